# Optimizing a Trainium2 kernel written in Bass

```python
import math
import jax
import jax.numpy as jnp
from jax import lax
import numpy as np

D_MODEL = 1024
BATCH = 2
SEQ = 16384
DEPTH = 2

N_EVEN = (DEPTH + 1) // 2
N_ODD = DEPTH // 2

SSD_HEADS = 16
SSD_HEAD_DIM = 64
SSD_INNER = SSD_HEADS * SSD_HEAD_DIM
SSD_GROUPS = 4
SSD_STATE = 128
SSD_CONV = 4
SSD_CONV_PAD = (2, 1)
SSD_CHUNK = 128
SSD_XBC = SSD_INNER + 2 * SSD_GROUPS * SSD_STATE
SC_WIDTH = 1024
SC_CONV = 3
MLA_HEADS = 8
MLA_Q_RANK = 256
MLA_KV_RANK = 128
MLA_NOPE = 64
MLA_ROPE = 32
MLA_V = 64
MLA_WIDTH = MLA_HEADS * MLA_V
ROPE_THETA = 10000.0
ATTN_BLOCK = 128
ATTN_SCALE = (MLA_NOPE + MLA_ROPE) ** -0.5
POOL_WINDOWS = (2, 4, 8, 16)
POOL_GROUP = 128
POOL_WIDTH = POOL_GROUP * len(POOL_WINDOWS)
EPS = 1e-5
ALPHA = (2 * DEPTH) ** 0.25
BETA = (8 * DEPTH) ** -0.25

EVEN_SIZES = (SSD_INNER, SSD_XBC, 2 * SSD_HEADS, SC_WIDTH, SC_WIDTH, SC_WIDTH, SC_WIDTH)
ODD_SIZES = (MLA_Q_RANK, MLA_KV_RANK, MLA_ROPE, MLA_WIDTH, POOL_WIDTH, POOL_WIDTH)
EVEN_PROJ = sum(EVEN_SIZES)
ODD_PROJ = sum(ODD_SIZES)
EVEN_OUT = SSD_INNER + SC_WIDTH
ODD_OUT = MLA_WIDTH + POOL_WIDTH

kernel_name = 'hybrid_ssd_shortconv_mla_pool_encoder'

F32 = jnp.float32


def _split(t, sizes):
    return jnp.split(t, [int(s) for s in np.cumsum(sizes)[:-1]], axis=-1)


def _layer_norm(x, g, b):
    xf = x.astype(F32)
    mu = jnp.mean(xf, -1, keepdims=True)
    var = jnp.mean(jnp.square(xf - mu), -1, keepdims=True)
    return ((xf - mu) * lax.rsqrt(var + EPS) * g + b).astype(x.dtype)


def _rms_norm(x, g):
    xf = x.astype(F32)
    return (xf * lax.rsqrt(jnp.mean(xf * xf, -1, keepdims=True) + EPS) * g).astype(x.dtype)


def _depthwise_conv(u, w, pad):
    return lax.conv_general_dilated(
        u, w[:, None, :].astype(u.dtype), window_strides=(1,), padding=(pad,),
        dimension_numbers=('NWC', 'WIO', 'NWC'), feature_group_count=u.shape[-1])


def _segsum_exp(cs):
    q = cs.shape[-1]
    mask = jnp.tril(jnp.ones((q, q), bool))
    return jnp.exp(jnp.where(mask, cs[..., :, None] - cs[..., None, :], -jnp.inf))


def _ssd_chunked(x, dt, a, bm, cm):
    b, S, H, P = x.shape
    G, N = bm.shape[2], bm.shape[3]
    R = H // G
    nc, Q = S // SSD_CHUNK, SSD_CHUNK
    xd = (x.astype(F32) * dt[..., None]).reshape(b, nc, Q, G, R, P)
    cs = jnp.cumsum(jnp.transpose((dt * a).reshape(b, nc, Q, G, R), (0, 1, 3, 4, 2)), axis=-1)
    bq = bm.astype(F32).reshape(b, nc, Q, G, N)
    cq = cm.astype(F32).reshape(b, nc, Q, G, N)
    cb = jnp.einsum('bclgn,bcsgn->bcgls', cq, bq)
    y_diag = jnp.einsum('bcgls,bcgrls,bcsgrp->bclgrp', cb, _segsum_exp(cs), xd)
    decay_states = jnp.exp(cs[..., -1:] - cs)
    states = jnp.einsum('bclgn,bcgrl,bclgrp->bcgrpn', bq, decay_states, xd)
    chunk_tot = cs[..., -1]

    def step(h, inp):
        st, tot = inp
        return h * jnp.exp(tot)[..., None, None] + st, h

    h0 = jnp.zeros((b, G, R, P, N), F32)
    _, h_in = lax.scan(step, h0, (jnp.moveaxis(states, 1, 0), jnp.moveaxis(chunk_tot, 1, 0)))
    h_in = jnp.moveaxis(h_in, 0, 1)
    y_off = jnp.einsum('bclgn,bcgrpn,bcgrl->bclgrp', cq, h_in, jnp.exp(cs))
    return (y_diag + y_off).reshape(b, S, H, P)


def _ssd_bidirectional(xs, bm, cm, dt_raw, a_log, dt_bias, d_skip):
    def one_dir(x_, b_, c_, dtr, k):
        dt = jax.nn.softplus(dtr.astype(F32) + dt_bias[k].astype(F32))
        a = -jnp.exp(a_log[k].astype(F32))
        return _ssd_chunked(x_, dt, a, b_, c_) + d_skip[k].astype(F32)[:, None] * x_.astype(F32)

    flip = lambda t: jnp.flip(t, axis=1)
    y_f = one_dir(xs, bm, cm, dt_raw[:, :, 0], 0)
    y_b = flip(one_dir(flip(xs), flip(bm), flip(cm), flip(dt_raw[:, :, 1]), 1))
    return y_f + y_b


def _even_layer(x, w_in, conv_w, conv_b, a_log, dt_bias, d_skip, norm_g, sc_conv_w, w_out):
    b, S, _ = x.shape
    proj = x @ w_in
    z, xbc, dt_raw, sc_bg, sc_cg, sc_h, sc_gate = _split(proj, EVEN_SIZES)
    xbc = jax.nn.silu(_depthwise_conv(xbc, conv_w, SSD_CONV_PAD) + conv_b)
    xs, bm, cm = _split(xbc, (SSD_INNER, SSD_GROUPS * SSD_STATE, SSD_GROUPS * SSD_STATE))
    y_a = _ssd_bidirectional(
        xs.reshape(b, S, SSD_HEADS, SSD_HEAD_DIM),
        bm.reshape(b, S, SSD_GROUPS, SSD_STATE), cm.reshape(b, S, SSD_GROUPS, SSD_STATE),
        dt_raw.reshape(b, S, 2, SSD_HEADS), a_log, dt_bias, d_skip)
    y_a = _rms_norm(y_a.reshape(b, S, SSD_INNER).astype(x.dtype) * jax.nn.silu(z), norm_g)
    y_b = sc_bg * _depthwise_conv(sc_cg * sc_h, sc_conv_w, (1, 1)) * jax.nn.silu(sc_gate)
    return jnp.concatenate([y_a, y_b], axis=-1) @ w_out


def _rope(t, cos, sin):
    half = t.shape[-1] // 2
    t1, t2 = t[..., :half].astype(F32), t[..., half:].astype(F32)
    return jnp.concatenate([t1 * cos - t2 * sin, t1 * sin + t2 * cos], axis=-1).astype(t.dtype)


def _mla_attention(q_nope, q_rope, k_nope, k_rope, v):
    b, S, H, _ = q_nope.shape
    nb = S // ATTN_BLOCK
    to_blocks = lambda t: jnp.moveaxis(t.reshape((b, nb, ATTN_BLOCK) + t.shape[2:]), 1, 0)

    def block(qs):
        qn, qr = qs
        s = (jnp.einsum('bqhd,bkhd->bhqk', qn, k_nope, preferred_element_type=F32)
             + jnp.einsum('bqhr,bkr->bhqk', qr, k_rope, preferred_element_type=F32)) * ATTN_SCALE
        p = jax.nn.softmax(s, axis=-1).astype(v.dtype)
        return jnp.einsum('bhqk,bkhd->bqhd', p, v)

    o = lax.map(block, (to_blocks(q_nope), to_blocks(q_rope)))
    return jnp.moveaxis(o, 0, 1).reshape(b, S, H * MLA_V)


def _multiscale_pool(u):
    S = u.shape[1]
    cs = jnp.pad(jnp.cumsum(u.astype(F32), axis=1), ((0, 0), (1, 0), (0, 0)))
    pos = jnp.arange(S)
    outs = []
    for gi, w in enumerate(POOL_WINDOWS):
        lo = jnp.clip(pos - w // 2, 0, S)
        hi = jnp.clip(pos + w - w // 2, 0, S)
        cg = cs[..., gi * POOL_GROUP:(gi + 1) * POOL_GROUP]
        cnt = (hi - lo).astype(F32)[None, :, None]
        outs.append((jnp.take(cg, hi, axis=1) - jnp.take(cg, lo, axis=1)) / cnt)
    return (jnp.concatenate(outs, axis=-1) - u.astype(F32)).astype(u.dtype)


def _odd_layer(x, positions, w_in, q_norm_g, w_uq, kv_norm_g, w_ukv, pool_w, pool_scale, w_out):
    b, S, _ = x.shape
    proj = x @ w_in
    cq, ckv, k_rope, gate_c, u_d, gate_d = _split(proj, ODD_SIZES)
    q = (_rms_norm(cq, q_norm_g) @ w_uq).reshape(b, S, MLA_HEADS, MLA_NOPE + MLA_ROPE)
    kv = (_rms_norm(ckv, kv_norm_g) @ w_ukv).reshape(b, S, MLA_HEADS, MLA_NOPE + MLA_V)
    q_nope, q_rope = q[..., :MLA_NOPE], q[..., MLA_NOPE:]
    k_nope, v = kv[..., :MLA_NOPE], kv[..., MLA_NOPE:]
    half = MLA_ROPE // 2
    inv_freq = ROPE_THETA ** (-jnp.arange(half, dtype=F32) / half)
    ang = positions.astype(F32)[..., None] * inv_freq
    cos, sin = jnp.cos(ang), jnp.sin(ang)
    q_rope = _rope(q_rope, cos[:, :, None, :], sin[:, :, None, :])
    k_rope = _rope(k_rope, cos, sin)
    y_c = _mla_attention(q_nope, q_rope, k_nope, k_rope, v) * jax.nn.silu(gate_c)
    pooled = _multiscale_pool(u_d).reshape(b, S, len(POOL_WINDOWS), POOL_GROUP)
    y_d = jnp.einsum('bsgc,gcd->bsgd', pooled, pool_w).reshape(b, S, POOL_WIDTH)
    y_d = y_d * pool_scale * jax.nn.silu(gate_d)
    return jnp.concatenate([y_c, y_d], axis=-1) @ w_out


def setup_inputs(seed: int = 0) -> dict:
    key = jax.random.key(seed)
    ks = iter(jax.random.split(key, 32))
    nrm = lambda shape, scale: jax.random.normal(next(ks), shape, F32) * scale
    NE, NO = N_EVEN, N_ODD
    x = jax.random.normal(next(ks), (BATCH, SEQ, D_MODEL), F32)
    positions = (jnp.arange(SEQ, dtype=jnp.int32)[None, :]
                 + jax.random.randint(next(ks), (BATCH, 1), 0, 4096, dtype=jnp.int32))
    dt0 = jnp.exp(jax.random.uniform(next(ks), (NE, 2, SSD_HEADS), F32, math.log(1e-3), math.log(1e-1)))
    ev_dt_bias = dt0 + jnp.log(-jnp.expm1(-dt0))
    ev_a_log = jnp.log(jax.random.uniform(next(ks), (NE, 2, SSD_HEADS), F32, 1.0, 16.0))
    return {
        'x': x,
        'positions': positions,
        'ev_w_in': nrm((NE, D_MODEL, EVEN_PROJ), D_MODEL ** -0.5),
        'ev_conv_w': nrm((NE, SSD_CONV, SSD_XBC), SSD_CONV ** -0.5),
        'ev_conv_b': nrm((NE, SSD_XBC), 0.02),
        'ev_a_log': ev_a_log,
        'ev_dt_bias': ev_dt_bias,
        'ev_d_skip': 1.0 + nrm((NE, 2, SSD_HEADS), 0.05),
        'ev_norm_g': 1.0 + nrm((NE, SSD_INNER), 0.02),
        'ev_sc_conv_w': nrm((NE, SC_CONV, SC_WIDTH), SC_CONV ** -0.5),
        'ev_w_out': nrm((NE, EVEN_OUT, D_MODEL), BETA * EVEN_OUT ** -0.5),
        'ev_ln_g': 1.0 + nrm((NE, D_MODEL), 0.02),
        'ev_ln_b': nrm((NE, D_MODEL), 0.02),
        'od_w_in': nrm((NO, D_MODEL, ODD_PROJ), D_MODEL ** -0.5),
        'od_q_norm_g': 1.0 + nrm((NO, MLA_Q_RANK), 0.02),
        'od_w_uq': nrm((NO, MLA_Q_RANK, MLA_HEADS * (MLA_NOPE + MLA_ROPE)), MLA_Q_RANK ** -0.5),
        'od_kv_norm_g': 1.0 + nrm((NO, MLA_KV_RANK), 0.02),
        'od_w_ukv': nrm((NO, MLA_KV_RANK, MLA_HEADS * (MLA_NOPE + MLA_V)), MLA_KV_RANK ** -0.5),
        'od_pool_w': nrm((NO, len(POOL_WINDOWS), POOL_GROUP, POOL_GROUP), POOL_GROUP ** -0.5),
        'od_pool_scale': 1.0 + nrm((NO, POOL_WIDTH), 0.02),
        'od_w_out': nrm((NO, ODD_OUT, D_MODEL), BETA * ODD_OUT ** -0.5),
        'od_ln_g': 1.0 + nrm((NO, D_MODEL), 0.02),
        'od_ln_b': nrm((NO, D_MODEL), 0.02),
    }


def reference(x, positions, ev_w_in, ev_conv_w, ev_conv_b, ev_a_log, ev_dt_bias, ev_d_skip,
              ev_norm_g, ev_sc_conv_w, ev_w_out, ev_ln_g, ev_ln_b, od_w_in, od_q_norm_g, od_w_uq,
              od_kv_norm_g, od_w_ukv, od_pool_w, od_pool_scale, od_w_out, od_ln_g, od_ln_b):
    for layer in range(DEPTH):
        i = layer // 2
        if layer % 2 == 0:
            h = _even_layer(x, ev_w_in[i], ev_conv_w[i], ev_conv_b[i], ev_a_log[i], ev_dt_bias[i],
                            ev_d_skip[i], ev_norm_g[i], ev_sc_conv_w[i], ev_w_out[i])
            x = _layer_norm(ALPHA * x + h, ev_ln_g[i], ev_ln_b[i])
        else:
            h = _odd_layer(x, positions, od_w_in[i], od_q_norm_g[i], od_w_uq[i], od_kv_norm_g[i],
                           od_w_ukv[i], od_pool_w[i], od_pool_scale[i], od_w_out[i])
            x = _layer_norm(ALPHA * x + h, od_ln_g[i], od_ln_b[i])
    return x
```

```python
import numpy as np
import concourse.bass as bass
import concourse.mybir as mybir
from concourse.bass_utils import run_bass_kernel_spmd
from contextlib import ExitStack

F32 = mybir.dt.float32
BF16 = mybir.dt.bfloat16
I32 = mybir.dt.int32
AF = mybir.ActivationFunctionType
ALU = mybir.AluOpType
AX = mybir.AxisListType

SAME_ENGINE_SYNC = True

D = 1024
SEQ = 16384
NCORES = 8
ALPHA = 4 ** 0.25
EPS = 1e-5


class Tok:
    __slots__ = ("w", "r", "name")

    def __init__(self, name=""):
        self.w = None
        self.r = {}
        self.name = name


class Prog:
    def __init__(self, nc, es):
        self.nc = nc
        self.es = es
        self.es_top = es
        self.eng = {"pe": nc.tensor, "act": nc.scalar, "dve": nc.vector,
                    "pool": nc.gpsimd, "sp": nc.sync}
        self.sems = {}
        self.cnt = {}
        for k in self.eng:
            self.sems[k] = es.enter_context(nc.semaphore("s_" + k))
            self.cnt[k] = 0
        self.seen = {k: {} for k in self.eng}
        self.ndma = 0
        self.out_dma = []
        self.n_ops = 0
        self.uid = 0

    prefix = ""

    def sb(self, name, shape, dt):
        return self.es.enter_context(self.nc.sbuf_tensor(self.prefix + name, list(shape), dt))

    def ps(self, name, shape, dt=F32):
        return self.es.enter_context(self.nc.psum_tensor(self.prefix + name, list(shape), dt))

    def dma_sem(self):
        k = "d%d" % self.ndma
        self.ndma += 1
        self.sems[k] = self.es_top.enter_context(self.nc.semaphore("s_" + k))
        self.cnt[k] = 0
        return k

    def _wait(self, e, deps):
        for (k, v) in deps:
            if k == e:
                if not SAME_ENGINE_SYNC or e == "pe" or e == "sp":
                    continue
            if self.seen[e].get(k, 0) >= v:
                continue
            self.eng[e].wait_ge(self.sems[k], v)
            self.seen[e][k] = v

    def _deps(self, r, w):
        m = {}
        for t in r:
            if t.w is not None:
                k, v = t.w
                if m.get(k, 0) < v:
                    m[k] = v
        for t in w:
            if t.w is not None:
                k, v = t.w
                if m.get(k, 0) < v:
                    m[k] = v
            for k, v in t.r.items():
                if m.get(k, 0) < v:
                    m[k] = v
        return list(m.items())

    def op(self, e, fn, r=(), w=()):
        self._wait(e, self._deps(r, w))
        ins = fn(self.eng[e])
        self.cnt[e] += 1
        v = self.cnt[e]
        ins.then_inc(self.sems[e], 1)
        for t in r:
            if t.r.get(e, 0) < v:
                t.r[e] = v
        for t in w:
            t.w = (e, v)
            t.r = {}
        self.n_ops += 1
        return ins

    def dma(self, q, out, in_, r=(), w=(), sem=None, is_out=False, nowait=False, **kw):
        if not nowait:
            self._wait(q, self._deps(r, w))
        ins = self.eng[q].dma_start(out=out, in_=in_, **kw)
        self.cnt[sem] += 16
        v = self.cnt[sem]
        ins.then_inc(self.sems[sem], 16)
        for t in r:
            if t.r.get(sem, 0) < v:
                t.r[sem] = v
        for t in w:
            t.w = (sem, v)
            t.r = {}
        if is_out:
            self.out_dma.append((sem, v))
        return ins

    def finish(self, e="sp"):
        m = {}
        for k, v in self.out_dma:
            if m.get(k, 0) < v:
                m[k] = v
        for k, v in m.items():
            self.eng[e].wait_ge(self.sems[k], v)


class Ring:
    def __init__(self, p, name, shape, dt, n, space="sb", dma=False):
        self.bufs = []
        for i in range(n):
            t = p.sb("%s%d" % (name, i), shape, dt) if space == "sb" else p.ps("%s%d" % (name, i), shape, dt)
            self.bufs.append((t, Tok(name + str(i)), p.dma_sem() if dma else None))
        self.i = 0

    def next(self):
        b = self.bufs[self.i % len(self.bufs)]
        self.i += 1
        return b


def load_cast_weight(p, src, dst, stage, K, C, engines=("pool", "act"), cw=1024, tok=None):
    n = 0
    for k in range(K):
        for c0 in range(0, C, cw):
            c1 = min(C, c0 + cw)
            st, stok, ssem = stage.next()
            p.dma("sp", st[:, 0:c1 - c0], src[k * 128:(k + 1) * 128, c0:c1], w=[stok], sem=ssem)
            e = engines[n % len(engines)]
            n += 1
            if e == "act":
                p.op(e, lambda en: en.copy(out=dst[:, k, c0:c1], in_=st[:, 0:c1 - c0]), r=[stok], w=[tok])
            else:
                p.op(e, lambda en: en.tensor_copy(out=dst[:, k, c0:c1], in_=st[:, 0:c1 - c0]), r=[stok], w=[tok])


PI = float(np.pi)
TWO_PI = float(2 * np.pi)
C1 = 6.28125
C2 = float(2 * np.pi - 6.28125)
ATT_SCALE = float(96 ** -0.5)
NEG = -30000.0
L0B_TB = 256


def barrier(p):
    for e in p.eng:
        for k, v in p.cnt.items():
            if k != e and v > 0 and p.seen[e].get(k, 0) < v:
                p.eng[e].wait_ge(p.sems[k], v)
                p.seen[e][k] = v


def layer_norm_tail(p, o, t_o, xk, t_xk, rr, r2, junk, st_r, lng_s, lnb_s, t_c, out_ap, post=None, is_out=True):
    r, t_r, _ = rr.next()
    p.op("dve", lambda e: e.scalar_tensor_tensor(out=r[:], in0=xk[:], scalar=float(ALPHA), in1=o[:], op0=ALU.mult, op1=ALU.add),
         r=[t_xk, t_o], w=[t_r])
    st, t_st, _ = st_r.next()
    jk, t_jk, _ = junk.next()
    p.op("act", lambda e: e.activation(out=jk[:], in_=r[:], func=AF.Identity, accum_out=st[:, 0:1]), r=[t_r], w=[t_jk, t_st])
    p.op("act", lambda e: e.activation(out=jk[:], in_=r[:], func=AF.Square, accum_out=st[:, 1:2]), r=[t_r], w=[t_jk, t_st])
    p.op("dve", lambda e: e.tensor_scalar(out=st[:, 2:3], in0=st[:, 0:1], scalar1=1.0 / D, scalar2=None, op0=ALU.mult), r=[t_st], w=[t_st])
    p.op("dve", lambda e: e.tensor_tensor(out=st[:, 3:4], in0=st[:, 2:3], in1=st[:, 2:3], op=ALU.mult), r=[t_st], w=[t_st])
    p.op("dve", lambda e: e.scalar_tensor_tensor(out=st[:, 4:5], in0=st[:, 1:2], scalar=1.0 / D, in1=st[:, 3:4], op0=ALU.mult, op1=ALU.subtract),
         r=[t_st], w=[t_st])
    p.op("dve", lambda e: e.tensor_scalar(out=st[:, 4:5], in0=st[:, 4:5], scalar1=float(EPS), scalar2=None, op0=ALU.add), r=[t_st], w=[t_st])
    p.op("act", lambda e: e.activation(out=st[:, 5:6], in_=st[:, 4:5], func=AF.Ln), r=[t_st], w=[t_st])
    p.op("act", lambda e: e.activation(out=st[:, 6:7], in_=st[:, 5:6], func=AF.Exp, scale=-0.5), r=[t_st], w=[t_st])
    q, t_q, osem = r2.next()
    p.op("dve", lambda e: e.tensor_scalar(out=q[:], in0=r[:], scalar1=st[:, 2:3], scalar2=st[:, 6:7], op0=ALU.subtract, op1=ALU.mult),
         r=[t_r, t_st], w=[t_q])
    p.op("pool", lambda e: e.tensor_tensor(out=q[:], in0=q[:], in1=lng_s[:], op=ALU.mult), r=[t_c], w=[t_q])
    p.op("pool", lambda e: e.tensor_tensor(out=q[:], in0=q[:], in1=lnb_s[:], op=ALU.add), r=[t_c], w=[t_q])
    if post is not None:
        post(q, t_q)
    p.dma("sp", out_ap, q[:], r=[t_q], w=[], sem=osem, is_out=is_out)


def emit_l0b(nc, p, T, A):
    TB = L0B_TB
    NB = T // TB
    xT, xtok, yaT, w1, wout = A["xT1"], A["xtok"], A["yaT"], A["w1"], A["wout"]
    normg, scw, lng, lnb = A["normg"], A["scw"], A["lng"], A["lnb"]
    out, x1T, cst = A["x1"], A["x1T"], A["cst"]
    x1HL, x1HR = A["x1HL"], A["x1HR"]

    def halo_v(tab, bnd):
        return tab[bnd * 1024:(bnd + 1) * 1024, :].rearrange("(k p) t -> p k t", p=128)
    yaT_v = yaT.rearrange("(k p) t -> p k t", p=128)
    NB5 = T // 512

    def x1T_blk(blk, c0, n):
        return x1T[blk * 1024:(blk + 1) * 1024, c0:c0 + n].rearrange("(k p) t -> p k t", p=128)

    with ExitStack() as es:
        p.es = es
        w1b = p.sb("w1b", [128, 8, 5120], BF16)
        woutb = p.sb("woutb", [128, 16, D], BF16)
        t_w1b, t_woutb = Tok(), Tok()
        stage = Ring(p, "wst", [128, 1024], F32, 1, dma=True)
        normg_s = p.sb("normg_s", [128, 8], F32)
        scw_s = p.sb("scw_s", [128, 24], F32)
        lng_s = p.sb("lng_s", [128, D], F32)
        lnb_s = p.sb("lnb_s", [128, D], F32)
        ones_f = p.sb("ones_f", [128, 128], F32)
        t_c = Tok()
        dc = p.dma_sem()
        p.dma("sp", normg_s[:], normg[:, :], w=[t_c], sem=dc)
        p.dma("sp", scw_s[:], scw[:, :], w=[t_c], sem=dc)
        p.dma("sp", lng_s[:], lng[:, :], w=[t_c], sem=dc)
        p.dma("sp", lnb_s[:], lnb[:, :], w=[t_c], sem=dc)
        t_ones = Tok()
        p.op("dve", lambda e: e.memset(ones_f[:], 1.0), w=[t_ones])
        idf = p.sb("idf", [128, 128], F32)
        p.dma("sp", idf[:], cst[:, 256:384], w=[t_c], sem=dc)
        zt = p.sb("zt", [128, 8, 8], F32)
        t_zt = Tok()
        p.op("dve", lambda e: e.memset(zt[:], 0.0), w=[t_zt])
        zsem = p.dma_sem()
        p.dma("sp", halo_v(x1HL, 0), zt[:], r=[t_zt], sem=zsem)
        p.dma("sp", halo_v(x1HR, NB5), zt[:], r=[t_zt], sem=zsem)
        xtt_r = Ring(p, "xtt", [128, 8, 128], F32, 1, dma=True)
        load_cast_weight(p, w1, w1b, stage, 8, 5120, tok=t_w1b)
        load_cast_weight(p, wout, woutb, stage, 16, D, tok=t_woutb)

        xst = Ring(p, "xst", [128, TB + 2], F32, 3, dma=True)
        xb_r = Ring(p, "xb", [128, 8, TB + 2], BF16, 2)
        yst = Ring(p, "yst", [128, 8, TB], F32, 2, dma=True)
        xtk = Ring(p, "xtk", [128, D], F32, 2, dma=True)
        pp = Ring(p, "pp", [128, 512], F32, 4, space="ps")
        pss = Ring(p, "pss", [128, 512], F32, 1, space="ps")
        ph = Ring(p, "ph", [128, 512], F32, 1, space="ps")
        po = Ring(p, "po", [128, D], F32, 1, space="ps")
        g_all = p.sb("g_all", [128, 8, TB], F32)
        t_g = [Tok() for _ in range(8)]
        cat = p.sb("cat", [128, 16, TB], BF16)
        t_cat = [Tok() for _ in range(16)]
        tmp = Ring(p, "tmp", [128, TB + 2], F32, 8)
        rstd = p.sb("rstd", [128, TB], F32)
        t_rstd = Tok()
        halo = Ring(p, "halo", [128, 4], F32, 2)
        rr = Ring(p, "rr", [128, D], F32, 1)
        r2 = Ring(p, "r2", [128, D], F32, 2, dma=True)
        junk = Ring(p, "junk", [128, D], BF16, 1)
        st_r = Ring(p, "stat", [128, 8], F32, 4)
        for bi in range(NB):
            t0 = bi * TB
            xb, t_xb, _ = xb_r.next()
            for k in range(8):
                st, stok, ssem = xst.next()
                p.dma("sp", st[:], xT[k * 128:(k + 1) * 128, t0:t0 + TB + 2], w=[stok], sem=ssem)
                p.op("pool", lambda e: e.tensor_copy(out=xb[:, k, :], in_=st[:]), r=[stok], w=[t_xb])
            ya, t_ya, ya_sem = yst.next()
            p.dma("sp", ya[:], yaT_v[:, :, t0:t0 + TB], w=[t_ya], sem=ya_sem)

            ss, t_ss, _ = pss.next()
            for j in range(8):
                z, t_z, _ = pp.next()
                for k in range(8):
                    p.op("pe", lambda e: e.matmul(z[:, 0:TB], lhsT=w1b[:, k, j * 128:(j + 1) * 128],
                                                  rhs=xb[:, k, 1:TB + 1], start=(k == 0), stop=(k == 7)),
                         r=[t_w1b, t_xb], w=[t_z])
                sz, t_sz, _ = tmp.next()
                p.op("act", lambda e: e.activation(out=sz[:, 0:TB], in_=z[:, 0:TB], func=AF.Silu), r=[t_z], w=[t_sz])
                p.op("dve", lambda e: e.tensor_tensor(out=g_all[:, j, :], in0=sz[:, 0:TB], in1=ya[:, j, :], op=ALU.mult),
                     r=[t_sz, t_ya], w=[t_g[j]])
                sq, t_sq, _ = tmp.next()
                p.op("act", lambda e: e.activation(out=sq[:, 0:TB], in_=g_all[:, j, :], func=AF.Square), r=[t_g[j]], w=[t_sq])
                p.op("pe", lambda e: e.matmul(ss[:, 0:TB], lhsT=ones_f[:], rhs=sq[:, 0:TB], start=(j == 0), stop=(j == 7)),
                     r=[t_ones, t_sq], w=[t_ss])
            lnv, t_lnv, _ = tmp.next()
            p.op("dve", lambda e: e.tensor_scalar(out=lnv[:, 0:TB], in0=ss[:, 0:TB], scalar1=1.0 / 1024, scalar2=EPS,
                                                  op0=ALU.mult, op1=ALU.add), r=[t_ss], w=[t_lnv])
            p.op("act", lambda e: e.activation(out=lnv[:, 0:TB], in_=lnv[:, 0:TB], func=AF.Ln), r=[t_lnv], w=[t_lnv])
            p.op("act", lambda e: e.activation(out=rstd[:], in_=lnv[:, 0:TB], func=AF.Exp, scale=-0.5), r=[t_lnv], w=[t_rstd])
            for j in range(8):
                p.op("dve", lambda e: e.scalar_tensor_tensor(out=cat[:, j, :], in0=g_all[:, j, :], scalar=normg_s[:, j:j + 1],
                                                             in1=rstd[:], op0=ALU.mult, op1=ALU.mult),
                     r=[t_g[j], t_rstd, t_c], w=[t_cat[j]])

            for j in range(8):
                def proj(grp, lo, n, dst, t_dst, first=True, last=True):
                    for k in range(8):
                        p.op("pe", lambda e: e.matmul(dst, lhsT=w1b[:, k, grp * 1024 + j * 128:grp * 1024 + (j + 1) * 128],
                                                      rhs=xb[:, k, lo:lo + n], start=(k == 0), stop=(k == 7)),
                             r=[t_w1b, t_xb], w=[t_dst])
                cg, t_cg, _ = pp.next()
                proj(2, 1, TB, cg[:, 0:TB], t_cg)
                hh, t_hh, _ = pp.next()
                proj(3, 1, TB, hh[:, 0:TB], t_hh)
                hl, t_hl, _ = ph.next()
                for k in range(8):
                    p.op("pe", lambda e: e.matmul(hl[:, 0:2], lhsT=w1b[:, k, 2048 + j * 128:2048 + (j + 1) * 128],
                                                  rhs=xb[:, k, 0:TB + 2:TB + 1], start=(k == 0), stop=(k == 7)),
                         r=[t_w1b, t_xb], w=[t_hl])
                for k in range(8):
                    p.op("pe", lambda e: e.matmul(hl[:, 2:4], lhsT=w1b[:, k, 3072 + j * 128:3072 + (j + 1) * 128],
                                                  rhs=xb[:, k, 0:TB + 2:TB + 1], start=(k == 0), stop=(k == 7)),
                         r=[t_w1b, t_xb], w=[t_hl])
                cgs, t_cgs, _ = tmp.next()
                p.op("act", lambda e: e.copy(out=cgs[:, 0:TB], in_=cg[:, 0:TB]), r=[t_cg], w=[t_cgs])
                hs, t_hs, _ = halo.next()
                p.op("act", lambda e: e.copy(out=hs[:], in_=hl[:, 0:4]), r=[t_hl], w=[t_hs])
                u, t_u, _ = tmp.next()
                p.op("dve", lambda e: e.tensor_tensor(out=u[:, 1:TB + 1], in0=cgs[:, 0:TB], in1=hh[:, 0:TB], op=ALU.mult),
                     r=[t_cgs, t_hh], w=[t_u])
                p.op("dve", lambda e: e.tensor_tensor(out=u[:, 0:TB + 2:TB + 1], in0=hs[:, 0:2], in1=hs[:, 2:4], op=ALU.mult),
                     r=[t_hs], w=[t_u])
                c, t_cc, _ = tmp.next()
                p.op("dve", lambda e: e.tensor_scalar(out=c[:, 0:TB], in0=u[:, 0:TB], scalar1=scw_s[:, j * 3:j * 3 + 1], scalar2=None,
                                                      op0=ALU.mult), r=[t_u, t_c], w=[t_cc])
                p.op("dve", lambda e: e.scalar_tensor_tensor(out=c[:, 0:TB], in0=u[:, 1:TB + 1], scalar=scw_s[:, j * 3 + 1:j * 3 + 2],
                                                             in1=c[:, 0:TB], op0=ALU.mult, op1=ALU.add), r=[t_u, t_c], w=[t_cc])
                p.op("dve", lambda e: e.scalar_tensor_tensor(out=c[:, 0:TB], in0=u[:, 2:TB + 2], scalar=scw_s[:, j * 3 + 2:j * 3 + 3],
                                                             in1=c[:, 0:TB], op0=ALU.mult, op1=ALU.add), r=[t_u, t_c], w=[t_cc])
                bg, t_bg, _ = pp.next()
                proj(1, 1, TB, bg[:, 0:TB], t_bg)
                gt, t_gt, _ = pp.next()
                proj(4, 1, TB, gt[:, 0:TB], t_gt)
                sg, t_sg, _ = tmp.next()
                p.op("act", lambda e: e.activation(out=sg[:, 0:TB], in_=gt[:, 0:TB], func=AF.Silu), r=[t_gt], w=[t_sg])
                p.op("dve", lambda e: e.tensor_tensor(out=c[:, 0:TB], in0=c[:, 0:TB], in1=bg[:, 0:TB], op=ALU.mult),
                     r=[t_bg], w=[t_cc])
                p.op("dve", lambda e: e.tensor_tensor(out=cat[:, 8 + j, :], in0=c[:, 0:TB], in1=sg[:, 0:TB], op=ALU.mult),
                     r=[t_cc, t_sg], w=[t_cat[8 + j]])

            for tt in range(TB // 128):
                xk, t_xk, xk_sem = xtk.next()
                p.dma("sp", xk[:], xtok[t0 + tt * 128:t0 + (tt + 1) * 128, :], w=[t_xk], sem=xk_sem)
                o, t_o, _ = po.next()
                for half in range(2):
                    for kc in range(16):
                        p.op("pe", lambda e: e.matmul(o[:, half * 512:(half + 1) * 512], lhsT=cat[:, kc, tt * 128:(tt + 1) * 128],
                                                      rhs=woutb[:, kc, half * 512:(half + 1) * 512], start=(kc == 0), stop=(kc == 15)),
                             r=[t_cat[kc], t_woutb], w=[t_o])
                tok0 = t0 + tt * 128

                def post(q, t_q):
                    xtt, t_xtt, xtt_sem = xtt_r.next()
                    for hf in range(2):
                        tp, t_tp, _ = pp.next()
                        for kq in range(4):
                            kk = hf * 4 + kq
                            p.op("pe", lambda e: e.transpose(tp[:, kq * 128:(kq + 1) * 128], q[:, kk * 128:(kk + 1) * 128], idf[:]),
                                 r=[t_q, t_c], w=[t_tp])
                        p.op("act", lambda e: e.copy(out=xtt[:, hf * 4:(hf + 1) * 4, :], in_=tp[:].rearrange("p (k t) -> p k t", k=4)),
                             r=[t_tp], w=[t_xtt])
                    p.dma("sp", x1T_blk(tok0 // 512 + 1, tok0 % 512, 128), xtt[:], r=[t_xtt], sem=xtt_sem)
                    if tok0 % 512 == 0:
                        p.dma("sp", halo_v(x1HR, tok0 // 512), xtt[:, :, 0:8], r=[t_xtt], sem=xtt_sem)
                    if (tok0 + 128) % 512 == 0:
                        p.dma("sp", halo_v(x1HL, (tok0 + 128) // 512), xtt[:, :, 120:128], r=[t_xtt], sem=xtt_sem)
                layer_norm_tail(p, o, t_o, xk, t_xk, rr, r2, junk, st_r, lng_s, lnb_s, t_c,
                                out[t0 + tt * 128:t0 + (tt + 1) * 128, :], post=post, is_out=False)
        barrier(p)


def emit_l0a(nc, p, S, A):
    BLK = 512
    NBLK = S // BLK
    xT, wg_all, cvw_all, cvb_all, hp_all, cst, msk, yaT = (A["xT"], A["wg"], A["cvw"], A["cvb"], A["hp"], A["cst"], A["msk"], A["yaT"])

    with ExitStack() as es:
        p.es = es
        wgb = p.sb("wgb", [128, 8, 520], BF16)
        t_wgb = Tok()
        stage = Ring(p, "wst", [128, 520], F32, 2, dma=True)
        cvw_s = p.sb("cvw_s", [128, 16], F32)
        cvb_s = p.sb("cvb_s", [128, 4], F32)
        hp_s = p.sb("hp_s", [128, 24], F32)
        cst_s = p.sb("cst_s", [128, 512], F32)
        msk_s = p.sb("msk_s", [128, 1024], F32)
        mskb = p.sb("mskb", [128, 1024], BF16)
        identb = p.sb("identb", [128, 128], BF16)
        a_s = p.sb("a_s", [128, 8], F32)
        bias32 = p.sb("bias32", [128, 2, 4, 4], F32)
        a32 = p.sb("a32", [128, 2, 4, 4], F32)
        dsum = p.sb("dsum", [128, 4], F32)
        t_c = Tok()
        dc = p.dma_sem()
        for dst, src in ((cst_s, cst), (msk_s, msk)):
            p.dma("sp", dst[:], src[:, :], w=[t_c], sem=dc)
        U = cst_s[:, 0:128]
        UT = cst_s[:, 128:256]
        IDF = cst_s[:, 256:384]
        ONES = cst_s[:, 384:512]
        p.op("dve", lambda e: e.tensor_copy(out=mskb[:], in_=msk_s[:]), r=[t_c], w=[t_c])
        p.op("dve", lambda e: e.tensor_copy(out=identb[:], in_=IDF), r=[t_c], w=[t_c])

        xst = Ring(p, "xst", [128, BLK + 3], F32, 3, dma=True)
        xb_r = Ring(p, "xb", [128, 8, BLK + 3], BF16, 2)
        pP = Ring(p, "pP", [128, 512], F32, 2, space="ps")
        pH = Ring(p, "pH", [128, 512], F32, 1, space="ps")
        pT = Ring(p, "pT", [128, 512], F32, 1, space="ps")
        pS = Ring(p, "pS", [128, 512], F32, 1, space="ps")
        pE = Ring(p, "pE", [128, 512], F32, 1, space="ps")
        pC = Ring(p, "pC", [128, 512], F32, 1, space="ps")
        pY = Ring(p, "pY", [128, 512], F32, 1, space="ps")
        pre_r = Ring(p, "pre", [128, BLK + 3], F32, 3)
        cv_r = Ring(p, "cv", [128, BLK], F32, 2)
        xsf_r = Ring(p, "xsf", [128, 3, BLK], F32, 2)
        btb_r = Ring(p, "btb", [128, BLK], BF16, 2)
        ctb_r = Ring(p, "ctb", [128, BLK], BF16, 2)
        hs_r = Ring(p, "hs", [128, 16], F32, 2)
        dtv_r = Ring(p, "dtv", [128, 6, 16], F32, 2)
        sm_r = Ring(p, "sm", [128, 8, 4], F32, 3)
        W_r = Ring(p, "W", [128, 4, 128], F32, 2)
        E_r = Ring(p, "E", [128, 4, 128], F32, 2)
        M_r = Ring(p, "M", [128, 4, 128], BF16, 2)
        btk_r = Ring(p, "btk", [128, 128], BF16, 2)
        xd_r = Ring(p, "xd", [128, 256], BF16, 2)
        xdw_r = Ring(p, "xdw", [128, 256], BF16, 2)
        y_r = Ring(p, "y", [128, 256], F32, 3, dma=True)
        yT_r = Ring(p, "yT", [128, 256], F32, 3, dma=True)
        yt_r = Ring(p, "yt", [128, 256], F32, 3)
        yl_r = Ring(p, "yl", [128, 256], F32, 2, dma=True)
        H = p.sb("H", [128, 256], F32)
        Hb = p.sb("Hb", [128, 256], BF16)
        t_H, t_Hb = Tok(), Tok()

        for g in range(4):
            wg = wg_all[g]
            for dst, src in ((cvw_s, cvw_all[g]), (cvb_s, cvb_all[g]), (hp_s, hp_all[g])):
                p.dma("sp", dst[:], src, w=[t_c], sem=dc)
            t_ya = [Tok() for _ in range(S // 128)]
            p.op("act", lambda e: e.activation(out=a_s[:], in_=hp_s[:, 0:8], func=AF.Exp), r=[t_c], w=[t_c])
            p.op("dve", lambda e: e.tensor_scalar(out=a_s[:], in0=a_s[:], scalar1=-1.0, scalar2=None, op0=ALU.mult), r=[t_c], w=[t_c])
            for k in range(2):
                for c in range(4):
                    p.op("dve", lambda e: e.tensor_copy(out=bias32[:, k, c, :], in_=hp_s[:, 8 + 4 * k:12 + 4 * k]), r=[t_c], w=[t_c])
                    p.op("dve", lambda e: e.tensor_copy(out=a32[:, k, c, :], in_=a_s[:, 4 * k:4 * k + 4]), r=[t_c], w=[t_c])
            p.op("dve", lambda e: e.tensor_tensor(out=dsum[:], in0=hp_s[:, 16:20], in1=hp_s[:, 20:24], op=ALU.add), r=[t_c], w=[t_c])
            load_cast_weight(p, wg, wgb, stage, 8, 520, cw=520, tok=t_wgb)
            for k in range(2):
                p.op("dve", lambda e: e.memset(H[:], 0.0), w=[t_H])
                p.op("pool", lambda e: e.memset(Hb[:], 0.0), w=[t_Hb])
                Tri = U if k == 0 else UT
                blocks = range(NBLK) if k == 0 else range(NBLK - 1, -1, -1)
                for blk in blocks:
                    e0 = blk * BLK
                    xb, t_xb, _ = xb_r.next()
                    for kk in range(8):
                        st, stok, ssem = xst.next()
                        p.dma("sp", st[:], xT[kk * 128:(kk + 1) * 128, e0:e0 + BLK + 3], w=[stok], sem=ssem)
                        p.op("pool", lambda e: e.tensor_copy(out=xb[:, kk, :], in_=st[:]), r=[stok], w=[t_xb])
                    hb, t_hb, _ = pH.next()
                    for c in range(4):
                        for kk in range(8):
                            p.op("pe", lambda e: e.matmul(hb[:, 16 + 4 * c:20 + 4 * c], lhsT=xb[:, kk, 2 + c * 128:2 + (c + 1) * 128],
                                                          rhs=wgb[:, kk, 512 + 4 * k:516 + 4 * k], start=(kk == 0), stop=(kk == 7)),
                                 r=[t_xb, t_wgb], w=[t_hb])
                    dtv, t_dtv, _ = dtv_r.next()
                    V, AV, EE, LL, DT, DA = [dtv[:, i, :] for i in range(6)]
                    b32 = bias32[:, k, :, :].rearrange("p c r -> p (c r)")
                    A32 = a32[:, k, :, :].rearrange("p c r -> p (c r)")
                    p.op("dve", lambda e: e.tensor_tensor(out=V, in0=hb[:, 16:32], in1=b32, op=ALU.add), r=[t_hb, t_c], w=[t_dtv])
                    p.op("dve", lambda e: e.tensor_scalar(out=AV, in0=V, scalar1=-1.0, scalar2=None, op0=ALU.mult), r=[t_dtv], w=[t_dtv])
                    p.op("dve", lambda e: e.tensor_tensor(out=AV, in0=AV, in1=V, op=ALU.max), r=[t_dtv], w=[t_dtv])
                    p.op("act", lambda e: e.activation(out=EE, in_=AV, func=AF.Exp, scale=-1.0), r=[t_dtv], w=[t_dtv])
                    p.op("act", lambda e: e.activation(out=LL, in_=EE, func=AF.Ln, bias=1.0), r=[t_dtv], w=[t_dtv])
                    p.op("dve", lambda e: e.scalar_tensor_tensor(out=DT, in0=V, scalar=0.0, in1=LL, op0=ALU.max, op1=ALU.add), r=[t_dtv], w=[t_dtv])
                    p.op("dve", lambda e: e.tensor_tensor(out=DA, in0=DT, in1=A32, op=ALU.mult), r=[t_dtv, t_c], w=[t_dtv])

                    xsf, t_xsf, _ = xsf_r.next()
                    btb, t_btb, _ = btb_r.next()
                    ctb, t_ctb, _ = ctb_r.next()
                    for m in range(4):
                        P, t_P, _ = pP.next()
                        for kk in range(8):
                            p.op("pe", lambda e: e.matmul(P[:, 0:BLK], lhsT=wgb[:, kk, m * 128:(m + 1) * 128], rhs=xb[:, kk, 2:BLK + 2],
                                                          start=(kk == 0), stop=(kk == 7)), r=[t_xb, t_wgb], w=[t_P])
                        for kk in range(8):
                            p.op("pe", lambda e: e.matmul(hb[:, 4 * m:4 * m + 2], lhsT=wgb[:, kk, m * 128:(m + 1) * 128], rhs=xb[:, kk, 0:2],
                                                          start=(kk == 0), stop=(kk == 7)), r=[t_xb, t_wgb], w=[t_hb])
                        for kk in range(8):
                            p.op("pe", lambda e: e.matmul(hb[:, 4 * m + 2:4 * m + 3], lhsT=wgb[:, kk, m * 128:(m + 1) * 128],
                                                          rhs=xb[:, kk, BLK + 2:BLK + 3], start=(kk == 0), stop=(kk == 7)),
                                 r=[t_xb, t_wgb], w=[t_hb])
                        pre, t_pre, _ = pre_r.next()
                        p.op("act", lambda e: e.copy(out=pre[:, 2:BLK + 2], in_=P[:, 0:BLK]), r=[t_P], w=[t_pre])
                        p.op("act", lambda e: e.copy(out=pre[:, 0:2], in_=hb[:, 4 * m:4 * m + 2]), r=[t_hb], w=[t_pre])
                        p.op("act", lambda e: e.copy(out=pre[:, BLK + 2:BLK + 3], in_=hb[:, 4 * m + 2:4 * m + 3]), r=[t_hb], w=[t_pre])
                        cv, t_cv, _ = cv_r.next()
                        p.op("dve", lambda e: e.tensor_scalar(out=cv[:], in0=pre[:, 0:BLK], scalar1=cvw_s[:, 4 * m:4 * m + 1], scalar2=None,
                                                              op0=ALU.mult), r=[t_pre, t_c], w=[t_cv])
                        for tap in range(1, 4):
                            p.op("dve", lambda e: e.scalar_tensor_tensor(out=cv[:], in0=pre[:, tap:tap + BLK],
                                                                         scalar=cvw_s[:, 4 * m + tap:4 * m + tap + 1], in1=cv[:],
                                                                         op0=ALU.mult, op1=ALU.add), r=[t_pre, t_c], w=[t_cv])
                        if m < 3:
                            p.op("act", lambda e: e.activation(out=xsf[:, m, :], in_=cv[:], func=AF.Silu, bias=cvb_s[:, m:m + 1]),
                                 r=[t_cv, t_c], w=[t_xsf])
                            if m == 2:
                                p.op("pool", lambda e: e.tensor_copy(out=btb[:], in_=xsf[:, 2, :]), r=[t_xsf], w=[t_btb])
                        else:
                            p.op("act", lambda e: e.activation(out=ctb[:], in_=cv[:], func=AF.Silu, bias=cvb_s[:, m:m + 1]),
                                 r=[t_cv, t_c], w=[t_ctb])

                    chunks = range(4) if k == 0 else range(3, -1, -1)
                    for c in chunks:
                        gc = blk * 4 + c
                        cs_ = slice(c * 128, (c + 1) * 128)
                        dA = dtv[:, 5, 4 * c:4 * c + 4]
                        dtc = dtv[:, 4, 4 * c:4 * c + 4]
                        T_, t_T, _ = pT.next()
                        for m in range(3):
                            p.op("pe", lambda e: e.transpose(T_[:, m * 128:(m + 1) * 128], xsf[:, m, cs_], IDF), r=[t_xsf, t_c], w=[t_T])
                        btk, t_btk, _ = btk_r.next()
                        p.op("act", lambda e: e.copy(out=btk[:], in_=T_[:, 256:384]), r=[t_T], w=[t_btk])
                        Sm, t_Sm, _ = pS.next()
                        p.op("pe", lambda e: e.matmul(Sm[:, 0:4], lhsT=Tri, rhs=dA, start=True, stop=True), r=[t_dtv, t_c], w=[t_Sm])
                        p.op("pe", lambda e: e.matmul(Sm[:, 4:8], lhsT=ONES, rhs=dA, start=True, stop=True), r=[t_dtv, t_c], w=[t_Sm])
                        sm, t_sm, _ = sm_r.next()
                        CS, TOT, NCS, ECS, DTE, ETOT, DTW, D_ = [sm[:, i, :] for i in range(8)]
                        p.op("act", lambda e: e.copy(out=sm[:, 0:2, :], in_=Sm[:, 0:8].rearrange("p (a r) -> p a r", a=2)), r=[t_Sm], w=[t_sm])
                        p.op("dve", lambda e: e.tensor_scalar(out=NCS, in0=CS, scalar1=-1.0, scalar2=None, op0=ALU.mult), r=[t_sm], w=[t_sm])
                        p.op("act", lambda e: e.activation(out=ECS, in_=CS, func=AF.Exp), r=[t_sm], w=[t_sm])
                        p.op("dve", lambda e: e.tensor_tensor(out=D_, in0=TOT, in1=CS, op=ALU.subtract), r=[t_sm], w=[t_sm])
                        p.op("act", lambda e: e.activation(out=DTE, in_=D_, func=AF.Exp), r=[t_sm], w=[t_sm])
                        p.op("act", lambda e: e.activation(out=ETOT, in_=TOT, func=AF.Exp), r=[t_sm], w=[t_sm])
                        p.op("dve", lambda e: e.tensor_tensor(out=DTW, in0=DTE, in1=dtc, op=ALU.mult), r=[t_sm, t_dtv], w=[t_sm])
                        W, t_W, _ = W_r.next()
                        p.op("dve", lambda e: e.tensor_tensor(out=W[:], in0=Tri.unsqueeze(1).to_broadcast([128, 4, 128]),
                                                              in1=dA.unsqueeze(2).to_broadcast([128, 4, 128]), op=ALU.mult),
                             r=[t_dtv, t_c], w=[t_W])
                        Eb, t_Eb, _ = pE.next()
                        p.op("pe", lambda e: e.matmul(Eb[:], lhsT=ONES, rhs=W[:].rearrange("p r l -> p (r l)"), start=True, stop=False),
                             r=[t_W, t_c], w=[t_Eb])
                        p.op("pe", lambda e: e.matmul(Eb[:], lhsT=identb[:], rhs=mskb[:, 512 * k:512 * (k + 1)], start=False, stop=True),
                             r=[t_c], w=[t_Eb])
                        E, t_E, _ = E_r.next()
                        for r_ in range(4):
                            p.op("act", lambda e: e.activation(out=E[:, r_, :], in_=Eb[:, r_ * 128:(r_ + 1) * 128], func=AF.Exp,
                                                               bias=sm[:, 2, r_:r_ + 1]), r=[t_Eb, t_sm], w=[t_E])
                        Cb, t_Cb, _ = pC.next()
                        p.op("pe", lambda e: e.matmul(Cb[:, 0:128], lhsT=btb[:, cs_], rhs=ctb[:, cs_], start=True, stop=True),
                             r=[t_btb, t_ctb], w=[t_Cb])
                        M, t_M, _ = M_r.next()
                        p.op("dve", lambda e: e.tensor_tensor(out=M[:], in0=E[:], in1=Cb[:, 0:128].unsqueeze(1).to_broadcast([128, 4, 128]),
                                                              op=ALU.mult), r=[t_E, t_Cb], w=[t_M])
                        xd, t_xd, _ = xd_r.next()
                        xdw, t_xdw, _ = xdw_r.next()
                        xs_tok = T_[:, 0:256].rearrange("p (r q) -> p r q", r=4)
                        p.op("dve", lambda e: e.tensor_tensor(out=xd[:].rearrange("p (r q) -> p r q", r=4), in0=xs_tok,
                                                              in1=dtc.unsqueeze(2).to_broadcast([128, 4, 64]), op=ALU.mult),
                             r=[t_T, t_dtv], w=[t_xd])
                        p.op("dve", lambda e: e.tensor_tensor(out=xdw[:].rearrange("p (r q) -> p r q", r=4), in0=xs_tok,
                                                              in1=DTW.unsqueeze(2).to_broadcast([128, 4, 64]), op=ALU.mult),
                             r=[t_T, t_sm], w=[t_xdw])
                        Y, t_Y, _ = pY.next()
                        for r_ in range(4):
                            p.op("pe", lambda e: e.matmul(Y[:, r_ * 64:(r_ + 1) * 64], lhsT=M[:, r_, :], rhs=xd[:, r_ * 64:(r_ + 1) * 64],
                                                          start=True, stop=True), r=[t_M, t_xd], w=[t_Y])
                        p.op("pe", lambda e: e.matmul(Y[:, 256:512], lhsT=ctb[:, cs_], rhs=Hb[:], start=True, stop=True),
                             r=[t_ctb, t_Hb], w=[t_Y])
                        p.op("pe", lambda e: e.matmul(Cb[:, 256:512], lhsT=btk[:], rhs=xdw[:], start=True, stop=True),
                             r=[t_btk, t_xdw], w=[t_Cb])
                        yt, t_yt, _ = yt_r.next()
                        p.op("dve", lambda e: e.tensor_tensor(out=yt[:].rearrange("p (r q) -> p r q", r=4),
                                                              in0=Y[:, 256:512].rearrange("p (r q) -> p r q", r=4),
                                                              in1=ECS.unsqueeze(2).to_broadcast([128, 4, 64]), op=ALU.mult),
                             r=[t_Y, t_sm], w=[t_yt])
                        yo, t_yo, yo_sem = y_r.next()
                        p.op("dve", lambda e: e.tensor_tensor(out=yo[:], in0=yt[:], in1=Y[:, 0:256], op=ALU.add), r=[t_yt, t_Y], w=[t_yo])
                        if k == 0:
                            p.op("dve", lambda e: e.tensor_tensor(out=yt[:].rearrange("p (r q) -> p r q", r=4), in0=xs_tok,
                                                                  in1=dsum[:].unsqueeze(2).to_broadcast([128, 4, 64]), op=ALU.mult),
                                 r=[t_T, t_c], w=[t_yt])
                            p.op("pool", lambda e: e.tensor_tensor(out=yo[:], in0=yo[:], in1=yt[:], op=ALU.add), r=[t_yt], w=[t_yo])
                        ydst = yaT[g * 256:(g + 1) * 256, gc * 128:(gc + 1) * 128].rearrange("(j q) t -> q j t", q=128)
                        T2, t_T2, _ = pT.next()
                        for j in range(2):
                            p.op("pe", lambda e: e.transpose(T2[:, j * 128:(j + 1) * 128], yo[:, j * 128:(j + 1) * 128], IDF), r=[t_yo, t_c], w=[t_T2])
                        yoT, t_yoT, yoT_sem = yT_r.next()
                        if k == 0:
                            p.op("act", lambda e: e.copy(out=yoT[:], in_=T2[:, 0:256]), r=[t_T2], w=[t_yoT])
                        else:
                            yl, t_yl, yl_sem = yl_r.next()
                            p.dma("sp", yl[:].rearrange("q (j t) -> q j t", j=2), ydst, r=[t_ya[gc]], w=[t_yl], sem=yl_sem)
                            p.op("dve", lambda e: e.tensor_tensor(out=yoT[:], in0=T2[:, 0:256], in1=yl[:], op=ALU.add), r=[t_T2, t_yl], w=[t_yoT])
                        p.dma("sp", ydst, yoT[:].rearrange("q (j t) -> q j t", j=2), r=[t_yoT], w=[t_ya[gc]], sem=yoT_sem)
                        p.op("dve", lambda e: e.tensor_tensor(out=H[:].rearrange("p (r q) -> p r q", r=4),
                                                              in0=H[:].rearrange("p (r q) -> p r q", r=4),
                                                              in1=ETOT.unsqueeze(2).to_broadcast([128, 4, 64]), op=ALU.mult),
                             r=[t_sm], w=[t_H])
                        p.op("dve", lambda e: e.tensor_tensor(out=H[:], in0=H[:], in1=Cb[:, 256:512], op=ALU.add), r=[t_Cb], w=[t_H])
                        p.op("act", lambda e: e.copy(out=Hb[:], in_=H[:]), r=[t_H], w=[t_Hb])
        barrier(p)


def l0a_consts():
    t = np.arange(128)
    U = (t[:, None] <= t[None, :]).astype(np.float32)
    UT = (t[:, None] >= t[None, :]).astype(np.float32)
    I = np.eye(128, dtype=np.float32)
    ones = np.ones((128, 128), np.float32)
    cst = np.ascontiguousarray(np.concatenate([U, UT, I, ones], axis=1))
    mf = np.where(t[None, :] < t[:, None], NEG, 0.0).astype(np.float32)
    mb = np.where(t[None, :] > t[:, None], NEG, 0.0).astype(np.float32)
    msk = np.ascontiguousarray(np.concatenate([np.tile(mf, (1, 4)), np.tile(mb, (1, 4))], axis=1))
    return cst, msk


def prep_l0a(x, ev_w_in, ev_conv_w, ev_conv_b, ev_a_log, ev_dt_bias, ev_d_skip, S=SEQ):
    w_in = ev_w_in[0]
    cw = ev_conv_w[0]
    cb = ev_conv_b[0]
    cst, msk = l0a_consts()
    maps = []
    xTs = []
    for b in range(2):
        xe = np.zeros((D, S + 3), np.float32)
        xe[:, 2:S + 2] = x[b, :S, :].T
        xTs.append(xe)
    for c in range(NCORES):
        b, g = c // 4, c % 4
        xs_cols = 1024 + g * 256 + np.arange(256)
        b_cols = 1024 + 1024 + g * 128 + np.arange(128)
        c_cols = 1024 + 1536 + g * 128 + np.arange(128)
        dt_cols = np.concatenate([3072 + k * 16 + 4 * g + np.arange(4) for k in range(2)])
        cols = np.concatenate([xs_cols, b_cols, c_cols, dt_cols])
        wg = np.ascontiguousarray(w_in[:, cols])
        xbc_idx = cols[:512] - 1024
        cvw = np.ascontiguousarray(cw[:, xbc_idx].reshape(4, 4, 128).transpose(2, 1, 0).reshape(128, 16))
        cvb = np.ascontiguousarray(cb[xbc_idx].reshape(4, 128).T)
        hsel = np.concatenate([np.stack([v[0][k, 4 * g:4 * g + 4] for k in range(2)]).reshape(-1)
                               for v in (ev_a_log, ev_dt_bias, ev_d_skip)])
        hp = np.ascontiguousarray(np.broadcast_to(hsel[None, :], (128, 24))).astype(np.float32)
        maps.append({"xT": xTs[b], "wg": wg, "cvw": cvw, "cvb": cvb, "hp": hp, "cst": cst, "msk": msk})
    return maps


def rope_tables(p, posi, t_posi, n, invf, sgn, t_c, tabs, cosd, sind, t_cos, t_sin):
    R = slice(64, 96)
    ang, t_a, _ = tabs.next()
    nf, t_n, _ = tabs.next()
    ni, t_ni, _ = tabs.next()
    mm, t_m, _ = tabs.next()
    A, N, M = ang[R, 0:n], nf[R, 0:n], mm[R, 0:n]
    NI = ni[R, 0:n].bitcast(I32)
    p.op("dve", lambda e: e.tensor_copy(out=A, in_=posi[R, 0:n]), r=[t_posi], w=[t_a])
    p.op("dve", lambda e: e.tensor_scalar(out=A, in0=A, scalar1=invf[R, 0:1], scalar2=None, op0=ALU.mult), r=[t_c], w=[t_a])
    p.op("dve", lambda e: e.tensor_scalar(out=N, in0=A, scalar1=1.0 / TWO_PI, scalar2=None, op0=ALU.mult), r=[t_a], w=[t_n])
    p.op("dve", lambda e: e.tensor_copy(out=NI, in_=N), r=[t_n], w=[t_ni])
    p.op("dve", lambda e: e.tensor_copy(out=N, in_=NI), r=[t_ni], w=[t_n])
    p.op("dve", lambda e: e.scalar_tensor_tensor(out=A, in0=N, scalar=-C1, in1=A, op0=ALU.mult, op1=ALU.add), r=[t_n], w=[t_a])
    p.op("dve", lambda e: e.scalar_tensor_tensor(out=A, in0=N, scalar=-C2, in1=A, op0=ALU.mult, op1=ALU.add), r=[t_n], w=[t_a])

    def wrap(X, t_x):
        p.op("dve", lambda e: e.tensor_scalar(out=M, in0=X, scalar1=PI, scalar2=None, op0=ALU.is_gt), r=[t_x], w=[t_m])
        p.op("dve", lambda e: e.scalar_tensor_tensor(out=X, in0=M, scalar=-TWO_PI, in1=X, op0=ALU.mult, op1=ALU.add), r=[t_m], w=[t_x])
        p.op("dve", lambda e: e.tensor_scalar(out=M, in0=X, scalar1=-PI, scalar2=None, op0=ALU.is_lt), r=[t_x], w=[t_m])
        p.op("dve", lambda e: e.scalar_tensor_tensor(out=X, in0=M, scalar=TWO_PI, in1=X, op0=ALU.mult, op1=ALU.add), r=[t_m], w=[t_x])
    wrap(A, t_a)
    p.op("act", lambda e: e.activation(out=N, in_=A, func=AF.Sin), r=[t_a], w=[t_n])
    p.op("dve", lambda e: e.tensor_scalar(out=sind, in0=N, scalar1=sgn[R, 0:1], scalar2=None, op0=ALU.mult), r=[t_n, t_c], w=[t_sin])
    p.op("dve", lambda e: e.tensor_scalar(out=A, in0=A, scalar1=PI / 2, scalar2=None, op0=ALU.add), r=[t_a], w=[t_a])
    wrap(A, t_a)
    p.op("act", lambda e: e.activation(out=cosd, in_=A, func=AF.Sin), r=[t_a], w=[t_cos])


def emit_l1(nc, p, S, A):
    T = S // 4
    BLK = 512
    NKB = S // BLK
    NQB = T // BLK
    NK128 = S // 128
    x1T, x1, posb, out = A["x1T"], A["x1"], A["posb"], A["out"]
    w_cq, w_kv, w_kr, w_g3, w_uq, w_ukv, w_pool, wout = (A["w_cq"], A["w_kv"], A["w_kr"], A["w_g3"], A["w_uq"], A["w_ukv"],
                                                         A["w_pool"], A["wout_od"])
    sm_c, sel, lng, lnb = A["sm_c"], A["sel"], A["lng_od"], A["lnb_od"]
    oxT, ox1, oHL, oHR, opos = A["own_x1T"], A["own_x1"], A["own_HL"], A["own_HR"], A["own_pos"]

    def xrows_static(blk, kk):
        return x1T[blk * 1024 + kk * 128:blk * 1024 + (kk + 1) * 128, :]

    def xrows_own(blk, kk):
        return oxT[blk * 1024 + kk * 128:blk * 1024 + (kk + 1) * 128, :]
    xtok = ox1

    with ExitStack() as es:
        p.es = es
        smc = p.sb("smc", [128, 80], F32)
        sel_s = p.sb("sel_s", [128, 384], F32)
        t_c = Tok()
        dc = p.dma_sem()
        p.dma("sp", smc[:], sm_c[:, :], w=[t_c], sem=dc)
        p.dma("sp", sel_s[:], sel[:, :], w=[t_c], sem=dc)
        QG, KVG, INVF, SGN, PSC = smc[:, 0:2], smc[:, 2:3], smc[:, 3:4], smc[:, 4:5], smc[:, 5:9]
        CORR = smc[:, 16:80]
        SEL_E, SEL_O, ONES = sel_s[:, 0:128], sel_s[:, 128:256], sel_s[:, 256:384]
        ycT = p.sb("ycT", [128, 4, T], BF16)
        t_yc = [[Tok() for _ in range(NQB)] for _ in range(8)]
        esA = ExitStack()
        p.es = esA
        ckvn = p.sb("ckvn", [128, S], BF16)
        t_ckvn = [Tok() for _ in range(NKB)]
        Kbuf = p.sb("Kbuf", [96, S], BF16)
        t_kn = [Tok() for _ in range(NKB)]
        t_kr = [Tok() for _ in range(NKB)]
        cqn = p.sb("cqn", [128, 2, T], BF16)
        t_cqn = [Tok() for _ in range(NQB)]
        cosq = p.sb("cosq", [96, T], BF16)
        sinq = p.sb("sinq", [96, T], BF16)
        t_cosq = [Tok() for _ in range(NQB)]
        t_sinq = [Tok() for _ in range(NQB)]
        wuqb = p.sb("wuqb", [128, 2, 1536], BF16)
        wukvb = p.sb("wukvb", [128, 1024], BF16)
        t_wuq, t_wukv = Tok(), Tok()

        with ExitStack() as es1:
            p.es = es1
            wst = Ring(p, "wst", [128, 1536], F32, 1, dma=True)
            wcqb = p.sb("wcqb", [128, 8, 256], BF16)
            wkvb = p.sb("wkvb", [128, 8, 128], BF16)
            wkrb = p.sb("wkrb", [128, 8, 192], BF16)
            t_wcq, t_wkv, t_wkr = Tok(), Tok(), Tok()
            load_cast_weight(p, w_cq, wcqb, wst, 8, 256, cw=256, tok=t_wcq)
            load_cast_weight(p, w_kv, wkvb, wst, 8, 128, cw=128, tok=t_wkv)
            load_cast_weight(p, w_kr, wkrb, wst, 8, 192, cw=192, tok=t_wkr)
            for kc in range(2):
                st, stok, ssem = wst.next()
                p.dma("sp", st[:, 0:1536], w_uq[kc * 128:(kc + 1) * 128, :], w=[stok], sem=ssem)
                p.op("dve", lambda e: e.tensor_scalar(out=wuqb[:, kc, :], in0=st[:, 0:1536], scalar1=QG[:, kc:kc + 1], scalar2=None,
                                                      op0=ALU.mult), r=[stok, t_c], w=[t_wuq])
            st, stok, ssem = wst.next()
            p.dma("sp", st[:, 0:1024], w_ukv[:, :], w=[stok], sem=ssem)
            p.op("dve", lambda e: e.tensor_scalar(out=wukvb[:], in0=st[:, 0:1024], scalar1=KVG, scalar2=None, op0=ALU.mult),
                 r=[stok, t_c], w=[t_wukv])

            xst = Ring(p, "xst", [128, BLK], F32, 3, dma=True)
            xb_r = Ring(p, "xb", [128, 8, BLK], BF16, 2)
            pos_r = Ring(p, "posr", [128, BLK], I32, 2, dma=True)
            tabs = Ring(p, "tabs", [128, BLK], F32, 4)
            cs_r = Ring(p, "csr", [128, BLK], F32, 2)
            sn_r = Ring(p, "snr", [128, BLK], F32, 2)
            sq_r = Ring(p, "sqr", [128, BLK], F32, 3)
            t1_r = Ring(p, "t1r", [128, BLK], F32, 2)
            pA = Ring(p, "pA", [128, 512], F32, 5, space="ps")
            pSS = Ring(p, "pSS", [128, 512], F32, 2, space="ps")

            def load_xblock(rows_fn, blk):
                xb, t_xb, _ = xb_r.next()
                for kk in range(8):
                    st, stok, ssem = xst.next()
                    p.dma("sp", st[:], rows_fn(blk, kk), w=[stok], sem=ssem)
                    p.op("pool", lambda e: e.tensor_copy(out=xb[:, kk, :], in_=st[:]), r=[stok], w=[t_xb])
                return xb, t_xb

            def rstd_of(ss, t_ss, nch):
                r_, t_r, _ = sq_r.next()
                p.op("dve", lambda e: e.tensor_scalar(out=r_[:], in0=ss[:], scalar1=1.0 / nch, scalar2=EPS, op0=ALU.mult, op1=ALU.add),
                     r=[t_ss], w=[t_r])
                p.op("act", lambda e: e.activation(out=r_[:], in_=r_[:], func=AF.Ln), r=[t_r], w=[t_r])
                p.op("act", lambda e: e.activation(out=r_[:], in_=r_[:], func=AF.Exp, scale=-0.5), r=[t_r], w=[t_r])
                return r_, t_r

            for kb in range(NKB):
                c0 = kb * BLK
                xb, t_xb = load_xblock(xrows_static, kb + 1)
                pi_, t_pi, pi_sem = pos_r.next()
                p.dma("sp", pi_[64:96, :], posb[kb * 32:(kb + 1) * 32, :], w=[t_pi], sem=pi_sem)
                ck, t_ck, _ = pA.next()
                ka, t_ka, _ = pA.next()
                kbs, t_kbs, _ = pA.next()
                for kk in range(8):
                    p.op("pe", lambda e: e.matmul(ck[:], lhsT=wkvb[:, kk, :], rhs=xb[:, kk, :], start=(kk == 0), stop=(kk == 7)),
                         r=[t_wkv, t_xb], w=[t_ck])
                for kk in range(8):
                    p.op("pe", lambda e: e.matmul(ka[0:96, :], lhsT=wkrb[:, kk, 0:96], rhs=xb[:, kk, :], start=(kk == 0), stop=(kk == 7)),
                         r=[t_wkr, t_xb], w=[t_ka])
                for kk in range(8):
                    p.op("pe", lambda e: e.matmul(kbs[0:96, :], lhsT=wkrb[:, kk, 96:192], rhs=xb[:, kk, :], start=(kk == 0), stop=(kk == 7)),
                         r=[t_wkr, t_xb], w=[t_kbs])
                sq, t_sq, _ = sq_r.next()
                p.op("act", lambda e: e.activation(out=sq[:], in_=ck[:], func=AF.Square), r=[t_ck], w=[t_sq])
                ss, t_ss, _ = pSS.next()
                p.op("pe", lambda e: e.matmul(ss[:], lhsT=ONES, rhs=sq[:], start=True, stop=True), r=[t_sq, t_c], w=[t_ss])
                rs, t_rs = rstd_of(ss, t_ss, 128)
                p.op("dve", lambda e: e.tensor_tensor(out=ckvn[:, c0:c0 + BLK], in0=ck[:], in1=rs[:], op=ALU.mult),
                     r=[t_ck, t_rs], w=[t_ckvn[kb]])
                cs_, t_cs, _ = cs_r.next()
                sn_, t_sn, _ = sn_r.next()
                rope_tables(p, pi_, t_pi, BLK, INVF, SGN, t_c, tabs, cs_[64:96, :], sn_[64:96, :], t_cs, t_sn)
                t1, t_t1, _ = t1_r.next()
                t2, t_t2, _ = t1_r.next()
                p.op("dve", lambda e: e.tensor_tensor(out=t1[64:96, :], in0=ka[64:96, :], in1=cs_[64:96, :], op=ALU.mult),
                     r=[t_ka, t_cs], w=[t_t1])
                p.op("dve", lambda e: e.tensor_tensor(out=t2[64:96, :], in0=kbs[64:96, :], in1=sn_[64:96, :], op=ALU.mult),
                     r=[t_kbs, t_sn], w=[t_t2])
                p.op("pool", lambda e: e.tensor_tensor(out=Kbuf[64:96, c0:c0 + BLK], in0=t1[64:96, :], in1=t2[64:96, :], op=ALU.add),
                     r=[t_t1, t_t2], w=[t_kr[kb]])

            for qb in range(NQB):
                c0 = qb * BLK
                xb, t_xb = load_xblock(xrows_own, qb)
                pi_, t_pi, pi_sem = pos_r.next()
                p.dma("sp", pi_[64:96, :], opos[qb * 32:(qb + 1) * 32, :], w=[t_pi], sem=pi_sem)
                cqs = []
                ss, t_ss, _ = pSS.next()
                for m in range(2):
                    cq, t_cq, _ = pA.next()
                    for kk in range(8):
                        p.op("pe", lambda e: e.matmul(cq[:], lhsT=wcqb[:, kk, m * 128:(m + 1) * 128], rhs=xb[:, kk, :],
                                                      start=(kk == 0), stop=(kk == 7)), r=[t_wcq, t_xb], w=[t_cq])
                    sq, t_sq, _ = sq_r.next()
                    p.op("act", lambda e: e.activation(out=sq[:], in_=cq[:], func=AF.Square), r=[t_cq], w=[t_sq])
                    p.op("pe", lambda e: e.matmul(ss[:], lhsT=ONES, rhs=sq[:], start=(m == 0), stop=(m == 1)), r=[t_sq, t_c], w=[t_ss])
                    cqs.append((cq, t_cq))
                rs, t_rs = rstd_of(ss, t_ss, 256)
                for m in range(2):
                    cq, t_cq = cqs[m]
                    p.op("dve", lambda e: e.tensor_tensor(out=cqn[:, m, c0:c0 + BLK], in0=cq[:], in1=rs[:], op=ALU.mult),
                         r=[t_cq, t_rs], w=[t_cqn[qb]])
                rope_tables(p, pi_, t_pi, BLK, INVF, SGN, t_c, tabs, cosq[64:96, c0:c0 + BLK], sinq[64:96, c0:c0 + BLK],
                            t_cosq[qb], t_sinq[qb])
        barrier(p)

        with ExitStack() as es3:
            p.es = es3
            Vbuf = p.sb("Vbuf", [128, NK128, 128], BF16)
            t_v = [Tok() for _ in range(NK128 // 8 if NK128 >= 8 else 1)]
            VG = min(8, NK128)
            Q_r = Ring(p, "Q", [96, T], BF16, 2)
            tq_r = Ring(p, "tq", [96, BLK], F32, 4)
            P_r = Ring(p, "P", [128, BLK], BF16, 3)
            osb_r = Ring(p, "osb", [128, BLK], F32, 2)
            rden_r = Ring(p, "rden", [128, BLK], F32, 2)
            pS = Ring(p, "pS", [128, 512], F32, 3, space="ps")
            pO = Ring(p, "pO", [128, 512], F32, 2, space="ps")
            pD = Ring(p, "pD", [128, 512], F32, 1, space="ps")
            pB = Ring(p, "pB", [128, 512], F32, 2, space="ps")
            for h in range(8):
                odd = h % 2
                voff = 64 * odd
                Q, _, _ = Q_r.next()
                t_Q = [Tok() for _ in range(NQB)]
                for qb in range(NQB):
                    c0 = qb * BLK
                    qa, t_qa, _ = pB.next()
                    qs, t_qs, _ = pB.next()
                    for kc in range(2):
                        p.op("pe", lambda e: e.matmul(qa[0:96, :], lhsT=wuqb[:, kc, h * 96:(h + 1) * 96], rhs=cqn[:, kc, c0:c0 + BLK],
                                                      start=(kc == 0), stop=(kc == 1)), r=[t_wuq, t_cqn[qb]], w=[t_qa])
                    for kc in range(2):
                        p.op("pe", lambda e: e.matmul(qs[0:96, :], lhsT=wuqb[:, kc, 768 + h * 96:768 + (h + 1) * 96],
                                                      rhs=cqn[:, kc, c0:c0 + BLK], start=(kc == 0), stop=(kc == 1)),
                             r=[t_wuq, t_cqn[qb]], w=[t_qs])
                    p.op("dve", lambda e: e.tensor_copy(out=Q[0:64, c0:c0 + BLK], in_=qa[0:64, :]), r=[t_qa], w=[t_Q[qb]])
                    t1, t_t1, _ = tq_r.next()
                    t2, t_t2, _ = tq_r.next()
                    p.op("dve", lambda e: e.tensor_tensor(out=t1[64:96, :], in0=qa[64:96, :], in1=cosq[64:96, c0:c0 + BLK], op=ALU.mult),
                         r=[t_qa, t_cosq[qb]], w=[t_t1])
                    p.op("dve", lambda e: e.tensor_tensor(out=t2[64:96, :], in0=qs[64:96, :], in1=sinq[64:96, c0:c0 + BLK], op=ALU.mult),
                         r=[t_qs, t_sinq[qb]], w=[t_t2])
                    p.op("pool", lambda e: e.tensor_tensor(out=Q[64:96, c0:c0 + BLK], in0=t1[64:96, :], in1=t2[64:96, :], op=ALU.add),
                         r=[t_t1, t_t2], w=[t_Q[qb]])
                for kb in range(NKB):
                    c0 = kb * BLK
                    kp, t_kp, _ = pB.next()
                    p.op("pe", lambda e: e.matmul(kp[0:64, :], lhsT=wukvb[:, h * 128:h * 128 + 64], rhs=ckvn[:, c0:c0 + BLK],
                                                  start=True, stop=True), r=[t_wukv, t_ckvn[kb]], w=[t_kp])
                    p.op("dve", lambda e: e.tensor_copy(out=Kbuf[0:64, c0:c0 + BLK], in_=kp[0:64, :]), r=[t_kp], w=[t_kn[kb]])
                for g in range(len(t_v)):
                    vp, t_vp, _ = pB.next()
                    for j in range(VG):
                        k128 = g * VG + j
                        p.op("pe", lambda e: e.matmul(vp[:, j * 64:(j + 1) * 64], lhsT=ckvn[:, k128 * 128:(k128 + 1) * 128],
                                                      rhs=wukvb[:, h * 128 + 64:h * 128 + 128], start=True, stop=True),
                             r=[t_wukv, t_ckvn[k128 // 4]], w=[t_vp])
                    vs = Vbuf[:, g * VG:(g + 1) * VG, :]
                    p.op("pool", lambda e: e.memset(vs[:, :, 64 - voff:128 - voff], 0.0), w=[t_v[g]])
                    p.op("pool", lambda e: e.memset(vs[:, :, 64 - voff:65 - voff], 1.0), w=[t_v[g]])
                    p.op("dve", lambda e: e.tensor_copy(out=vs[:, :, voff:voff + 64], in_=vp[:, 0:VG * 64].rearrange("p (j v) -> p j v", v=64)),
                         r=[t_vp], w=[t_v[g]])
                MV = 128 if odd else 65
                for qb in range(NQB):
                    q0 = qb * BLK
                    O, t_O, _ = pO.next()
                    Sq = {}

                    def issue_S(k128):
                        Sx, t_S, _ = pS.next()
                        kb = k128 // 4
                        p.op("pe", lambda e: e.matmul(Sx[:], lhsT=Kbuf[0:96, k128 * 128:(k128 + 1) * 128], rhs=Q[0:96, q0:q0 + BLK],
                                                      start=True, stop=True), r=[t_kn[kb], t_kr[kb], t_Q[qb]], w=[t_S])
                        Sq[k128] = (Sx, t_S)
                    for k128 in range(min(2, NK128)):
                        issue_S(k128)
                    for k128 in range(NK128):
                        Sx, t_S = Sq.pop(k128)
                        Pt, t_P, _ = P_r.next()
                        p.op("act", lambda e: e.activation(out=Pt[:], in_=Sx[:], func=AF.Exp, scale=ATT_SCALE), r=[t_S], w=[t_P])
                        if k128 + 2 < NK128:
                            issue_S(k128 + 2)
                        p.op("pe", lambda e: e.matmul(O[0:MV, :], lhsT=Vbuf[:, k128, 0:MV], rhs=Pt[:], start=(k128 == 0),
                                                      stop=(k128 == NK128 - 1)), r=[t_v[k128 // VG], t_P], w=[t_O])
                    osb, t_osb, _ = osb_r.next()
                    p.op("dve", lambda e: e.tensor_copy(out=osb[0:MV, :], in_=O[0:MV, :]), r=[t_O], w=[t_osb])
                    Dn, t_D, _ = pD.next()
                    if odd:
                        p.op("pe", lambda e: e.matmul(Dn[:], lhsT=SEL_O, rhs=osb[:], start=True, stop=True), r=[t_osb, t_c], w=[t_D])
                    else:
                        p.op("pe", lambda e: e.matmul(Dn[0:64, :], lhsT=SEL_E[0:65, 0:64], rhs=osb[0:65, :], start=True, stop=True),
                             r=[t_osb, t_c], w=[t_D])
                    rd, t_rd, _ = rden_r.next()
                    PR = slice(voff, voff + 64)
                    p.op("dve", lambda e: e.reciprocal(out=rd[PR, :], in_=Dn[PR, :]), r=[t_D], w=[t_rd])
                    p.op("dve", lambda e: e.tensor_tensor(out=ycT[PR, h // 2, q0:q0 + BLK], in0=osb[PR, :], in1=rd[PR, :], op=ALU.mult),
                         r=[t_osb, t_rd], w=[t_yc[h][qb]])
        barrier(p)
        esA.close()

        with ExitStack() as es4:
            p.es = es4
            wst = Ring(p, "wst4", [128, 1024], F32, 2, dma=True)
            wg3b = p.sb("wg3b", [128, 8, 1536], BF16)
            woutb = p.sb("woutb", [128, 8, D], BF16)
            wpoolb = p.sb("wpoolb", [128, 512], BF16)
            lng_s = p.sb("lng_s", [128, D], F32)
            lnb_s = p.sb("lnb_s", [128, D], F32)
            t_wg3, t_wout, t_wpool, t_ln = Tok(), Tok(), Tok(), Tok()
            dl = p.dma_sem()
            p.dma("sp", lng_s[:], lng[:, :], w=[t_ln], sem=dl)
            p.dma("sp", lnb_s[:], lnb[:, :], w=[t_ln], sem=dl)
            load_cast_weight(p, w_g3, wg3b, wst, 8, 1536, cw=768, tok=t_wg3)
            load_cast_weight(p, wout, woutb, wst, 8, D, tok=t_wout)
            st, stok, ssem = wst.next()
            p.dma("sp", st[:, 0:512], w_pool[:, :], w=[stok], sem=ssem)
            p.op("dve", lambda e: e.tensor_copy(out=wpoolb[:], in_=st[:, 0:512]), r=[stok], w=[t_wpool])
            xst = Ring(p, "xst4", [128, BLK + 16], F32, 3, dma=True)
            xb_r = Ring(p, "xb4", [128, 8, BLK + 16], BF16, 2)
            xtk = Ring(p, "xtk", [128, D], F32, 2, dma=True)
            cat = p.sb("cat", [128, 8, BLK], BF16)
            t_cat = [Tok() for _ in range(8)]
            tmp = Ring(p, "tmp4", [128, BLK + 16], F32, 10)
            hl_r = Ring(p, "hl4", [128, 16], F32, 2)
            pl_r = Ring(p, "pl4", [128, BLK], BF16, 2)
            rr = Ring(p, "rr", [128, D], F32, 2)
            r2 = Ring(p, "r2", [128, D], F32, 2, dma=True)
            junk = Ring(p, "junk", [128, D], BF16, 1)
            st_r = Ring(p, "stat", [128, 8], F32, 4)
            pp = Ring(p, "pp4", [128, 512], F32, 5, space="ps")
            ph = Ring(p, "ph4", [128, 512], F32, 1, space="ps")
            po = Ring(p, "po4", [128, D], F32, 1, space="ps")
            WIN = (2, 4, 8, 16)
            for qb in range(NQB):
                c0 = qb * BLK
                xb, t_xb, _ = xb_r.next()
                for kk in range(8):
                    st, stok, ssem = xst.next()
                    rws = slice(qb * 1024 + kk * 128, qb * 1024 + (kk + 1) * 128)
                    p.dma("sp", st[:, 8:BLK + 8], oxT[rws, :], w=[stok], sem=ssem)
                    p.dma("sp", st[:, 0:8], oHL[rws, :], w=[stok], sem=ssem, nowait=True)
                    p.dma("sp", st[:, BLK + 8:BLK + 16], oHR[rws, :], w=[stok], sem=ssem, nowait=True)
                    p.op("pool", lambda e: e.tensor_copy(out=xb[:, kk, :], in_=st[:]), r=[stok], w=[t_xb])

                def proj(col0, lo, n, dst, t_dst):
                    for kk in range(8):
                        p.op("pe", lambda e: e.matmul(dst, lhsT=wg3b[:, kk, col0:col0 + 128], rhs=xb[:, kk, lo:lo + n],
                                                      start=(kk == 0), stop=(kk == 7)), r=[t_wg3, t_xb], w=[t_dst])
                for j in range(4):
                    gc, t_gc, _ = pp.next()
                    proj(j * 128, 8, BLK, gc[:], t_gc)
                    sg, t_sg, _ = tmp.next()
                    p.op("act", lambda e: e.activation(out=sg[:, 0:BLK], in_=gc[:], func=AF.Silu), r=[t_gc], w=[t_sg])
                    p.op("dve", lambda e: e.tensor_tensor(out=cat[:, j, :], in0=sg[:, 0:BLK], in1=ycT[:, j, c0:c0 + BLK], op=ALU.mult),
                         r=[t_sg, t_yc[2 * j][qb], t_yc[2 * j + 1][qb]], w=[t_cat[j]])
                for gi in range(4):
                    w = WIN[gi]
                    um, t_um, _ = pp.next()
                    proj(512 + gi * 128, 8, BLK, um[:], t_um)
                    hl, t_hl, _ = ph.next()
                    proj(512 + gi * 128, 0, 8, hl[:, 0:8], t_hl)
                    proj(512 + gi * 128, BLK + 8, 8, hl[:, 8:16], t_hl)
                    u, t_u, _ = tmp.next()
                    p.op("act", lambda e: e.copy(out=u[:, 8:BLK + 8], in_=um[:]), r=[t_um], w=[t_u])
                    p.op("act", lambda e: e.copy(out=u[:, 0:8], in_=hl[:, 0:8]), r=[t_hl], w=[t_u])
                    p.op("act", lambda e: e.copy(out=u[:, BLK + 8:BLK + 16], in_=hl[:, 8:16]), r=[t_hl], w=[t_u])
                    cur, t_cur, n, width = u, t_u, BLK + 16, 1
                    while width < w:
                        nxt, t_nxt, _ = tmp.next()
                        n2 = n - width
                        p.op("dve", lambda e: e.tensor_tensor(out=nxt[:, 0:n2], in0=cur[:, 0:n2], in1=cur[:, width:width + n2], op=ALU.add),
                             r=[t_cur], w=[t_nxt])
                        cur, t_cur, n, width = nxt, t_nxt, n2, width * 2
                    s0 = 8 - w // 2
                    pm, t_pm, _ = tmp.next()
                    p.op("dve", lambda e: e.tensor_scalar(out=pm[:, 0:BLK], in0=cur[:, s0:s0 + BLK], scalar1=1.0 / w, scalar2=None, op0=ALU.mult),
                         r=[t_cur], w=[t_pm])
                    if qb == 0:
                        p.op("dve", lambda e: e.tensor_tensor(out=pm[:, 0:8], in0=pm[:, 0:8], in1=CORR[:, gi * 16:gi * 16 + 8], op=ALU.mult),
                             r=[t_c], w=[t_pm])
                    if qb == NQB - 1:
                        p.op("dve", lambda e: e.tensor_tensor(out=pm[:, BLK - 8:BLK], in0=pm[:, BLK - 8:BLK],
                                                              in1=CORR[:, gi * 16 + 8:gi * 16 + 16], op=ALU.mult), r=[t_c], w=[t_pm])
                    pl, t_pl, _ = pl_r.next()
                    p.op("dve", lambda e: e.tensor_tensor(out=pl[:], in0=pm[:, 0:BLK], in1=u[:, 8:BLK + 8], op=ALU.subtract),
                         r=[t_pm, t_u], w=[t_pl])
                    yd, t_yd, _ = pp.next()
                    p.op("pe", lambda e: e.matmul(yd[:], lhsT=wpoolb[:, gi * 128:(gi + 1) * 128], rhs=pl[:], start=True, stop=True),
                         r=[t_wpool, t_pl], w=[t_yd])
                    gd, t_gd, _ = pp.next()
                    proj(1024 + gi * 128, 8, BLK, gd[:], t_gd)
                    sg, t_sg, _ = tmp.next()
                    p.op("act", lambda e: e.activation(out=sg[:, 0:BLK], in_=gd[:], func=AF.Silu), r=[t_gd], w=[t_sg])
                    p.op("dve", lambda e: e.scalar_tensor_tensor(out=cat[:, 4 + gi, :], in0=yd[:], scalar=PSC[:, gi:gi + 1], in1=sg[:, 0:BLK],
                                                                 op0=ALU.mult, op1=ALU.mult), r=[t_yd, t_sg, t_c], w=[t_cat[4 + gi]])
                for tt in range(BLK // 128):
                    xk, t_xk, xk_sem = xtk.next()
                    p.dma("sp", xk[:], xtok[c0 + tt * 128:c0 + (tt + 1) * 128, :], w=[t_xk], sem=xk_sem)
                    o, t_o, _ = po.next()
                    for half in range(2):
                        for kc in range(8):
                            p.op("pe", lambda e: e.matmul(o[:, half * 512:(half + 1) * 512], lhsT=cat[:, kc, tt * 128:(tt + 1) * 128],
                                                          rhs=woutb[:, kc, half * 512:(half + 1) * 512], start=(kc == 0), stop=(kc == 7)),
                                 r=[t_cat[kc], t_wout], w=[t_o])
                    layer_norm_tail(p, o, t_o, xk, t_xk, rr, r2, junk, st_r, lng_s, lnb_s, t_ln,
                                    out[c0 + tt * 128:c0 + (tt + 1) * 128, :])
        barrier(p)


def prep_l1(x1, positions, od_w_in, od_q_norm_g, od_w_uq, od_kv_norm_g, od_w_ukv, od_pool_w, od_pool_scale, od_w_out,
            od_ln_g, od_ln_b, S=SEQ):
    T = S // 4
    w_in = od_w_in[0]
    w_cq = np.ascontiguousarray(w_in[:, 0:256])
    w_kv = np.ascontiguousarray(w_in[:, 256:384])
    kr = w_in[:, 384:416]
    krs = np.concatenate([kr[:, 16:32], kr[:, 0:16]], axis=1)
    z64 = np.zeros((D, 64), np.float32)
    w_kr = np.ascontiguousarray(np.concatenate([z64, kr, z64, krs], axis=1))
    w_g3 = np.ascontiguousarray(w_in[:, 416:1952])
    uq = od_w_uq[0].reshape(256, 8, 96)
    uqs = np.zeros_like(uq)
    uqs[:, :, 64:80] = uq[:, :, 80:96]
    uqs[:, :, 80:96] = uq[:, :, 64:80]
    w_uq = np.ascontiguousarray(np.concatenate([uq.reshape(256, 768), uqs.reshape(256, 768)], axis=1))
    w_ukv = np.ascontiguousarray(od_w_ukv[0])
    w_pool = np.ascontiguousarray(od_pool_w[0].transpose(1, 0, 2).reshape(128, 512))
    wout = np.ascontiguousarray(od_w_out[0])
    half = 16
    inv_freq = (np.float32(10000.0) ** (-np.arange(half, dtype=np.float32) / np.float32(half))).astype(np.float32)
    sel = np.zeros((128, 384), np.float32)
    sel[64, 0:64] = 1.0
    sel[0, 128 + 64:128 + 128] = 1.0
    sel[:, 256:384] = 1.0
    lng = np.ascontiguousarray(np.broadcast_to(od_ln_g[0][None, :], (128, D)))
    lnb = np.ascontiguousarray(np.broadcast_to(od_ln_b[0][None, :], (128, D)))
    maps = []
    xTbs = [np.ascontiguousarray(x1[b, :S, :].T) for b in range(2)] if x1 is not None else [None, None]
    posbs = [np.ascontiguousarray(np.broadcast_to(positions[b, :S].reshape(S // 512, 1, 512), (S // 512, 32, 512))
                                  .reshape((S // 512) * 32, 512)).astype(np.int32) for b in range(2)]
    for c in range(NCORES):
        b, s0 = c // 4, (c % 4) * T
        xe = None
        if x1 is not None:
            xe = np.zeros((D, T + 16), np.float32)
            lo, hi = max(0, s0 - 8), min(S, s0 + T + 8)
            xe[:, lo - (s0 - 8):hi - (s0 - 8)] = x1[b, lo:hi, :].T
        smc = np.zeros((128, 80), np.float32)
        smc[:, 0:2] = od_q_norm_g[0].reshape(2, 128).T
        smc[:, 2] = od_kv_norm_g[0]
        smc[64:80, 3] = inv_freq
        smc[80:96, 3] = inv_freq
        smc[64:80, 4] = -1.0
        smc[80:96, 4] = 1.0
        smc[:, 5:9] = od_pool_scale[0].reshape(4, 128).T
        for gi, w in enumerate((2, 4, 8, 16)):
            for j in range(8):
                for side, t in ((0, s0 + j), (1, s0 + T - 8 + j)):
                    lo_ = min(max(t - w // 2, 0), S)
                    hi_ = min(max(t + w - w // 2, 0), S)
                    smc[:, 16 + gi * 16 + side * 8 + j] = np.float32(w) / np.float32(hi_ - lo_)
        maps.append({"xTb": xTbs[b], "xTo": xe, "xtok": (np.ascontiguousarray(x1[b, s0:s0 + T, :]) if x1 is not None else None),
                     "posb": posbs[b],
                     "w_cq": w_cq, "w_kv": w_kv, "w_kr": w_kr, "w_g3": w_g3, "w_uq": w_uq, "w_ukv": w_ukv,
                     "w_pool": w_pool, "wout": wout, "sm_c": smc, "sel": sel, "lng": lng, "lnb": lnb})
    return maps


def build_fused(S=SEQ):
    T = S // 4
    nc = bass.Bass("TRN2", target_bir_lowering=False)

    def inp(name, shape, dt=F32):
        return nc.dram_tensor(name, list(shape), dt, kind="ExternalInput").ap()
    A = {}
    A["xT"] = inp("xT", [D, S + 3])
    A["xT1"] = A["xT"][:, 1:S + 3]
    A["xtok"] = inp("xtok", [S, D])
    A["wg"] = [inp("wg%d" % g, [D, 520]) for g in range(4)]
    A["cvw"] = [inp("cvw%d" % g, [128, 16])[:, :] for g in range(4)]
    A["cvb"] = [inp("cvb%d" % g, [128, 4])[:, :] for g in range(4)]
    A["hp"] = [inp("hp%d" % g, [128, 24])[:, :] for g in range(4)]
    A["cst"] = inp("cst", [128, 512])
    A["msk"] = inp("msk", [128, 1024])
    A["w1"] = inp("w1", [D, 5120])
    A["wout"] = inp("wout", [2048, D])
    A["normg"] = inp("normg", [128, 8])
    A["scw"] = inp("scw", [128, 24])
    A["lng"] = inp("lng", [128, D])
    A["lnb"] = inp("lnb", [128, D])
    A["posb"] = inp("posb", [(S // 512) * 32, 512], I32)
    A["w_cq"] = inp("w_cq", [D, 256])
    A["w_kv"] = inp("w_kv", [D, 128])
    A["w_kr"] = inp("w_kr", [D, 192])
    A["w_g3"] = inp("w_g3", [D, 1536])
    A["w_uq"] = inp("w_uq", [256, 1536])
    A["w_ukv"] = inp("w_ukv", [128, 1024])
    A["w_pool"] = inp("w_pool", [128, 512])
    A["wout_od"] = inp("wout_od", [D, D])
    A["sm_c"] = inp("sm_c", [128, 80])
    A["sel"] = inp("sel", [128, 384])
    A["lng_od"] = inp("lng_od", [128, D])
    A["lnb_od"] = inp("lnb_od", [128, D])
    off = inp("off", [1, 4], I32)
    A["out"] = nc.dram_tensor("out", [T, D], F32, kind="ExternalOutput").ap()
    A["yaT"] = nc.dram_tensor("yaT_s", [D, S], F32).ap()
    A["x1"] = nc.dram_tensor("x1_s", [S, D], F32).ap()
    A["x1T"] = nc.dram_tensor("x1T_s", [(S // 512 + 2) * 1024, 512], F32).ap()
    A["x1HL"] = nc.dram_tensor("x1HL_s", [(S // 512 + 1) * 1024, 8], F32).ap()
    A["x1HR"] = nc.dram_tensor("x1HR_s", [(S // 512 + 1) * 1024, 8], F32).ap()
    A["own_x1T"] = nc.dram_tensor("own_x1T_s", [(T // 512) * 1024, 512], F32).ap()
    A["own_x1"] = nc.dram_tensor("own_x1_s", [T, D], F32).ap()
    A["own_HL"] = nc.dram_tensor("own_HL_s", [(T // 512) * 1024, 8], F32).ap()
    A["own_HR"] = nc.dram_tensor("own_HR_s", [(T // 512) * 1024, 8], F32).ap()
    A["own_pos"] = nc.dram_tensor("own_pos_s", [(T // 512) * 32, 512], I32).ap()

    with ExitStack() as es:
        p = Prog(nc, es)
        regs = [es.enter_context(nc.sync.register("offr%d" % i)) for i in range(3)]
        for i in range(3):
            nc.sync.reg_load(regs[i], off[0:1, i:i + 1])
        NB, NQB = S // 512, T // 512
        b0v = nc.sync.snap(regs[0], min_val=0, max_val=NB - NQB)
        u0v = nc.sync.snap(regs[1], min_val=0, max_val=(NB - NQB) * 64)
        t0v = nc.sync.snap(regs[2], min_val=0, max_val=(S - T) // 8)
        p.prefix = "a_"
        emit_l0a(nc, p, S, A)
        p.prefix = "b_"
        emit_l0b(nc, p, S, A)
        csem = p.dma_sem()
        v = lambda ap, b: ap.rearrange("(a b) t -> a (b t)", b=b)
        p.dma("sp", v(A["own_x1T"], 16), v(A["x1T"], 16)[bass.ds(u0v + 64, NQB * 64), :], sem=csem)
        p.dma("sp", v(A["own_x1"], 8), v(A["x1"], 8)[bass.ds(t0v, T // 8), :], sem=csem)
        p.dma("sp", v(A["own_HL"], 1024), v(A["x1HL"], 1024)[bass.ds(b0v, NQB), :], sem=csem)
        p.dma("sp", v(A["own_HR"], 1024), v(A["x1HR"], 1024)[bass.ds(b0v + 1, NQB), :], sem=csem)
        p.dma("sp", v(A["own_pos"], 32), v(A["posb"], 32)[bass.ds(b0v, NQB), :], sem=csem)
        barrier(p)
        p.prefix = "c_"
        emit_l1(nc, p, S, A)
        p.es = es
        p.finish()
    return nc


def prep_fused(inputs, S=SEQ):
    f = lambda a: np.asarray(a, dtype=np.float32)
    x = f(inputs["x"])[:, :S]
    positions = np.asarray(inputs["positions"], dtype=np.int32)[:, :S]
    T = S // 4
    l0a = prep_l0a(x, f(inputs["ev_w_in"]), f(inputs["ev_conv_w"]), f(inputs["ev_conv_b"]), f(inputs["ev_a_log"]),
                   f(inputs["ev_dt_bias"]), f(inputs["ev_d_skip"]), S=S)
    w_in = f(inputs["ev_w_in"])[0]
    w1 = np.ascontiguousarray(np.concatenate([w_in[:, 0:1024], w_in[:, 3104:7200]], axis=1))
    wout = np.ascontiguousarray(f(inputs["ev_w_out"])[0])
    normg = np.ascontiguousarray(f(inputs["ev_norm_g"])[0].reshape(8, 128).T)
    scw = np.ascontiguousarray(f(inputs["ev_sc_conv_w"])[0].reshape(3, 8, 128).transpose(2, 1, 0).reshape(128, 24))
    lng = np.ascontiguousarray(np.broadcast_to(f(inputs["ev_ln_g"])[0][None, :], (128, D)))
    lnb = np.ascontiguousarray(np.broadcast_to(f(inputs["ev_ln_b"])[0][None, :], (128, D)))
    dummy_x1 = np.zeros((2, 16, D), np.float32)
    l1 = prep_l1(None, positions, f(inputs["od_w_in"]), f(inputs["od_q_norm_g"]), f(inputs["od_w_uq"]), f(inputs["od_kv_norm_g"]),
                 f(inputs["od_w_ukv"]), f(inputs["od_pool_w"]), f(inputs["od_pool_scale"]), f(inputs["od_w_out"]),
                 f(inputs["od_ln_g"]), f(inputs["od_ln_b"]), S=S)
    xtoks = [np.ascontiguousarray(x[b]) for b in range(2)]
    maps = []
    for c in range(NCORES):
        b, q = c // 4, c % 4
        m = {"xT": l0a[4 * b]["xT"], "xtok": xtoks[b], "cst": l0a[0]["cst"], "msk": l0a[0]["msk"],
             "w1": w1, "wout": wout, "normg": normg, "scw": scw, "lng": lng, "lnb": lnb,
             "off": np.array([[q * T // 512, (q * T // 512) * 64, q * T // 8, 0]], np.int32)}
        for g in range(4):
            src = l0a[4 * b + g]
            m["wg%d" % g] = src["wg"]
            m["cvw%d" % g] = src["cvw"]
            m["cvb%d" % g] = src["cvb"]
            m["hp%d" % g] = src["hp"]
        lm = l1[c]
        for k in ("posb", "w_cq", "w_kv", "w_kr", "w_g3", "w_uq", "w_ukv", "w_pool", "sm_c", "sel"):
            m[k] = lm[k]
        m["wout_od"] = lm["wout"]
        m["lng_od"] = lm["lng"]
        m["lnb_od"] = lm["lnb"]
        maps.append(m)
    return maps


def kernel(**inputs):
    T = SEQ // 4
    maps = prep_fused(inputs)
    res = run_bass_kernel_spmd(build_fused(), maps, core_ids=list(range(NCORES)))
    out = np.empty((2, SEQ, D), np.float32)
    for c in range(NCORES):
        out[c // 4, (c % 4) * T:(c % 4 + 1) * T, :] = res.results[c]["out"]
    return out
```

```python
import numpy as np
import concourse.bass as bass
import concourse.mybir as mybir
from concourse.bass_utils import run_bass_kernel_spmd
from contextlib import ExitStack

F32 = mybir.dt.float32
BF16 = mybir.dt.bfloat16
I32 = mybir.dt.int32
AF = mybir.ActivationFunctionType
ALU = mybir.AluOpType
AX = mybir.AxisListType

SAME_ENGINE_SYNC = True

D = 1024
SEQ = 16384
NCORES = 8
ALPHA = 4 ** 0.25
EPS = 1e-5


class Tok:
    __slots__ = ("w", "r", "name")

    def __init__(self, name=""):
        self.w = None
        self.r = {}
        self.name = name


class Prog:
    def __init__(self, nc, es):
        self.nc = nc
        self.es = es
        self.es_top = es
        self.eng = {"pe": nc.tensor, "act": nc.scalar, "dve": nc.vector,
                    "pool": nc.gpsimd, "sp": nc.sync}
        self.sems = {}
        self.cnt = {}
        for k in self.eng:
            self.sems[k] = es.enter_context(nc.semaphore("s_" + k))
            self.cnt[k] = 0
        self.seen = {k: {} for k in self.eng}
        self.ndma = 0
        self.out_dma = []
        self.n_ops = 0
        self.uid = 0

    prefix = ""

    def sb(self, name, shape, dt):
        return self.es.enter_context(self.nc.sbuf_tensor(self.prefix + name, list(shape), dt))

    def ps(self, name, shape, dt=F32):
        return self.es.enter_context(self.nc.psum_tensor(self.prefix + name, list(shape), dt))

    def dma_sem(self):
        k = "d%d" % self.ndma
        self.ndma += 1
        self.sems[k] = self.es_top.enter_context(self.nc.semaphore("s_" + k))
        self.cnt[k] = 0
        return k

    def _wait(self, e, deps):
        for (k, v) in deps:
            if k == e:
                if not SAME_ENGINE_SYNC or e == "pe" or e == "sp":
                    continue
            if self.seen[e].get(k, 0) >= v:
                continue
            self.eng[e].wait_ge(self.sems[k], v)
            self.seen[e][k] = v

    def _deps(self, r, w):
        m = {}
        for t in r:
            if t.w is not None:
                k, v = t.w
                if m.get(k, 0) < v:
                    m[k] = v
        for t in w:
            if t.w is not None:
                k, v = t.w
                if m.get(k, 0) < v:
                    m[k] = v
            for k, v in t.r.items():
                if m.get(k, 0) < v:
                    m[k] = v
        return list(m.items())

    def op(self, e, fn, r=(), w=(), multi=False):
        deps = self._deps(r, w)
        att = None
        if e != "pe" and not multi:
            need = [(k, v) for (k, v) in deps
                    if not (k == e and not SAME_ENGINE_SYNC) and self.seen[e].get(k, 0) < v]
            if need:
                att = need[-1]
                self._wait(e, need[:-1])
        else:
            self._wait(e, deps)
        ins = fn(self.eng[e])
        if att is not None:
            ins._wait_ge(self.sems[att[0]], att[1])
            self.seen[e][att[0]] = att[1]
        self.cnt[e] += 1
        v = self.cnt[e]
        ins.then_inc(self.sems[e], 1)
        for t in r:
            if t.r.get(e, 0) < v:
                t.r[e] = v
        for t in w:
            t.w = (e, v)
            t.r = {}
        self.n_ops += 1
        return ins

    def dma(self, q, out, in_, r=(), w=(), sem=None, is_out=False, nowait=False, **kw):
        if not nowait:
            self._wait(q, self._deps(r, w))
        ins = self.eng[q].dma_start(out=out, in_=in_, **kw)
        self.cnt[sem] += 16
        v = self.cnt[sem]
        ins.then_inc(self.sems[sem], 16)
        for t in r:
            if t.r.get(sem, 0) < v:
                t.r[sem] = v
        for t in w:
            t.w = (sem, v)
            t.r = {}
        if is_out:
            self.out_dma.append((sem, v))
        return ins

    def finish(self, e="sp"):
        m = {}
        for k, v in self.out_dma:
            if m.get(k, 0) < v:
                m[k] = v
        for k, v in m.items():
            self.eng[e].wait_ge(self.sems[k], v)


class Ring:
    def __init__(self, p, name, shape, dt, n, space="sb", dma=False):
        self.bufs = []
        for i in range(n):
            t = p.sb("%s%d" % (name, i), shape, dt) if space == "sb" else p.ps("%s%d" % (name, i), shape, dt)
            self.bufs.append((t, Tok(name + str(i)), p.dma_sem() if dma else None))
        self.i = 0

    def next(self):
        b = self.bufs[self.i % len(self.bufs)]
        self.i += 1
        return b


def load_cast_weight(p, src, dst, stage, K, C, engines=("pool", "act"), cw=1024, tok=None):
    n = 0
    for k in range(K):
        for c0 in range(0, C, cw):
            c1 = min(C, c0 + cw)
            st, stok, ssem = stage.next()
            p.dma("sp", st[:, 0:c1 - c0], src[k * 128:(k + 1) * 128, c0:c1], w=[stok], sem=ssem)
            e = engines[n % len(engines)]
            n += 1
            if e == "act":
                p.op(e, lambda en: en.copy(out=dst[:, k, c0:c1], in_=st[:, 0:c1 - c0]), r=[stok], w=[tok])
            else:
                p.op(e, lambda en: en.tensor_copy(out=dst[:, k, c0:c1], in_=st[:, 0:c1 - c0]), r=[stok], w=[tok])


PI = float(np.pi)
TWO_PI = float(2 * np.pi)
C1 = 6.28125
C2 = float(2 * np.pi - 6.28125)
ATT_SCALE = float(96 ** -0.5)
NEG = -30000.0
L0B_TB = 256


def barrier(p):
    for e in p.eng:
        for k, v in p.cnt.items():
            if k != e and v > 0 and p.seen[e].get(k, 0) < v:
                p.eng[e].wait_ge(p.sems[k], v)
                p.seen[e][k] = v


def layer_norm_tail(p, o, t_o, xk, t_xk, rr, r2, junk, st_r, lng_s, lnb_s, t_c, out_ap, post=None, is_out=True):
    r, t_r, _ = rr.next()
    p.op("dve", lambda e: e.scalar_tensor_tensor(out=r[:], in0=xk[:], scalar=float(ALPHA), in1=o[:], op0=ALU.mult, op1=ALU.add),
         r=[t_xk, t_o], w=[t_r])
    st, t_st, _ = st_r.next()
    jk, t_jk, _ = junk.next()
    p.op("act", lambda e: e.activation(out=jk[:], in_=r[:], func=AF.Identity, accum_out=st[:, 0:1]), r=[t_r], w=[t_jk, t_st], multi=True)
    p.op("act", lambda e: e.activation(out=jk[:], in_=r[:], func=AF.Square, accum_out=st[:, 1:2]), r=[t_r], w=[t_jk, t_st], multi=True)
    p.op("dve", lambda e: e.tensor_scalar(out=st[:, 2:3], in0=st[:, 0:1], scalar1=1.0 / D, scalar2=None, op0=ALU.mult), r=[t_st], w=[t_st])
    p.op("dve", lambda e: e.tensor_tensor(out=st[:, 3:4], in0=st[:, 2:3], in1=st[:, 2:3], op=ALU.mult), r=[t_st], w=[t_st])
    p.op("dve", lambda e: e.scalar_tensor_tensor(out=st[:, 4:5], in0=st[:, 1:2], scalar=1.0 / D, in1=st[:, 3:4], op0=ALU.mult, op1=ALU.subtract),
         r=[t_st], w=[t_st])
    p.op("dve", lambda e: e.tensor_scalar(out=st[:, 4:5], in0=st[:, 4:5], scalar1=float(EPS), scalar2=None, op0=ALU.add), r=[t_st], w=[t_st])
    p.op("act", lambda e: e.activation(out=st[:, 5:6], in_=st[:, 4:5], func=AF.Ln), r=[t_st], w=[t_st])
    p.op("act", lambda e: e.activation(out=st[:, 6:7], in_=st[:, 5:6], func=AF.Exp, scale=-0.5), r=[t_st], w=[t_st])
    q, t_q, osem = r2.next()
    p.op("dve", lambda e: e.tensor_scalar(out=q[:], in0=r[:], scalar1=st[:, 2:3], scalar2=st[:, 6:7], op0=ALU.subtract, op1=ALU.mult),
         r=[t_r, t_st], w=[t_q])
    p.op("pool", lambda e: e.tensor_tensor(out=q[:], in0=q[:], in1=lng_s[:], op=ALU.mult), r=[t_c], w=[t_q])
    p.op("pool", lambda e: e.tensor_tensor(out=q[:], in0=q[:], in1=lnb_s[:], op=ALU.add), r=[t_c], w=[t_q])
    if post is not None:
        post(q, t_q)
    p.dma("sp", out_ap, q[:], r=[t_q], w=[], sem=osem, is_out=is_out)


def emit_l0b(nc, p, T, A):
    TB = L0B_TB
    NB = T // TB
    xT, xtok, yaT, w1, wout = A["xT1"], A["xtok"], A["yaT"], A["w1"], A["wout"]
    normg, scw, lng, lnb = A["normg"], A["scw"], A["lng"], A["lnb"]
    out, x1T, cst = A["x1"], A["x1T"], A["cst"]
    x1HL, x1HR = A["x1HL"], A["x1HR"]

    def halo_v(tab, bnd):
        return tab[bnd * 1024:(bnd + 1) * 1024, :].rearrange("(k p) t -> p k t", p=128)
    yaT_v = yaT.rearrange("(k p) t -> p k t", p=128)
    NB5 = T // 512

    def x1T_blk(blk, c0, n):
        return x1T[blk * 1024:(blk + 1) * 1024, c0:c0 + n].rearrange("(k p) t -> p k t", p=128)

    with ExitStack() as es:
        p.es = es
        w1b = p.sb("w1b", [128, 8, 5120], BF16)
        woutb = p.sb("woutb", [128, 16, D], BF16)
        t_w1b, t_woutb = Tok(), Tok()
        stage = Ring(p, "wst", [128, 1024], F32, 1, dma=True)
        normg_s = p.sb("normg_s", [128, 8], F32)
        scw_s = p.sb("scw_s", [128, 24], F32)
        lng_s = p.sb("lng_s", [128, D], F32)
        lnb_s = p.sb("lnb_s", [128, D], F32)
        ones_f = p.sb("ones_f", [128, 128], F32)
        t_c = Tok()
        dc = p.dma_sem()
        p.dma("sp", normg_s[:], normg[:, :], w=[t_c], sem=dc)
        p.dma("sp", scw_s[:], scw[:, :], w=[t_c], sem=dc)
        p.dma("sp", lng_s[:], lng[:, :], w=[t_c], sem=dc)
        p.dma("sp", lnb_s[:], lnb[:, :], w=[t_c], sem=dc)
        t_ones = Tok()
        p.op("dve", lambda e: e.memset(ones_f[:], 1.0), w=[t_ones])
        idf = p.sb("idf", [128, 128], F32)
        p.dma("sp", idf[:], cst[:, 256:384], w=[t_c], sem=dc)
        zt = p.sb("zt", [128, 8, 8], F32)
        t_zt = Tok()
        p.op("dve", lambda e: e.memset(zt[:], 0.0), w=[t_zt])
        zsem = p.dma_sem()
        p.dma("sp", halo_v(x1HL, 0), zt[:], r=[t_zt], sem=zsem)
        p.dma("sp", halo_v(x1HR, NB5), zt[:], r=[t_zt], sem=zsem)
        xtt_r = Ring(p, "xtt", [128, 8, 128], F32, 1, dma=True)
        load_cast_weight(p, w1, w1b, stage, 8, 5120, tok=t_w1b)
        load_cast_weight(p, wout, woutb, stage, 16, D, tok=t_woutb)

        xst = Ring(p, "xst", [128, TB + 2], F32, 3, dma=True)
        xb_r = Ring(p, "xb", [128, 8, TB + 2], BF16, 2)
        yst = Ring(p, "yst", [128, 8, TB], F32, 2, dma=True)
        xtk = Ring(p, "xtk", [128, D], F32, 2, dma=True)
        pp = Ring(p, "pp", [128, 512], F32, 4, space="ps")
        pss = Ring(p, "pss", [128, 512], F32, 1, space="ps")
        ph = Ring(p, "ph", [128, 512], F32, 1, space="ps")
        po = Ring(p, "po", [128, D], F32, 1, space="ps")
        g_all = p.sb("g_all", [128, 8, TB], F32)
        t_g = [Tok() for _ in range(8)]
        cat = p.sb("cat", [128, 16, TB], BF16)
        t_cat = [Tok() for _ in range(16)]
        tmp = Ring(p, "tmp", [128, TB + 2], F32, 8)
        rstd = p.sb("rstd", [128, TB], F32)
        t_rstd = Tok()
        halo = Ring(p, "halo", [128, 4], F32, 2)
        rr = Ring(p, "rr", [128, D], F32, 1)
        r2 = Ring(p, "r2", [128, D], F32, 2, dma=True)
        junk = Ring(p, "junk", [128, D], BF16, 1)
        st_r = Ring(p, "stat", [128, 8], F32, 4)
        for bi in range(NB):
            t0 = bi * TB
            xb, t_xb, _ = xb_r.next()
            for k in range(8):
                st, stok, ssem = xst.next()
                p.dma("sp", st[:], xT[k * 128:(k + 1) * 128, t0:t0 + TB + 2], w=[stok], sem=ssem)
                p.op("pool", lambda e: e.tensor_copy(out=xb[:, k, :], in_=st[:]), r=[stok], w=[t_xb])
            ya, t_ya, ya_sem = yst.next()
            p.dma("sp", ya[:], yaT_v[:, :, t0:t0 + TB], w=[t_ya], sem=ya_sem)

            ss, t_ss, _ = pss.next()
            for j in range(8):
                z, t_z, _ = pp.next()
                for k in range(8):
                    p.op("pe", lambda e: e.matmul(z[:, 0:TB], lhsT=w1b[:, k, j * 128:(j + 1) * 128],
                                                  rhs=xb[:, k, 1:TB + 1], start=(k == 0), stop=(k == 7)),
                         r=[t_w1b, t_xb], w=[t_z])
                sz, t_sz, _ = tmp.next()
                p.op("act", lambda e: e.activation(out=sz[:, 0:TB], in_=z[:, 0:TB], func=AF.Silu), r=[t_z], w=[t_sz])
                p.op("dve", lambda e: e.tensor_tensor(out=g_all[:, j, :], in0=sz[:, 0:TB], in1=ya[:, j, :], op=ALU.mult),
                     r=[t_sz, t_ya], w=[t_g[j]])
                sq, t_sq, _ = tmp.next()
                p.op("act", lambda e: e.activation(out=sq[:, 0:TB], in_=g_all[:, j, :], func=AF.Square), r=[t_g[j]], w=[t_sq])
                p.op("pe", lambda e: e.matmul(ss[:, 0:TB], lhsT=ones_f[:], rhs=sq[:, 0:TB], start=(j == 0), stop=(j == 7)),
                     r=[t_ones, t_sq], w=[t_ss])
            lnv, t_lnv, _ = tmp.next()
            p.op("dve", lambda e: e.tensor_scalar(out=lnv[:, 0:TB], in0=ss[:, 0:TB], scalar1=1.0 / 1024, scalar2=EPS,
                                                  op0=ALU.mult, op1=ALU.add), r=[t_ss], w=[t_lnv])
            p.op("act", lambda e: e.activation(out=lnv[:, 0:TB], in_=lnv[:, 0:TB], func=AF.Ln), r=[t_lnv], w=[t_lnv])
            p.op("act", lambda e: e.activation(out=rstd[:], in_=lnv[:, 0:TB], func=AF.Exp, scale=-0.5), r=[t_lnv], w=[t_rstd])
            for j in range(8):
                p.op("dve", lambda e: e.scalar_tensor_tensor(out=cat[:, j, :], in0=g_all[:, j, :], scalar=normg_s[:, j:j + 1],
                                                             in1=rstd[:], op0=ALU.mult, op1=ALU.mult),
                     r=[t_g[j], t_rstd, t_c], w=[t_cat[j]])

            for j in range(8):
                def proj(grp, lo, n, dst, t_dst, first=True, last=True):
                    for k in range(8):
                        p.op("pe", lambda e: e.matmul(dst, lhsT=w1b[:, k, grp * 1024 + j * 128:grp * 1024 + (j + 1) * 128],
                                                      rhs=xb[:, k, lo:lo + n], start=(k == 0), stop=(k == 7)),
                             r=[t_w1b, t_xb], w=[t_dst])
                cg, t_cg, _ = pp.next()
                proj(2, 0, TB + 2, cg[:, 0:TB + 2], t_cg)
                hh, t_hh, _ = pp.next()
                proj(3, 0, TB + 2, hh[:, 0:TB + 2], t_hh)
                cgs, t_cgs, _ = tmp.next()
                p.op("act", lambda e: e.copy(out=cgs[:, 0:TB + 2], in_=cg[:, 0:TB + 2]), r=[t_cg], w=[t_cgs])
                u, t_u, _ = tmp.next()
                p.op("dve", lambda e: e.tensor_tensor(out=u[:, 0:TB + 2], in0=cgs[:, 0:TB + 2], in1=hh[:, 0:TB + 2], op=ALU.mult),
                     r=[t_cgs, t_hh], w=[t_u])
                c, t_cc, _ = tmp.next()
                p.op("dve", lambda e: e.tensor_scalar(out=c[:, 0:TB], in0=u[:, 0:TB], scalar1=scw_s[:, j * 3:j * 3 + 1], scalar2=None,
                                                      op0=ALU.mult), r=[t_u, t_c], w=[t_cc])
                p.op("dve", lambda e: e.scalar_tensor_tensor(out=c[:, 0:TB], in0=u[:, 1:TB + 1], scalar=scw_s[:, j * 3 + 1:j * 3 + 2],
                                                             in1=c[:, 0:TB], op0=ALU.mult, op1=ALU.add), r=[t_u, t_c], w=[t_cc])
                p.op("dve", lambda e: e.scalar_tensor_tensor(out=c[:, 0:TB], in0=u[:, 2:TB + 2], scalar=scw_s[:, j * 3 + 2:j * 3 + 3],
                                                             in1=c[:, 0:TB], op0=ALU.mult, op1=ALU.add), r=[t_u, t_c], w=[t_cc])
                bg, t_bg, _ = pp.next()
                proj(1, 1, TB, bg[:, 0:TB], t_bg)
                gt, t_gt, _ = pp.next()
                proj(4, 1, TB, gt[:, 0:TB], t_gt)
                sg, t_sg, _ = tmp.next()
                p.op("act", lambda e: e.activation(out=sg[:, 0:TB], in_=gt[:, 0:TB], func=AF.Silu), r=[t_gt], w=[t_sg])
                p.op("dve", lambda e: e.tensor_tensor(out=c[:, 0:TB], in0=c[:, 0:TB], in1=bg[:, 0:TB], op=ALU.mult),
                     r=[t_bg], w=[t_cc])
                p.op("dve", lambda e: e.tensor_tensor(out=cat[:, 8 + j, :], in0=c[:, 0:TB], in1=sg[:, 0:TB], op=ALU.mult),
                     r=[t_cc, t_sg], w=[t_cat[8 + j]])

            for tt in range(TB // 128):
                xk, t_xk, xk_sem = xtk.next()
                p.dma("sp", xk[:], xtok[t0 + tt * 128:t0 + (tt + 1) * 128, :], w=[t_xk], sem=xk_sem)
                o, t_o, _ = po.next()
                for half in range(2):
                    for kc in range(16):
                        p.op("pe", lambda e: e.matmul(o[:, half * 512:(half + 1) * 512], lhsT=cat[:, kc, tt * 128:(tt + 1) * 128],
                                                      rhs=woutb[:, kc, half * 512:(half + 1) * 512], start=(kc == 0), stop=(kc == 15)),
                             r=[t_cat[kc], t_woutb], w=[t_o])
                tok0 = t0 + tt * 128

                def post(q, t_q):
                    xtt, t_xtt, xtt_sem = xtt_r.next()
                    for hf in range(2):
                        tp, t_tp, _ = pp.next()
                        for kq in range(4):
                            kk = hf * 4 + kq
                            p.op("pe", lambda e: e.transpose(tp[:, kq * 128:(kq + 1) * 128], q[:, kk * 128:(kk + 1) * 128], idf[:]),
                                 r=[t_q, t_c], w=[t_tp])
                        p.op("act", lambda e: e.copy(out=xtt[:, hf * 4:(hf + 1) * 4, :], in_=tp[:].rearrange("p (k t) -> p k t", k=4)),
                             r=[t_tp], w=[t_xtt])
                    p.dma("sp", x1T_blk(tok0 // 512 + 1, tok0 % 512, 128), xtt[:], r=[t_xtt], sem=xtt_sem)
                    if tok0 % 512 == 0:
                        p.dma("sp", halo_v(x1HR, tok0 // 512), xtt[:, :, 0:8], r=[t_xtt], sem=xtt_sem)
                    if (tok0 + 128) % 512 == 0:
                        p.dma("sp", halo_v(x1HL, (tok0 + 128) // 512), xtt[:, :, 120:128], r=[t_xtt], sem=xtt_sem)
                layer_norm_tail(p, o, t_o, xk, t_xk, rr, r2, junk, st_r, lng_s, lnb_s, t_c,
                                out[t0 + tt * 128:t0 + (tt + 1) * 128, :], post=post, is_out=False)
        barrier(p)


def emit_l0a(nc, p, S, A):
    BLK = 512
    NBLK = S // BLK
    xT, wg_all, cvw_all, cvb_all, hp_all, cst, msk, yaT = (A["xT"], A["wg"], A["cvw"], A["cvb"], A["hp"], A["cst"], A["msk"], A["yaT"])

    with ExitStack() as es:
        p.es = es
        wgb = p.sb("wgb", [128, 8, 520], BF16)
        t_wgb = Tok()
        stage = Ring(p, "wst", [128, 520], F32, 2, dma=True)
        cvw_s = p.sb("cvw_s", [128, 16], F32)
        cvb_s = p.sb("cvb_s", [128, 4], F32)
        hp_s = p.sb("hp_s", [128, 24], F32)
        cst_s = p.sb("cst_s", [128, 512], F32)
        msk_s = p.sb("msk_s", [128, 1024], F32)
        mskb = p.sb("mskb", [128, 1024], BF16)
        identb = p.sb("identb", [128, 128], BF16)
        a_s = p.sb("a_s", [128, 8], F32)
        bias32 = p.sb("bias32", [128, 2, 4, 4], F32)
        a32 = p.sb("a32", [128, 2, 4, 4], F32)
        dsum = p.sb("dsum", [128, 4], F32)
        t_c = Tok()
        dc = p.dma_sem()
        for dst, src in ((cst_s, cst), (msk_s, msk)):
            p.dma("sp", dst[:], src[:, :], w=[t_c], sem=dc)
        U = cst_s[:, 0:128]
        UT = cst_s[:, 128:256]
        IDF = cst_s[:, 256:384]
        ONES = cst_s[:, 384:512]
        p.op("dve", lambda e: e.tensor_copy(out=mskb[:], in_=msk_s[:]), r=[t_c], w=[t_c])
        p.op("dve", lambda e: e.tensor_copy(out=identb[:], in_=IDF), r=[t_c], w=[t_c])

        xst = Ring(p, "xst", [128, BLK + 3], F32, 3, dma=True)
        xb_r = Ring(p, "xb", [128, 8, BLK + 3], BF16, 2)
        pP = Ring(p, "pP", [128, 512], F32, 2, space="ps")
        pH = Ring(p, "pH", [128, 512], F32, 1, space="ps")
        pT = Ring(p, "pT", [128, 512], F32, 1, space="ps")
        pS = Ring(p, "pS", [128, 512], F32, 1, space="ps")
        pE = Ring(p, "pE", [128, 512], F32, 1, space="ps")
        pC = Ring(p, "pC", [128, 512], F32, 1, space="ps")
        pY = Ring(p, "pY", [128, 512], F32, 1, space="ps")
        pre_r = Ring(p, "pre", [128, BLK + 3], F32, 3)
        cv_r = Ring(p, "cv", [128, BLK], F32, 2)
        xsf_r = Ring(p, "xsf", [128, 3, BLK], F32, 2)
        btb_r = Ring(p, "btb", [128, BLK], BF16, 2)
        ctb_r = Ring(p, "ctb", [128, BLK], BF16, 2)
        hs_r = Ring(p, "hs", [128, 16], F32, 2)
        dtv_r = Ring(p, "dtv", [128, 6, 16], F32, 2)
        sm_r = Ring(p, "sm", [128, 8, 4], F32, 3)
        W_r = Ring(p, "W", [128, 4, 128], F32, 2)
        E_r = Ring(p, "E", [128, 4, 128], F32, 2)
        M_r = Ring(p, "M", [128, 4, 128], BF16, 2)
        btk_r = Ring(p, "btk", [128, 128], BF16, 2)
        xd_r = Ring(p, "xd", [128, 256], BF16, 2)
        xdw_r = Ring(p, "xdw", [128, 256], BF16, 2)
        y_r = Ring(p, "y", [128, 256], F32, 3, dma=True)
        yT_r = Ring(p, "yT", [128, 256], F32, 3, dma=True)
        yt_r = Ring(p, "yt", [128, 256], F32, 3)
        yl_r = Ring(p, "yl", [128, 256], F32, 2, dma=True)
        H = p.sb("H", [128, 256], F32)
        Hb = p.sb("Hb", [128, 256], BF16)
        t_H, t_Hb = Tok(), Tok()

        for g in range(4):
            wg = wg_all[g]
            for dst, src in ((cvw_s, cvw_all[g]), (cvb_s, cvb_all[g]), (hp_s, hp_all[g])):
                p.dma("sp", dst[:], src, w=[t_c], sem=dc)
            t_ya = [Tok() for _ in range(S // 128)]
            p.op("act", lambda e: e.activation(out=a_s[:], in_=hp_s[:, 0:8], func=AF.Exp), r=[t_c], w=[t_c])
            p.op("dve", lambda e: e.tensor_scalar(out=a_s[:], in0=a_s[:], scalar1=-1.0, scalar2=None, op0=ALU.mult), r=[t_c], w=[t_c])
            for k in range(2):
                for c in range(4):
                    p.op("dve", lambda e: e.tensor_copy(out=bias32[:, k, c, :], in_=hp_s[:, 8 + 4 * k:12 + 4 * k]), r=[t_c], w=[t_c])
                    p.op("dve", lambda e: e.tensor_copy(out=a32[:, k, c, :], in_=a_s[:, 4 * k:4 * k + 4]), r=[t_c], w=[t_c])
            p.op("dve", lambda e: e.tensor_tensor(out=dsum[:], in0=hp_s[:, 16:20], in1=hp_s[:, 20:24], op=ALU.add), r=[t_c], w=[t_c])
            load_cast_weight(p, wg, wgb, stage, 8, 520, cw=520, tok=t_wgb)
            for k in range(2):
                p.op("dve", lambda e: e.memset(H[:], 0.0), w=[t_H])
                p.op("pool", lambda e: e.memset(Hb[:], 0.0), w=[t_Hb])
                Tri = U if k == 0 else UT
                blocks = range(NBLK) if k == 0 else range(NBLK - 1, -1, -1)
                for blk in blocks:
                    e0 = blk * BLK
                    xb, t_xb, _ = xb_r.next()
                    for kk in range(8):
                        st, stok, ssem = xst.next()
                        p.dma("sp", st[:], xT[kk * 128:(kk + 1) * 128, e0:e0 + BLK + 3], w=[stok], sem=ssem)
                        p.op("pool", lambda e: e.tensor_copy(out=xb[:, kk, :], in_=st[:]), r=[stok], w=[t_xb])
                    hb, t_hb, _ = pH.next()
                    for c in range(4):
                        for kk in range(8):
                            p.op("pe", lambda e: e.matmul(hb[:, 16 + 4 * c:20 + 4 * c], lhsT=xb[:, kk, 2 + c * 128:2 + (c + 1) * 128],
                                                          rhs=wgb[:, kk, 512 + 4 * k:516 + 4 * k], start=(kk == 0), stop=(kk == 7)),
                                 r=[t_xb, t_wgb], w=[t_hb])
                    dtv, t_dtv, _ = dtv_r.next()
                    V, AV, EE, LL, DT, DA = [dtv[:, i, :] for i in range(6)]
                    b32 = bias32[:, k, :, :].rearrange("p c r -> p (c r)")
                    A32 = a32[:, k, :, :].rearrange("p c r -> p (c r)")
                    p.op("dve", lambda e: e.tensor_tensor(out=V, in0=hb[:, 16:32], in1=b32, op=ALU.add), r=[t_hb, t_c], w=[t_dtv])
                    p.op("dve", lambda e: e.tensor_scalar(out=AV, in0=V, scalar1=-1.0, scalar2=None, op0=ALU.mult), r=[t_dtv], w=[t_dtv])
                    p.op("dve", lambda e: e.tensor_tensor(out=AV, in0=AV, in1=V, op=ALU.max), r=[t_dtv], w=[t_dtv])
                    p.op("act", lambda e: e.activation(out=EE, in_=AV, func=AF.Exp, scale=-1.0), r=[t_dtv], w=[t_dtv])
                    p.op("act", lambda e: e.activation(out=LL, in_=EE, func=AF.Ln, bias=1.0), r=[t_dtv], w=[t_dtv])
                    p.op("dve", lambda e: e.scalar_tensor_tensor(out=DT, in0=V, scalar=0.0, in1=LL, op0=ALU.max, op1=ALU.add), r=[t_dtv], w=[t_dtv])
                    p.op("dve", lambda e: e.tensor_tensor(out=DA, in0=DT, in1=A32, op=ALU.mult), r=[t_dtv, t_c], w=[t_dtv])

                    xsf, t_xsf, _ = xsf_r.next()
                    btb, t_btb, _ = btb_r.next()
                    ctb, t_ctb, _ = ctb_r.next()
                    for m in range(4):
                        P, t_P, _ = pP.next()
                        for kk in range(8):
                            p.op("pe", lambda e: e.matmul(P[:, 0:BLK], lhsT=wgb[:, kk, m * 128:(m + 1) * 128], rhs=xb[:, kk, 0:BLK],
                                                          start=(kk == 0), stop=(kk == 7)), r=[t_xb, t_wgb], w=[t_P])
                        for kk in range(8):
                            p.op("pe", lambda e: e.matmul(hb[:, 4 * m:4 * m + 3], lhsT=wgb[:, kk, m * 128:(m + 1) * 128],
                                                          rhs=xb[:, kk, BLK:BLK + 3], start=(kk == 0), stop=(kk == 7)),
                                 r=[t_xb, t_wgb], w=[t_hb])
                        pre, t_pre, _ = pre_r.next()
                        p.op("act", lambda e: e.copy(out=pre[:, 0:BLK], in_=P[:, 0:BLK]), r=[t_P], w=[t_pre])
                        p.op("act", lambda e: e.copy(out=pre[:, BLK:BLK + 3], in_=hb[:, 4 * m:4 * m + 3]), r=[t_hb], w=[t_pre])
                        cv, t_cv, _ = cv_r.next()
                        p.op("dve", lambda e: e.tensor_scalar(out=cv[:], in0=pre[:, 0:BLK], scalar1=cvw_s[:, 4 * m:4 * m + 1], scalar2=None,
                                                              op0=ALU.mult), r=[t_pre, t_c], w=[t_cv])
                        for tap in range(1, 4):
                            p.op("dve", lambda e: e.scalar_tensor_tensor(out=cv[:], in0=pre[:, tap:tap + BLK],
                                                                         scalar=cvw_s[:, 4 * m + tap:4 * m + tap + 1], in1=cv[:],
                                                                         op0=ALU.mult, op1=ALU.add), r=[t_pre, t_c], w=[t_cv])
                        if m < 3:
                            p.op("act", lambda e: e.activation(out=xsf[:, m, :], in_=cv[:], func=AF.Silu, bias=cvb_s[:, m:m + 1]),
                                 r=[t_cv, t_c], w=[t_xsf])
                            if m == 2:
                                p.op("pool", lambda e: e.tensor_copy(out=btb[:], in_=xsf[:, 2, :]), r=[t_xsf], w=[t_btb])
                        else:
                            p.op("act", lambda e: e.activation(out=ctb[:], in_=cv[:], func=AF.Silu, bias=cvb_s[:, m:m + 1]),
                                 r=[t_cv, t_c], w=[t_ctb])

                    chunks = range(4) if k == 0 else range(3, -1, -1)
                    for c in chunks:
                        gc = blk * 4 + c
                        cs_ = slice(c * 128, (c + 1) * 128)
                        dA = dtv[:, 5, 4 * c:4 * c + 4]
                        dtc = dtv[:, 4, 4 * c:4 * c + 4]
                        T_, t_T, _ = pT.next()
                        for m in range(3):
                            p.op("pe", lambda e: e.transpose(T_[:, m * 128:(m + 1) * 128], xsf[:, m, cs_], IDF), r=[t_xsf, t_c], w=[t_T])
                        btk, t_btk, _ = btk_r.next()
                        p.op("act", lambda e: e.copy(out=btk[:], in_=T_[:, 256:384]), r=[t_T], w=[t_btk])
                        Sm, t_Sm, _ = pS.next()
                        p.op("pe", lambda e: e.matmul(Sm[:, 0:4], lhsT=Tri, rhs=dA, start=True, stop=True), r=[t_dtv, t_c], w=[t_Sm])
                        p.op("pe", lambda e: e.matmul(Sm[:, 4:8], lhsT=ONES, rhs=dA, start=True, stop=True), r=[t_dtv, t_c], w=[t_Sm])
                        sm, t_sm, _ = sm_r.next()
                        CS, TOT, NCS, ECS, DTE, ETOT, DTW, D_ = [sm[:, i, :] for i in range(8)]
                        p.op("act", lambda e: e.copy(out=sm[:, 0:2, :], in_=Sm[:, 0:8].rearrange("p (a r) -> p a r", a=2)), r=[t_Sm], w=[t_sm])
                        p.op("dve", lambda e: e.tensor_scalar(out=NCS, in0=CS, scalar1=-1.0, scalar2=None, op0=ALU.mult), r=[t_sm], w=[t_sm])
                        p.op("act", lambda e: e.activation(out=ECS, in_=CS, func=AF.Exp), r=[t_sm], w=[t_sm])
                        p.op("dve", lambda e: e.tensor_tensor(out=D_, in0=TOT, in1=CS, op=ALU.subtract), r=[t_sm], w=[t_sm])
                        p.op("act", lambda e: e.activation(out=DTE, in_=D_, func=AF.Exp), r=[t_sm], w=[t_sm])
                        p.op("act", lambda e: e.activation(out=ETOT, in_=TOT, func=AF.Exp), r=[t_sm], w=[t_sm])
                        p.op("dve", lambda e: e.tensor_tensor(out=DTW, in0=DTE, in1=dtc, op=ALU.mult), r=[t_sm, t_dtv], w=[t_sm])
                        W, t_W, _ = W_r.next()
                        p.op("dve", lambda e: e.tensor_tensor(out=W[:], in0=Tri.unsqueeze(1).to_broadcast([128, 4, 128]),
                                                              in1=dA.unsqueeze(2).to_broadcast([128, 4, 128]), op=ALU.mult),
                             r=[t_dtv, t_c], w=[t_W])
                        Eb, t_Eb, _ = pE.next()
                        p.op("pe", lambda e: e.matmul(Eb[:], lhsT=ONES, rhs=W[:].rearrange("p r l -> p (r l)"), start=True, stop=False),
                             r=[t_W, t_c], w=[t_Eb])
                        p.op("pe", lambda e: e.matmul(Eb[:], lhsT=identb[:], rhs=mskb[:, 512 * k:512 * (k + 1)], start=False, stop=True),
                             r=[t_c], w=[t_Eb])
                        E, t_E, _ = E_r.next()
                        for r_ in range(4):
                            p.op("act", lambda e: e.activation(out=E[:, r_, :], in_=Eb[:, r_ * 128:(r_ + 1) * 128], func=AF.Exp,
                                                               bias=sm[:, 2, r_:r_ + 1]), r=[t_Eb, t_sm], w=[t_E])
                        Cb, t_Cb, _ = pC.next()
                        p.op("pe", lambda e: e.matmul(Cb[:, 0:128], lhsT=btb[:, cs_], rhs=ctb[:, cs_], start=True, stop=True),
                             r=[t_btb, t_ctb], w=[t_Cb])
                        M, t_M, _ = M_r.next()
                        p.op("dve", lambda e: e.tensor_tensor(out=M[:], in0=E[:], in1=Cb[:, 0:128].unsqueeze(1).to_broadcast([128, 4, 128]),
                                                              op=ALU.mult), r=[t_E, t_Cb], w=[t_M])
                        xd, t_xd, _ = xd_r.next()
                        xdw, t_xdw, _ = xdw_r.next()
                        xs_tok = T_[:, 0:256].rearrange("p (r q) -> p r q", r=4)
                        p.op("dve", lambda e: e.tensor_tensor(out=xd[:].rearrange("p (r q) -> p r q", r=4), in0=xs_tok,
                                                              in1=dtc.unsqueeze(2).to_broadcast([128, 4, 64]), op=ALU.mult),
                             r=[t_T, t_dtv], w=[t_xd])
                        p.op("dve", lambda e: e.tensor_tensor(out=xdw[:].rearrange("p (r q) -> p r q", r=4), in0=xs_tok,
                                                              in1=DTW.unsqueeze(2).to_broadcast([128, 4, 64]), op=ALU.mult),
                             r=[t_T, t_sm], w=[t_xdw])
                        Y, t_Y, _ = pY.next()
                        for r_ in range(4):
                            p.op("pe", lambda e: e.matmul(Y[:, r_ * 64:(r_ + 1) * 64], lhsT=M[:, r_, :], rhs=xd[:, r_ * 64:(r_ + 1) * 64],
                                                          start=True, stop=True), r=[t_M, t_xd], w=[t_Y])
                        p.op("pe", lambda e: e.matmul(Y[:, 256:512], lhsT=ctb[:, cs_], rhs=Hb[:], start=True, stop=True),
                             r=[t_ctb, t_Hb], w=[t_Y])
                        p.op("pe", lambda e: e.matmul(Cb[:, 256:512], lhsT=btk[:], rhs=xdw[:], start=True, stop=True),
                             r=[t_btk, t_xdw], w=[t_Cb])
                        yt, t_yt, _ = yt_r.next()
                        p.op("dve", lambda e: e.tensor_tensor(out=yt[:].rearrange("p (r q) -> p r q", r=4),
                                                              in0=Y[:, 256:512].rearrange("p (r q) -> p r q", r=4),
                                                              in1=ECS.unsqueeze(2).to_broadcast([128, 4, 64]), op=ALU.mult),
                             r=[t_Y, t_sm], w=[t_yt])
                        yo, t_yo, yo_sem = y_r.next()
                        p.op("dve", lambda e: e.tensor_tensor(out=yo[:], in0=yt[:], in1=Y[:, 0:256], op=ALU.add), r=[t_yt, t_Y], w=[t_yo])
                        if k == 0:
                            p.op("dve", lambda e: e.tensor_tensor(out=yt[:].rearrange("p (r q) -> p r q", r=4), in0=xs_tok,
                                                                  in1=dsum[:].unsqueeze(2).to_broadcast([128, 4, 64]), op=ALU.mult),
                                 r=[t_T, t_c], w=[t_yt])
                            p.op("pool", lambda e: e.tensor_tensor(out=yo[:], in0=yo[:], in1=yt[:], op=ALU.add), r=[t_yt], w=[t_yo])
                        ydst = yaT[g * 256:(g + 1) * 256, gc * 128:(gc + 1) * 128].rearrange("(j q) t -> q j t", q=128)
                        T2, t_T2, _ = pT.next()
                        for j in range(2):
                            p.op("pe", lambda e: e.transpose(T2[:, j * 128:(j + 1) * 128], yo[:, j * 128:(j + 1) * 128], IDF), r=[t_yo, t_c], w=[t_T2])
                        yoT, t_yoT, yoT_sem = yT_r.next()
                        if k == 0:
                            p.op("act", lambda e: e.copy(out=yoT[:], in_=T2[:, 0:256]), r=[t_T2], w=[t_yoT])
                        else:
                            yl, t_yl, yl_sem = yl_r.next()
                            p.dma("sp", yl[:].rearrange("q (j t) -> q j t", j=2), ydst, r=[t_ya[gc]], w=[t_yl], sem=yl_sem)
                            p.op("dve", lambda e: e.tensor_tensor(out=yoT[:], in0=T2[:, 0:256], in1=yl[:], op=ALU.add), r=[t_T2, t_yl], w=[t_yoT])
                        p.dma("sp", ydst, yoT[:].rearrange("q (j t) -> q j t", j=2), r=[t_yoT], w=[t_ya[gc]], sem=yoT_sem)
                        p.op("dve", lambda e: e.tensor_tensor(out=H[:].rearrange("p (r q) -> p r q", r=4),
                                                              in0=H[:].rearrange("p (r q) -> p r q", r=4),
                                                              in1=ETOT.unsqueeze(2).to_broadcast([128, 4, 64]), op=ALU.mult),
                             r=[t_sm], w=[t_H])
                        p.op("dve", lambda e: e.tensor_tensor(out=H[:], in0=H[:], in1=Cb[:, 256:512], op=ALU.add), r=[t_Cb], w=[t_H])
                        p.op("act", lambda e: e.copy(out=Hb[:], in_=H[:]), r=[t_H], w=[t_Hb])
        barrier(p)


def l0a_consts():
    t = np.arange(128)
    U = (t[:, None] <= t[None, :]).astype(np.float32)
    UT = (t[:, None] >= t[None, :]).astype(np.float32)
    I = np.eye(128, dtype=np.float32)
    ones = np.ones((128, 128), np.float32)
    cst = np.ascontiguousarray(np.concatenate([U, UT, I, ones], axis=1))
    mf = np.where(t[None, :] < t[:, None], NEG, 0.0).astype(np.float32)
    mb = np.where(t[None, :] > t[:, None], NEG, 0.0).astype(np.float32)
    msk = np.ascontiguousarray(np.concatenate([np.tile(mf, (1, 4)), np.tile(mb, (1, 4))], axis=1))
    return cst, msk


def prep_l0a(x, ev_w_in, ev_conv_w, ev_conv_b, ev_a_log, ev_dt_bias, ev_d_skip, S=SEQ):
    w_in = ev_w_in[0]
    cw = ev_conv_w[0]
    cb = ev_conv_b[0]
    cst, msk = l0a_consts()
    maps = []
    xTs = []
    for b in range(2):
        xe = np.zeros((D, S + 3), np.float32)
        xe[:, 2:S + 2] = x[b, :S, :].T
        xTs.append(xe)
    for c in range(NCORES):
        b, g = c // 4, c % 4
        xs_cols = 1024 + g * 256 + np.arange(256)
        b_cols = 1024 + 1024 + g * 128 + np.arange(128)
        c_cols = 1024 + 1536 + g * 128 + np.arange(128)
        dt_cols = np.concatenate([3072 + k * 16 + 4 * g + np.arange(4) for k in range(2)])
        cols = np.concatenate([xs_cols, b_cols, c_cols, dt_cols])
        wg = np.ascontiguousarray(w_in[:, cols])
        xbc_idx = cols[:512] - 1024
        cvw = np.ascontiguousarray(cw[:, xbc_idx].reshape(4, 4, 128).transpose(2, 1, 0).reshape(128, 16))
        cvb = np.ascontiguousarray(cb[xbc_idx].reshape(4, 128).T)
        hsel = np.concatenate([np.stack([v[0][k, 4 * g:4 * g + 4] for k in range(2)]).reshape(-1)
                               for v in (ev_a_log, ev_dt_bias, ev_d_skip)])
        hp = np.ascontiguousarray(np.broadcast_to(hsel[None, :], (128, 24))).astype(np.float32)
        maps.append({"xT": xTs[b], "wg": wg, "cvw": cvw, "cvb": cvb, "hp": hp, "cst": cst, "msk": msk})
    return maps


def rope_tables(p, posi, t_posi, n, invf, sgn, t_c, tabs, cosd, sind, t_cos, t_sin):
    R = slice(64, 96)
    ang, t_a, _ = tabs.next()
    nf, t_n, _ = tabs.next()
    ni, t_ni, _ = tabs.next()
    mm, t_m, _ = tabs.next()
    A, N, M = ang[R, 0:n], nf[R, 0:n], mm[R, 0:n]
    NI = ni[R, 0:n].bitcast(I32)
    p.op("dve", lambda e: e.tensor_copy(out=A, in_=posi[R, 0:n]), r=[t_posi], w=[t_a])
    p.op("dve", lambda e: e.tensor_scalar(out=A, in0=A, scalar1=invf[R, 0:1], scalar2=None, op0=ALU.mult), r=[t_c], w=[t_a])
    p.op("dve", lambda e: e.tensor_scalar(out=N, in0=A, scalar1=1.0 / TWO_PI, scalar2=None, op0=ALU.mult), r=[t_a], w=[t_n])
    p.op("dve", lambda e: e.tensor_copy(out=NI, in_=N), r=[t_n], w=[t_ni])
    p.op("dve", lambda e: e.tensor_copy(out=N, in_=NI), r=[t_ni], w=[t_n])
    p.op("dve", lambda e: e.scalar_tensor_tensor(out=A, in0=N, scalar=-C1, in1=A, op0=ALU.mult, op1=ALU.add), r=[t_n], w=[t_a])
    p.op("dve", lambda e: e.scalar_tensor_tensor(out=A, in0=N, scalar=-C2, in1=A, op0=ALU.mult, op1=ALU.add), r=[t_n], w=[t_a])

    def wrap(X, t_x):
        p.op("dve", lambda e: e.tensor_scalar(out=M, in0=X, scalar1=PI, scalar2=None, op0=ALU.is_gt), r=[t_x], w=[t_m])
        p.op("dve", lambda e: e.scalar_tensor_tensor(out=X, in0=M, scalar=-TWO_PI, in1=X, op0=ALU.mult, op1=ALU.add), r=[t_m], w=[t_x])
        p.op("dve", lambda e: e.tensor_scalar(out=M, in0=X, scalar1=-PI, scalar2=None, op0=ALU.is_lt), r=[t_x], w=[t_m])
        p.op("dve", lambda e: e.scalar_tensor_tensor(out=X, in0=M, scalar=TWO_PI, in1=X, op0=ALU.mult, op1=ALU.add), r=[t_m], w=[t_x])
    wrap(A, t_a)
    p.op("act", lambda e: e.activation(out=N, in_=A, func=AF.Sin), r=[t_a], w=[t_n])
    p.op("dve", lambda e: e.tensor_scalar(out=sind, in0=N, scalar1=sgn[R, 0:1], scalar2=None, op0=ALU.mult), r=[t_n, t_c], w=[t_sin])
    p.op("dve", lambda e: e.tensor_scalar(out=A, in0=A, scalar1=PI / 2, scalar2=None, op0=ALU.add), r=[t_a], w=[t_a])
    wrap(A, t_a)
    p.op("act", lambda e: e.activation(out=cosd, in_=A, func=AF.Sin), r=[t_a], w=[t_cos])


def emit_l1(nc, p, S, A):
    T = S // 4
    BLK = 512
    NKB = S // BLK
    NQB = T // BLK
    NK128 = S // 128
    x1T, x1, posb, out = A["x1T"], A["x1"], A["posb"], A["out"]
    w_cq, w_kv, w_kr, w_g3, w_uq, w_ukv, w_pool, wout = (A["w_cq"], A["w_kv"], A["w_kr"], A["w_g3"], A["w_uq"], A["w_ukv"],
                                                         A["w_pool"], A["wout_od"])
    sm_c, sel, lng, lnb = A["sm_c"], A["sel"], A["lng_od"], A["lnb_od"]
    oxT, ox1, oHL, oHR, opos = A["own_x1T"], A["own_x1"], A["own_HL"], A["own_HR"], A["own_pos"]

    def xrows_static(blk, kk):
        return x1T[blk * 1024 + kk * 128:blk * 1024 + (kk + 1) * 128, :]

    def xrows_own(blk, kk):
        return oxT[blk * 1024 + kk * 128:blk * 1024 + (kk + 1) * 128, :]
    xtok = ox1

    with ExitStack() as es:
        p.es = es
        smc = p.sb("smc", [128, 80], F32)
        sel_s = p.sb("sel_s", [128, 384], F32)
        t_c = Tok()
        dc = p.dma_sem()
        p.dma("sp", smc[:], sm_c[:, :], w=[t_c], sem=dc)
        p.dma("sp", sel_s[:], sel[:, :], w=[t_c], sem=dc)
        QG, KVG, INVF, SGN, PSC = smc[:, 0:2], smc[:, 2:3], smc[:, 3:4], smc[:, 4:5], smc[:, 5:9]
        CORR = smc[:, 16:80]
        SEL_E, SEL_O, ONES = sel_s[:, 0:128], sel_s[:, 128:256], sel_s[:, 256:384]
        ycT = p.sb("ycT", [128, 4, T], BF16)
        t_yc = [[Tok() for _ in range(NQB)] for _ in range(8)]
        esA = ExitStack()
        p.es = esA
        ckvn = p.sb("ckvn", [128, S], BF16)
        t_ckvn = [Tok() for _ in range(NKB)]
        Kbuf = p.sb("Kbuf", [96, S], BF16)
        t_kn = [Tok() for _ in range(NKB)]
        t_kr = [Tok() for _ in range(NKB)]
        cqn = p.sb("cqn", [128, 2, T], BF16)
        t_cqn = [Tok() for _ in range(NQB)]
        cosq = p.sb("cosq", [96, T], BF16)
        sinq = p.sb("sinq", [96, T], BF16)
        t_cosq = [Tok() for _ in range(NQB)]
        t_sinq = [Tok() for _ in range(NQB)]
        wuqb = p.sb("wuqb", [128, 2, 1536], BF16)
        wukvb = p.sb("wukvb", [128, 1024], BF16)
        t_wuq, t_wukv = Tok(), Tok()

        with ExitStack() as es1:
            p.es = es1
            wst = Ring(p, "wst", [128, 1536], F32, 1, dma=True)
            wcqb = p.sb("wcqb", [128, 8, 256], BF16)
            wkvb = p.sb("wkvb", [128, 8, 128], BF16)
            wkrb = p.sb("wkrb", [128, 8, 192], BF16)
            t_wcq, t_wkv, t_wkr = Tok(), Tok(), Tok()
            load_cast_weight(p, w_cq, wcqb, wst, 8, 256, cw=256, tok=t_wcq)
            load_cast_weight(p, w_kv, wkvb, wst, 8, 128, cw=128, tok=t_wkv)
            load_cast_weight(p, w_kr, wkrb, wst, 8, 192, cw=192, tok=t_wkr)
            for kc in range(2):
                st, stok, ssem = wst.next()
                p.dma("sp", st[:, 0:1536], w_uq[kc * 128:(kc + 1) * 128, :], w=[stok], sem=ssem)
                p.op("dve", lambda e: e.tensor_scalar(out=wuqb[:, kc, :], in0=st[:, 0:1536], scalar1=QG[:, kc:kc + 1], scalar2=None,
                                                      op0=ALU.mult), r=[stok, t_c], w=[t_wuq])
            st, stok, ssem = wst.next()
            p.dma("sp", st[:, 0:1024], w_ukv[:, :], w=[stok], sem=ssem)
            p.op("dve", lambda e: e.tensor_scalar(out=wukvb[:], in0=st[:, 0:1024], scalar1=KVG, scalar2=None, op0=ALU.mult),
                 r=[stok, t_c], w=[t_wukv])

            xst = Ring(p, "xst", [128, BLK], F32, 3, dma=True)
            xb_r = Ring(p, "xb", [128, 8, BLK], BF16, 2)
            pos_r = Ring(p, "posr", [128, BLK], I32, 2, dma=True)
            tabs = Ring(p, "tabs", [128, BLK], F32, 4)
            cs_r = Ring(p, "csr", [128, BLK], F32, 2)
            sn_r = Ring(p, "snr", [128, BLK], F32, 2)
            sq_r = Ring(p, "sqr", [128, BLK], F32, 3)
            t1_r = Ring(p, "t1r", [128, BLK], F32, 2)
            pA = Ring(p, "pA", [128, 512], F32, 5, space="ps")
            pSS = Ring(p, "pSS", [128, 512], F32, 2, space="ps")

            def load_xblock(rows_fn, blk):
                xb, t_xb, _ = xb_r.next()
                for kk in range(8):
                    st, stok, ssem = xst.next()
                    p.dma("sp", st[:], rows_fn(blk, kk), w=[stok], sem=ssem)
                    p.op("pool", lambda e: e.tensor_copy(out=xb[:, kk, :], in_=st[:]), r=[stok], w=[t_xb])
                return xb, t_xb

            def rstd_of(ss, t_ss, nch):
                r_, t_r, _ = sq_r.next()
                p.op("dve", lambda e: e.tensor_scalar(out=r_[:], in0=ss[:], scalar1=1.0 / nch, scalar2=EPS, op0=ALU.mult, op1=ALU.add),
                     r=[t_ss], w=[t_r])
                p.op("act", lambda e: e.activation(out=r_[:], in_=r_[:], func=AF.Ln), r=[t_r], w=[t_r])
                p.op("act", lambda e: e.activation(out=r_[:], in_=r_[:], func=AF.Exp, scale=-0.5), r=[t_r], w=[t_r])
                return r_, t_r

            for kb in range(NKB):
                c0 = kb * BLK
                xb, t_xb = load_xblock(xrows_static, kb + 1)
                pi_, t_pi, pi_sem = pos_r.next()
                p.dma("sp", pi_[64:96, :], posb[kb * 32:(kb + 1) * 32, :], w=[t_pi], sem=pi_sem)
                ck, t_ck, _ = pA.next()
                ka, t_ka, _ = pA.next()
                kbs, t_kbs, _ = pA.next()
                for kk in range(8):
                    p.op("pe", lambda e: e.matmul(ck[:], lhsT=wkvb[:, kk, :], rhs=xb[:, kk, :], start=(kk == 0), stop=(kk == 7)),
                         r=[t_wkv, t_xb], w=[t_ck])
                for kk in range(8):
                    p.op("pe", lambda e: e.matmul(ka[0:96, :], lhsT=wkrb[:, kk, 0:96], rhs=xb[:, kk, :], start=(kk == 0), stop=(kk == 7)),
                         r=[t_wkr, t_xb], w=[t_ka])
                for kk in range(8):
                    p.op("pe", lambda e: e.matmul(kbs[0:96, :], lhsT=wkrb[:, kk, 96:192], rhs=xb[:, kk, :], start=(kk == 0), stop=(kk == 7)),
                         r=[t_wkr, t_xb], w=[t_kbs])
                sq, t_sq, _ = sq_r.next()
                p.op("act", lambda e: e.activation(out=sq[:], in_=ck[:], func=AF.Square), r=[t_ck], w=[t_sq])
                ss, t_ss, _ = pSS.next()
                p.op("pe", lambda e: e.matmul(ss[:], lhsT=ONES, rhs=sq[:], start=True, stop=True), r=[t_sq, t_c], w=[t_ss])
                rs, t_rs = rstd_of(ss, t_ss, 128)
                p.op("dve", lambda e: e.tensor_tensor(out=ckvn[:, c0:c0 + BLK], in0=ck[:], in1=rs[:], op=ALU.mult),
                     r=[t_ck, t_rs], w=[t_ckvn[kb]])
                cs_, t_cs, _ = cs_r.next()
                sn_, t_sn, _ = sn_r.next()
                rope_tables(p, pi_, t_pi, BLK, INVF, SGN, t_c, tabs, cs_[64:96, :], sn_[64:96, :], t_cs, t_sn)
                t1, t_t1, _ = t1_r.next()
                t2, t_t2, _ = t1_r.next()
                p.op("dve", lambda e: e.tensor_tensor(out=t1[64:96, :], in0=ka[64:96, :], in1=cs_[64:96, :], op=ALU.mult),
                     r=[t_ka, t_cs], w=[t_t1])
                p.op("dve", lambda e: e.tensor_tensor(out=t2[64:96, :], in0=kbs[64:96, :], in1=sn_[64:96, :], op=ALU.mult),
                     r=[t_kbs, t_sn], w=[t_t2])
                p.op("pool", lambda e: e.tensor_tensor(out=Kbuf[64:96, c0:c0 + BLK], in0=t1[64:96, :], in1=t2[64:96, :], op=ALU.add),
                     r=[t_t1, t_t2], w=[t_kr[kb]])

            for qb in range(NQB):
                c0 = qb * BLK
                xb, t_xb = load_xblock(xrows_own, qb)
                pi_, t_pi, pi_sem = pos_r.next()
                p.dma("sp", pi_[64:96, :], opos[qb * 32:(qb + 1) * 32, :], w=[t_pi], sem=pi_sem)
                cqs = []
                ss, t_ss, _ = pSS.next()
                for m in range(2):
                    cq, t_cq, _ = pA.next()
                    for kk in range(8):
                        p.op("pe", lambda e: e.matmul(cq[:], lhsT=wcqb[:, kk, m * 128:(m + 1) * 128], rhs=xb[:, kk, :],
                                                      start=(kk == 0), stop=(kk == 7)), r=[t_wcq, t_xb], w=[t_cq])
                    sq, t_sq, _ = sq_r.next()
                    p.op("act", lambda e: e.activation(out=sq[:], in_=cq[:], func=AF.Square), r=[t_cq], w=[t_sq])
                    p.op("pe", lambda e: e.matmul(ss[:], lhsT=ONES, rhs=sq[:], start=(m == 0), stop=(m == 1)), r=[t_sq, t_c], w=[t_ss])
                    cqs.append((cq, t_cq))
                rs, t_rs = rstd_of(ss, t_ss, 256)
                for m in range(2):
                    cq, t_cq = cqs[m]
                    p.op("dve", lambda e: e.tensor_tensor(out=cqn[:, m, c0:c0 + BLK], in0=cq[:], in1=rs[:], op=ALU.mult),
                         r=[t_cq, t_rs], w=[t_cqn[qb]])
                rope_tables(p, pi_, t_pi, BLK, INVF, SGN, t_c, tabs, cosq[64:96, c0:c0 + BLK], sinq[64:96, c0:c0 + BLK],
                            t_cosq[qb], t_sinq[qb])
        barrier(p)

        with ExitStack() as es3:
            p.es = es3
            Vbuf = p.sb("Vbuf", [128, NK128, 128], BF16)
            t_v = [Tok() for _ in range(NK128 // 8 if NK128 >= 8 else 1)]
            VG = min(8, NK128)
            Q_r = Ring(p, "Q", [96, T], BF16, 2)
            tq_r = Ring(p, "tq", [96, BLK], F32, 4)
            P_r = Ring(p, "P", [128, BLK], BF16, 3)
            osb_r = Ring(p, "osb", [128, BLK], F32, 2)
            rden_r = Ring(p, "rden", [128, BLK], F32, 2)
            pS = Ring(p, "pS", [128, 512], F32, 3, space="ps")
            pO = Ring(p, "pO", [128, 512], F32, 2, space="ps")
            pD = Ring(p, "pD", [128, 512], F32, 1, space="ps")
            pB = Ring(p, "pB", [128, 512], F32, 2, space="ps")
            for h in range(8):
                odd = h % 2
                voff = 64 * odd
                Q, _, _ = Q_r.next()
                t_Q = [Tok() for _ in range(NQB)]
                for qb in range(NQB):
                    c0 = qb * BLK
                    qa, t_qa, _ = pB.next()
                    qs, t_qs, _ = pB.next()
                    for kc in range(2):
                        p.op("pe", lambda e: e.matmul(qa[0:96, :], lhsT=wuqb[:, kc, h * 96:(h + 1) * 96], rhs=cqn[:, kc, c0:c0 + BLK],
                                                      start=(kc == 0), stop=(kc == 1)), r=[t_wuq, t_cqn[qb]], w=[t_qa])
                    for kc in range(2):
                        p.op("pe", lambda e: e.matmul(qs[0:96, :], lhsT=wuqb[:, kc, 768 + h * 96:768 + (h + 1) * 96],
                                                      rhs=cqn[:, kc, c0:c0 + BLK], start=(kc == 0), stop=(kc == 1)),
                             r=[t_wuq, t_cqn[qb]], w=[t_qs])
                    p.op("dve", lambda e: e.tensor_copy(out=Q[0:64, c0:c0 + BLK], in_=qa[0:64, :]), r=[t_qa], w=[t_Q[qb]])
                    t1, t_t1, _ = tq_r.next()
                    t2, t_t2, _ = tq_r.next()
                    p.op("dve", lambda e: e.tensor_tensor(out=t1[64:96, :], in0=qa[64:96, :], in1=cosq[64:96, c0:c0 + BLK], op=ALU.mult),
                         r=[t_qa, t_cosq[qb]], w=[t_t1])
                    p.op("dve", lambda e: e.tensor_tensor(out=t2[64:96, :], in0=qs[64:96, :], in1=sinq[64:96, c0:c0 + BLK], op=ALU.mult),
                         r=[t_qs, t_sinq[qb]], w=[t_t2])
                    p.op("pool", lambda e: e.tensor_tensor(out=Q[64:96, c0:c0 + BLK], in0=t1[64:96, :], in1=t2[64:96, :], op=ALU.add),
                         r=[t_t1, t_t2], w=[t_Q[qb]])
                for kb in range(NKB):
                    c0 = kb * BLK
                    kp, t_kp, _ = pB.next()
                    p.op("pe", lambda e: e.matmul(kp[0:64, :], lhsT=wukvb[:, h * 128:h * 128 + 64], rhs=ckvn[:, c0:c0 + BLK],
                                                  start=True, stop=True), r=[t_wukv, t_ckvn[kb]], w=[t_kp])
                    p.op("dve", lambda e: e.tensor_copy(out=Kbuf[0:64, c0:c0 + BLK], in_=kp[0:64, :]), r=[t_kp], w=[t_kn[kb]])
                for g in range(len(t_v)):
                    vp, t_vp, _ = pB.next()
                    for j in range(VG):
                        k128 = g * VG + j
                        p.op("pe", lambda e: e.matmul(vp[:, j * 64:(j + 1) * 64], lhsT=ckvn[:, k128 * 128:(k128 + 1) * 128],
                                                      rhs=wukvb[:, h * 128 + 64:h * 128 + 128], start=True, stop=True),
                             r=[t_wukv, t_ckvn[k128 // 4]], w=[t_vp])
                    vs = Vbuf[:, g * VG:(g + 1) * VG, :]
                    p.op("pool", lambda e: e.memset(vs[:, :, 64 - voff:128 - voff], 0.0), w=[t_v[g]])
                    p.op("pool", lambda e: e.memset(vs[:, :, 64 - voff:65 - voff], 1.0), w=[t_v[g]])
                    p.op("dve", lambda e: e.tensor_copy(out=vs[:, :, voff:voff + 64], in_=vp[:, 0:VG * 64].rearrange("p (j v) -> p j v", v=64)),
                         r=[t_vp], w=[t_v[g]])
                MV = 128 if odd else 65
                for qb in range(NQB):
                    q0 = qb * BLK
                    O, t_O, _ = pO.next()
                    Sq = {}

                    def issue_S(k128):
                        Sx, t_S, _ = pS.next()
                        kb = k128 // 4
                        p.op("pe", lambda e: e.matmul(Sx[:], lhsT=Kbuf[0:96, k128 * 128:(k128 + 1) * 128], rhs=Q[0:96, q0:q0 + BLK],
                                                      start=True, stop=True), r=[t_kn[kb], t_kr[kb], t_Q[qb]], w=[t_S])
                        Sq[k128] = (Sx, t_S)
                    for k128 in range(min(2, NK128)):
                        issue_S(k128)
                    for k128 in range(NK128):
                        Sx, t_S = Sq.pop(k128)
                        Pt, t_P, _ = P_r.next()
                        p.op("act", lambda e: e.activation(out=Pt[:], in_=Sx[:], func=AF.Exp, scale=ATT_SCALE), r=[t_S], w=[t_P])
                        if k128 + 2 < NK128:
                            issue_S(k128 + 2)
                        p.op("pe", lambda e: e.matmul(O[0:MV, :], lhsT=Vbuf[:, k128, 0:MV], rhs=Pt[:], start=(k128 == 0),
                                                      stop=(k128 == NK128 - 1)), r=[t_v[k128 // VG], t_P], w=[t_O])
                    osb, t_osb, _ = osb_r.next()
                    p.op("dve", lambda e: e.tensor_copy(out=osb[0:MV, :], in_=O[0:MV, :]), r=[t_O], w=[t_osb])
                    Dn, t_D, _ = pD.next()
                    if odd:
                        p.op("pe", lambda e: e.matmul(Dn[:], lhsT=SEL_O, rhs=osb[:], start=True, stop=True), r=[t_osb, t_c], w=[t_D])
                    else:
                        p.op("pe", lambda e: e.matmul(Dn[0:64, :], lhsT=SEL_E[0:65, 0:64], rhs=osb[0:65, :], start=True, stop=True),
                             r=[t_osb, t_c], w=[t_D])
                    rd, t_rd, _ = rden_r.next()
                    PR = slice(voff, voff + 64)
                    p.op("dve", lambda e: e.reciprocal(out=rd[PR, :], in_=Dn[PR, :]), r=[t_D], w=[t_rd])
                    p.op("dve", lambda e: e.tensor_tensor(out=ycT[PR, h // 2, q0:q0 + BLK], in0=osb[PR, :], in1=rd[PR, :], op=ALU.mult),
                         r=[t_osb, t_rd], w=[t_yc[h][qb]])
        barrier(p)
        esA.close()

        with ExitStack() as es4:
            p.es = es4
            wst = Ring(p, "wst4", [128, 1024], F32, 2, dma=True)
            wg3b = p.sb("wg3b", [128, 8, 1536], BF16)
            woutb = p.sb("woutb", [128, 8, D], BF16)
            wpoolb = p.sb("wpoolb", [128, 512], BF16)
            lng_s = p.sb("lng_s", [128, D], F32)
            lnb_s = p.sb("lnb_s", [128, D], F32)
            t_wg3, t_wout, t_wpool, t_ln = Tok(), Tok(), Tok(), Tok()
            dl = p.dma_sem()
            p.dma("sp", lng_s[:], lng[:, :], w=[t_ln], sem=dl)
            p.dma("sp", lnb_s[:], lnb[:, :], w=[t_ln], sem=dl)
            load_cast_weight(p, w_g3, wg3b, wst, 8, 1536, cw=768, tok=t_wg3)
            load_cast_weight(p, wout, woutb, wst, 8, D, tok=t_wout)
            st, stok, ssem = wst.next()
            p.dma("sp", st[:, 0:512], w_pool[:, :], w=[stok], sem=ssem)
            p.op("dve", lambda e: e.tensor_copy(out=wpoolb[:], in_=st[:, 0:512]), r=[stok], w=[t_wpool])
            xst = Ring(p, "xst4", [128, BLK + 16], F32, 3, dma=True)
            xb_r = Ring(p, "xb4", [128, 8, BLK + 16], BF16, 2)
            xtk = Ring(p, "xtk", [128, D], F32, 2, dma=True)
            cat = p.sb("cat", [128, 8, BLK], BF16)
            t_cat = [Tok() for _ in range(8)]
            tmp = Ring(p, "tmp4", [128, BLK + 16], F32, 10)
            hl_r = Ring(p, "hl4", [128, 16], F32, 2)
            pl_r = Ring(p, "pl4", [128, BLK], BF16, 2)
            rr = Ring(p, "rr", [128, D], F32, 2)
            r2 = Ring(p, "r2", [128, D], F32, 2, dma=True)
            junk = Ring(p, "junk", [128, D], BF16, 1)
            st_r = Ring(p, "stat", [128, 8], F32, 4)
            pp = Ring(p, "pp4", [128, 512], F32, 5, space="ps")
            ph = Ring(p, "ph4", [128, 512], F32, 1, space="ps")
            po = Ring(p, "po4", [128, D], F32, 1, space="ps")
            WIN = (2, 4, 8, 16)
            for qb in range(NQB):
                c0 = qb * BLK
                xb, t_xb, _ = xb_r.next()
                for kk in range(8):
                    st, stok, ssem = xst.next()
                    rws = slice(qb * 1024 + kk * 128, qb * 1024 + (kk + 1) * 128)
                    p.dma("sp", st[:, 8:BLK + 8], oxT[rws, :], w=[stok], sem=ssem)
                    p.dma("sp", st[:, 0:8], oHL[rws, :], w=[stok], sem=ssem, nowait=True)
                    p.dma("sp", st[:, BLK + 8:BLK + 16], oHR[rws, :], w=[stok], sem=ssem, nowait=True)
                    p.op("pool", lambda e: e.tensor_copy(out=xb[:, kk, :], in_=st[:]), r=[stok], w=[t_xb])

                def proj(col0, lo, n, dst, t_dst):
                    for kk in range(8):
                        p.op("pe", lambda e: e.matmul(dst, lhsT=wg3b[:, kk, col0:col0 + 128], rhs=xb[:, kk, lo:lo + n],
                                                      start=(kk == 0), stop=(kk == 7)), r=[t_wg3, t_xb], w=[t_dst])
                for j in range(4):
                    gc, t_gc, _ = pp.next()
                    proj(j * 128, 8, BLK, gc[:], t_gc)
                    sg, t_sg, _ = tmp.next()
                    p.op("act", lambda e: e.activation(out=sg[:, 0:BLK], in_=gc[:], func=AF.Silu), r=[t_gc], w=[t_sg])
                    p.op("dve", lambda e: e.tensor_tensor(out=cat[:, j, :], in0=sg[:, 0:BLK], in1=ycT[:, j, c0:c0 + BLK], op=ALU.mult),
                         r=[t_sg, t_yc[2 * j][qb], t_yc[2 * j + 1][qb]], w=[t_cat[j]])
                for gi in range(4):
                    w = WIN[gi]
                    um, t_um, _ = pp.next()
                    proj(512 + gi * 128, 8, BLK, um[:], t_um)
                    hl, t_hl, _ = ph.next()
                    proj(512 + gi * 128, 0, 8, hl[:, 0:8], t_hl)
                    proj(512 + gi * 128, BLK + 8, 8, hl[:, 8:16], t_hl)
                    u, t_u, _ = tmp.next()
                    p.op("act", lambda e: e.copy(out=u[:, 8:BLK + 8], in_=um[:]), r=[t_um], w=[t_u])
                    p.op("act", lambda e: e.copy(out=u[:, 0:8], in_=hl[:, 0:8]), r=[t_hl], w=[t_u])
                    p.op("act", lambda e: e.copy(out=u[:, BLK + 8:BLK + 16], in_=hl[:, 8:16]), r=[t_hl], w=[t_u])
                    cur, t_cur, n, width = u, t_u, BLK + 16, 1
                    while width < w:
                        nxt, t_nxt, _ = tmp.next()
                        n2 = n - width
                        p.op("dve", lambda e: e.tensor_tensor(out=nxt[:, 0:n2], in0=cur[:, 0:n2], in1=cur[:, width:width + n2], op=ALU.add),
                             r=[t_cur], w=[t_nxt])
                        cur, t_cur, n, width = nxt, t_nxt, n2, width * 2
                    s0 = 8 - w // 2
                    pm, t_pm, _ = tmp.next()
                    p.op("dve", lambda e: e.tensor_scalar(out=pm[:, 0:BLK], in0=cur[:, s0:s0 + BLK], scalar1=1.0 / w, scalar2=None, op0=ALU.mult),
                         r=[t_cur], w=[t_pm])
                    if qb == 0:
                        p.op("dve", lambda e: e.tensor_tensor(out=pm[:, 0:8], in0=pm[:, 0:8], in1=CORR[:, gi * 16:gi * 16 + 8], op=ALU.mult),
                             r=[t_c], w=[t_pm])
                    if qb == NQB - 1:
                        p.op("dve", lambda e: e.tensor_tensor(out=pm[:, BLK - 8:BLK], in0=pm[:, BLK - 8:BLK],
                                                              in1=CORR[:, gi * 16 + 8:gi * 16 + 16], op=ALU.mult), r=[t_c], w=[t_pm])
                    pl, t_pl, _ = pl_r.next()
                    p.op("dve", lambda e: e.tensor_tensor(out=pl[:], in0=pm[:, 0:BLK], in1=u[:, 8:BLK + 8], op=ALU.subtract),
                         r=[t_pm, t_u], w=[t_pl])
                    yd, t_yd, _ = pp.next()
                    p.op("pe", lambda e: e.matmul(yd[:], lhsT=wpoolb[:, gi * 128:(gi + 1) * 128], rhs=pl[:], start=True, stop=True),
                         r=[t_wpool, t_pl], w=[t_yd])
                    gd, t_gd, _ = pp.next()
                    proj(1024 + gi * 128, 8, BLK, gd[:], t_gd)
                    sg, t_sg, _ = tmp.next()
                    p.op("act", lambda e: e.activation(out=sg[:, 0:BLK], in_=gd[:], func=AF.Silu), r=[t_gd], w=[t_sg])
                    p.op("dve", lambda e: e.scalar_tensor_tensor(out=cat[:, 4 + gi, :], in0=yd[:], scalar=PSC[:, gi:gi + 1], in1=sg[:, 0:BLK],
                                                                 op0=ALU.mult, op1=ALU.mult), r=[t_yd, t_sg, t_c], w=[t_cat[4 + gi]])
                for tt in range(BLK // 128):
                    xk, t_xk, xk_sem = xtk.next()
                    p.dma("sp", xk[:], xtok[c0 + tt * 128:c0 + (tt + 1) * 128, :], w=[t_xk], sem=xk_sem)
                    o, t_o, _ = po.next()
                    for half in range(2):
                        for kc in range(8):
                            p.op("pe", lambda e: e.matmul(o[:, half * 512:(half + 1) * 512], lhsT=cat[:, kc, tt * 128:(tt + 1) * 128],
                                                          rhs=woutb[:, kc, half * 512:(half + 1) * 512], start=(kc == 0), stop=(kc == 7)),
                                 r=[t_cat[kc], t_wout], w=[t_o])
                    layer_norm_tail(p, o, t_o, xk, t_xk, rr, r2, junk, st_r, lng_s, lnb_s, t_ln,
                                    out[c0 + tt * 128:c0 + (tt + 1) * 128, :])
        barrier(p)


def prep_l1(x1, positions, od_w_in, od_q_norm_g, od_w_uq, od_kv_norm_g, od_w_ukv, od_pool_w, od_pool_scale, od_w_out,
            od_ln_g, od_ln_b, S=SEQ):
    T = S // 4
    w_in = od_w_in[0]
    w_cq = np.ascontiguousarray(w_in[:, 0:256])
    w_kv = np.ascontiguousarray(w_in[:, 256:384])
    kr = w_in[:, 384:416]
    krs = np.concatenate([kr[:, 16:32], kr[:, 0:16]], axis=1)
    z64 = np.zeros((D, 64), np.float32)
    w_kr = np.ascontiguousarray(np.concatenate([z64, kr, z64, krs], axis=1))
    w_g3 = np.ascontiguousarray(w_in[:, 416:1952])
    uq = od_w_uq[0].reshape(256, 8, 96)
    uqs = np.zeros_like(uq)
    uqs[:, :, 64:80] = uq[:, :, 80:96]
    uqs[:, :, 80:96] = uq[:, :, 64:80]
    w_uq = np.ascontiguousarray(np.concatenate([uq.reshape(256, 768), uqs.reshape(256, 768)], axis=1))
    w_ukv = np.ascontiguousarray(od_w_ukv[0])
    w_pool = np.ascontiguousarray(od_pool_w[0].transpose(1, 0, 2).reshape(128, 512))
    wout = np.ascontiguousarray(od_w_out[0])
    half = 16
    inv_freq = (np.float32(10000.0) ** (-np.arange(half, dtype=np.float32) / np.float32(half))).astype(np.float32)
    sel = np.zeros((128, 384), np.float32)
    sel[64, 0:64] = 1.0
    sel[0, 128 + 64:128 + 128] = 1.0
    sel[:, 256:384] = 1.0
    lng = np.ascontiguousarray(np.broadcast_to(od_ln_g[0][None, :], (128, D)))
    lnb = np.ascontiguousarray(np.broadcast_to(od_ln_b[0][None, :], (128, D)))
    maps = []
    xTbs = [np.ascontiguousarray(x1[b, :S, :].T) for b in range(2)] if x1 is not None else [None, None]
    posbs = [np.ascontiguousarray(np.broadcast_to(positions[b, :S].reshape(S // 512, 1, 512), (S // 512, 32, 512))
                                  .reshape((S // 512) * 32, 512)).astype(np.int32) for b in range(2)]
    for c in range(NCORES):
        b, s0 = c // 4, (c % 4) * T
        xe = None
        if x1 is not None:
            xe = np.zeros((D, T + 16), np.float32)
            lo, hi = max(0, s0 - 8), min(S, s0 + T + 8)
            xe[:, lo - (s0 - 8):hi - (s0 - 8)] = x1[b, lo:hi, :].T
        smc = np.zeros((128, 80), np.float32)
        smc[:, 0:2] = od_q_norm_g[0].reshape(2, 128).T
        smc[:, 2] = od_kv_norm_g[0]
        smc[64:80, 3] = inv_freq
        smc[80:96, 3] = inv_freq
        smc[64:80, 4] = -1.0
        smc[80:96, 4] = 1.0
        smc[:, 5:9] = od_pool_scale[0].reshape(4, 128).T
        for gi, w in enumerate((2, 4, 8, 16)):
            for j in range(8):
                for side, t in ((0, s0 + j), (1, s0 + T - 8 + j)):
                    lo_ = min(max(t - w // 2, 0), S)
                    hi_ = min(max(t + w - w // 2, 0), S)
                    smc[:, 16 + gi * 16 + side * 8 + j] = np.float32(w) / np.float32(hi_ - lo_)
        maps.append({"xTb": xTbs[b], "xTo": xe, "xtok": (np.ascontiguousarray(x1[b, s0:s0 + T, :]) if x1 is not None else None),
                     "posb": posbs[b],
                     "w_cq": w_cq, "w_kv": w_kv, "w_kr": w_kr, "w_g3": w_g3, "w_uq": w_uq, "w_ukv": w_ukv,
                     "w_pool": w_pool, "wout": wout, "sm_c": smc, "sel": sel, "lng": lng, "lnb": lnb})
    return maps


def build_fused(S=SEQ):
    T = S // 4
    nc = bass.Bass("TRN2", target_bir_lowering=False)

    def inp(name, shape, dt=F32):
        return nc.dram_tensor(name, list(shape), dt, kind="ExternalInput").ap()
    A = {}
    A["xT"] = inp("xT", [D, S + 3])
    A["xT1"] = A["xT"][:, 1:S + 3]
    A["xtok"] = inp("xtok", [S, D])
    A["wg"] = [inp("wg%d" % g, [D, 520]) for g in range(4)]
    A["cvw"] = [inp("cvw%d" % g, [128, 16])[:, :] for g in range(4)]
    A["cvb"] = [inp("cvb%d" % g, [128, 4])[:, :] for g in range(4)]
    A["hp"] = [inp("hp%d" % g, [128, 24])[:, :] for g in range(4)]
    A["cst"] = inp("cst", [128, 512])
    A["msk"] = inp("msk", [128, 1024])
    A["w1"] = inp("w1", [D, 5120])
    A["wout"] = inp("wout", [2048, D])
    A["normg"] = inp("normg", [128, 8])
    A["scw"] = inp("scw", [128, 24])
    A["lng"] = inp("lng", [128, D])
    A["lnb"] = inp("lnb", [128, D])
    A["posb"] = inp("posb", [(S // 512) * 32, 512], I32)
    A["w_cq"] = inp("w_cq", [D, 256])
    A["w_kv"] = inp("w_kv", [D, 128])
    A["w_kr"] = inp("w_kr", [D, 192])
    A["w_g3"] = inp("w_g3", [D, 1536])
    A["w_uq"] = inp("w_uq", [256, 1536])
    A["w_ukv"] = inp("w_ukv", [128, 1024])
    A["w_pool"] = inp("w_pool", [128, 512])
    A["wout_od"] = inp("wout_od", [D, D])
    A["sm_c"] = inp("sm_c", [128, 80])
    A["sel"] = inp("sel", [128, 384])
    A["lng_od"] = inp("lng_od", [128, D])
    A["lnb_od"] = inp("lnb_od", [128, D])
    off = inp("off", [1, 4], I32)
    A["out"] = nc.dram_tensor("out", [T, D], F32, kind="ExternalOutput").ap()
    A["yaT"] = nc.dram_tensor("yaT_s", [D, S], F32).ap()
    A["x1"] = nc.dram_tensor("x1_s", [S, D], F32).ap()
    A["x1T"] = nc.dram_tensor("x1T_s", [(S // 512 + 2) * 1024, 512], F32).ap()
    A["x1HL"] = nc.dram_tensor("x1HL_s", [(S // 512 + 1) * 1024, 8], F32).ap()
    A["x1HR"] = nc.dram_tensor("x1HR_s", [(S // 512 + 1) * 1024, 8], F32).ap()
    A["own_x1T"] = nc.dram_tensor("own_x1T_s", [(T // 512) * 1024, 512], F32).ap()
    A["own_x1"] = nc.dram_tensor("own_x1_s", [T, D], F32).ap()
    A["own_HL"] = nc.dram_tensor("own_HL_s", [(T // 512) * 1024, 8], F32).ap()
    A["own_HR"] = nc.dram_tensor("own_HR_s", [(T // 512) * 1024, 8], F32).ap()
    A["own_pos"] = nc.dram_tensor("own_pos_s", [(T // 512) * 32, 512], I32).ap()

    with ExitStack() as es:
        p = Prog(nc, es)
        regs = [es.enter_context(nc.sync.register("offr%d" % i)) for i in range(3)]
        for i in range(3):
            nc.sync.reg_load(regs[i], off[0:1, i:i + 1])
        NB, NQB = S // 512, T // 512
        b0v = nc.sync.snap(regs[0], min_val=0, max_val=NB - NQB)
        u0v = nc.sync.snap(regs[1], min_val=0, max_val=(NB - NQB) * 64)
        t0v = nc.sync.snap(regs[2], min_val=0, max_val=(S - T) // 8)
        p.prefix = "a_"
        emit_l0a(nc, p, S, A)
        p.prefix = "b_"
        emit_l0b(nc, p, S, A)
        csem = p.dma_sem()
        v = lambda ap, b: ap.rearrange("(a b) t -> a (b t)", b=b)
        p.dma("sp", v(A["own_x1T"], 16), v(A["x1T"], 16)[bass.ds(u0v + 64, NQB * 64), :], sem=csem)
        p.dma("sp", v(A["own_x1"], 8), v(A["x1"], 8)[bass.ds(t0v, T // 8), :], sem=csem)
        p.dma("sp", v(A["own_HL"], 1024), v(A["x1HL"], 1024)[bass.ds(b0v, NQB), :], sem=csem)
        p.dma("sp", v(A["own_HR"], 1024), v(A["x1HR"], 1024)[bass.ds(b0v + 1, NQB), :], sem=csem)
        p.dma("sp", v(A["own_pos"], 32), v(A["posb"], 32)[bass.ds(b0v, NQB), :], sem=csem)
        barrier(p)
        p.prefix = "c_"
        emit_l1(nc, p, S, A)
        p.es = es
        p.finish()
    return nc


def prep_fused(inputs, S=SEQ):
    f = lambda a: np.asarray(a, dtype=np.float32)
    x = f(inputs["x"])[:, :S]
    positions = np.asarray(inputs["positions"], dtype=np.int32)[:, :S]
    T = S // 4
    l0a = prep_l0a(x, f(inputs["ev_w_in"]), f(inputs["ev_conv_w"]), f(inputs["ev_conv_b"]), f(inputs["ev_a_log"]),
                   f(inputs["ev_dt_bias"]), f(inputs["ev_d_skip"]), S=S)
    w_in = f(inputs["ev_w_in"])[0]
    w1 = np.ascontiguousarray(np.concatenate([w_in[:, 0:1024], w_in[:, 3104:7200]], axis=1))
    wout = np.ascontiguousarray(f(inputs["ev_w_out"])[0])
    normg = np.ascontiguousarray(f(inputs["ev_norm_g"])[0].reshape(8, 128).T)
    scw = np.ascontiguousarray(f(inputs["ev_sc_conv_w"])[0].reshape(3, 8, 128).transpose(2, 1, 0).reshape(128, 24))
    lng = np.ascontiguousarray(np.broadcast_to(f(inputs["ev_ln_g"])[0][None, :], (128, D)))
    lnb = np.ascontiguousarray(np.broadcast_to(f(inputs["ev_ln_b"])[0][None, :], (128, D)))
    dummy_x1 = np.zeros((2, 16, D), np.float32)
    l1 = prep_l1(None, positions, f(inputs["od_w_in"]), f(inputs["od_q_norm_g"]), f(inputs["od_w_uq"]), f(inputs["od_kv_norm_g"]),
                 f(inputs["od_w_ukv"]), f(inputs["od_pool_w"]), f(inputs["od_pool_scale"]), f(inputs["od_w_out"]),
                 f(inputs["od_ln_g"]), f(inputs["od_ln_b"]), S=S)
    xtoks = [np.ascontiguousarray(x[b]) for b in range(2)]
    maps = []
    for c in range(NCORES):
        b, q = c // 4, c % 4
        m = {"xT": l0a[4 * b]["xT"], "xtok": xtoks[b], "cst": l0a[0]["cst"], "msk": l0a[0]["msk"],
             "w1": w1, "wout": wout, "normg": normg, "scw": scw, "lng": lng, "lnb": lnb,
             "off": np.array([[q * T // 512, (q * T // 512) * 64, q * T // 8, 0]], np.int32)}
        for g in range(4):
            src = l0a[4 * b + g]
            m["wg%d" % g] = src["wg"]
            m["cvw%d" % g] = src["cvw"]
            m["cvb%d" % g] = src["cvb"]
            m["hp%d" % g] = src["hp"]
        lm = l1[c]
        for k in ("posb", "w_cq", "w_kv", "w_kr", "w_g3", "w_uq", "w_ukv", "w_pool", "sm_c", "sel"):
            m[k] = lm[k]
        m["wout_od"] = lm["wout"]
        m["lng_od"] = lm["lng"]
        m["lnb_od"] = lm["lnb"]
        maps.append(m)
    return maps


def kernel(**inputs):
    T = SEQ // 4
    maps = prep_fused(inputs)
    res = run_bass_kernel_spmd(build_fused(), maps, core_ids=list(range(NCORES)))
    out = np.empty((2, SEQ, D), np.float32)
    for c in range(NCORES):
        out[c // 4, (c % 4) * T:(c % 4 + 1) * T, :] = res.results[c]["out"]
    return out
```

```python
import numpy as np
import concourse.bass as bass
import concourse.mybir as mybir
from concourse.bass_utils import run_bass_kernel_spmd
from contextlib import ExitStack

F32 = mybir.dt.float32
BF16 = mybir.dt.bfloat16
I32 = mybir.dt.int32
AF = mybir.ActivationFunctionType
ALU = mybir.AluOpType
AX = mybir.AxisListType

SAME_ENGINE_SYNC = True

D = 1024
SEQ = 16384
NCORES = 8
ALPHA = 4 ** 0.25
EPS = 1e-5


class Tok:
    __slots__ = ("w", "r", "name")

    def __init__(self, name=""):
        self.w = None
        self.r = {}
        self.name = name


class Prog:
    def __init__(self, nc, es):
        self.nc = nc
        self.es = es
        self.es_top = es
        self.eng = {"pe": nc.tensor, "act": nc.scalar, "dve": nc.vector,
                    "pool": nc.gpsimd, "sp": nc.sync}
        self.sems = {}
        self.cnt = {}
        for k in self.eng:
            self.sems[k] = es.enter_context(nc.semaphore("s_" + k))
            self.cnt[k] = 0
        self.seen = {k: {} for k in self.eng}
        self.ndma = 0
        self.out_dma = []
        self.n_ops = 0
        self.uid = 0

    prefix = ""

    def sb(self, name, shape, dt):
        return self.es.enter_context(self.nc.sbuf_tensor(self.prefix + name, list(shape), dt))

    def ps(self, name, shape, dt=F32):
        return self.es.enter_context(self.nc.psum_tensor(self.prefix + name, list(shape), dt))

    def dma_sem(self):
        k = "d%d" % self.ndma
        self.ndma += 1
        self.sems[k] = self.es_top.enter_context(self.nc.semaphore("s_" + k))
        self.cnt[k] = 0
        return k

    def _wait(self, e, deps):
        for (k, v) in deps:
            if k == e:
                if not SAME_ENGINE_SYNC or e == "pe" or e == "sp":
                    continue
            if self.seen[e].get(k, 0) >= v:
                continue
            self.eng[e].wait_ge(self.sems[k], v)
            self.seen[e][k] = v

    def _deps(self, r, w):
        m = {}
        for t in r:
            if t.w is not None:
                k, v = t.w
                if m.get(k, 0) < v:
                    m[k] = v
        for t in w:
            if t.w is not None:
                k, v = t.w
                if m.get(k, 0) < v:
                    m[k] = v
            for k, v in t.r.items():
                if m.get(k, 0) < v:
                    m[k] = v
        return list(m.items())

    def op(self, e, fn, r=(), w=(), multi=False):
        deps = self._deps(r, w)
        att = None
        if e != "pe" and not multi:
            need = [(k, v) for (k, v) in deps
                    if not (k == e and not SAME_ENGINE_SYNC) and self.seen[e].get(k, 0) < v]
            if need:
                att = need[-1]
                self._wait(e, need[:-1])
        else:
            self._wait(e, deps)
        ins = fn(self.eng[e])
        if att is not None:
            ins._wait_ge(self.sems[att[0]], att[1])
            self.seen[e][att[0]] = att[1]
        self.cnt[e] += 1
        v = self.cnt[e]
        ins.then_inc(self.sems[e], 1)
        for t in r:
            if t.r.get(e, 0) < v:
                t.r[e] = v
        for t in w:
            t.w = (e, v)
            t.r = {}
        self.n_ops += 1
        return ins

    def dma(self, q, out, in_, r=(), w=(), sem=None, is_out=False, nowait=False, **kw):
        if not nowait:
            self._wait(q, self._deps(r, w))
        ins = self.eng[q].dma_start(out=out, in_=in_, **kw)
        self.cnt[sem] += 16
        v = self.cnt[sem]
        ins.then_inc(self.sems[sem], 16)
        for t in r:
            if t.r.get(sem, 0) < v:
                t.r[sem] = v
        for t in w:
            t.w = (sem, v)
            t.r = {}
        if is_out:
            self.out_dma.append((sem, v))
        return ins

    def finish(self, e="sp"):
        m = {}
        for k, v in self.out_dma:
            if m.get(k, 0) < v:
                m[k] = v
        for k, v in m.items():
            self.eng[e].wait_ge(self.sems[k], v)


class Ring:
    def __init__(self, p, name, shape, dt, n, space="sb", dma=False):
        self.bufs = []
        for i in range(n):
            t = p.sb("%s%d" % (name, i), shape, dt) if space == "sb" else p.ps("%s%d" % (name, i), shape, dt)
            self.bufs.append((t, Tok(name + str(i)), p.dma_sem() if dma else None))
        self.i = 0

    def next(self):
        b = self.bufs[self.i % len(self.bufs)]
        self.i += 1
        return b


def load_cast_weight(p, src, dst, stage, K, C, engines=("pool", "act"), cw=1024, tok=None):
    n = 0
    for k in range(K):
        for c0 in range(0, C, cw):
            c1 = min(C, c0 + cw)
            st, stok, ssem = stage.next()
            p.dma("sp", st[:, 0:c1 - c0], src[k * 128:(k + 1) * 128, c0:c1], w=[stok], sem=ssem)
            e = engines[n % len(engines)]
            n += 1
            if e == "act":
                p.op(e, lambda en: en.copy(out=dst[:, k, c0:c1], in_=st[:, 0:c1 - c0]), r=[stok], w=[tok])
            else:
                p.op(e, lambda en: en.tensor_copy(out=dst[:, k, c0:c1], in_=st[:, 0:c1 - c0]), r=[stok], w=[tok])


PI = float(np.pi)
TWO_PI = float(2 * np.pi)
C1 = 6.28125
C2 = float(2 * np.pi - 6.28125)
ATT_SCALE = float(96 ** -0.5)
NEG = -30000.0
L0B_TB = 256


def barrier(p):
    for e in p.eng:
        for k, v in p.cnt.items():
            if k != e and v > 0 and p.seen[e].get(k, 0) < v:
                p.eng[e].wait_ge(p.sems[k], v)
                p.seen[e][k] = v


def layer_norm_tail(p, o, t_o, xk, t_xk, rr, r2, junk, st_r, lng_s, lnb_s, t_c, out_ap, post=None, is_out=True):
    r, t_r, _ = rr.next()
    p.op("dve", lambda e: e.scalar_tensor_tensor(out=r[:], in0=xk[:], scalar=float(ALPHA), in1=o[:], op0=ALU.mult, op1=ALU.add),
         r=[t_xk, t_o], w=[t_r])
    st, t_st, _ = st_r.next()
    jk, t_jk, _ = junk.next()
    p.op("act", lambda e: e.activation(out=jk[:], in_=r[:], func=AF.Identity, accum_out=st[:, 0:1]), r=[t_r], w=[t_jk, t_st], multi=True)
    p.op("act", lambda e: e.activation(out=jk[:], in_=r[:], func=AF.Square, accum_out=st[:, 1:2]), r=[t_r], w=[t_jk, t_st], multi=True)
    p.op("dve", lambda e: e.tensor_scalar(out=st[:, 2:3], in0=st[:, 0:1], scalar1=1.0 / D, scalar2=None, op0=ALU.mult), r=[t_st], w=[t_st])
    p.op("dve", lambda e: e.tensor_tensor(out=st[:, 3:4], in0=st[:, 2:3], in1=st[:, 2:3], op=ALU.mult), r=[t_st], w=[t_st])
    p.op("dve", lambda e: e.scalar_tensor_tensor(out=st[:, 4:5], in0=st[:, 1:2], scalar=1.0 / D, in1=st[:, 3:4], op0=ALU.mult, op1=ALU.subtract),
         r=[t_st], w=[t_st])
    p.op("dve", lambda e: e.tensor_scalar(out=st[:, 4:5], in0=st[:, 4:5], scalar1=float(EPS), scalar2=None, op0=ALU.add), r=[t_st], w=[t_st])
    p.op("act", lambda e: e.activation(out=st[:, 5:6], in_=st[:, 4:5], func=AF.Ln), r=[t_st], w=[t_st])
    p.op("act", lambda e: e.activation(out=st[:, 6:7], in_=st[:, 5:6], func=AF.Exp, scale=-0.5), r=[t_st], w=[t_st])
    q, t_q, osem = r2.next()
    p.op("dve", lambda e: e.tensor_scalar(out=q[:], in0=r[:], scalar1=st[:, 2:3], scalar2=st[:, 6:7], op0=ALU.subtract, op1=ALU.mult),
         r=[t_r, t_st], w=[t_q])
    p.op("pool", lambda e: e.tensor_tensor(out=q[:], in0=q[:], in1=lng_s[:], op=ALU.mult), r=[t_c], w=[t_q])
    p.op("pool", lambda e: e.tensor_tensor(out=q[:], in0=q[:], in1=lnb_s[:], op=ALU.add), r=[t_c], w=[t_q])
    if post is not None:
        post(q, t_q)
    p.dma("act", out_ap, q[:], r=[t_q], w=[], sem=osem, is_out=is_out)


def emit_l0b(nc, p, T, A):
    TB = L0B_TB
    NB = T // TB
    xT, xtok, yaT, w1, wout = A["xT1"], A["xtok"], A["yaT"], A["w1"], A["wout"]
    normg, scw, lng, lnb = A["normg"], A["scw"], A["lng"], A["lnb"]
    out, x1T, cst = A["x1"], A["x1T"], A["cst"]
    x1HL, x1HR = A["x1HL"], A["x1HR"]

    def halo_v(tab, bnd):
        return tab[bnd * 1024:(bnd + 1) * 1024, :].rearrange("(k p) t -> p k t", p=128)
    yaT_v = yaT.rearrange("(k p) t -> p k t", p=128)
    NB5 = T // 512

    def x1T_blk(blk, c0, n):
        return x1T[blk * 1024:(blk + 1) * 1024, c0:c0 + n].rearrange("(k p) t -> p k t", p=128)

    with ExitStack() as es:
        p.es = es
        w1b = p.sb("w1b", [128, 8, 5120], BF16)
        woutb = p.sb("woutb", [128, 16, D], BF16)
        t_w1b, t_woutb = Tok(), Tok()
        stage = Ring(p, "wst", [128, 1024], F32, 1, dma=True)
        normg_s = p.sb("normg_s", [128, 8], F32)
        scw_s = p.sb("scw_s", [128, 24], F32)
        lng_s = p.sb("lng_s", [128, D], F32)
        lnb_s = p.sb("lnb_s", [128, D], F32)
        ones_f = p.sb("ones_f", [128, 128], F32)
        t_c = Tok()
        dc = p.dma_sem()
        p.dma("sp", normg_s[:], normg[:, :], w=[t_c], sem=dc)
        p.dma("sp", scw_s[:], scw[:, :], w=[t_c], sem=dc)
        p.dma("sp", lng_s[:], lng[:, :], w=[t_c], sem=dc)
        p.dma("sp", lnb_s[:], lnb[:, :], w=[t_c], sem=dc)
        t_ones = Tok()
        p.op("dve", lambda e: e.memset(ones_f[:], 1.0), w=[t_ones])
        idf = p.sb("idf", [128, 128], F32)
        p.dma("sp", idf[:], cst[:, 256:384], w=[t_c], sem=dc)
        zt = p.sb("zt", [128, 8, 8], F32)
        t_zt = Tok()
        p.op("dve", lambda e: e.memset(zt[:], 0.0), w=[t_zt])
        zsem = p.dma_sem()
        p.dma("sp", halo_v(x1HL, 0), zt[:], r=[t_zt], sem=zsem)
        p.dma("sp", halo_v(x1HR, NB5), zt[:], r=[t_zt], sem=zsem)
        xtt_r = Ring(p, "xtt", [128, 8, 128], F32, 1, dma=True)
        load_cast_weight(p, w1, w1b, stage, 8, 5120, tok=t_w1b)
        load_cast_weight(p, wout, woutb, stage, 16, D, tok=t_woutb)

        xst = Ring(p, "xst", [128, TB + 2], F32, 3, dma=True)
        xb_r = Ring(p, "xb", [128, 8, TB + 2], BF16, 2)
        yst = Ring(p, "yst", [128, 8, TB], F32, 2, dma=True)
        xtk = Ring(p, "xtk", [128, D], F32, 2, dma=True)
        pp = Ring(p, "pp", [128, 512], F32, 4, space="ps")
        pss = Ring(p, "pss", [128, 512], F32, 1, space="ps")
        ph = Ring(p, "ph", [128, 512], F32, 1, space="ps")
        po = Ring(p, "po", [128, D], F32, 1, space="ps")
        g_all = p.sb("g_all", [128, 8, TB], F32)
        t_g = [Tok() for _ in range(8)]
        cat = p.sb("cat", [128, 16, TB], BF16)
        t_cat = [Tok() for _ in range(16)]
        tmp = Ring(p, "tmp", [128, TB + 2], F32, 8)
        rstd = p.sb("rstd", [128, TB], F32)
        t_rstd = Tok()
        halo = Ring(p, "halo", [128, 4], F32, 2)
        rr = Ring(p, "rr", [128, D], F32, 1)
        r2 = Ring(p, "r2", [128, D], F32, 2, dma=True)
        junk = Ring(p, "junk", [128, D], BF16, 1)
        st_r = Ring(p, "stat", [128, 8], F32, 4)
        for bi in range(NB):
            t0 = bi * TB
            xb, t_xb, _ = xb_r.next()
            for k in range(8):
                st, stok, ssem = xst.next()
                p.dma("sp", st[:], xT[k * 128:(k + 1) * 128, t0:t0 + TB + 2], w=[stok], sem=ssem)
                p.op("pool", lambda e: e.tensor_copy(out=xb[:, k, :], in_=st[:]), r=[stok], w=[t_xb])
            ya, t_ya, ya_sem = yst.next()
            p.dma("sp", ya[:], yaT_v[:, :, t0:t0 + TB], w=[t_ya], sem=ya_sem)

            ss, t_ss, _ = pss.next()
            for j in range(8):
                z, t_z, _ = pp.next()
                for k in range(8):
                    p.op("pe", lambda e: e.matmul(z[:, 0:TB], lhsT=w1b[:, k, j * 128:(j + 1) * 128],
                                                  rhs=xb[:, k, 1:TB + 1], start=(k == 0), stop=(k == 7)),
                         r=[t_w1b, t_xb], w=[t_z])
                sz, t_sz, _ = tmp.next()
                p.op("act", lambda e: e.activation(out=sz[:, 0:TB], in_=z[:, 0:TB], func=AF.Silu), r=[t_z], w=[t_sz])
                p.op("dve", lambda e: e.tensor_tensor(out=g_all[:, j, :], in0=sz[:, 0:TB], in1=ya[:, j, :], op=ALU.mult),
                     r=[t_sz, t_ya], w=[t_g[j]])
                sq, t_sq, _ = tmp.next()
                p.op("act", lambda e: e.activation(out=sq[:, 0:TB], in_=g_all[:, j, :], func=AF.Square), r=[t_g[j]], w=[t_sq])
                p.op("pe", lambda e: e.matmul(ss[:, 0:TB], lhsT=ones_f[:], rhs=sq[:, 0:TB], start=(j == 0), stop=(j == 7)),
                     r=[t_ones, t_sq], w=[t_ss])
            lnv, t_lnv, _ = tmp.next()
            p.op("dve", lambda e: e.tensor_scalar(out=lnv[:, 0:TB], in0=ss[:, 0:TB], scalar1=1.0 / 1024, scalar2=EPS,
                                                  op0=ALU.mult, op1=ALU.add), r=[t_ss], w=[t_lnv])
            p.op("act", lambda e: e.activation(out=lnv[:, 0:TB], in_=lnv[:, 0:TB], func=AF.Ln), r=[t_lnv], w=[t_lnv])
            p.op("act", lambda e: e.activation(out=rstd[:], in_=lnv[:, 0:TB], func=AF.Exp, scale=-0.5), r=[t_lnv], w=[t_rstd])
            for j in range(8):
                p.op("dve", lambda e: e.scalar_tensor_tensor(out=cat[:, j, :], in0=g_all[:, j, :], scalar=normg_s[:, j:j + 1],
                                                             in1=rstd[:], op0=ALU.mult, op1=ALU.mult),
                     r=[t_g[j], t_rstd, t_c], w=[t_cat[j]])

            for j in range(8):
                def proj(grp, lo, n, dst, t_dst, first=True, last=True):
                    for k in range(8):
                        p.op("pe", lambda e: e.matmul(dst, lhsT=w1b[:, k, grp * 1024 + j * 128:grp * 1024 + (j + 1) * 128],
                                                      rhs=xb[:, k, lo:lo + n], start=(k == 0), stop=(k == 7)),
                             r=[t_w1b, t_xb], w=[t_dst])
                cg, t_cg, _ = pp.next()
                proj(2, 0, TB + 2, cg[:, 0:TB + 2], t_cg)
                hh, t_hh, _ = pp.next()
                proj(3, 0, TB + 2, hh[:, 0:TB + 2], t_hh)
                cgs, t_cgs, _ = tmp.next()
                p.op("act", lambda e: e.copy(out=cgs[:, 0:TB + 2], in_=cg[:, 0:TB + 2]), r=[t_cg], w=[t_cgs])
                u, t_u, _ = tmp.next()
                p.op("dve", lambda e: e.tensor_tensor(out=u[:, 0:TB + 2], in0=cgs[:, 0:TB + 2], in1=hh[:, 0:TB + 2], op=ALU.mult),
                     r=[t_cgs, t_hh], w=[t_u])
                c, t_cc, _ = tmp.next()
                p.op("dve", lambda e: e.tensor_scalar(out=c[:, 0:TB], in0=u[:, 0:TB], scalar1=scw_s[:, j * 3:j * 3 + 1], scalar2=None,
                                                      op0=ALU.mult), r=[t_u, t_c], w=[t_cc])
                p.op("dve", lambda e: e.scalar_tensor_tensor(out=c[:, 0:TB], in0=u[:, 1:TB + 1], scalar=scw_s[:, j * 3 + 1:j * 3 + 2],
                                                             in1=c[:, 0:TB], op0=ALU.mult, op1=ALU.add), r=[t_u, t_c], w=[t_cc])
                p.op("dve", lambda e: e.scalar_tensor_tensor(out=c[:, 0:TB], in0=u[:, 2:TB + 2], scalar=scw_s[:, j * 3 + 2:j * 3 + 3],
                                                             in1=c[:, 0:TB], op0=ALU.mult, op1=ALU.add), r=[t_u, t_c], w=[t_cc])
                bg, t_bg, _ = pp.next()
                proj(1, 1, TB, bg[:, 0:TB], t_bg)
                gt, t_gt, _ = pp.next()
                proj(4, 1, TB, gt[:, 0:TB], t_gt)
                sg, t_sg, _ = tmp.next()
                p.op("act", lambda e: e.activation(out=sg[:, 0:TB], in_=gt[:, 0:TB], func=AF.Silu), r=[t_gt], w=[t_sg])
                p.op("dve", lambda e: e.tensor_tensor(out=c[:, 0:TB], in0=c[:, 0:TB], in1=bg[:, 0:TB], op=ALU.mult),
                     r=[t_bg], w=[t_cc])
                p.op("dve", lambda e: e.tensor_tensor(out=cat[:, 8 + j, :], in0=c[:, 0:TB], in1=sg[:, 0:TB], op=ALU.mult),
                     r=[t_cc, t_sg], w=[t_cat[8 + j]])

            for tt in range(TB // 128):
                xk, t_xk, xk_sem = xtk.next()
                p.dma("sp", xk[:], xtok[t0 + tt * 128:t0 + (tt + 1) * 128, :], w=[t_xk], sem=xk_sem)
                o, t_o, _ = po.next()
                for half in range(2):
                    for kc in range(16):
                        p.op("pe", lambda e: e.matmul(o[:, half * 512:(half + 1) * 512], lhsT=cat[:, kc, tt * 128:(tt + 1) * 128],
                                                      rhs=woutb[:, kc, half * 512:(half + 1) * 512], start=(kc == 0), stop=(kc == 15)),
                             r=[t_cat[kc], t_woutb], w=[t_o])
                tok0 = t0 + tt * 128

                def post(q, t_q):
                    xtt, t_xtt, xtt_sem = xtt_r.next()
                    for hf in range(2):
                        tp, t_tp, _ = pp.next()
                        for kq in range(4):
                            kk = hf * 4 + kq
                            p.op("pe", lambda e: e.transpose(tp[:, kq * 128:(kq + 1) * 128], q[:, kk * 128:(kk + 1) * 128], idf[:]),
                                 r=[t_q, t_c], w=[t_tp])
                        p.op("act", lambda e: e.copy(out=xtt[:, hf * 4:(hf + 1) * 4, :], in_=tp[:].rearrange("p (k t) -> p k t", k=4)),
                             r=[t_tp], w=[t_xtt])
                    p.dma("act", x1T_blk(tok0 // 512 + 1, tok0 % 512, 128), xtt[:], r=[t_xtt], sem=xtt_sem)
                    if tok0 % 512 == 0:
                        p.dma("act", halo_v(x1HR, tok0 // 512), xtt[:, :, 0:8], r=[t_xtt], sem=xtt_sem)
                    if (tok0 + 128) % 512 == 0:
                        p.dma("act", halo_v(x1HL, (tok0 + 128) // 512), xtt[:, :, 120:128], r=[t_xtt], sem=xtt_sem)
                layer_norm_tail(p, o, t_o, xk, t_xk, rr, r2, junk, st_r, lng_s, lnb_s, t_c,
                                out[t0 + tt * 128:t0 + (tt + 1) * 128, :], post=post, is_out=False)
        barrier(p)


def emit_l0a(nc, p, S, A):
    BLK = 512
    NBLK = S // BLK
    xT, wg_all, cvw_all, cvb_all, hp_all, cst, msk, yaT = (A["xT"], A["wg"], A["cvw"], A["cvb"], A["hp"], A["cst"], A["msk"], A["yaT"])

    with ExitStack() as es:
        p.es = es
        wgb = p.sb("wgb", [128, 8, 520], BF16)
        t_wgb = Tok()
        stage = Ring(p, "wst", [128, 520], F32, 2, dma=True)
        cvw_s = p.sb("cvw_s", [128, 16], F32)
        cvb_s = p.sb("cvb_s", [128, 4], F32)
        hp_s = p.sb("hp_s", [128, 24], F32)
        cst_s = p.sb("cst_s", [128, 512], F32)
        msk_s = p.sb("msk_s", [128, 1024], F32)
        mskb = p.sb("mskb", [128, 1024], BF16)
        identb = p.sb("identb", [128, 128], BF16)
        a_s = p.sb("a_s", [128, 8], F32)
        bias32 = p.sb("bias32", [128, 2, 4, 4], F32)
        a32 = p.sb("a32", [128, 2, 4, 4], F32)
        dsum = p.sb("dsum", [128, 4], F32)
        t_c = Tok()
        dc = p.dma_sem()
        for dst, src in ((cst_s, cst), (msk_s, msk)):
            p.dma("sp", dst[:], src[:, :], w=[t_c], sem=dc)
        U = cst_s[:, 0:128]
        UT = cst_s[:, 128:256]
        IDF = cst_s[:, 256:384]
        ONES = cst_s[:, 384:512]
        p.op("dve", lambda e: e.tensor_copy(out=mskb[:], in_=msk_s[:]), r=[t_c], w=[t_c])
        p.op("dve", lambda e: e.tensor_copy(out=identb[:], in_=IDF), r=[t_c], w=[t_c])

        xst = Ring(p, "xst", [128, BLK + 3], F32, 3, dma=True)
        xb_r = Ring(p, "xb", [128, 8, BLK + 3], BF16, 2)
        pG = Ring(p, "pG", [128, 512], F32, 6, space="ps")
        pH = Ring(p, "pHb", [128, 512], F32, 1, space="ps")
        pP = Ring(p, "pPp", [128, 512], F32, 1, space="ps")
        pre_r = Ring(p, "pre", [128, BLK + 3], F32, 3)
        cv_r = Ring(p, "cv", [128, BLK], F32, 2)
        xsf_r = Ring(p, "xsf", [128, 3, BLK], F32, 2)
        btb_r = Ring(p, "btb", [128, BLK], BF16, 2)
        ctb_r = Ring(p, "ctb", [128, BLK], BF16, 2)
        hs_r = Ring(p, "hs", [128, 16], F32, 2)
        dtv_r = Ring(p, "dtv", [128, 6, 16], F32, 2)
        sm_r = Ring(p, "sm", [128, 8, 4], F32, 3)
        W_r = Ring(p, "W", [128, 4, 128], F32, 2)
        E_r = Ring(p, "E", [128, 4, 128], F32, 2)
        M_r = Ring(p, "M", [128, 4, 128], BF16, 2)
        btk_r = Ring(p, "btk", [128, 128], BF16, 2)
        xd_r = Ring(p, "xd", [128, 256], BF16, 2)
        xdw_r = Ring(p, "xdw", [128, 256], BF16, 2)
        y_r = Ring(p, "y", [128, 256], F32, 3, dma=True)
        yT_r = Ring(p, "yT", [128, 256], F32, 3, dma=True)
        yt_r = Ring(p, "yt", [128, 256], F32, 3)
        yl_r = Ring(p, "yl", [128, 256], F32, 2, dma=True)
        H = p.sb("H", [128, 256], F32)
        Hb = p.sb("Hb", [128, 256], BF16)
        t_H, t_Hb = Tok(), Tok()

        yds_r = Ring(p, "yds", [128, 256], F32, 2)

        def front(k, g, blk, c, dtv, t_dtv, xsf, t_xsf, btb, t_btb, ctb, t_ctb, Tri, t_ya):
            gc = blk * 4 + c
            cs_ = slice(c * 128, (c + 1) * 128)
            dA = dtv[:, 5, 4 * c:4 * c + 4]
            dtc = dtv[:, 4, 4 * c:4 * c + 4]
            T_, t_T, _ = pG.next()
            for m in range(3):
                p.op("pe", lambda e: e.transpose(T_[:, m * 128:(m + 1) * 128], xsf[:, m, cs_], IDF), r=[t_xsf, t_c], w=[t_T])
            btk, t_btk, _ = btk_r.next()
            p.op("act", lambda e: e.copy(out=btk[:], in_=T_[:, 256:384]), r=[t_T], w=[t_btk])
            Sm, t_Sm = T_[:, 384:512], t_T
            p.op("pe", lambda e: e.matmul(Sm[:, 0:4], lhsT=Tri, rhs=dA, start=True, stop=True), r=[t_dtv, t_c], w=[t_Sm])
            p.op("pe", lambda e: e.matmul(Sm[:, 4:8], lhsT=ONES, rhs=dA, start=True, stop=True), r=[t_dtv, t_c], w=[t_Sm])
            sm, t_sm, _ = sm_r.next()
            CS, TOT, NCS, ECS, DTE, ETOT, DTW, D_ = [sm[:, i, :] for i in range(8)]
            p.op("act", lambda e: e.copy(out=sm[:, 0:2, :], in_=Sm[:, 0:8].rearrange("p (a r) -> p a r", a=2)), r=[t_Sm], w=[t_sm])
            p.op("dve", lambda e: e.tensor_scalar(out=NCS, in0=CS, scalar1=-1.0, scalar2=None, op0=ALU.mult), r=[t_sm], w=[t_sm])
            p.op("act", lambda e: e.activation(out=ECS, in_=CS, func=AF.Exp), r=[t_sm], w=[t_sm])
            p.op("dve", lambda e: e.tensor_tensor(out=D_, in0=TOT, in1=CS, op=ALU.subtract), r=[t_sm], w=[t_sm])
            p.op("act", lambda e: e.activation(out=DTE, in_=D_, func=AF.Exp), r=[t_sm], w=[t_sm])
            p.op("act", lambda e: e.activation(out=ETOT, in_=TOT, func=AF.Exp), r=[t_sm], w=[t_sm])
            p.op("dve", lambda e: e.tensor_tensor(out=DTW, in0=DTE, in1=dtc, op=ALU.mult), r=[t_sm, t_dtv], w=[t_sm])
            W, t_W, _ = W_r.next()
            p.op("dve", lambda e: e.tensor_tensor(out=W[:], in0=Tri.unsqueeze(1).to_broadcast([128, 4, 128]),
                                                  in1=dA.unsqueeze(2).to_broadcast([128, 4, 128]), op=ALU.mult),
                 r=[t_dtv, t_c], w=[t_W])
            Eb, t_Eb, _ = pG.next()
            p.op("pe", lambda e: e.matmul(Eb[:], lhsT=ONES, rhs=W[:].rearrange("p r l -> p (r l)"), start=True, stop=False),
                 r=[t_W, t_c], w=[t_Eb])
            p.op("pe", lambda e: e.matmul(Eb[:], lhsT=identb[:], rhs=mskb[:, 512 * k:512 * (k + 1)], start=False, stop=True),
                 r=[t_c], w=[t_Eb])
            E, t_E, _ = E_r.next()
            for r_ in range(4):
                p.op("act", lambda e: e.activation(out=E[:, r_, :], in_=Eb[:, r_ * 128:(r_ + 1) * 128], func=AF.Exp,
                                                   bias=sm[:, 2, r_:r_ + 1]), r=[t_Eb, t_sm], w=[t_E])
            Cb, t_Cb, _ = pG.next()
            p.op("pe", lambda e: e.matmul(Cb[:, 0:128], lhsT=btb[:, cs_], rhs=ctb[:, cs_], start=True, stop=True),
                 r=[t_btb, t_ctb], w=[t_Cb])
            M, t_M, _ = M_r.next()
            p.op("dve", lambda e: e.tensor_tensor(out=M[:], in0=E[:], in1=Cb[:, 0:128].unsqueeze(1).to_broadcast([128, 4, 128]),
                                                  op=ALU.mult), r=[t_E, t_Cb], w=[t_M])
            xd, t_xd, _ = xd_r.next()
            xdw, t_xdw, _ = xdw_r.next()
            xs_tok = T_[:, 0:256].rearrange("p (r q) -> p r q", r=4)
            p.op("dve", lambda e: e.tensor_tensor(out=xd[:].rearrange("p (r q) -> p r q", r=4), in0=xs_tok,
                                                  in1=dtc.unsqueeze(2).to_broadcast([128, 4, 64]), op=ALU.mult),
                 r=[t_T, t_dtv], w=[t_xd])
            p.op("dve", lambda e: e.tensor_tensor(out=xdw[:].rearrange("p (r q) -> p r q", r=4), in0=xs_tok,
                                                  in1=DTW.unsqueeze(2).to_broadcast([128, 4, 64]), op=ALU.mult),
                 r=[t_T, t_sm], w=[t_xdw])
            yds, t_yds = None, None
            if k == 0:
                yds, t_yds, _ = yds_r.next()
                p.op("dve", lambda e: e.tensor_tensor(out=yds[:].rearrange("p (r q) -> p r q", r=4), in0=xs_tok,
                                                      in1=dsum[:].unsqueeze(2).to_broadcast([128, 4, 64]), op=ALU.mult),
                     r=[t_T, t_c], w=[t_yds])
            return dict(k=k, g=g, gc=gc, cs_=cs_, ctb=ctb, t_ctb=t_ctb, M=M, t_M=t_M, xd=xd, t_xd=t_xd, xdw=xdw, t_xdw=t_xdw,
                        btk=btk, t_btk=t_btk, ECS=ECS, ETOT=ETOT, t_sm=t_sm, yds=yds, t_yds=t_yds, t_ya=t_ya)

        def back(s_):
            k, g, gc, cs_ = s_["k"], s_["g"], s_["gc"], s_["cs_"]
            ctb, t_ctb, M, t_M, xd, t_xd, xdw, t_xdw = (s_["ctb"], s_["t_ctb"], s_["M"], s_["t_M"], s_["xd"], s_["t_xd"],
                                                        s_["xdw"], s_["t_xdw"])
            btk, t_btk, ECS, ETOT, t_sm, yds, t_yds, t_ya = (s_["btk"], s_["t_btk"], s_["ECS"], s_["ETOT"], s_["t_sm"],
                                                             s_["yds"], s_["t_yds"], s_["t_ya"])
            Y, t_Y, _ = pG.next()
            for r_ in range(4):
                p.op("pe", lambda e: e.matmul(Y[:, r_ * 64:(r_ + 1) * 64], lhsT=M[:, r_, :], rhs=xd[:, r_ * 64:(r_ + 1) * 64],
                                              start=True, stop=True), r=[t_M, t_xd], w=[t_Y])
            p.op("pe", lambda e: e.matmul(Y[:, 256:512], lhsT=ctb[:, cs_], rhs=Hb[:], start=True, stop=True),
                 r=[t_ctb, t_Hb], w=[t_Y])
            ST, t_ST, _ = pG.next()
            p.op("pe", lambda e: e.matmul(ST[:, 0:256], lhsT=btk[:], rhs=xdw[:], start=True, stop=True),
                 r=[t_btk, t_xdw], w=[t_ST])
            yt, t_yt, _ = yt_r.next()
            p.op("dve", lambda e: e.tensor_tensor(out=yt[:].rearrange("p (r q) -> p r q", r=4),
                                                  in0=Y[:, 256:512].rearrange("p (r q) -> p r q", r=4),
                                                  in1=ECS.unsqueeze(2).to_broadcast([128, 4, 64]), op=ALU.mult),
                 r=[t_Y, t_sm], w=[t_yt])
            yo, t_yo, yo_sem = y_r.next()
            p.op("dve", lambda e: e.tensor_tensor(out=yo[:], in0=yt[:], in1=Y[:, 0:256], op=ALU.add), r=[t_yt, t_Y], w=[t_yo])
            if k == 0:
                p.op("dve", lambda e: e.tensor_tensor(out=yo[:], in0=yo[:], in1=yds[:], op=ALU.add), r=[t_yds], w=[t_yo])
            p.op("dve", lambda e: e.tensor_tensor(out=H[:].rearrange("p (r q) -> p r q", r=4),
                                                  in0=H[:].rearrange("p (r q) -> p r q", r=4),
                                                  in1=ETOT.unsqueeze(2).to_broadcast([128, 4, 64]), op=ALU.mult),
                 r=[t_sm], w=[t_H])
            p.op("dve", lambda e: e.tensor_tensor(out=H[:], in0=H[:], in1=ST[:, 0:256], op=ALU.add), r=[t_ST], w=[t_H])
            p.op("act", lambda e: e.copy(out=Hb[:], in_=H[:]), r=[t_H], w=[t_Hb])
            ydst = yaT[g * 256:(g + 1) * 256, gc * 128:(gc + 1) * 128].rearrange("(j q) t -> q j t", q=128)
            T2, t_T2, _ = pG.next()
            for j in range(2):
                p.op("pe", lambda e: e.transpose(T2[:, j * 128:(j + 1) * 128], yo[:, j * 128:(j + 1) * 128], IDF), r=[t_yo, t_c], w=[t_T2])
            yoT, t_yoT, yoT_sem = yT_r.next()
            if k == 0:
                p.op("act", lambda e: e.copy(out=yoT[:], in_=T2[:, 0:256]), r=[t_T2], w=[t_yoT])
            else:
                yl, t_yl, yl_sem = yl_r.next()
                p.dma("act", yl[:].rearrange("q (j t) -> q j t", j=2), ydst, r=[t_ya[gc]], w=[t_yl], sem=yl_sem)
                p.op("dve", lambda e: e.tensor_tensor(out=yoT[:], in0=T2[:, 0:256], in1=yl[:], op=ALU.add), r=[t_T2, t_yl], w=[t_yoT])
            p.dma("act", ydst, yoT[:].rearrange("q (j t) -> q j t", j=2), r=[t_yoT], w=[t_ya[gc]], sem=yoT_sem)

        for g in range(4):
            wg = wg_all[g]
            for dst, src in ((cvw_s, cvw_all[g]), (cvb_s, cvb_all[g]), (hp_s, hp_all[g])):
                p.dma("sp", dst[:], src, w=[t_c], sem=dc)
            t_ya = [Tok() for _ in range(S // 128)]
            p.op("act", lambda e: e.activation(out=a_s[:], in_=hp_s[:, 0:8], func=AF.Exp), r=[t_c], w=[t_c])
            p.op("dve", lambda e: e.tensor_scalar(out=a_s[:], in0=a_s[:], scalar1=-1.0, scalar2=None, op0=ALU.mult), r=[t_c], w=[t_c])
            for k in range(2):
                for c in range(4):
                    p.op("dve", lambda e: e.tensor_copy(out=bias32[:, k, c, :], in_=hp_s[:, 8 + 4 * k:12 + 4 * k]), r=[t_c], w=[t_c])
                    p.op("dve", lambda e: e.tensor_copy(out=a32[:, k, c, :], in_=a_s[:, 4 * k:4 * k + 4]), r=[t_c], w=[t_c])
            p.op("dve", lambda e: e.tensor_tensor(out=dsum[:], in0=hp_s[:, 16:20], in1=hp_s[:, 20:24], op=ALU.add), r=[t_c], w=[t_c])
            load_cast_weight(p, wg, wgb, stage, 8, 520, cw=520, tok=t_wgb)
            for k in range(2):
                pend = None
                p.op("dve", lambda e: e.memset(H[:], 0.0), w=[t_H])
                p.op("dve", lambda e: e.memset(Hb[:], 0.0), w=[t_Hb])
                Tri = U if k == 0 else UT
                blocks = range(NBLK) if k == 0 else range(NBLK - 1, -1, -1)
                for blk in blocks:
                    e0 = blk * BLK
                    xb, t_xb, _ = xb_r.next()
                    for kk in range(8):
                        st, stok, ssem = xst.next()
                        p.dma("sp", st[:], xT[kk * 128:(kk + 1) * 128, e0:e0 + BLK + 3], w=[stok], sem=ssem)
                        p.op("pool", lambda e: e.tensor_copy(out=xb[:, kk, :], in_=st[:]), r=[stok], w=[t_xb])
                    hb, t_hb, _ = pH.next()
                    for c in range(4):
                        for kk in range(8):
                            p.op("pe", lambda e: e.matmul(hb[:, 16 + 4 * c:20 + 4 * c], lhsT=xb[:, kk, 2 + c * 128:2 + (c + 1) * 128],
                                                          rhs=wgb[:, kk, 512 + 4 * k:516 + 4 * k], start=(kk == 0), stop=(kk == 7)),
                                 r=[t_xb, t_wgb], w=[t_hb])
                    dtv, t_dtv, _ = dtv_r.next()
                    V, AV, EE, LL, DT, DA = [dtv[:, i, :] for i in range(6)]
                    b32 = bias32[:, k, :, :].rearrange("p c r -> p (c r)")
                    A32 = a32[:, k, :, :].rearrange("p c r -> p (c r)")
                    p.op("dve", lambda e: e.tensor_tensor(out=V, in0=hb[:, 16:32], in1=b32, op=ALU.add), r=[t_hb, t_c], w=[t_dtv])
                    p.op("dve", lambda e: e.tensor_scalar(out=AV, in0=V, scalar1=-1.0, scalar2=None, op0=ALU.mult), r=[t_dtv], w=[t_dtv])
                    p.op("dve", lambda e: e.tensor_tensor(out=AV, in0=AV, in1=V, op=ALU.max), r=[t_dtv], w=[t_dtv])
                    p.op("act", lambda e: e.activation(out=EE, in_=AV, func=AF.Exp, scale=-1.0), r=[t_dtv], w=[t_dtv])
                    p.op("act", lambda e: e.activation(out=LL, in_=EE, func=AF.Ln, bias=1.0), r=[t_dtv], w=[t_dtv])
                    p.op("dve", lambda e: e.scalar_tensor_tensor(out=DT, in0=V, scalar=0.0, in1=LL, op0=ALU.max, op1=ALU.add), r=[t_dtv], w=[t_dtv])
                    p.op("dve", lambda e: e.tensor_tensor(out=DA, in0=DT, in1=A32, op=ALU.mult), r=[t_dtv, t_c], w=[t_dtv])

                    xsf, t_xsf, _ = xsf_r.next()
                    btb, t_btb, _ = btb_r.next()
                    ctb, t_ctb, _ = ctb_r.next()
                    for m in range(4):
                        P, t_P, _ = pP.next()
                        for kk in range(8):
                            p.op("pe", lambda e: e.matmul(P[:, 0:BLK], lhsT=wgb[:, kk, m * 128:(m + 1) * 128], rhs=xb[:, kk, 0:BLK],
                                                          start=(kk == 0), stop=(kk == 7)), r=[t_xb, t_wgb], w=[t_P])
                        for kk in range(8):
                            p.op("pe", lambda e: e.matmul(hb[:, 4 * m:4 * m + 3], lhsT=wgb[:, kk, m * 128:(m + 1) * 128],
                                                          rhs=xb[:, kk, BLK:BLK + 3], start=(kk == 0), stop=(kk == 7)),
                                 r=[t_xb, t_wgb], w=[t_hb])
                        pre, t_pre, _ = pre_r.next()
                        p.op("act", lambda e: e.copy(out=pre[:, 0:BLK], in_=P[:, 0:BLK]), r=[t_P], w=[t_pre])
                        p.op("act", lambda e: e.copy(out=pre[:, BLK:BLK + 3], in_=hb[:, 4 * m:4 * m + 3]), r=[t_hb], w=[t_pre])
                        cv, t_cv, _ = cv_r.next()
                        p.op("dve", lambda e: e.tensor_scalar(out=cv[:], in0=pre[:, 0:BLK], scalar1=cvw_s[:, 4 * m:4 * m + 1], scalar2=None,
                                                              op0=ALU.mult), r=[t_pre, t_c], w=[t_cv])
                        for tap in range(1, 4):
                            p.op("dve", lambda e: e.scalar_tensor_tensor(out=cv[:], in0=pre[:, tap:tap + BLK],
                                                                         scalar=cvw_s[:, 4 * m + tap:4 * m + tap + 1], in1=cv[:],
                                                                         op0=ALU.mult, op1=ALU.add), r=[t_pre, t_c], w=[t_cv])
                        if m < 3:
                            p.op("act", lambda e: e.activation(out=xsf[:, m, :], in_=cv[:], func=AF.Silu, bias=cvb_s[:, m:m + 1]),
                                 r=[t_cv, t_c], w=[t_xsf])
                            if m == 2:
                                p.op("act", lambda e: e.copy(out=btb[:], in_=xsf[:, 2, :]), r=[t_xsf], w=[t_btb])
                        else:
                            p.op("act", lambda e: e.activation(out=ctb[:], in_=cv[:], func=AF.Silu, bias=cvb_s[:, m:m + 1]),
                                 r=[t_cv, t_c], w=[t_ctb])

                    chunks = range(4) if k == 0 else range(3, -1, -1)
                    for c in chunks:
                        st_ = front(k, g, blk, c, dtv, t_dtv, xsf, t_xsf, btb, t_btb, ctb, t_ctb, Tri, t_ya)
                        if pend is not None:
                            back(pend)
                        pend = st_
                back(pend)
                pend = None
        barrier(p)


def l0a_consts():
    t = np.arange(128)
    U = (t[:, None] <= t[None, :]).astype(np.float32)
    UT = (t[:, None] >= t[None, :]).astype(np.float32)
    I = np.eye(128, dtype=np.float32)
    ones = np.ones((128, 128), np.float32)
    cst = np.ascontiguousarray(np.concatenate([U, UT, I, ones], axis=1))
    mf = np.where(t[None, :] < t[:, None], NEG, 0.0).astype(np.float32)
    mb = np.where(t[None, :] > t[:, None], NEG, 0.0).astype(np.float32)
    msk = np.ascontiguousarray(np.concatenate([np.tile(mf, (1, 4)), np.tile(mb, (1, 4))], axis=1))
    return cst, msk


def prep_l0a(x, ev_w_in, ev_conv_w, ev_conv_b, ev_a_log, ev_dt_bias, ev_d_skip, S=SEQ):
    w_in = ev_w_in[0]
    cw = ev_conv_w[0]
    cb = ev_conv_b[0]
    cst, msk = l0a_consts()
    maps = []
    xTs = []
    for b in range(2):
        xe = np.zeros((D, S + 3), np.float32)
        xe[:, 2:S + 2] = x[b, :S, :].T
        xTs.append(xe)
    for c in range(NCORES):
        b, g = c // 4, c % 4
        xs_cols = 1024 + g * 256 + np.arange(256)
        b_cols = 1024 + 1024 + g * 128 + np.arange(128)
        c_cols = 1024 + 1536 + g * 128 + np.arange(128)
        dt_cols = np.concatenate([3072 + k * 16 + 4 * g + np.arange(4) for k in range(2)])
        cols = np.concatenate([xs_cols, b_cols, c_cols, dt_cols])
        wg = np.ascontiguousarray(w_in[:, cols])
        xbc_idx = cols[:512] - 1024
        cvw = np.ascontiguousarray(cw[:, xbc_idx].reshape(4, 4, 128).transpose(2, 1, 0).reshape(128, 16))
        cvb = np.ascontiguousarray(cb[xbc_idx].reshape(4, 128).T)
        hsel = np.concatenate([np.stack([v[0][k, 4 * g:4 * g + 4] for k in range(2)]).reshape(-1)
                               for v in (ev_a_log, ev_dt_bias, ev_d_skip)])
        hp = np.ascontiguousarray(np.broadcast_to(hsel[None, :], (128, 24))).astype(np.float32)
        maps.append({"xT": xTs[b], "wg": wg, "cvw": cvw, "cvb": cvb, "hp": hp, "cst": cst, "msk": msk})
    return maps


def rope_tables(p, posi, t_posi, n, invf, sgn, t_c, tabs, cosd, sind, t_cos, t_sin):
    R = slice(64, 96)
    ang, t_a, _ = tabs.next()
    nf, t_n, _ = tabs.next()
    ni, t_ni, _ = tabs.next()
    mm, t_m, _ = tabs.next()
    A, N, M = ang[R, 0:n], nf[R, 0:n], mm[R, 0:n]
    NI = ni[R, 0:n].bitcast(I32)
    p.op("dve", lambda e: e.tensor_copy(out=A, in_=posi[R, 0:n]), r=[t_posi], w=[t_a])
    p.op("dve", lambda e: e.tensor_scalar(out=A, in0=A, scalar1=invf[R, 0:1], scalar2=None, op0=ALU.mult), r=[t_c], w=[t_a])
    p.op("dve", lambda e: e.tensor_scalar(out=N, in0=A, scalar1=1.0 / TWO_PI, scalar2=None, op0=ALU.mult), r=[t_a], w=[t_n])
    p.op("dve", lambda e: e.tensor_copy(out=NI, in_=N), r=[t_n], w=[t_ni])
    p.op("dve", lambda e: e.tensor_copy(out=N, in_=NI), r=[t_ni], w=[t_n])
    p.op("dve", lambda e: e.scalar_tensor_tensor(out=A, in0=N, scalar=-C1, in1=A, op0=ALU.mult, op1=ALU.add), r=[t_n], w=[t_a])
    p.op("dve", lambda e: e.scalar_tensor_tensor(out=A, in0=N, scalar=-C2, in1=A, op0=ALU.mult, op1=ALU.add), r=[t_n], w=[t_a])

    def wrap(X, t_x):
        p.op("dve", lambda e: e.tensor_scalar(out=M, in0=X, scalar1=PI, scalar2=None, op0=ALU.is_gt), r=[t_x], w=[t_m])
        p.op("dve", lambda e: e.scalar_tensor_tensor(out=X, in0=M, scalar=-TWO_PI, in1=X, op0=ALU.mult, op1=ALU.add), r=[t_m], w=[t_x])
        p.op("dve", lambda e: e.tensor_scalar(out=M, in0=X, scalar1=-PI, scalar2=None, op0=ALU.is_lt), r=[t_x], w=[t_m])
        p.op("dve", lambda e: e.scalar_tensor_tensor(out=X, in0=M, scalar=TWO_PI, in1=X, op0=ALU.mult, op1=ALU.add), r=[t_m], w=[t_x])
    wrap(A, t_a)
    p.op("act", lambda e: e.activation(out=N, in_=A, func=AF.Sin), r=[t_a], w=[t_n])
    p.op("dve", lambda e: e.tensor_scalar(out=sind, in0=N, scalar1=sgn[R, 0:1], scalar2=None, op0=ALU.mult), r=[t_n, t_c], w=[t_sin])
    p.op("dve", lambda e: e.tensor_scalar(out=A, in0=A, scalar1=PI / 2, scalar2=None, op0=ALU.add), r=[t_a], w=[t_a])
    wrap(A, t_a)
    p.op("act", lambda e: e.activation(out=cosd, in_=A, func=AF.Sin), r=[t_a], w=[t_cos])


def emit_l1(nc, p, S, A):
    T = S // 4
    BLK = 512
    NKB = S // BLK
    NQB = T // BLK
    NK128 = S // 128
    x1T, x1, posb, out = A["x1T"], A["x1"], A["posb"], A["out"]
    w_cq, w_kv, w_kr, w_g3, w_uq, w_ukv, w_pool, wout = (A["w_cq"], A["w_kv"], A["w_kr"], A["w_g3"], A["w_uq"], A["w_ukv"],
                                                         A["w_pool"], A["wout_od"])
    sm_c, sel, lng, lnb = A["sm_c"], A["sel"], A["lng_od"], A["lnb_od"]
    oxT, ox1, oHL, oHR, opos = A["own_x1T"], A["own_x1"], A["own_HL"], A["own_HR"], A["own_pos"]

    def xrows_static(blk, kk):
        return x1T[blk * 1024 + kk * 128:blk * 1024 + (kk + 1) * 128, :]

    def xrows_own(blk, kk):
        return oxT[blk * 1024 + kk * 128:blk * 1024 + (kk + 1) * 128, :]
    xtok = ox1

    with ExitStack() as es:
        p.es = es
        smc = p.sb("smc", [128, 80], F32)
        sel_s = p.sb("sel_s", [128, 384], F32)
        t_c = Tok()
        dc = p.dma_sem()
        p.dma("sp", smc[:], sm_c[:, :], w=[t_c], sem=dc)
        p.dma("sp", sel_s[:], sel[:, :], w=[t_c], sem=dc)
        QG, KVG, INVF, SGN, PSC = smc[:, 0:2], smc[:, 2:3], smc[:, 3:4], smc[:, 4:5], smc[:, 5:9]
        CORR = smc[:, 16:80]
        SEL_E, SEL_O, ONES = sel_s[:, 0:128], sel_s[:, 128:256], sel_s[:, 256:384]
        ycT = p.sb("ycT", [128, 4, T], BF16)
        t_yc = [[Tok() for _ in range(NQB)] for _ in range(8)]
        esA = ExitStack()
        p.es = esA
        ckvn = p.sb("ckvn", [128, S], BF16)
        t_ckvn = [Tok() for _ in range(NKB)]
        Kbuf = p.sb("Kbuf", [96, S], BF16)
        t_kn = [Tok() for _ in range(NKB)]
        t_kr = [Tok() for _ in range(NKB)]
        cqn = p.sb("cqn", [128, 2, T], BF16)
        t_cqn = [Tok() for _ in range(NQB)]
        cosq = p.sb("cosq", [96, T], BF16)
        sinq = p.sb("sinq", [96, T], BF16)
        t_cosq = [Tok() for _ in range(NQB)]
        t_sinq = [Tok() for _ in range(NQB)]
        wuqb = p.sb("wuqb", [128, 2, 1536], BF16)
        wukvb = p.sb("wukvb", [128, 1024], BF16)
        t_wuq, t_wukv = Tok(), Tok()

        with ExitStack() as es1:
            p.es = es1
            wst = Ring(p, "wst", [128, 1536], F32, 1, dma=True)
            wcqb = p.sb("wcqb", [128, 8, 256], BF16)
            wkvb = p.sb("wkvb", [128, 8, 128], BF16)
            wkrb = p.sb("wkrb", [128, 8, 192], BF16)
            t_wcq, t_wkv, t_wkr = Tok(), Tok(), Tok()
            load_cast_weight(p, w_cq, wcqb, wst, 8, 256, cw=256, tok=t_wcq)
            load_cast_weight(p, w_kv, wkvb, wst, 8, 128, cw=128, tok=t_wkv)
            load_cast_weight(p, w_kr, wkrb, wst, 8, 192, cw=192, tok=t_wkr)
            for kc in range(2):
                st, stok, ssem = wst.next()
                p.dma("sp", st[:, 0:1536], w_uq[kc * 128:(kc + 1) * 128, :], w=[stok], sem=ssem)
                p.op("dve", lambda e: e.tensor_scalar(out=wuqb[:, kc, :], in0=st[:, 0:1536], scalar1=QG[:, kc:kc + 1], scalar2=None,
                                                      op0=ALU.mult), r=[stok, t_c], w=[t_wuq])
            st, stok, ssem = wst.next()
            p.dma("sp", st[:, 0:1024], w_ukv[:, :], w=[stok], sem=ssem)
            p.op("dve", lambda e: e.tensor_scalar(out=wukvb[:], in0=st[:, 0:1024], scalar1=KVG, scalar2=None, op0=ALU.mult),
                 r=[stok, t_c], w=[t_wukv])

            xst = Ring(p, "xst", [128, BLK], F32, 3, dma=True)
            xb_r = Ring(p, "xb", [128, 8, BLK], BF16, 2)
            pos_r = Ring(p, "posr", [128, BLK], I32, 2, dma=True)
            tabs = Ring(p, "tabs", [128, BLK], F32, 4)
            cs_r = Ring(p, "csr", [128, BLK], F32, 2)
            sn_r = Ring(p, "snr", [128, BLK], F32, 2)
            sq_r = Ring(p, "sqr", [128, BLK], F32, 3)
            t1_r = Ring(p, "t1r", [128, BLK], F32, 2)
            pA = Ring(p, "pA", [128, 512], F32, 5, space="ps")
            pSS = Ring(p, "pSS", [128, 512], F32, 2, space="ps")

            def load_xblock(rows_fn, blk):
                xb, t_xb, _ = xb_r.next()
                for kk in range(8):
                    st, stok, ssem = xst.next()
                    p.dma("sp", st[:], rows_fn(blk, kk), w=[stok], sem=ssem)
                    p.op("pool", lambda e: e.tensor_copy(out=xb[:, kk, :], in_=st[:]), r=[stok], w=[t_xb])
                return xb, t_xb

            def rstd_of(ss, t_ss, nch):
                r_, t_r, _ = sq_r.next()
                p.op("dve", lambda e: e.tensor_scalar(out=r_[:], in0=ss[:], scalar1=1.0 / nch, scalar2=EPS, op0=ALU.mult, op1=ALU.add),
                     r=[t_ss], w=[t_r])
                p.op("act", lambda e: e.activation(out=r_[:], in_=r_[:], func=AF.Ln), r=[t_r], w=[t_r])
                p.op("act", lambda e: e.activation(out=r_[:], in_=r_[:], func=AF.Exp, scale=-0.5), r=[t_r], w=[t_r])
                return r_, t_r

            for kb in range(NKB):
                c0 = kb * BLK
                xb, t_xb = load_xblock(xrows_static, kb + 1)
                pi_, t_pi, pi_sem = pos_r.next()
                p.dma("sp", pi_[64:96, :], posb[kb * 32:(kb + 1) * 32, :], w=[t_pi], sem=pi_sem)
                ck, t_ck, _ = pA.next()
                ka, t_ka, _ = pA.next()
                kbs, t_kbs, _ = pA.next()
                for kk in range(8):
                    p.op("pe", lambda e: e.matmul(ck[:], lhsT=wkvb[:, kk, :], rhs=xb[:, kk, :], start=(kk == 0), stop=(kk == 7)),
                         r=[t_wkv, t_xb], w=[t_ck])
                for kk in range(8):
                    p.op("pe", lambda e: e.matmul(ka[0:96, :], lhsT=wkrb[:, kk, 0:96], rhs=xb[:, kk, :], start=(kk == 0), stop=(kk == 7)),
                         r=[t_wkr, t_xb], w=[t_ka])
                for kk in range(8):
                    p.op("pe", lambda e: e.matmul(kbs[0:96, :], lhsT=wkrb[:, kk, 96:192], rhs=xb[:, kk, :], start=(kk == 0), stop=(kk == 7)),
                         r=[t_wkr, t_xb], w=[t_kbs])
                sq, t_sq, _ = sq_r.next()
                p.op("act", lambda e: e.activation(out=sq[:], in_=ck[:], func=AF.Square), r=[t_ck], w=[t_sq])
                ss, t_ss, _ = pSS.next()
                p.op("pe", lambda e: e.matmul(ss[:], lhsT=ONES, rhs=sq[:], start=True, stop=True), r=[t_sq, t_c], w=[t_ss])
                rs, t_rs = rstd_of(ss, t_ss, 128)
                p.op("dve", lambda e: e.tensor_tensor(out=ckvn[:, c0:c0 + BLK], in0=ck[:], in1=rs[:], op=ALU.mult),
                     r=[t_ck, t_rs], w=[t_ckvn[kb]])
                cs_, t_cs, _ = cs_r.next()
                sn_, t_sn, _ = sn_r.next()
                rope_tables(p, pi_, t_pi, BLK, INVF, SGN, t_c, tabs, cs_[64:96, :], sn_[64:96, :], t_cs, t_sn)
                t1, t_t1, _ = t1_r.next()
                t2, t_t2, _ = t1_r.next()
                p.op("dve", lambda e: e.tensor_tensor(out=t1[64:96, :], in0=ka[64:96, :], in1=cs_[64:96, :], op=ALU.mult),
                     r=[t_ka, t_cs], w=[t_t1])
                p.op("dve", lambda e: e.tensor_tensor(out=t2[64:96, :], in0=kbs[64:96, :], in1=sn_[64:96, :], op=ALU.mult),
                     r=[t_kbs, t_sn], w=[t_t2])
                p.op("pool", lambda e: e.tensor_tensor(out=Kbuf[64:96, c0:c0 + BLK], in0=t1[64:96, :], in1=t2[64:96, :], op=ALU.add),
                     r=[t_t1, t_t2], w=[t_kr[kb]])

            for qb in range(NQB):
                c0 = qb * BLK
                xb, t_xb = load_xblock(xrows_own, qb)
                pi_, t_pi, pi_sem = pos_r.next()
                p.dma("sp", pi_[64:96, :], opos[qb * 32:(qb + 1) * 32, :], w=[t_pi], sem=pi_sem)
                cqs = []
                ss, t_ss, _ = pSS.next()
                for m in range(2):
                    cq, t_cq, _ = pA.next()
                    for kk in range(8):
                        p.op("pe", lambda e: e.matmul(cq[:], lhsT=wcqb[:, kk, m * 128:(m + 1) * 128], rhs=xb[:, kk, :],
                                                      start=(kk == 0), stop=(kk == 7)), r=[t_wcq, t_xb], w=[t_cq])
                    sq, t_sq, _ = sq_r.next()
                    p.op("act", lambda e: e.activation(out=sq[:], in_=cq[:], func=AF.Square), r=[t_cq], w=[t_sq])
                    p.op("pe", lambda e: e.matmul(ss[:], lhsT=ONES, rhs=sq[:], start=(m == 0), stop=(m == 1)), r=[t_sq, t_c], w=[t_ss])
                    cqs.append((cq, t_cq))
                rs, t_rs = rstd_of(ss, t_ss, 256)
                for m in range(2):
                    cq, t_cq = cqs[m]
                    p.op("dve", lambda e: e.tensor_tensor(out=cqn[:, m, c0:c0 + BLK], in0=cq[:], in1=rs[:], op=ALU.mult),
                         r=[t_cq, t_rs], w=[t_cqn[qb]])
                rope_tables(p, pi_, t_pi, BLK, INVF, SGN, t_c, tabs, cosq[64:96, c0:c0 + BLK], sinq[64:96, c0:c0 + BLK],
                            t_cosq[qb], t_sinq[qb])
        barrier(p)

        with ExitStack() as es3:
            p.es = es3
            Vbuf = p.sb("Vbuf", [128, NK128, 128], BF16)
            t_v = [Tok() for _ in range(NK128 // 8 if NK128 >= 8 else 1)]
            VG = min(8, NK128)
            Q_r = Ring(p, "Q", [96, T], BF16, 2)
            tq_r = Ring(p, "tq", [96, BLK], F32, 4)
            P_r = Ring(p, "P", [128, BLK], BF16, 3)
            osb_r = Ring(p, "osb", [128, BLK], F32, 2)
            rden_r = Ring(p, "rden", [128, BLK], F32, 2)
            pS = Ring(p, "pS", [128, 512], F32, 3, space="ps")
            pO = Ring(p, "pO", [128, 512], F32, 2, space="ps")
            pD = Ring(p, "pD", [128, 512], F32, 1, space="ps")
            pB = Ring(p, "pB", [128, 512], F32, 2, space="ps")
            for h in range(8):
                odd = h % 2
                voff = 64 * odd
                Q, _, _ = Q_r.next()
                t_Q = [Tok() for _ in range(NQB)]
                for qb in range(NQB):
                    c0 = qb * BLK
                    qa, t_qa, _ = pB.next()
                    qs, t_qs, _ = pB.next()
                    for kc in range(2):
                        p.op("pe", lambda e: e.matmul(qa[0:96, :], lhsT=wuqb[:, kc, h * 96:(h + 1) * 96], rhs=cqn[:, kc, c0:c0 + BLK],
                                                      start=(kc == 0), stop=(kc == 1)), r=[t_wuq, t_cqn[qb]], w=[t_qa])
                    for kc in range(2):
                        p.op("pe", lambda e: e.matmul(qs[0:96, :], lhsT=wuqb[:, kc, 768 + h * 96:768 + (h + 1) * 96],
                                                      rhs=cqn[:, kc, c0:c0 + BLK], start=(kc == 0), stop=(kc == 1)),
                             r=[t_wuq, t_cqn[qb]], w=[t_qs])
                    p.op("dve", lambda e: e.tensor_copy(out=Q[0:64, c0:c0 + BLK], in_=qa[0:64, :]), r=[t_qa], w=[t_Q[qb]])
                    t1, t_t1, _ = tq_r.next()
                    t2, t_t2, _ = tq_r.next()
                    p.op("dve", lambda e: e.tensor_tensor(out=t1[64:96, :], in0=qa[64:96, :], in1=cosq[64:96, c0:c0 + BLK], op=ALU.mult),
                         r=[t_qa, t_cosq[qb]], w=[t_t1])
                    p.op("dve", lambda e: e.tensor_tensor(out=t2[64:96, :], in0=qs[64:96, :], in1=sinq[64:96, c0:c0 + BLK], op=ALU.mult),
                         r=[t_qs, t_sinq[qb]], w=[t_t2])
                    p.op("pool", lambda e: e.tensor_tensor(out=Q[64:96, c0:c0 + BLK], in0=t1[64:96, :], in1=t2[64:96, :], op=ALU.add),
                         r=[t_t1, t_t2], w=[t_Q[qb]])
                for kb in range(NKB):
                    c0 = kb * BLK
                    kp, t_kp, _ = pB.next()
                    p.op("pe", lambda e: e.matmul(kp[0:64, :], lhsT=wukvb[:, h * 128:h * 128 + 64], rhs=ckvn[:, c0:c0 + BLK],
                                                  start=True, stop=True), r=[t_wukv, t_ckvn[kb]], w=[t_kp])
                    p.op("dve", lambda e: e.tensor_copy(out=Kbuf[0:64, c0:c0 + BLK], in_=kp[0:64, :]), r=[t_kp], w=[t_kn[kb]])
                for g in range(len(t_v)):
                    vp, t_vp, _ = pB.next()
                    for j in range(VG):
                        k128 = g * VG + j
                        p.op("pe", lambda e: e.matmul(vp[:, j * 64:(j + 1) * 64], lhsT=ckvn[:, k128 * 128:(k128 + 1) * 128],
                                                      rhs=wukvb[:, h * 128 + 64:h * 128 + 128], start=True, stop=True),
                             r=[t_wukv, t_ckvn[k128 // 4]], w=[t_vp])
                    vs = Vbuf[:, g * VG:(g + 1) * VG, :]
                    p.op("pool", lambda e: e.memset(vs[:, :, 64 - voff:128 - voff], 0.0), w=[t_v[g]])
                    p.op("pool", lambda e: e.memset(vs[:, :, 64 - voff:65 - voff], 1.0), w=[t_v[g]])
                    p.op("dve", lambda e: e.tensor_copy(out=vs[:, :, voff:voff + 64], in_=vp[:, 0:VG * 64].rearrange("p (j v) -> p j v", v=64)),
                         r=[t_vp], w=[t_v[g]])
                MV = 128 if odd else 65
                for qb in range(NQB):
                    q0 = qb * BLK
                    O, t_O, _ = pO.next()
                    Sq = {}

                    def issue_S(k128):
                        Sx, t_S, _ = pS.next()
                        kb = k128 // 4
                        p.op("pe", lambda e: e.matmul(Sx[:], lhsT=Kbuf[0:96, k128 * 128:(k128 + 1) * 128], rhs=Q[0:96, q0:q0 + BLK],
                                                      start=True, stop=True), r=[t_kn[kb], t_kr[kb], t_Q[qb]], w=[t_S])
                        Sq[k128] = (Sx, t_S)
                    for k128 in range(min(2, NK128)):
                        issue_S(k128)
                    for k128 in range(NK128):
                        Sx, t_S = Sq.pop(k128)
                        Pt, t_P, _ = P_r.next()
                        p.op("act", lambda e: e.activation(out=Pt[:], in_=Sx[:], func=AF.Exp, scale=ATT_SCALE), r=[t_S], w=[t_P])
                        if k128 + 2 < NK128:
                            issue_S(k128 + 2)
                        p.op("pe", lambda e: e.matmul(O[0:MV, :], lhsT=Vbuf[:, k128, 0:MV], rhs=Pt[:], start=(k128 == 0),
                                                      stop=(k128 == NK128 - 1)), r=[t_v[k128 // VG], t_P], w=[t_O])
                    osb, t_osb, _ = osb_r.next()
                    p.op("dve", lambda e: e.tensor_copy(out=osb[0:MV, :], in_=O[0:MV, :]), r=[t_O], w=[t_osb])
                    Dn, t_D, _ = pD.next()
                    if odd:
                        p.op("pe", lambda e: e.matmul(Dn[:], lhsT=SEL_O, rhs=osb[:], start=True, stop=True), r=[t_osb, t_c], w=[t_D])
                    else:
                        p.op("pe", lambda e: e.matmul(Dn[0:64, :], lhsT=SEL_E[0:65, 0:64], rhs=osb[0:65, :], start=True, stop=True),
                             r=[t_osb, t_c], w=[t_D])
                    rd, t_rd, _ = rden_r.next()
                    PR = slice(voff, voff + 64)
                    p.op("dve", lambda e: e.reciprocal(out=rd[PR, :], in_=Dn[PR, :]), r=[t_D], w=[t_rd])
                    p.op("dve", lambda e: e.tensor_tensor(out=ycT[PR, h // 2, q0:q0 + BLK], in0=osb[PR, :], in1=rd[PR, :], op=ALU.mult),
                         r=[t_osb, t_rd], w=[t_yc[h][qb]])
        barrier(p)
        esA.close()

        with ExitStack() as es4:
            p.es = es4
            wst = Ring(p, "wst4", [128, 1024], F32, 2, dma=True)
            wg3b = p.sb("wg3b", [128, 8, 1536], BF16)
            woutb = p.sb("woutb", [128, 8, D], BF16)
            wpoolb = p.sb("wpoolb", [128, 512], BF16)
            lng_s = p.sb("lng_s", [128, D], F32)
            lnb_s = p.sb("lnb_s", [128, D], F32)
            t_wg3, t_wout, t_wpool, t_ln = Tok(), Tok(), Tok(), Tok()
            dl = p.dma_sem()
            p.dma("sp", lng_s[:], lng[:, :], w=[t_ln], sem=dl)
            p.dma("sp", lnb_s[:], lnb[:, :], w=[t_ln], sem=dl)
            load_cast_weight(p, w_g3, wg3b, wst, 8, 1536, cw=768, tok=t_wg3)
            load_cast_weight(p, wout, woutb, wst, 8, D, tok=t_wout)
            st, stok, ssem = wst.next()
            p.dma("sp", st[:, 0:512], w_pool[:, :], w=[stok], sem=ssem)
            p.op("dve", lambda e: e.tensor_copy(out=wpoolb[:], in_=st[:, 0:512]), r=[stok], w=[t_wpool])
            xst = Ring(p, "xst4", [128, BLK + 16], F32, 3, dma=True)
            xb_r = Ring(p, "xb4", [128, 8, BLK + 16], BF16, 2)
            xtk = Ring(p, "xtk", [128, D], F32, 2, dma=True)
            cat = p.sb("cat", [128, 8, BLK], BF16)
            t_cat = [Tok() for _ in range(8)]
            tmp = Ring(p, "tmp4", [128, BLK + 16], F32, 10)
            hl_r = Ring(p, "hl4", [128, 16], F32, 2)
            pl_r = Ring(p, "pl4", [128, BLK], BF16, 2)
            rr = Ring(p, "rr", [128, D], F32, 2)
            r2 = Ring(p, "r2", [128, D], F32, 2, dma=True)
            junk = Ring(p, "junk", [128, D], BF16, 1)
            st_r = Ring(p, "stat", [128, 8], F32, 4)
            pp = Ring(p, "pp4", [128, 512], F32, 5, space="ps")
            ph = Ring(p, "ph4", [128, 512], F32, 1, space="ps")
            po = Ring(p, "po4", [128, D], F32, 1, space="ps")
            WIN = (2, 4, 8, 16)
            for qb in range(NQB):
                c0 = qb * BLK
                xb, t_xb, _ = xb_r.next()
                for kk in range(8):
                    st, stok, ssem = xst.next()
                    rws = slice(qb * 1024 + kk * 128, qb * 1024 + (kk + 1) * 128)
                    p.dma("sp", st[:, 8:BLK + 8], oxT[rws, :], w=[stok], sem=ssem)
                    p.dma("sp", st[:, 0:8], oHL[rws, :], w=[stok], sem=ssem, nowait=True)
                    p.dma("sp", st[:, BLK + 8:BLK + 16], oHR[rws, :], w=[stok], sem=ssem, nowait=True)
                    p.op("pool", lambda e: e.tensor_copy(out=xb[:, kk, :], in_=st[:]), r=[stok], w=[t_xb])

                def proj(col0, lo, n, dst, t_dst):
                    for kk in range(8):
                        p.op("pe", lambda e: e.matmul(dst, lhsT=wg3b[:, kk, col0:col0 + 128], rhs=xb[:, kk, lo:lo + n],
                                                      start=(kk == 0), stop=(kk == 7)), r=[t_wg3, t_xb], w=[t_dst])
                for j in range(4):
                    gc, t_gc, _ = pp.next()
                    proj(j * 128, 8, BLK, gc[:], t_gc)
                    sg, t_sg, _ = tmp.next()
                    p.op("act", lambda e: e.activation(out=sg[:, 0:BLK], in_=gc[:], func=AF.Silu), r=[t_gc], w=[t_sg])
                    p.op("dve", lambda e: e.tensor_tensor(out=cat[:, j, :], in0=sg[:, 0:BLK], in1=ycT[:, j, c0:c0 + BLK], op=ALU.mult),
                         r=[t_sg, t_yc[2 * j][qb], t_yc[2 * j + 1][qb]], w=[t_cat[j]])
                for gi in range(4):
                    w = WIN[gi]
                    um, t_um, _ = pp.next()
                    proj(512 + gi * 128, 8, BLK, um[:], t_um)
                    hl, t_hl, _ = ph.next()
                    proj(512 + gi * 128, 0, 8, hl[:, 0:8], t_hl)
                    proj(512 + gi * 128, BLK + 8, 8, hl[:, 8:16], t_hl)
                    u, t_u, _ = tmp.next()
                    p.op("act", lambda e: e.copy(out=u[:, 8:BLK + 8], in_=um[:]), r=[t_um], w=[t_u])
                    p.op("act", lambda e: e.copy(out=u[:, 0:8], in_=hl[:, 0:8]), r=[t_hl], w=[t_u])
                    p.op("act", lambda e: e.copy(out=u[:, BLK + 8:BLK + 16], in_=hl[:, 8:16]), r=[t_hl], w=[t_u])
                    cur, t_cur, n, width = u, t_u, BLK + 16, 1
                    while width < w:
                        nxt, t_nxt, _ = tmp.next()
                        n2 = n - width
                        p.op("dve", lambda e: e.tensor_tensor(out=nxt[:, 0:n2], in0=cur[:, 0:n2], in1=cur[:, width:width + n2], op=ALU.add),
                             r=[t_cur], w=[t_nxt])
                        cur, t_cur, n, width = nxt, t_nxt, n2, width * 2
                    s0 = 8 - w // 2
                    pm, t_pm, _ = tmp.next()
                    p.op("dve", lambda e: e.tensor_scalar(out=pm[:, 0:BLK], in0=cur[:, s0:s0 + BLK], scalar1=1.0 / w, scalar2=None, op0=ALU.mult),
                         r=[t_cur], w=[t_pm])
                    if qb == 0:
                        p.op("dve", lambda e: e.tensor_tensor(out=pm[:, 0:8], in0=pm[:, 0:8], in1=CORR[:, gi * 16:gi * 16 + 8], op=ALU.mult),
                             r=[t_c], w=[t_pm])
                    if qb == NQB - 1:
                        p.op("dve", lambda e: e.tensor_tensor(out=pm[:, BLK - 8:BLK], in0=pm[:, BLK - 8:BLK],
                                                              in1=CORR[:, gi * 16 + 8:gi * 16 + 16], op=ALU.mult), r=[t_c], w=[t_pm])
                    pl, t_pl, _ = pl_r.next()
                    p.op("dve", lambda e: e.tensor_tensor(out=pl[:], in0=pm[:, 0:BLK], in1=u[:, 8:BLK + 8], op=ALU.subtract),
                         r=[t_pm, t_u], w=[t_pl])
                    yd, t_yd, _ = pp.next()
                    p.op("pe", lambda e: e.matmul(yd[:], lhsT=wpoolb[:, gi * 128:(gi + 1) * 128], rhs=pl[:], start=True, stop=True),
                         r=[t_wpool, t_pl], w=[t_yd])
                    gd, t_gd, _ = pp.next()
                    proj(1024 + gi * 128, 8, BLK, gd[:], t_gd)
                    sg, t_sg, _ = tmp.next()
                    p.op("act", lambda e: e.activation(out=sg[:, 0:BLK], in_=gd[:], func=AF.Silu), r=[t_gd], w=[t_sg])
                    p.op("dve", lambda e: e.scalar_tensor_tensor(out=cat[:, 4 + gi, :], in0=yd[:], scalar=PSC[:, gi:gi + 1], in1=sg[:, 0:BLK],
                                                                 op0=ALU.mult, op1=ALU.mult), r=[t_yd, t_sg, t_c], w=[t_cat[4 + gi]])
                for tt in range(BLK // 128):
                    xk, t_xk, xk_sem = xtk.next()
                    p.dma("sp", xk[:], xtok[c0 + tt * 128:c0 + (tt + 1) * 128, :], w=[t_xk], sem=xk_sem)
                    o, t_o, _ = po.next()
                    for half in range(2):
                        for kc in range(8):
                            p.op("pe", lambda e: e.matmul(o[:, half * 512:(half + 1) * 512], lhsT=cat[:, kc, tt * 128:(tt + 1) * 128],
                                                          rhs=woutb[:, kc, half * 512:(half + 1) * 512], start=(kc == 0), stop=(kc == 7)),
                                 r=[t_cat[kc], t_wout], w=[t_o])
                    layer_norm_tail(p, o, t_o, xk, t_xk, rr, r2, junk, st_r, lng_s, lnb_s, t_ln,
                                    out[c0 + tt * 128:c0 + (tt + 1) * 128, :])
        barrier(p)


def prep_l1(x1, positions, od_w_in, od_q_norm_g, od_w_uq, od_kv_norm_g, od_w_ukv, od_pool_w, od_pool_scale, od_w_out,
            od_ln_g, od_ln_b, S=SEQ):
    T = S // 4
    w_in = od_w_in[0]
    w_cq = np.ascontiguousarray(w_in[:, 0:256])
    w_kv = np.ascontiguousarray(w_in[:, 256:384])
    kr = w_in[:, 384:416]
    krs = np.concatenate([kr[:, 16:32], kr[:, 0:16]], axis=1)
    z64 = np.zeros((D, 64), np.float32)
    w_kr = np.ascontiguousarray(np.concatenate([z64, kr, z64, krs], axis=1))
    w_g3 = np.ascontiguousarray(w_in[:, 416:1952])
    uq = od_w_uq[0].reshape(256, 8, 96)
    uqs = np.zeros_like(uq)
    uqs[:, :, 64:80] = uq[:, :, 80:96]
    uqs[:, :, 80:96] = uq[:, :, 64:80]
    w_uq = np.ascontiguousarray(np.concatenate([uq.reshape(256, 768), uqs.reshape(256, 768)], axis=1))
    w_ukv = np.ascontiguousarray(od_w_ukv[0])
    w_pool = np.ascontiguousarray(od_pool_w[0].transpose(1, 0, 2).reshape(128, 512))
    wout = np.ascontiguousarray(od_w_out[0])
    half = 16
    inv_freq = (np.float32(10000.0) ** (-np.arange(half, dtype=np.float32) / np.float32(half))).astype(np.float32)
    sel = np.zeros((128, 384), np.float32)
    sel[64, 0:64] = 1.0
    sel[0, 128 + 64:128 + 128] = 1.0
    sel[:, 256:384] = 1.0
    lng = np.ascontiguousarray(np.broadcast_to(od_ln_g[0][None, :], (128, D)))
    lnb = np.ascontiguousarray(np.broadcast_to(od_ln_b[0][None, :], (128, D)))
    maps = []
    xTbs = [np.ascontiguousarray(x1[b, :S, :].T) for b in range(2)] if x1 is not None else [None, None]
    posbs = [np.ascontiguousarray(np.broadcast_to(positions[b, :S].reshape(S // 512, 1, 512), (S // 512, 32, 512))
                                  .reshape((S // 512) * 32, 512)).astype(np.int32) for b in range(2)]
    for c in range(NCORES):
        b, s0 = c // 4, (c % 4) * T
        xe = None
        if x1 is not None:
            xe = np.zeros((D, T + 16), np.float32)
            lo, hi = max(0, s0 - 8), min(S, s0 + T + 8)
            xe[:, lo - (s0 - 8):hi - (s0 - 8)] = x1[b, lo:hi, :].T
        smc = np.zeros((128, 80), np.float32)
        smc[:, 0:2] = od_q_norm_g[0].reshape(2, 128).T
        smc[:, 2] = od_kv_norm_g[0]
        smc[64:80, 3] = inv_freq
        smc[80:96, 3] = inv_freq
        smc[64:80, 4] = -1.0
        smc[80:96, 4] = 1.0
        smc[:, 5:9] = od_pool_scale[0].reshape(4, 128).T
        for gi, w in enumerate((2, 4, 8, 16)):
            for j in range(8):
                for side, t in ((0, s0 + j), (1, s0 + T - 8 + j)):
                    lo_ = min(max(t - w // 2, 0), S)
                    hi_ = min(max(t + w - w // 2, 0), S)
                    smc[:, 16 + gi * 16 + side * 8 + j] = np.float32(w) / np.float32(hi_ - lo_)
        maps.append({"xTb": xTbs[b], "xTo": xe, "xtok": (np.ascontiguousarray(x1[b, s0:s0 + T, :]) if x1 is not None else None),
                     "posb": posbs[b],
                     "w_cq": w_cq, "w_kv": w_kv, "w_kr": w_kr, "w_g3": w_g3, "w_uq": w_uq, "w_ukv": w_ukv,
                     "w_pool": w_pool, "wout": wout, "sm_c": smc, "sel": sel, "lng": lng, "lnb": lnb})
    return maps


def build_fused(S=SEQ):
    T = S // 4
    nc = bass.Bass("TRN2", target_bir_lowering=False)

    def inp(name, shape, dt=F32):
        return nc.dram_tensor(name, list(shape), dt, kind="ExternalInput").ap()
    A = {}
    A["xT"] = inp("xT", [D, S + 3])
    A["xT1"] = A["xT"][:, 1:S + 3]
    A["xtok"] = inp("xtok", [S, D])
    A["wg"] = [inp("wg%d" % g, [D, 520]) for g in range(4)]
    A["cvw"] = [inp("cvw%d" % g, [128, 16])[:, :] for g in range(4)]
    A["cvb"] = [inp("cvb%d" % g, [128, 4])[:, :] for g in range(4)]
    A["hp"] = [inp("hp%d" % g, [128, 24])[:, :] for g in range(4)]
    A["cst"] = inp("cst", [128, 512])
    A["msk"] = inp("msk", [128, 1024])
    A["w1"] = inp("w1", [D, 5120])
    A["wout"] = inp("wout", [2048, D])
    A["normg"] = inp("normg", [128, 8])
    A["scw"] = inp("scw", [128, 24])
    A["lng"] = inp("lng", [128, D])
    A["lnb"] = inp("lnb", [128, D])
    A["posb"] = inp("posb", [(S // 512) * 32, 512], I32)
    A["w_cq"] = inp("w_cq", [D, 256])
    A["w_kv"] = inp("w_kv", [D, 128])
    A["w_kr"] = inp("w_kr", [D, 192])
    A["w_g3"] = inp("w_g3", [D, 1536])
    A["w_uq"] = inp("w_uq", [256, 1536])
    A["w_ukv"] = inp("w_ukv", [128, 1024])
    A["w_pool"] = inp("w_pool", [128, 512])
    A["wout_od"] = inp("wout_od", [D, D])
    A["sm_c"] = inp("sm_c", [128, 80])
    A["sel"] = inp("sel", [128, 384])
    A["lng_od"] = inp("lng_od", [128, D])
    A["lnb_od"] = inp("lnb_od", [128, D])
    off = inp("off", [1, 4], I32)
    A["out"] = nc.dram_tensor("out", [T, D], F32, kind="ExternalOutput").ap()
    A["yaT"] = nc.dram_tensor("yaT_s", [D, S], F32).ap()
    A["x1"] = nc.dram_tensor("x1_s", [S, D], F32).ap()
    A["x1T"] = nc.dram_tensor("x1T_s", [(S // 512 + 2) * 1024, 512], F32).ap()
    A["x1HL"] = nc.dram_tensor("x1HL_s", [(S // 512 + 1) * 1024, 8], F32).ap()
    A["x1HR"] = nc.dram_tensor("x1HR_s", [(S // 512 + 1) * 1024, 8], F32).ap()
    A["own_x1T"] = nc.dram_tensor("own_x1T_s", [(T // 512) * 1024, 512], F32).ap()
    A["own_x1"] = nc.dram_tensor("own_x1_s", [T, D], F32).ap()
    A["own_HL"] = nc.dram_tensor("own_HL_s", [(T // 512) * 1024, 8], F32).ap()
    A["own_HR"] = nc.dram_tensor("own_HR_s", [(T // 512) * 1024, 8], F32).ap()
    A["own_pos"] = nc.dram_tensor("own_pos_s", [(T // 512) * 32, 512], I32).ap()

    with ExitStack() as es:
        p = Prog(nc, es)
        regs = [es.enter_context(nc.sync.register("offr%d" % i)) for i in range(3)]
        for i in range(3):
            nc.sync.reg_load(regs[i], off[0:1, i:i + 1])
        NB, NQB = S // 512, T // 512
        b0v = nc.sync.snap(regs[0], min_val=0, max_val=NB - NQB)
        u0v = nc.sync.snap(regs[1], min_val=0, max_val=(NB - NQB) * 64)
        t0v = nc.sync.snap(regs[2], min_val=0, max_val=(S - T) // 8)
        p.prefix = "a_"
        emit_l0a(nc, p, S, A)
        p.prefix = "b_"
        emit_l0b(nc, p, S, A)
        csem = p.dma_sem()
        v = lambda ap, b: ap.rearrange("(a b) t -> a (b t)", b=b)
        p.dma("sp", v(A["own_x1T"], 16), v(A["x1T"], 16)[bass.ds(u0v + 64, NQB * 64), :], sem=csem)
        p.dma("sp", v(A["own_x1"], 8), v(A["x1"], 8)[bass.ds(t0v, T // 8), :], sem=csem)
        p.dma("sp", v(A["own_HL"], 1024), v(A["x1HL"], 1024)[bass.ds(b0v, NQB), :], sem=csem)
        p.dma("sp", v(A["own_HR"], 1024), v(A["x1HR"], 1024)[bass.ds(b0v + 1, NQB), :], sem=csem)
        p.dma("sp", v(A["own_pos"], 32), v(A["posb"], 32)[bass.ds(b0v, NQB), :], sem=csem)
        barrier(p)
        p.prefix = "c_"
        emit_l1(nc, p, S, A)
        p.es = es
        p.finish()
    return nc


def prep_fused(inputs, S=SEQ):
    f = lambda a: np.asarray(a, dtype=np.float32)
    x = f(inputs["x"])[:, :S]
    positions = np.asarray(inputs["positions"], dtype=np.int32)[:, :S]
    T = S // 4
    l0a = prep_l0a(x, f(inputs["ev_w_in"]), f(inputs["ev_conv_w"]), f(inputs["ev_conv_b"]), f(inputs["ev_a_log"]),
                   f(inputs["ev_dt_bias"]), f(inputs["ev_d_skip"]), S=S)
    w_in = f(inputs["ev_w_in"])[0]
    w1 = np.ascontiguousarray(np.concatenate([w_in[:, 0:1024], w_in[:, 3104:7200]], axis=1))
    wout = np.ascontiguousarray(f(inputs["ev_w_out"])[0])
    normg = np.ascontiguousarray(f(inputs["ev_norm_g"])[0].reshape(8, 128).T)
    scw = np.ascontiguousarray(f(inputs["ev_sc_conv_w"])[0].reshape(3, 8, 128).transpose(2, 1, 0).reshape(128, 24))
    lng = np.ascontiguousarray(np.broadcast_to(f(inputs["ev_ln_g"])[0][None, :], (128, D)))
    lnb = np.ascontiguousarray(np.broadcast_to(f(inputs["ev_ln_b"])[0][None, :], (128, D)))
    dummy_x1 = np.zeros((2, 16, D), np.float32)
    l1 = prep_l1(None, positions, f(inputs["od_w_in"]), f(inputs["od_q_norm_g"]), f(inputs["od_w_uq"]), f(inputs["od_kv_norm_g"]),
                 f(inputs["od_w_ukv"]), f(inputs["od_pool_w"]), f(inputs["od_pool_scale"]), f(inputs["od_w_out"]),
                 f(inputs["od_ln_g"]), f(inputs["od_ln_b"]), S=S)
    xtoks = [np.ascontiguousarray(x[b]) for b in range(2)]
    maps = []
    for c in range(NCORES):
        b, q = c // 4, c % 4
        m = {"xT": l0a[4 * b]["xT"], "xtok": xtoks[b], "cst": l0a[0]["cst"], "msk": l0a[0]["msk"],
             "w1": w1, "wout": wout, "normg": normg, "scw": scw, "lng": lng, "lnb": lnb,
             "off": np.array([[q * T // 512, (q * T // 512) * 64, q * T // 8, 0]], np.int32)}
        for g in range(4):
            src = l0a[4 * b + g]
            m["wg%d" % g] = src["wg"]
            m["cvw%d" % g] = src["cvw"]
            m["cvb%d" % g] = src["cvb"]
            m["hp%d" % g] = src["hp"]
        lm = l1[c]
        for k in ("posb", "w_cq", "w_kv", "w_kr", "w_g3", "w_uq", "w_ukv", "w_pool", "sm_c", "sel"):
            m[k] = lm[k]
        m["wout_od"] = lm["wout"]
        m["lng_od"] = lm["lng"]
        m["lnb_od"] = lm["lnb"]
        maps.append(m)
    return maps


def kernel(**inputs):
    T = SEQ // 4
    maps = prep_fused(inputs)
    res = run_bass_kernel_spmd(build_fused(), maps, core_ids=list(range(NCORES)))
    out = np.empty((2, SEQ, D), np.float32)
    for c in range(NCORES):
        out[c // 4, (c % 4) * T:(c % 4 + 1) * T, :] = res.results[c]["out"]
    return out
```

```python
import numpy as np
import concourse.bass as bass
import concourse.mybir as mybir
from concourse.bass_utils import run_bass_kernel_spmd
from contextlib import ExitStack

F32 = mybir.dt.float32
BF16 = mybir.dt.bfloat16
I32 = mybir.dt.int32
AF = mybir.ActivationFunctionType
ALU = mybir.AluOpType
AX = mybir.AxisListType

SAME_ENGINE_SYNC = True

D = 1024
SEQ = 16384
NCORES = 8
ALPHA = 4 ** 0.25
EPS = 1e-5


class Tok:
    __slots__ = ("w", "r", "name")

    def __init__(self, name=""):
        self.w = None
        self.r = {}
        self.name = name


class Prog:
    def __init__(self, nc, es):
        self.nc = nc
        self.es = es
        self.es_top = es
        self.eng = {"pe": nc.tensor, "act": nc.scalar, "dve": nc.vector,
                    "pool": nc.gpsimd, "sp": nc.sync}
        self.sems = {}
        self.cnt = {}
        for k in self.eng:
            self.sems[k] = es.enter_context(nc.semaphore("s_" + k))
            self.cnt[k] = 0
        self.seen = {k: {} for k in self.eng}
        self.ndma = 0
        self.out_dma = []
        self.n_ops = 0
        self.uid = 0

    prefix = ""

    def sb(self, name, shape, dt):
        return self.es.enter_context(self.nc.sbuf_tensor(self.prefix + name, list(shape), dt))

    def ps(self, name, shape, dt=F32):
        return self.es.enter_context(self.nc.psum_tensor(self.prefix + name, list(shape), dt))

    def dma_sem(self):
        k = "d%d" % self.ndma
        self.ndma += 1
        self.sems[k] = self.es_top.enter_context(self.nc.semaphore("s_" + k))
        self.cnt[k] = 0
        return k

    def _wait(self, e, deps):
        for (k, v) in deps:
            if k == e:
                if not SAME_ENGINE_SYNC or e == "pe" or e == "sp":
                    continue
            if self.seen[e].get(k, 0) >= v:
                continue
            self.eng[e].wait_ge(self.sems[k], v)
            self.seen[e][k] = v

    def _deps(self, r, w):
        m = {}
        for t in r:
            if t.w is not None:
                k, v = t.w
                if m.get(k, 0) < v:
                    m[k] = v
        for t in w:
            if t.w is not None:
                k, v = t.w
                if m.get(k, 0) < v:
                    m[k] = v
            for k, v in t.r.items():
                if m.get(k, 0) < v:
                    m[k] = v
        return list(m.items())

    def op(self, e, fn, r=(), w=(), multi=False):
        deps = self._deps(r, w)
        att = None
        if e != "pe" and not multi:
            need = [(k, v) for (k, v) in deps
                    if not (k == e and not SAME_ENGINE_SYNC) and self.seen[e].get(k, 0) < v]
            if need:
                att = need[-1]
                self._wait(e, need[:-1])
        else:
            self._wait(e, deps)
        ins = fn(self.eng[e])
        if att is not None:
            ins._wait_ge(self.sems[att[0]], att[1])
            self.seen[e][att[0]] = att[1]
        self.cnt[e] += 1
        v = self.cnt[e]
        ins.then_inc(self.sems[e], 1)
        for t in r:
            if t.r.get(e, 0) < v:
                t.r[e] = v
        for t in w:
            t.w = (e, v)
            t.r = {}
        self.n_ops += 1
        return ins

    def dma(self, q, out, in_, r=(), w=(), sem=None, is_out=False, nowait=False, **kw):
        if not nowait:
            self._wait(q, self._deps(r, w))
        ins = self.eng[q].dma_start(out=out, in_=in_, **kw)
        self.cnt[sem] += 16
        v = self.cnt[sem]
        ins.then_inc(self.sems[sem], 16)
        for t in r:
            if t.r.get(sem, 0) < v:
                t.r[sem] = v
        for t in w:
            t.w = (sem, v)
            t.r = {}
        if is_out:
            self.out_dma.append((sem, v))
        return ins

    def finish(self, e="sp"):
        m = {}
        for k, v in self.out_dma:
            if m.get(k, 0) < v:
                m[k] = v
        for k, v in m.items():
            self.eng[e].wait_ge(self.sems[k], v)


class Ring:
    def __init__(self, p, name, shape, dt, n, space="sb", dma=False):
        self.bufs = []
        for i in range(n):
            t = p.sb("%s%d" % (name, i), shape, dt) if space == "sb" else p.ps("%s%d" % (name, i), shape, dt)
            self.bufs.append((t, Tok(name + str(i)), p.dma_sem() if dma else None))
        self.i = 0

    def next(self):
        b = self.bufs[self.i % len(self.bufs)]
        self.i += 1
        return b


def load_cast_weight(p, src, dst, stage, K, C, engines=("pool", "act"), cw=1024, tok=None):
    n = 0
    for k in range(K):
        for c0 in range(0, C, cw):
            c1 = min(C, c0 + cw)
            st, stok, ssem = stage.next()
            p.dma("sp", st[:, 0:c1 - c0], src[k * 128:(k + 1) * 128, c0:c1], w=[stok], sem=ssem)
            e = engines[n % len(engines)]
            n += 1
            if e == "act":
                p.op(e, lambda en: en.copy(out=dst[:, k, c0:c1], in_=st[:, 0:c1 - c0]), r=[stok], w=[tok])
            else:
                p.op(e, lambda en: en.tensor_copy(out=dst[:, k, c0:c1], in_=st[:, 0:c1 - c0]), r=[stok], w=[tok])


PI = float(np.pi)
TWO_PI = float(2 * np.pi)
C1 = 6.28125
C2 = float(2 * np.pi - 6.28125)
ATT_SCALE = float(96 ** -0.5)
NEG = -30000.0
L0B_TB = 256


def barrier(p):
    for e in p.eng:
        for k, v in p.cnt.items():
            if k != e and v > 0 and p.seen[e].get(k, 0) < v:
                p.eng[e].wait_ge(p.sems[k], v)
                p.seen[e][k] = v


def layer_norm_tail(p, o, t_o, xk, t_xk, rr, r2, junk, st_r, lng_s, lnb_s, t_c, out_ap, post=None, is_out=True):
    r, t_r, _ = rr.next()
    p.op("dve", lambda e: e.scalar_tensor_tensor(out=r[:], in0=xk[:], scalar=float(ALPHA), in1=o[:], op0=ALU.mult, op1=ALU.add),
         r=[t_xk, t_o], w=[t_r])
    st, t_st, _ = st_r.next()
    jk, t_jk, _ = junk.next()
    p.op("act", lambda e: e.activation(out=jk[:], in_=r[:], func=AF.Identity, accum_out=st[:, 0:1]), r=[t_r], w=[t_jk, t_st], multi=True)
    p.op("act", lambda e: e.activation(out=jk[:], in_=r[:], func=AF.Square, accum_out=st[:, 1:2]), r=[t_r], w=[t_jk, t_st], multi=True)
    p.op("dve", lambda e: e.tensor_scalar(out=st[:, 2:3], in0=st[:, 0:1], scalar1=1.0 / D, scalar2=None, op0=ALU.mult), r=[t_st], w=[t_st])
    p.op("dve", lambda e: e.tensor_tensor(out=st[:, 3:4], in0=st[:, 2:3], in1=st[:, 2:3], op=ALU.mult), r=[t_st], w=[t_st])
    p.op("dve", lambda e: e.scalar_tensor_tensor(out=st[:, 4:5], in0=st[:, 1:2], scalar=1.0 / D, in1=st[:, 3:4], op0=ALU.mult, op1=ALU.subtract),
         r=[t_st], w=[t_st])
    p.op("dve", lambda e: e.tensor_scalar(out=st[:, 4:5], in0=st[:, 4:5], scalar1=float(EPS), scalar2=None, op0=ALU.add), r=[t_st], w=[t_st])
    p.op("act", lambda e: e.activation(out=st[:, 5:6], in_=st[:, 4:5], func=AF.Ln), r=[t_st], w=[t_st])
    p.op("act", lambda e: e.activation(out=st[:, 6:7], in_=st[:, 5:6], func=AF.Exp, scale=-0.5), r=[t_st], w=[t_st])
    q, t_q, osem = r2.next()
    p.op("dve", lambda e: e.tensor_scalar(out=q[:], in0=r[:], scalar1=st[:, 2:3], scalar2=st[:, 6:7], op0=ALU.subtract, op1=ALU.mult),
         r=[t_r, t_st], w=[t_q])
    p.op("pool", lambda e: e.tensor_tensor(out=q[:], in0=q[:], in1=lng_s[:], op=ALU.mult), r=[t_c], w=[t_q])
    p.op("pool", lambda e: e.tensor_tensor(out=q[:], in0=q[:], in1=lnb_s[:], op=ALU.add), r=[t_c], w=[t_q])
    if post is not None:
        post(q, t_q)
    p.dma("act", out_ap, q[:], r=[t_q], w=[], sem=osem, is_out=is_out)


def emit_l0b(nc, p, T, A):
    TB = L0B_TB
    NB = T // TB
    xT, xtok, yaT, w1, wout = A["xT1"], A["xtok"], A["yaT"], A["w1"], A["wout"]
    normg, scw, lng, lnb = A["normg"], A["scw"], A["lng"], A["lnb"]
    out, x1T, cst = A["x1"], A["x1T"], A["cst"]
    x1HL, x1HR = A["x1HL"], A["x1HR"]

    def halo_v(tab, bnd):
        return tab[bnd * 1024:(bnd + 1) * 1024, :].rearrange("(k p) t -> p k t", p=128)
    yaT_v = yaT.rearrange("(k p) t -> p k t", p=128)
    NB5 = T // 512

    def x1T_blk(blk, c0, n):
        return x1T[blk * 1024:(blk + 1) * 1024, c0:c0 + n].rearrange("(k p) t -> p k t", p=128)

    with ExitStack() as es:
        p.es = es
        w1b = p.sb("w1b", [128, 8, 5120], BF16)
        woutb = p.sb("woutb", [128, 16, D], BF16)
        t_w1b, t_woutb = Tok(), Tok()
        stage = Ring(p, "wst", [128, 1024], F32, 1, dma=True)
        normg_s = p.sb("normg_s", [128, 8], F32)
        scw_s = p.sb("scw_s", [128, 24], F32)
        lng_s = p.sb("lng_s", [128, D], F32)
        lnb_s = p.sb("lnb_s", [128, D], F32)
        ones_f = p.sb("ones_f", [128, 128], F32)
        t_c = Tok()
        dc = p.dma_sem()
        p.dma("sp", normg_s[:], normg[:, :], w=[t_c], sem=dc)
        p.dma("sp", scw_s[:], scw[:, :], w=[t_c], sem=dc)
        p.dma("sp", lng_s[:], lng[:, :], w=[t_c], sem=dc)
        p.dma("sp", lnb_s[:], lnb[:, :], w=[t_c], sem=dc)
        t_ones = Tok()
        p.op("dve", lambda e: e.memset(ones_f[:], 1.0), w=[t_ones])
        idf = p.sb("idf", [128, 128], F32)
        p.dma("sp", idf[:], cst[:, 256:384], w=[t_c], sem=dc)
        zt = p.sb("zt", [128, 8, 8], F32)
        t_zt = Tok()
        p.op("dve", lambda e: e.memset(zt[:], 0.0), w=[t_zt])
        zsem = p.dma_sem()
        p.dma("sp", halo_v(x1HL, 0), zt[:], r=[t_zt], sem=zsem)
        p.dma("sp", halo_v(x1HR, NB5), zt[:], r=[t_zt], sem=zsem)
        xtt_r = Ring(p, "xtt", [128, 8, 128], F32, 1, dma=True)
        load_cast_weight(p, w1, w1b, stage, 8, 5120, tok=t_w1b)
        load_cast_weight(p, wout, woutb, stage, 16, D, tok=t_woutb)

        xst = Ring(p, "xst", [128, TB + 2], F32, 3, dma=True)
        xb_r = Ring(p, "xb", [128, 8, TB + 2], BF16, 2)
        yst = Ring(p, "yst", [128, 8, TB], F32, 2, dma=True)
        xtk = Ring(p, "xtk", [128, D], F32, 2, dma=True)
        pp = Ring(p, "pp", [128, 512], F32, 4, space="ps")
        pss = Ring(p, "pss", [128, 512], F32, 1, space="ps")
        ph = Ring(p, "ph", [128, 512], F32, 1, space="ps")
        po = Ring(p, "po", [128, D], F32, 1, space="ps")
        g_all = p.sb("g_all", [128, 8, TB], F32)
        t_g = [Tok() for _ in range(8)]
        cat = p.sb("cat", [128, 16, TB], BF16)
        t_cat = [Tok() for _ in range(16)]
        tmp = Ring(p, "tmp", [128, TB + 2], F32, 8)
        rstd = p.sb("rstd", [128, TB], F32)
        t_rstd = Tok()
        halo = Ring(p, "halo", [128, 4], F32, 2)
        rr = Ring(p, "rr", [128, D], F32, 1)
        r2 = Ring(p, "r2", [128, D], F32, 2, dma=True)
        junk = Ring(p, "junk", [128, D], BF16, 1)
        st_r = Ring(p, "stat", [128, 8], F32, 4)
        for bi in range(NB):
            t0 = bi * TB
            xb, t_xb, _ = xb_r.next()
            for k in range(8):
                st, stok, ssem = xst.next()
                p.dma("sp", st[:], xT[k * 128:(k + 1) * 128, t0:t0 + TB + 2], w=[stok], sem=ssem)
                p.op("pool", lambda e: e.tensor_copy(out=xb[:, k, :], in_=st[:]), r=[stok], w=[t_xb])
            ya, t_ya, ya_sem = yst.next()
            p.dma("sp", ya[:], yaT_v[:, :, t0:t0 + TB], w=[t_ya], sem=ya_sem)

            ss, t_ss, _ = pss.next()
            for j in range(8):
                z, t_z, _ = pp.next()
                for k in range(8):
                    p.op("pe", lambda e: e.matmul(z[:, 0:TB], lhsT=w1b[:, k, j * 128:(j + 1) * 128],
                                                  rhs=xb[:, k, 1:TB + 1], start=(k == 0), stop=(k == 7)),
                         r=[t_w1b, t_xb], w=[t_z])
                sz, t_sz, _ = tmp.next()
                p.op("act", lambda e: e.activation(out=sz[:, 0:TB], in_=z[:, 0:TB], func=AF.Silu), r=[t_z], w=[t_sz])
                p.op("dve", lambda e: e.tensor_tensor(out=g_all[:, j, :], in0=sz[:, 0:TB], in1=ya[:, j, :], op=ALU.mult),
                     r=[t_sz, t_ya], w=[t_g[j]])
                sq, t_sq, _ = tmp.next()
                p.op("act", lambda e: e.activation(out=sq[:, 0:TB], in_=g_all[:, j, :], func=AF.Square), r=[t_g[j]], w=[t_sq])
                p.op("pe", lambda e: e.matmul(ss[:, 0:TB], lhsT=ones_f[:], rhs=sq[:, 0:TB], start=(j == 0), stop=(j == 7)),
                     r=[t_ones, t_sq], w=[t_ss])
            lnv, t_lnv, _ = tmp.next()
            p.op("dve", lambda e: e.tensor_scalar(out=lnv[:, 0:TB], in0=ss[:, 0:TB], scalar1=1.0 / 1024, scalar2=EPS,
                                                  op0=ALU.mult, op1=ALU.add), r=[t_ss], w=[t_lnv])
            p.op("act", lambda e: e.activation(out=lnv[:, 0:TB], in_=lnv[:, 0:TB], func=AF.Ln), r=[t_lnv], w=[t_lnv])
            p.op("act", lambda e: e.activation(out=rstd[:], in_=lnv[:, 0:TB], func=AF.Exp, scale=-0.5), r=[t_lnv], w=[t_rstd])
            for j in range(8):
                p.op("dve", lambda e: e.scalar_tensor_tensor(out=cat[:, j, :], in0=g_all[:, j, :], scalar=normg_s[:, j:j + 1],
                                                             in1=rstd[:], op0=ALU.mult, op1=ALU.mult),
                     r=[t_g[j], t_rstd, t_c], w=[t_cat[j]])

            for j in range(8):
                def proj(grp, lo, n, dst, t_dst, first=True, last=True):
                    for k in range(8):
                        p.op("pe", lambda e: e.matmul(dst, lhsT=w1b[:, k, grp * 1024 + j * 128:grp * 1024 + (j + 1) * 128],
                                                      rhs=xb[:, k, lo:lo + n], start=(k == 0), stop=(k == 7)),
                             r=[t_w1b, t_xb], w=[t_dst])
                cg, t_cg, _ = pp.next()
                proj(2, 0, TB + 2, cg[:, 0:TB + 2], t_cg)
                hh, t_hh, _ = pp.next()
                proj(3, 0, TB + 2, hh[:, 0:TB + 2], t_hh)
                cgs, t_cgs, _ = tmp.next()
                p.op("act", lambda e: e.copy(out=cgs[:, 0:TB + 2], in_=cg[:, 0:TB + 2]), r=[t_cg], w=[t_cgs])
                u, t_u, _ = tmp.next()
                p.op("dve", lambda e: e.tensor_tensor(out=u[:, 0:TB + 2], in0=cgs[:, 0:TB + 2], in1=hh[:, 0:TB + 2], op=ALU.mult),
                     r=[t_cgs, t_hh], w=[t_u])
                c, t_cc, _ = tmp.next()
                p.op("dve", lambda e: e.tensor_scalar(out=c[:, 0:TB], in0=u[:, 0:TB], scalar1=scw_s[:, j * 3:j * 3 + 1], scalar2=None,
                                                      op0=ALU.mult), r=[t_u, t_c], w=[t_cc])
                p.op("dve", lambda e: e.scalar_tensor_tensor(out=c[:, 0:TB], in0=u[:, 1:TB + 1], scalar=scw_s[:, j * 3 + 1:j * 3 + 2],
                                                             in1=c[:, 0:TB], op0=ALU.mult, op1=ALU.add), r=[t_u, t_c], w=[t_cc])
                p.op("dve", lambda e: e.scalar_tensor_tensor(out=c[:, 0:TB], in0=u[:, 2:TB + 2], scalar=scw_s[:, j * 3 + 2:j * 3 + 3],
                                                             in1=c[:, 0:TB], op0=ALU.mult, op1=ALU.add), r=[t_u, t_c], w=[t_cc])
                bg, t_bg, _ = pp.next()
                proj(1, 1, TB, bg[:, 0:TB], t_bg)
                gt, t_gt, _ = pp.next()
                proj(4, 1, TB, gt[:, 0:TB], t_gt)
                sg, t_sg, _ = tmp.next()
                p.op("act", lambda e: e.activation(out=sg[:, 0:TB], in_=gt[:, 0:TB], func=AF.Silu), r=[t_gt], w=[t_sg])
                p.op("dve", lambda e: e.tensor_tensor(out=c[:, 0:TB], in0=c[:, 0:TB], in1=bg[:, 0:TB], op=ALU.mult),
                     r=[t_bg], w=[t_cc])
                p.op("dve", lambda e: e.tensor_tensor(out=cat[:, 8 + j, :], in0=c[:, 0:TB], in1=sg[:, 0:TB], op=ALU.mult),
                     r=[t_cc, t_sg], w=[t_cat[8 + j]])

            for tt in range(TB // 128):
                xk, t_xk, xk_sem = xtk.next()
                p.dma("sp", xk[:], xtok[t0 + tt * 128:t0 + (tt + 1) * 128, :], w=[t_xk], sem=xk_sem)
                o, t_o, _ = po.next()
                for half in range(2):
                    for kc in range(16):
                        p.op("pe", lambda e: e.matmul(o[:, half * 512:(half + 1) * 512], lhsT=cat[:, kc, tt * 128:(tt + 1) * 128],
                                                      rhs=woutb[:, kc, half * 512:(half + 1) * 512], start=(kc == 0), stop=(kc == 15)),
                             r=[t_cat[kc], t_woutb], w=[t_o])
                tok0 = t0 + tt * 128

                def post(q, t_q):
                    xtt, t_xtt, xtt_sem = xtt_r.next()
                    for hf in range(2):
                        tp, t_tp, _ = pp.next()
                        for kq in range(4):
                            kk = hf * 4 + kq
                            p.op("pe", lambda e: e.transpose(tp[:, kq * 128:(kq + 1) * 128], q[:, kk * 128:(kk + 1) * 128], idf[:]),
                                 r=[t_q, t_c], w=[t_tp])
                        p.op("act", lambda e: e.copy(out=xtt[:, hf * 4:(hf + 1) * 4, :], in_=tp[:].rearrange("p (k t) -> p k t", k=4)),
                             r=[t_tp], w=[t_xtt])
                    p.dma("act", x1T_blk(tok0 // 512 + 1, tok0 % 512, 128), xtt[:], r=[t_xtt], sem=xtt_sem)
                    if tok0 % 512 == 0:
                        p.dma("act", halo_v(x1HR, tok0 // 512), xtt[:, :, 0:8], r=[t_xtt], sem=xtt_sem)
                    if (tok0 + 128) % 512 == 0:
                        p.dma("act", halo_v(x1HL, (tok0 + 128) // 512), xtt[:, :, 120:128], r=[t_xtt], sem=xtt_sem)
                layer_norm_tail(p, o, t_o, xk, t_xk, rr, r2, junk, st_r, lng_s, lnb_s, t_c,
                                out[t0 + tt * 128:t0 + (tt + 1) * 128, :], post=post, is_out=False)
        barrier(p)


def emit_l0a(nc, p, S, A):
    BLK = 512
    NBLK = S // BLK
    xT, wg_all, cvw_all, cvb_all, hp_all, cst, msk, yaT = (A["xT"], A["wg"], A["cvw"], A["cvb"], A["hp"], A["cst"], A["msk"], A["yaT"])

    with ExitStack() as es:
        p.es = es
        wgb_l = [p.sb("wgb%d" % g, [128, 8, 520], BF16) for g in range(4)]
        t_wgb_l = [Tok() for g in range(4)]
        stage = Ring(p, "wst", [128, 520], F32, 2, dma=True)
        cvw_l = [p.sb("cvw_s%d" % g, [128, 16], F32) for g in range(4)]
        cvb_l = [p.sb("cvb_s%d" % g, [128, 4], F32) for g in range(4)]
        hp_l = [p.sb("hp_s%d" % g, [128, 24], F32) for g in range(4)]
        cst_s = p.sb("cst_s", [128, 512], F32)
        msk_s = p.sb("msk_s", [128, 1024], F32)
        mskb = p.sb("mskb", [128, 1024], BF16)
        identb = p.sb("identb", [128, 128], BF16)
        a_l = [p.sb("a_s%d" % g, [128, 8], F32) for g in range(4)]
        bias32_l = [p.sb("bias32_%d" % g, [128, 2, 4, 4], F32) for g in range(4)]
        a32_l = [p.sb("a32_%d" % g, [128, 2, 4, 4], F32) for g in range(4)]
        dsum_l = [p.sb("dsum%d" % g, [128, 4], F32) for g in range(4)]
        t_c = Tok()
        dc = p.dma_sem()
        for dst, src in ((cst_s, cst), (msk_s, msk)):
            p.dma("sp", dst[:], src[:, :], w=[t_c], sem=dc)
        U = cst_s[:, 0:128]
        UT = cst_s[:, 128:256]
        IDF = cst_s[:, 256:384]
        ONES = cst_s[:, 384:512]
        p.op("dve", lambda e: e.tensor_copy(out=mskb[:], in_=msk_s[:]), r=[t_c], w=[t_c])
        p.op("dve", lambda e: e.tensor_copy(out=identb[:], in_=IDF), r=[t_c], w=[t_c])

        xst = Ring(p, "xst", [128, BLK + 3], F32, 3, dma=True)
        xb_r = Ring(p, "xb", [128, 8, BLK + 3], BF16, 2)
        pG = Ring(p, "pG", [128, 512], F32, 6, space="ps")
        pH = Ring(p, "pHb", [128, 512], F32, 1, space="ps")
        pP = Ring(p, "pPp", [128, 512], F32, 1, space="ps")
        pre_r = Ring(p, "pre", [128, BLK + 3], F32, 3)
        cv_r = Ring(p, "cv", [128, BLK], F32, 2)
        xsf_r = Ring(p, "xsf", [128, 3, BLK], F32, 2)
        btb_r = Ring(p, "btb", [128, BLK], BF16, 2)
        ctb_r = Ring(p, "ctb", [128, BLK], BF16, 2)
        hs_r = Ring(p, "hs", [128, 16], F32, 2)
        dtv_r = Ring(p, "dtv", [128, 6, 16], F32, 2)
        sm_r = Ring(p, "sm", [128, 8, 4], F32, 3)
        W_r = Ring(p, "W", [128, 4, 128], F32, 2)
        E_r = Ring(p, "E", [128, 4, 128], F32, 2)
        M_r = Ring(p, "M", [128, 4, 128], BF16, 2)
        btk_r = Ring(p, "btk", [128, 128], BF16, 2)
        xd_r = Ring(p, "xd", [128, 256], BF16, 2)
        xdw_r = Ring(p, "xdw", [128, 256], BF16, 2)
        y_r = Ring(p, "y", [128, 256], F32, 3, dma=True)
        yT_r = Ring(p, "yT", [128, 256], F32, 3, dma=True)
        yt_r = Ring(p, "yt", [128, 256], F32, 3)
        yl_r = Ring(p, "yl", [128, 256], F32, 2, dma=True)
        H_l = [p.sb("H%d" % g, [128, 256], F32) for g in range(4)]
        Hb_l = [p.sb("Hb%d" % g, [128, 256], BF16) for g in range(4)]
        t_H_l = [Tok() for g in range(4)]
        t_Hb_l = [Tok() for g in range(4)]

        yds_r = Ring(p, "yds", [128, 256], F32, 2)

        def front(k, g, blk, c, dtv, t_dtv, xsf, t_xsf, btb, t_btb, ctb, t_ctb, Tri, t_ya):
            gc = blk * 4 + c
            cs_ = slice(c * 128, (c + 1) * 128)
            dA = dtv[:, 5, 4 * c:4 * c + 4]
            dtc = dtv[:, 4, 4 * c:4 * c + 4]
            W, t_W, _ = W_r.next()
            p.op("dve", lambda e: e.tensor_tensor(out=W[:], in0=Tri.unsqueeze(1).to_broadcast([128, 4, 128]),
                                                  in1=dA.unsqueeze(2).to_broadcast([128, 4, 128]), op=ALU.mult),
                 r=[t_dtv, t_c], w=[t_W])
            Eb, t_Eb, _ = pG.next()
            p.op("pe", lambda e: e.matmul(Eb[:], lhsT=ONES, rhs=W[:].rearrange("p r l -> p (r l)"), start=True, stop=False),
                 r=[t_W, t_c], w=[t_Eb])
            p.op("pe", lambda e: e.matmul(Eb[:], lhsT=identb[:], rhs=mskb[:, 512 * k:512 * (k + 1)], start=False, stop=True),
                 r=[t_c], w=[t_Eb])
            T_, t_T, _ = pG.next()
            Sm, t_Sm = T_[:, 384:512], t_T
            p.op("pe", lambda e: e.matmul(Sm[:, 0:4], lhsT=Tri, rhs=dA, start=True, stop=True), r=[t_dtv, t_c], w=[t_Sm])
            p.op("pe", lambda e: e.matmul(Sm[:, 4:8], lhsT=ONES, rhs=dA, start=True, stop=True), r=[t_dtv, t_c], w=[t_Sm])
            for m in range(3):
                p.op("pe", lambda e: e.transpose(T_[:, m * 128:(m + 1) * 128], xsf[:, m, cs_], IDF), r=[t_xsf, t_c], w=[t_T])
            Cb, t_Cb, _ = pG.next()
            p.op("pe", lambda e: e.matmul(Cb[:, 0:128], lhsT=btb[:, cs_], rhs=ctb[:, cs_], start=True, stop=True),
                 r=[t_btb, t_ctb], w=[t_Cb])
            sm, t_sm, _ = sm_r.next()
            CS, TOT, NCS, ECS, DTE, ETOT, DTW, D_ = [sm[:, i, :] for i in range(8)]
            p.op("act", lambda e: e.copy(out=sm[:, 0:2, :], in_=Sm[:, 0:8].rearrange("p (a r) -> p a r", a=2)), r=[t_Sm], w=[t_sm])
            p.op("dve", lambda e: e.tensor_scalar(out=NCS, in0=CS, scalar1=-1.0, scalar2=None, op0=ALU.mult), r=[t_sm], w=[t_sm])
            E, t_E, _ = E_r.next()
            for r_ in range(4):
                p.op("act", lambda e: e.activation(out=E[:, r_, :], in_=Eb[:, r_ * 128:(r_ + 1) * 128], func=AF.Exp,
                                                   bias=sm[:, 2, r_:r_ + 1]), r=[t_Eb, t_sm], w=[t_E])
            btk, t_btk, _ = btk_r.next()
            p.op("act", lambda e: e.copy(out=btk[:], in_=T_[:, 256:384]), r=[t_T], w=[t_btk])
            M, t_M, _ = M_r.next()
            p.op("dve", lambda e: e.tensor_tensor(out=M[:], in0=E[:], in1=Cb[:, 0:128].unsqueeze(1).to_broadcast([128, 4, 128]),
                                                  op=ALU.mult), r=[t_E, t_Cb], w=[t_M])
            p.op("act", lambda e: e.activation(out=ECS, in_=CS, func=AF.Exp), r=[t_sm], w=[t_sm])
            p.op("dve", lambda e: e.tensor_tensor(out=D_, in0=TOT, in1=CS, op=ALU.subtract), r=[t_sm], w=[t_sm])
            p.op("act", lambda e: e.activation(out=DTE, in_=D_, func=AF.Exp), r=[t_sm], w=[t_sm])
            p.op("act", lambda e: e.activation(out=ETOT, in_=TOT, func=AF.Exp), r=[t_sm], w=[t_sm])
            p.op("dve", lambda e: e.tensor_tensor(out=DTW, in0=DTE, in1=dtc, op=ALU.mult), r=[t_sm, t_dtv], w=[t_sm])
            xd, t_xd, _ = xd_r.next()
            xdw, t_xdw, _ = xdw_r.next()
            xs_tok = T_[:, 0:256].rearrange("p (r q) -> p r q", r=4)
            p.op("dve", lambda e: e.tensor_tensor(out=xd[:].rearrange("p (r q) -> p r q", r=4), in0=xs_tok,
                                                  in1=dtc.unsqueeze(2).to_broadcast([128, 4, 64]), op=ALU.mult),
                 r=[t_T, t_dtv], w=[t_xd])
            p.op("dve", lambda e: e.tensor_tensor(out=xdw[:].rearrange("p (r q) -> p r q", r=4), in0=xs_tok,
                                                  in1=DTW.unsqueeze(2).to_broadcast([128, 4, 64]), op=ALU.mult),
                 r=[t_T, t_sm], w=[t_xdw])
            yds, t_yds = None, None
            if k == 0:
                yds, t_yds, _ = yds_r.next()
                p.op("dve", lambda e: e.tensor_tensor(out=yds[:].rearrange("p (r q) -> p r q", r=4), in0=xs_tok,
                                                      in1=dsum_l[g][:].unsqueeze(2).to_broadcast([128, 4, 64]), op=ALU.mult),
                     r=[t_T, t_c], w=[t_yds])
            return dict(k=k, g=g, gc=gc, cs_=cs_, ctb=ctb, t_ctb=t_ctb, M=M, t_M=t_M, xd=xd, t_xd=t_xd, xdw=xdw, t_xdw=t_xdw,
                        btk=btk, t_btk=t_btk, ECS=ECS, ETOT=ETOT, t_sm=t_sm, yds=yds, t_yds=t_yds, t_ya=t_ya)

        def back(s_):
            k, g, gc, cs_ = s_["k"], s_["g"], s_["gc"], s_["cs_"]
            ctb, t_ctb, M, t_M, xd, t_xd, xdw, t_xdw = (s_["ctb"], s_["t_ctb"], s_["M"], s_["t_M"], s_["xd"], s_["t_xd"],
                                                        s_["xdw"], s_["t_xdw"])
            btk, t_btk, ECS, ETOT, t_sm, yds, t_yds, t_ya = (s_["btk"], s_["t_btk"], s_["ECS"], s_["ETOT"], s_["t_sm"],
                                                             s_["yds"], s_["t_yds"], s_["t_ya"])
            H, Hb, t_H, t_Hb = H_l[g], Hb_l[g], t_H_l[g], t_Hb_l[g]
            Y, t_Y, _ = pG.next()
            for r_ in range(4):
                p.op("pe", lambda e: e.matmul(Y[:, r_ * 64:(r_ + 1) * 64], lhsT=M[:, r_, :], rhs=xd[:, r_ * 64:(r_ + 1) * 64],
                                              start=True, stop=True), r=[t_M, t_xd], w=[t_Y])
            p.op("pe", lambda e: e.matmul(Y[:, 256:512], lhsT=ctb[:, cs_], rhs=Hb[:], start=True, stop=True),
                 r=[t_ctb, t_Hb], w=[t_Y])
            ST, t_ST, _ = pG.next()
            p.op("pe", lambda e: e.matmul(ST[:, 0:256], lhsT=btk[:], rhs=xdw[:], start=True, stop=True),
                 r=[t_btk, t_xdw], w=[t_ST])
            yt, t_yt, _ = yt_r.next()
            p.op("dve", lambda e: e.tensor_tensor(out=yt[:].rearrange("p (r q) -> p r q", r=4),
                                                  in0=Y[:, 256:512].rearrange("p (r q) -> p r q", r=4),
                                                  in1=ECS.unsqueeze(2).to_broadcast([128, 4, 64]), op=ALU.mult),
                 r=[t_Y, t_sm], w=[t_yt])
            yo, t_yo, yo_sem = y_r.next()
            p.op("dve", lambda e: e.tensor_tensor(out=yo[:], in0=yt[:], in1=Y[:, 0:256], op=ALU.add), r=[t_yt, t_Y], w=[t_yo])
            if k == 0:
                p.op("dve", lambda e: e.tensor_tensor(out=yo[:], in0=yo[:], in1=yds[:], op=ALU.add), r=[t_yds], w=[t_yo])
            p.op("dve", lambda e: e.tensor_tensor(out=H[:].rearrange("p (r q) -> p r q", r=4),
                                                  in0=H[:].rearrange("p (r q) -> p r q", r=4),
                                                  in1=ETOT.unsqueeze(2).to_broadcast([128, 4, 64]), op=ALU.mult),
                 r=[t_sm], w=[t_H])
            p.op("dve", lambda e: e.tensor_tensor(out=H[:], in0=H[:], in1=ST[:, 0:256], op=ALU.add), r=[t_ST], w=[t_H])
            p.op("act", lambda e: e.copy(out=Hb[:], in_=H[:]), r=[t_H], w=[t_Hb])
            ydst = yaT[g * 256:(g + 1) * 256, gc * 128:(gc + 1) * 128].rearrange("(j q) t -> q j t", q=128)
            T2, t_T2, _ = pG.next()
            for j in range(2):
                p.op("pe", lambda e: e.transpose(T2[:, j * 128:(j + 1) * 128], yo[:, j * 128:(j + 1) * 128], IDF), r=[t_yo, t_c], w=[t_T2])
            yoT, t_yoT, yoT_sem = yT_r.next()
            if k == 0:
                p.op("act", lambda e: e.copy(out=yoT[:], in_=T2[:, 0:256]), r=[t_T2], w=[t_yoT])
            else:
                yl, t_yl, yl_sem = yl_r.next()
                p.dma("act", yl[:].rearrange("q (j t) -> q j t", j=2), ydst, r=[t_ya[gc]], w=[t_yl], sem=yl_sem)
                p.op("dve", lambda e: e.tensor_tensor(out=yoT[:], in0=T2[:, 0:256], in1=yl[:], op=ALU.add), r=[t_T2, t_yl], w=[t_yoT])
            p.dma("act", ydst, yoT[:].rearrange("q (j t) -> q j t", j=2), r=[t_yoT], w=[t_ya[gc]], sem=yoT_sem)

        for g in range(4):
            cvw_s, cvb_s, hp_s, a_s, bias32, a32, dsum = cvw_l[g], cvb_l[g], hp_l[g], a_l[g], bias32_l[g], a32_l[g], dsum_l[g]
            for dst, src in ((cvw_s, cvw_all[g]), (cvb_s, cvb_all[g]), (hp_s, hp_all[g])):
                p.dma("sp", dst[:], src, w=[t_c], sem=dc)
            p.op("act", lambda e: e.activation(out=a_s[:], in_=hp_s[:, 0:8], func=AF.Exp), r=[t_c], w=[t_c])
            p.op("dve", lambda e: e.tensor_scalar(out=a_s[:], in0=a_s[:], scalar1=-1.0, scalar2=None, op0=ALU.mult), r=[t_c], w=[t_c])
            for k in range(2):
                for c in range(4):
                    p.op("dve", lambda e: e.tensor_copy(out=bias32[:, k, c, :], in_=hp_s[:, 8 + 4 * k:12 + 4 * k]), r=[t_c], w=[t_c])
                    p.op("dve", lambda e: e.tensor_copy(out=a32[:, k, c, :], in_=a_s[:, 4 * k:4 * k + 4]), r=[t_c], w=[t_c])
            p.op("dve", lambda e: e.tensor_tensor(out=dsum[:], in0=hp_s[:, 16:20], in1=hp_s[:, 20:24], op=ALU.add), r=[t_c], w=[t_c])
            load_cast_weight(p, wg_all[g], wgb_l[g], stage, 8, 520, cw=520, engines=("dve", "act"), tok=t_wgb_l[g])
        t_ya_l = [[Tok() for _ in range(S // 128)] for g in range(4)]
        for k in range(2):
            pend = None
            for g in range(4):
                p.op("dve", lambda e: e.memset(H_l[g][:], 0.0), w=[t_H_l[g]])
                p.op("dve", lambda e: e.memset(Hb_l[g][:], 0.0), w=[t_Hb_l[g]])
            Tri = U if k == 0 else UT
            blocks = range(NBLK) if k == 0 else range(NBLK - 1, -1, -1)
            for blk in blocks:
                e0 = blk * BLK
                xb, t_xb, _ = xb_r.next()
                for kk in range(8):
                    st, stok, ssem = xst.next()
                    p.dma("sp", st[:], xT[kk * 128:(kk + 1) * 128, e0:e0 + BLK + 3], w=[stok], sem=ssem)
                    p.op("pool", lambda e: e.tensor_copy(out=xb[:, kk, :], in_=st[:]), r=[stok], w=[t_xb])
                for g in range(4):
                    wgb, t_wgb, cvw_s, cvb_s, bias32, a32, t_ya = wgb_l[g], t_wgb_l[g], cvw_l[g], cvb_l[g], bias32_l[g], a32_l[g], t_ya_l[g]
                    hb, t_hb, _ = pH.next()
                    for c in range(4):
                        for kk in range(8):
                            p.op("pe", lambda e: e.matmul(hb[:, 16 + 4 * c:20 + 4 * c], lhsT=xb[:, kk, 2 + c * 128:2 + (c + 1) * 128],
                                                          rhs=wgb[:, kk, 512 + 4 * k:516 + 4 * k], start=(kk == 0), stop=(kk == 7)),
                                 r=[t_xb, t_wgb], w=[t_hb])
                    dtv, t_dtv, _ = dtv_r.next()
                    V, AV, EE, LL, DT, DA = [dtv[:, i, :] for i in range(6)]
                    b32 = bias32[:, k, :, :].rearrange("p c r -> p (c r)")
                    A32 = a32[:, k, :, :].rearrange("p c r -> p (c r)")
                    p.op("dve", lambda e: e.tensor_tensor(out=V, in0=hb[:, 16:32], in1=b32, op=ALU.add), r=[t_hb, t_c], w=[t_dtv])
                    p.op("dve", lambda e: e.tensor_scalar(out=AV, in0=V, scalar1=-1.0, scalar2=None, op0=ALU.mult), r=[t_dtv], w=[t_dtv])
                    p.op("dve", lambda e: e.tensor_tensor(out=AV, in0=AV, in1=V, op=ALU.max), r=[t_dtv], w=[t_dtv])
                    p.op("act", lambda e: e.activation(out=EE, in_=AV, func=AF.Exp, scale=-1.0), r=[t_dtv], w=[t_dtv])
                    p.op("act", lambda e: e.activation(out=LL, in_=EE, func=AF.Ln, bias=1.0), r=[t_dtv], w=[t_dtv])
                    p.op("dve", lambda e: e.scalar_tensor_tensor(out=DT, in0=V, scalar=0.0, in1=LL, op0=ALU.max, op1=ALU.add), r=[t_dtv], w=[t_dtv])
                    p.op("dve", lambda e: e.tensor_tensor(out=DA, in0=DT, in1=A32, op=ALU.mult), r=[t_dtv, t_c], w=[t_dtv])

                    xsf, t_xsf, _ = xsf_r.next()
                    btb, t_btb, _ = btb_r.next()
                    ctb, t_ctb, _ = ctb_r.next()
                    for m in range(4):
                        P, t_P, _ = pP.next()
                        for kk in range(8):
                            p.op("pe", lambda e: e.matmul(P[:, 0:BLK], lhsT=wgb[:, kk, m * 128:(m + 1) * 128], rhs=xb[:, kk, 0:BLK],
                                                          start=(kk == 0), stop=(kk == 7)), r=[t_xb, t_wgb], w=[t_P])
                        for kk in range(8):
                            p.op("pe", lambda e: e.matmul(hb[:, 4 * m:4 * m + 3], lhsT=wgb[:, kk, m * 128:(m + 1) * 128],
                                                          rhs=xb[:, kk, BLK:BLK + 3], start=(kk == 0), stop=(kk == 7)),
                                 r=[t_xb, t_wgb], w=[t_hb])
                        pre, t_pre, _ = pre_r.next()
                        p.op("act", lambda e: e.copy(out=pre[:, 0:BLK], in_=P[:, 0:BLK]), r=[t_P], w=[t_pre])
                        p.op("act", lambda e: e.copy(out=pre[:, BLK:BLK + 3], in_=hb[:, 4 * m:4 * m + 3]), r=[t_hb], w=[t_pre])
                        cv, t_cv, _ = cv_r.next()
                        p.op("dve", lambda e: e.tensor_scalar(out=cv[:], in0=pre[:, 0:BLK], scalar1=cvw_s[:, 4 * m:4 * m + 1], scalar2=None,
                                                              op0=ALU.mult), r=[t_pre, t_c], w=[t_cv])
                        for tap in range(1, 4):
                            p.op("dve", lambda e: e.scalar_tensor_tensor(out=cv[:], in0=pre[:, tap:tap + BLK],
                                                                         scalar=cvw_s[:, 4 * m + tap:4 * m + tap + 1], in1=cv[:],
                                                                         op0=ALU.mult, op1=ALU.add), r=[t_pre, t_c], w=[t_cv])
                        if m < 3:
                            p.op("act", lambda e: e.activation(out=xsf[:, m, :], in_=cv[:], func=AF.Silu, bias=cvb_s[:, m:m + 1]),
                                 r=[t_cv, t_c], w=[t_xsf])
                            if m == 2:
                                p.op("act", lambda e: e.copy(out=btb[:], in_=xsf[:, 2, :]), r=[t_xsf], w=[t_btb])
                        else:
                            p.op("act", lambda e: e.activation(out=ctb[:], in_=cv[:], func=AF.Silu, bias=cvb_s[:, m:m + 1]),
                                 r=[t_cv, t_c], w=[t_ctb])

                    chunks = range(4) if k == 0 else range(3, -1, -1)
                    for c in chunks:
                        st_ = front(k, g, blk, c, dtv, t_dtv, xsf, t_xsf, btb, t_btb, ctb, t_ctb, Tri, t_ya)
                        if pend is not None:
                            back(pend)
                        pend = st_
            back(pend)
            pend = None
        barrier(p)


def l0a_consts():
    t = np.arange(128)
    U = (t[:, None] <= t[None, :]).astype(np.float32)
    UT = (t[:, None] >= t[None, :]).astype(np.float32)
    I = np.eye(128, dtype=np.float32)
    ones = np.ones((128, 128), np.float32)
    cst = np.ascontiguousarray(np.concatenate([U, UT, I, ones], axis=1))
    mf = np.where(t[None, :] < t[:, None], NEG, 0.0).astype(np.float32)
    mb = np.where(t[None, :] > t[:, None], NEG, 0.0).astype(np.float32)
    msk = np.ascontiguousarray(np.concatenate([np.tile(mf, (1, 4)), np.tile(mb, (1, 4))], axis=1))
    return cst, msk


def prep_l0a(x, ev_w_in, ev_conv_w, ev_conv_b, ev_a_log, ev_dt_bias, ev_d_skip, S=SEQ):
    w_in = ev_w_in[0]
    cw = ev_conv_w[0]
    cb = ev_conv_b[0]
    cst, msk = l0a_consts()
    maps = []
    xTs = []
    for b in range(2):
        xe = np.zeros((D, S + 3), np.float32)
        xe[:, 2:S + 2] = x[b, :S, :].T
        xTs.append(xe)
    for c in range(NCORES):
        b, g = c // 4, c % 4
        xs_cols = 1024 + g * 256 + np.arange(256)
        b_cols = 1024 + 1024 + g * 128 + np.arange(128)
        c_cols = 1024 + 1536 + g * 128 + np.arange(128)
        dt_cols = np.concatenate([3072 + k * 16 + 4 * g + np.arange(4) for k in range(2)])
        cols = np.concatenate([xs_cols, b_cols, c_cols, dt_cols])
        wg = np.ascontiguousarray(w_in[:, cols])
        xbc_idx = cols[:512] - 1024
        cvw = np.ascontiguousarray(cw[:, xbc_idx].reshape(4, 4, 128).transpose(2, 1, 0).reshape(128, 16))
        cvb = np.ascontiguousarray(cb[xbc_idx].reshape(4, 128).T)
        hsel = np.concatenate([np.stack([v[0][k, 4 * g:4 * g + 4] for k in range(2)]).reshape(-1)
                               for v in (ev_a_log, ev_dt_bias, ev_d_skip)])
        hp = np.ascontiguousarray(np.broadcast_to(hsel[None, :], (128, 24))).astype(np.float32)
        maps.append({"xT": xTs[b], "wg": wg, "cvw": cvw, "cvb": cvb, "hp": hp, "cst": cst, "msk": msk})
    return maps


def rope_tables(p, posi, t_posi, n, invf, sgn, t_c, tabs, cosd, sind, t_cos, t_sin):
    R = slice(64, 96)
    ang, t_a, _ = tabs.next()
    nf, t_n, _ = tabs.next()
    ni, t_ni, _ = tabs.next()
    mm, t_m, _ = tabs.next()
    A, N, M = ang[R, 0:n], nf[R, 0:n], mm[R, 0:n]
    NI = ni[R, 0:n].bitcast(I32)
    p.op("dve", lambda e: e.tensor_copy(out=A, in_=posi[R, 0:n]), r=[t_posi], w=[t_a])
    p.op("dve", lambda e: e.tensor_scalar(out=A, in0=A, scalar1=invf[R, 0:1], scalar2=None, op0=ALU.mult), r=[t_c], w=[t_a])
    p.op("dve", lambda e: e.tensor_scalar(out=N, in0=A, scalar1=1.0 / TWO_PI, scalar2=None, op0=ALU.mult), r=[t_a], w=[t_n])
    p.op("dve", lambda e: e.tensor_copy(out=NI, in_=N), r=[t_n], w=[t_ni])
    p.op("dve", lambda e: e.tensor_copy(out=N, in_=NI), r=[t_ni], w=[t_n])
    p.op("dve", lambda e: e.scalar_tensor_tensor(out=A, in0=N, scalar=-C1, in1=A, op0=ALU.mult, op1=ALU.add), r=[t_n], w=[t_a])
    p.op("dve", lambda e: e.scalar_tensor_tensor(out=A, in0=N, scalar=-C2, in1=A, op0=ALU.mult, op1=ALU.add), r=[t_n], w=[t_a])

    def wrap(X, t_x):
        p.op("dve", lambda e: e.tensor_scalar(out=M, in0=X, scalar1=PI, scalar2=None, op0=ALU.is_gt), r=[t_x], w=[t_m])
        p.op("dve", lambda e: e.scalar_tensor_tensor(out=X, in0=M, scalar=-TWO_PI, in1=X, op0=ALU.mult, op1=ALU.add), r=[t_m], w=[t_x])
        p.op("dve", lambda e: e.tensor_scalar(out=M, in0=X, scalar1=-PI, scalar2=None, op0=ALU.is_lt), r=[t_x], w=[t_m])
        p.op("dve", lambda e: e.scalar_tensor_tensor(out=X, in0=M, scalar=TWO_PI, in1=X, op0=ALU.mult, op1=ALU.add), r=[t_m], w=[t_x])
    wrap(A, t_a)
    p.op("act", lambda e: e.activation(out=N, in_=A, func=AF.Sin), r=[t_a], w=[t_n])
    p.op("dve", lambda e: e.tensor_scalar(out=sind, in0=N, scalar1=sgn[R, 0:1], scalar2=None, op0=ALU.mult), r=[t_n, t_c], w=[t_sin])
    p.op("dve", lambda e: e.tensor_scalar(out=A, in0=A, scalar1=PI / 2, scalar2=None, op0=ALU.add), r=[t_a], w=[t_a])
    wrap(A, t_a)
    p.op("act", lambda e: e.activation(out=cosd, in_=A, func=AF.Sin), r=[t_a], w=[t_cos])


def emit_l1(nc, p, S, A):
    T = S // 4
    BLK = 512
    NKB = S // BLK
    NQB = T // BLK
    NK128 = S // 128
    x1T, x1, posb, out = A["x1T"], A["x1"], A["posb"], A["out"]
    w_cq, w_kv, w_kr, w_g3, w_uq, w_ukv, w_pool, wout = (A["w_cq"], A["w_kv"], A["w_kr"], A["w_g3"], A["w_uq"], A["w_ukv"],
                                                         A["w_pool"], A["wout_od"])
    sm_c, sel, lng, lnb = A["sm_c"], A["sel"], A["lng_od"], A["lnb_od"]
    oxT, ox1, oHL, oHR, opos = A["own_x1T"], A["own_x1"], A["own_HL"], A["own_HR"], A["own_pos"]

    def xrows_static(blk, kk):
        return x1T[blk * 1024 + kk * 128:blk * 1024 + (kk + 1) * 128, :]

    def xrows_own(blk, kk):
        return oxT[blk * 1024 + kk * 128:blk * 1024 + (kk + 1) * 128, :]
    xtok = ox1

    with ExitStack() as es:
        p.es = es
        smc = p.sb("smc", [128, 80], F32)
        sel_s = p.sb("sel_s", [128, 384], F32)
        t_c = Tok()
        dc = p.dma_sem()
        p.dma("sp", smc[:], sm_c[:, :], w=[t_c], sem=dc)
        p.dma("sp", sel_s[:], sel[:, :], w=[t_c], sem=dc)
        QG, KVG, INVF, SGN, PSC = smc[:, 0:2], smc[:, 2:3], smc[:, 3:4], smc[:, 4:5], smc[:, 5:9]
        CORR = smc[:, 16:80]
        SEL_E, SEL_O, ONES = sel_s[:, 0:128], sel_s[:, 128:256], sel_s[:, 256:384]
        ycT = p.sb("ycT", [128, 4, T], BF16)
        t_yc = [[Tok() for _ in range(NQB)] for _ in range(8)]
        esA = ExitStack()
        p.es = esA
        ckvn = p.sb("ckvn", [128, S], BF16)
        t_ckvn = [Tok() for _ in range(NKB)]
        Kbuf = p.sb("Kbuf", [96, S], BF16)
        t_kn = [Tok() for _ in range(NKB)]
        t_kr = [Tok() for _ in range(NKB)]
        cqn = p.sb("cqn", [128, 2, T], BF16)
        t_cqn = [Tok() for _ in range(NQB)]
        cosq = p.sb("cosq", [96, T], BF16)
        sinq = p.sb("sinq", [96, T], BF16)
        t_cosq = [Tok() for _ in range(NQB)]
        t_sinq = [Tok() for _ in range(NQB)]
        wuqb = p.sb("wuqb", [128, 2, 1536], BF16)
        wukvb = p.sb("wukvb", [128, 1024], BF16)
        t_wuq, t_wukv = Tok(), Tok()

        with ExitStack() as es1:
            p.es = es1
            wst = Ring(p, "wst", [128, 1536], F32, 1, dma=True)
            wcqb = p.sb("wcqb", [128, 8, 256], BF16)
            wkvb = p.sb("wkvb", [128, 8, 128], BF16)
            wkrb = p.sb("wkrb", [128, 8, 192], BF16)
            t_wcq, t_wkv, t_wkr = Tok(), Tok(), Tok()
            load_cast_weight(p, w_cq, wcqb, wst, 8, 256, cw=256, tok=t_wcq)
            load_cast_weight(p, w_kv, wkvb, wst, 8, 128, cw=128, tok=t_wkv)
            load_cast_weight(p, w_kr, wkrb, wst, 8, 192, cw=192, tok=t_wkr)
            for kc in range(2):
                st, stok, ssem = wst.next()
                p.dma("sp", st[:, 0:1536], w_uq[kc * 128:(kc + 1) * 128, :], w=[stok], sem=ssem)
                p.op("dve", lambda e: e.tensor_scalar(out=wuqb[:, kc, :], in0=st[:, 0:1536], scalar1=QG[:, kc:kc + 1], scalar2=None,
                                                      op0=ALU.mult), r=[stok, t_c], w=[t_wuq])
            st, stok, ssem = wst.next()
            p.dma("sp", st[:, 0:1024], w_ukv[:, :], w=[stok], sem=ssem)
            p.op("dve", lambda e: e.tensor_scalar(out=wukvb[:], in0=st[:, 0:1024], scalar1=KVG, scalar2=None, op0=ALU.mult),
                 r=[stok, t_c], w=[t_wukv])

            xst = Ring(p, "xst", [128, BLK], F32, 3, dma=True)
            xb_r = Ring(p, "xb", [128, 8, BLK], BF16, 2)
            pos_r = Ring(p, "posr", [128, BLK], I32, 2, dma=True)
            tabs = Ring(p, "tabs", [128, BLK], F32, 4)
            cs_r = Ring(p, "csr", [128, BLK], F32, 2)
            sn_r = Ring(p, "snr", [128, BLK], F32, 2)
            sq_r = Ring(p, "sqr", [128, BLK], F32, 3)
            t1_r = Ring(p, "t1r", [128, BLK], F32, 2)
            pA = Ring(p, "pA", [128, 512], F32, 5, space="ps")
            pSS = Ring(p, "pSS", [128, 512], F32, 2, space="ps")

            def load_xblock(rows_fn, blk):
                xb, t_xb, _ = xb_r.next()
                for kk in range(8):
                    st, stok, ssem = xst.next()
                    p.dma("sp", st[:], rows_fn(blk, kk), w=[stok], sem=ssem)
                    p.op("pool", lambda e: e.tensor_copy(out=xb[:, kk, :], in_=st[:]), r=[stok], w=[t_xb])
                return xb, t_xb

            def rstd_of(ss, t_ss, nch):
                r_, t_r, _ = sq_r.next()
                p.op("dve", lambda e: e.tensor_scalar(out=r_[:], in0=ss[:], scalar1=1.0 / nch, scalar2=EPS, op0=ALU.mult, op1=ALU.add),
                     r=[t_ss], w=[t_r])
                p.op("act", lambda e: e.activation(out=r_[:], in_=r_[:], func=AF.Ln), r=[t_r], w=[t_r])
                p.op("act", lambda e: e.activation(out=r_[:], in_=r_[:], func=AF.Exp, scale=-0.5), r=[t_r], w=[t_r])
                return r_, t_r

            for kb in range(NKB):
                c0 = kb * BLK
                xb, t_xb = load_xblock(xrows_static, kb + 1)
                pi_, t_pi, pi_sem = pos_r.next()
                p.dma("sp", pi_[64:96, :], posb[kb * 32:(kb + 1) * 32, :], w=[t_pi], sem=pi_sem)
                ck, t_ck, _ = pA.next()
                ka, t_ka, _ = pA.next()
                kbs, t_kbs, _ = pA.next()
                for kk in range(8):
                    p.op("pe", lambda e: e.matmul(ck[:], lhsT=wkvb[:, kk, :], rhs=xb[:, kk, :], start=(kk == 0), stop=(kk == 7)),
                         r=[t_wkv, t_xb], w=[t_ck])
                for kk in range(8):
                    p.op("pe", lambda e: e.matmul(ka[0:96, :], lhsT=wkrb[:, kk, 0:96], rhs=xb[:, kk, :], start=(kk == 0), stop=(kk == 7)),
                         r=[t_wkr, t_xb], w=[t_ka])
                for kk in range(8):
                    p.op("pe", lambda e: e.matmul(kbs[0:96, :], lhsT=wkrb[:, kk, 96:192], rhs=xb[:, kk, :], start=(kk == 0), stop=(kk == 7)),
                         r=[t_wkr, t_xb], w=[t_kbs])
                sq, t_sq, _ = sq_r.next()
                p.op("act", lambda e: e.activation(out=sq[:], in_=ck[:], func=AF.Square), r=[t_ck], w=[t_sq])
                ss, t_ss, _ = pSS.next()
                p.op("pe", lambda e: e.matmul(ss[:], lhsT=ONES, rhs=sq[:], start=True, stop=True), r=[t_sq, t_c], w=[t_ss])
                rs, t_rs = rstd_of(ss, t_ss, 128)
                p.op("dve", lambda e: e.tensor_tensor(out=ckvn[:, c0:c0 + BLK], in0=ck[:], in1=rs[:], op=ALU.mult),
                     r=[t_ck, t_rs], w=[t_ckvn[kb]])
                cs_, t_cs, _ = cs_r.next()
                sn_, t_sn, _ = sn_r.next()
                rope_tables(p, pi_, t_pi, BLK, INVF, SGN, t_c, tabs, cs_[64:96, :], sn_[64:96, :], t_cs, t_sn)
                t1, t_t1, _ = t1_r.next()
                t2, t_t2, _ = t1_r.next()
                p.op("dve", lambda e: e.tensor_tensor(out=t1[64:96, :], in0=ka[64:96, :], in1=cs_[64:96, :], op=ALU.mult),
                     r=[t_ka, t_cs], w=[t_t1])
                p.op("dve", lambda e: e.tensor_tensor(out=t2[64:96, :], in0=kbs[64:96, :], in1=sn_[64:96, :], op=ALU.mult),
                     r=[t_kbs, t_sn], w=[t_t2])
                p.op("pool", lambda e: e.tensor_tensor(out=Kbuf[64:96, c0:c0 + BLK], in0=t1[64:96, :], in1=t2[64:96, :], op=ALU.add),
                     r=[t_t1, t_t2], w=[t_kr[kb]])

            for qb in range(NQB):
                c0 = qb * BLK
                xb, t_xb = load_xblock(xrows_own, qb)
                pi_, t_pi, pi_sem = pos_r.next()
                p.dma("sp", pi_[64:96, :], opos[qb * 32:(qb + 1) * 32, :], w=[t_pi], sem=pi_sem)
                cqs = []
                ss, t_ss, _ = pSS.next()
                for m in range(2):
                    cq, t_cq, _ = pA.next()
                    for kk in range(8):
                        p.op("pe", lambda e: e.matmul(cq[:], lhsT=wcqb[:, kk, m * 128:(m + 1) * 128], rhs=xb[:, kk, :],
                                                      start=(kk == 0), stop=(kk == 7)), r=[t_wcq, t_xb], w=[t_cq])
                    sq, t_sq, _ = sq_r.next()
                    p.op("act", lambda e: e.activation(out=sq[:], in_=cq[:], func=AF.Square), r=[t_cq], w=[t_sq])
                    p.op("pe", lambda e: e.matmul(ss[:], lhsT=ONES, rhs=sq[:], start=(m == 0), stop=(m == 1)), r=[t_sq, t_c], w=[t_ss])
                    cqs.append((cq, t_cq))
                rs, t_rs = rstd_of(ss, t_ss, 256)
                for m in range(2):
                    cq, t_cq = cqs[m]
                    p.op("dve", lambda e: e.tensor_tensor(out=cqn[:, m, c0:c0 + BLK], in0=cq[:], in1=rs[:], op=ALU.mult),
                         r=[t_cq, t_rs], w=[t_cqn[qb]])
                rope_tables(p, pi_, t_pi, BLK, INVF, SGN, t_c, tabs, cosq[64:96, c0:c0 + BLK], sinq[64:96, c0:c0 + BLK],
                            t_cosq[qb], t_sinq[qb])
        barrier(p)

        with ExitStack() as es3:
            p.es = es3
            Vbuf = p.sb("Vbuf", [128, NK128, 128], BF16)
            t_v = [Tok() for _ in range(NK128 // 8 if NK128 >= 8 else 1)]
            VG = min(8, NK128)
            Q_r = Ring(p, "Q", [96, T], BF16, 2)
            tq_r = Ring(p, "tq", [96, BLK], F32, 4)
            P_r = Ring(p, "P", [128, BLK], BF16, 3)
            osb_r = Ring(p, "osb", [128, BLK], F32, 2)
            rden_r = Ring(p, "rden", [128, BLK], F32, 2)
            pS = Ring(p, "pS", [128, 512], F32, 3, space="ps")
            pO = Ring(p, "pO", [128, 512], F32, 2, space="ps")
            pD = Ring(p, "pD", [128, 512], F32, 1, space="ps")
            pB = Ring(p, "pB", [128, 512], F32, 2, space="ps")
            for h in range(8):
                odd = h % 2
                voff = 64 * odd
                Q, _, _ = Q_r.next()
                t_Q = [Tok() for _ in range(NQB)]
                for qb in range(NQB):
                    c0 = qb * BLK
                    qa, t_qa, _ = pB.next()
                    qs, t_qs, _ = pB.next()
                    for kc in range(2):
                        p.op("pe", lambda e: e.matmul(qa[0:96, :], lhsT=wuqb[:, kc, h * 96:(h + 1) * 96], rhs=cqn[:, kc, c0:c0 + BLK],
                                                      start=(kc == 0), stop=(kc == 1)), r=[t_wuq, t_cqn[qb]], w=[t_qa])
                    for kc in range(2):
                        p.op("pe", lambda e: e.matmul(qs[0:96, :], lhsT=wuqb[:, kc, 768 + h * 96:768 + (h + 1) * 96],
                                                      rhs=cqn[:, kc, c0:c0 + BLK], start=(kc == 0), stop=(kc == 1)),
                             r=[t_wuq, t_cqn[qb]], w=[t_qs])
                    p.op("dve", lambda e: e.tensor_copy(out=Q[0:64, c0:c0 + BLK], in_=qa[0:64, :]), r=[t_qa], w=[t_Q[qb]])
                    t1, t_t1, _ = tq_r.next()
                    t2, t_t2, _ = tq_r.next()
                    p.op("dve", lambda e: e.tensor_tensor(out=t1[64:96, :], in0=qa[64:96, :], in1=cosq[64:96, c0:c0 + BLK], op=ALU.mult),
                         r=[t_qa, t_cosq[qb]], w=[t_t1])
                    p.op("dve", lambda e: e.tensor_tensor(out=t2[64:96, :], in0=qs[64:96, :], in1=sinq[64:96, c0:c0 + BLK], op=ALU.mult),
                         r=[t_qs, t_sinq[qb]], w=[t_t2])
                    p.op("pool", lambda e: e.tensor_tensor(out=Q[64:96, c0:c0 + BLK], in0=t1[64:96, :], in1=t2[64:96, :], op=ALU.add),
                         r=[t_t1, t_t2], w=[t_Q[qb]])
                for kb in range(NKB):
                    c0 = kb * BLK
                    kp, t_kp, _ = pB.next()
                    p.op("pe", lambda e: e.matmul(kp[0:64, :], lhsT=wukvb[:, h * 128:h * 128 + 64], rhs=ckvn[:, c0:c0 + BLK],
                                                  start=True, stop=True), r=[t_wukv, t_ckvn[kb]], w=[t_kp])
                    p.op("dve", lambda e: e.tensor_copy(out=Kbuf[0:64, c0:c0 + BLK], in_=kp[0:64, :]), r=[t_kp], w=[t_kn[kb]])
                for g in range(len(t_v)):
                    vp, t_vp, _ = pB.next()
                    for j in range(VG):
                        k128 = g * VG + j
                        p.op("pe", lambda e: e.matmul(vp[:, j * 64:(j + 1) * 64], lhsT=ckvn[:, k128 * 128:(k128 + 1) * 128],
                                                      rhs=wukvb[:, h * 128 + 64:h * 128 + 128], start=True, stop=True),
                             r=[t_wukv, t_ckvn[k128 // 4]], w=[t_vp])
                    vs = Vbuf[:, g * VG:(g + 1) * VG, :]
                    p.op("pool", lambda e: e.memset(vs[:, :, 64 - voff:128 - voff], 0.0), w=[t_v[g]])
                    p.op("pool", lambda e: e.memset(vs[:, :, 64 - voff:65 - voff], 1.0), w=[t_v[g]])
                    p.op("dve", lambda e: e.tensor_copy(out=vs[:, :, voff:voff + 64], in_=vp[:, 0:VG * 64].rearrange("p (j v) -> p j v", v=64)),
                         r=[t_vp], w=[t_v[g]])
                MV = 128 if odd else 65
                for qb in range(NQB):
                    q0 = qb * BLK
                    O, t_O, _ = pO.next()
                    Sq = {}

                    def issue_S(k128):
                        Sx, t_S, _ = pS.next()
                        kb = k128 // 4
                        p.op("pe", lambda e: e.matmul(Sx[:], lhsT=Kbuf[0:96, k128 * 128:(k128 + 1) * 128], rhs=Q[0:96, q0:q0 + BLK],
                                                      start=True, stop=True), r=[t_kn[kb], t_kr[kb], t_Q[qb]], w=[t_S])
                        Sq[k128] = (Sx, t_S)
                    for k128 in range(min(2, NK128)):
                        issue_S(k128)
                    for k128 in range(NK128):
                        Sx, t_S = Sq.pop(k128)
                        Pt, t_P, _ = P_r.next()
                        p.op("act", lambda e: e.activation(out=Pt[:], in_=Sx[:], func=AF.Exp, scale=ATT_SCALE), r=[t_S], w=[t_P])
                        if k128 + 2 < NK128:
                            issue_S(k128 + 2)
                        p.op("pe", lambda e: e.matmul(O[0:MV, :], lhsT=Vbuf[:, k128, 0:MV], rhs=Pt[:], start=(k128 == 0),
                                                      stop=(k128 == NK128 - 1)), r=[t_v[k128 // VG], t_P], w=[t_O])
                    osb, t_osb, _ = osb_r.next()
                    p.op("dve", lambda e: e.tensor_copy(out=osb[0:MV, :], in_=O[0:MV, :]), r=[t_O], w=[t_osb])
                    Dn, t_D, _ = pD.next()
                    if odd:
                        p.op("pe", lambda e: e.matmul(Dn[:], lhsT=SEL_O, rhs=osb[:], start=True, stop=True), r=[t_osb, t_c], w=[t_D])
                    else:
                        p.op("pe", lambda e: e.matmul(Dn[0:64, :], lhsT=SEL_E[0:65, 0:64], rhs=osb[0:65, :], start=True, stop=True),
                             r=[t_osb, t_c], w=[t_D])
                    rd, t_rd, _ = rden_r.next()
                    PR = slice(voff, voff + 64)
                    p.op("dve", lambda e: e.reciprocal(out=rd[PR, :], in_=Dn[PR, :]), r=[t_D], w=[t_rd])
                    p.op("dve", lambda e: e.tensor_tensor(out=ycT[PR, h // 2, q0:q0 + BLK], in0=osb[PR, :], in1=rd[PR, :], op=ALU.mult),
                         r=[t_osb, t_rd], w=[t_yc[h][qb]])
        barrier(p)
        esA.close()

        with ExitStack() as es4:
            p.es = es4
            wst = Ring(p, "wst4", [128, 1024], F32, 2, dma=True)
            wg3b = p.sb("wg3b", [128, 8, 1536], BF16)
            woutb = p.sb("woutb", [128, 8, D], BF16)
            wpoolb = p.sb("wpoolb", [128, 512], BF16)
            lng_s = p.sb("lng_s", [128, D], F32)
            lnb_s = p.sb("lnb_s", [128, D], F32)
            t_wg3, t_wout, t_wpool, t_ln = Tok(), Tok(), Tok(), Tok()
            dl = p.dma_sem()
            p.dma("sp", lng_s[:], lng[:, :], w=[t_ln], sem=dl)
            p.dma("sp", lnb_s[:], lnb[:, :], w=[t_ln], sem=dl)
            load_cast_weight(p, w_g3, wg3b, wst, 8, 1536, cw=768, tok=t_wg3)
            load_cast_weight(p, wout, woutb, wst, 8, D, tok=t_wout)
            st, stok, ssem = wst.next()
            p.dma("sp", st[:, 0:512], w_pool[:, :], w=[stok], sem=ssem)
            p.op("dve", lambda e: e.tensor_copy(out=wpoolb[:], in_=st[:, 0:512]), r=[stok], w=[t_wpool])
            xst = Ring(p, "xst4", [128, BLK + 16], F32, 3, dma=True)
            xb_r = Ring(p, "xb4", [128, 8, BLK + 16], BF16, 2)
            xtk = Ring(p, "xtk", [128, D], F32, 2, dma=True)
            cat = p.sb("cat", [128, 8, BLK], BF16)
            t_cat = [Tok() for _ in range(8)]
            tmp = Ring(p, "tmp4", [128, BLK + 16], F32, 10)
            hl_r = Ring(p, "hl4", [128, 16], F32, 2)
            pl_r = Ring(p, "pl4", [128, BLK], BF16, 2)
            rr = Ring(p, "rr", [128, D], F32, 2)
            r2 = Ring(p, "r2", [128, D], F32, 2, dma=True)
            junk = Ring(p, "junk", [128, D], BF16, 1)
            st_r = Ring(p, "stat", [128, 8], F32, 4)
            pp = Ring(p, "pp4", [128, 512], F32, 5, space="ps")
            ph = Ring(p, "ph4", [128, 512], F32, 1, space="ps")
            po = Ring(p, "po4", [128, D], F32, 1, space="ps")
            WIN = (2, 4, 8, 16)
            for qb in range(NQB):
                c0 = qb * BLK
                xb, t_xb, _ = xb_r.next()
                for kk in range(8):
                    st, stok, ssem = xst.next()
                    rws = slice(qb * 1024 + kk * 128, qb * 1024 + (kk + 1) * 128)
                    p.dma("sp", st[:, 8:BLK + 8], oxT[rws, :], w=[stok], sem=ssem)
                    p.dma("sp", st[:, 0:8], oHL[rws, :], w=[stok], sem=ssem, nowait=True)
                    p.dma("sp", st[:, BLK + 8:BLK + 16], oHR[rws, :], w=[stok], sem=ssem, nowait=True)
                    p.op("pool", lambda e: e.tensor_copy(out=xb[:, kk, :], in_=st[:]), r=[stok], w=[t_xb])

                def proj(col0, lo, n, dst, t_dst):
                    for kk in range(8):
                        p.op("pe", lambda e: e.matmul(dst, lhsT=wg3b[:, kk, col0:col0 + 128], rhs=xb[:, kk, lo:lo + n],
                                                      start=(kk == 0), stop=(kk == 7)), r=[t_wg3, t_xb], w=[t_dst])
                for j in range(4):
                    gc, t_gc, _ = pp.next()
                    proj(j * 128, 8, BLK, gc[:], t_gc)
                    sg, t_sg, _ = tmp.next()
                    p.op("act", lambda e: e.activation(out=sg[:, 0:BLK], in_=gc[:], func=AF.Silu), r=[t_gc], w=[t_sg])
                    p.op("dve", lambda e: e.tensor_tensor(out=cat[:, j, :], in0=sg[:, 0:BLK], in1=ycT[:, j, c0:c0 + BLK], op=ALU.mult),
                         r=[t_sg, t_yc[2 * j][qb], t_yc[2 * j + 1][qb]], w=[t_cat[j]])
                for gi in range(4):
                    w = WIN[gi]
                    um, t_um, _ = pp.next()
                    proj(512 + gi * 128, 8, BLK, um[:], t_um)
                    hl, t_hl, _ = ph.next()
                    proj(512 + gi * 128, 0, 8, hl[:, 0:8], t_hl)
                    proj(512 + gi * 128, BLK + 8, 8, hl[:, 8:16], t_hl)
                    u, t_u, _ = tmp.next()
                    p.op("act", lambda e: e.copy(out=u[:, 8:BLK + 8], in_=um[:]), r=[t_um], w=[t_u])
                    p.op("act", lambda e: e.copy(out=u[:, 0:8], in_=hl[:, 0:8]), r=[t_hl], w=[t_u])
                    p.op("act", lambda e: e.copy(out=u[:, BLK + 8:BLK + 16], in_=hl[:, 8:16]), r=[t_hl], w=[t_u])
                    cur, t_cur, n, width = u, t_u, BLK + 16, 1
                    while width < w:
                        nxt, t_nxt, _ = tmp.next()
                        n2 = n - width
                        p.op("dve", lambda e: e.tensor_tensor(out=nxt[:, 0:n2], in0=cur[:, 0:n2], in1=cur[:, width:width + n2], op=ALU.add),
                             r=[t_cur], w=[t_nxt])
                        cur, t_cur, n, width = nxt, t_nxt, n2, width * 2
                    s0 = 8 - w // 2
                    pm, t_pm, _ = tmp.next()
                    p.op("dve", lambda e: e.tensor_scalar(out=pm[:, 0:BLK], in0=cur[:, s0:s0 + BLK], scalar1=1.0 / w, scalar2=None, op0=ALU.mult),
                         r=[t_cur], w=[t_pm])
                    if qb == 0:
                        p.op("dve", lambda e: e.tensor_tensor(out=pm[:, 0:8], in0=pm[:, 0:8], in1=CORR[:, gi * 16:gi * 16 + 8], op=ALU.mult),
                             r=[t_c], w=[t_pm])
                    if qb == NQB - 1:
                        p.op("dve", lambda e: e.tensor_tensor(out=pm[:, BLK - 8:BLK], in0=pm[:, BLK - 8:BLK],
                                                              in1=CORR[:, gi * 16 + 8:gi * 16 + 16], op=ALU.mult), r=[t_c], w=[t_pm])
                    pl, t_pl, _ = pl_r.next()
                    p.op("dve", lambda e: e.tensor_tensor(out=pl[:], in0=pm[:, 0:BLK], in1=u[:, 8:BLK + 8], op=ALU.subtract),
                         r=[t_pm, t_u], w=[t_pl])
                    yd, t_yd, _ = pp.next()
                    p.op("pe", lambda e: e.matmul(yd[:], lhsT=wpoolb[:, gi * 128:(gi + 1) * 128], rhs=pl[:], start=True, stop=True),
                         r=[t_wpool, t_pl], w=[t_yd])
                    gd, t_gd, _ = pp.next()
                    proj(1024 + gi * 128, 8, BLK, gd[:], t_gd)
                    sg, t_sg, _ = tmp.next()
                    p.op("act", lambda e: e.activation(out=sg[:, 0:BLK], in_=gd[:], func=AF.Silu), r=[t_gd], w=[t_sg])
                    p.op("dve", lambda e: e.scalar_tensor_tensor(out=cat[:, 4 + gi, :], in0=yd[:], scalar=PSC[:, gi:gi + 1], in1=sg[:, 0:BLK],
                                                                 op0=ALU.mult, op1=ALU.mult), r=[t_yd, t_sg, t_c], w=[t_cat[4 + gi]])
                for tt in range(BLK // 128):
                    xk, t_xk, xk_sem = xtk.next()
                    p.dma("sp", xk[:], xtok[c0 + tt * 128:c0 + (tt + 1) * 128, :], w=[t_xk], sem=xk_sem)
                    o, t_o, _ = po.next()
                    for half in range(2):
                        for kc in range(8):
                            p.op("pe", lambda e: e.matmul(o[:, half * 512:(half + 1) * 512], lhsT=cat[:, kc, tt * 128:(tt + 1) * 128],
                                                          rhs=woutb[:, kc, half * 512:(half + 1) * 512], start=(kc == 0), stop=(kc == 7)),
                                 r=[t_cat[kc], t_wout], w=[t_o])
                    layer_norm_tail(p, o, t_o, xk, t_xk, rr, r2, junk, st_r, lng_s, lnb_s, t_ln,
                                    out[c0 + tt * 128:c0 + (tt + 1) * 128, :])
        barrier(p)


def prep_l1(x1, positions, od_w_in, od_q_norm_g, od_w_uq, od_kv_norm_g, od_w_ukv, od_pool_w, od_pool_scale, od_w_out,
            od_ln_g, od_ln_b, S=SEQ):
    T = S // 4
    w_in = od_w_in[0]
    w_cq = np.ascontiguousarray(w_in[:, 0:256])
    w_kv = np.ascontiguousarray(w_in[:, 256:384])
    kr = w_in[:, 384:416]
    krs = np.concatenate([kr[:, 16:32], kr[:, 0:16]], axis=1)
    z64 = np.zeros((D, 64), np.float32)
    w_kr = np.ascontiguousarray(np.concatenate([z64, kr, z64, krs], axis=1))
    w_g3 = np.ascontiguousarray(w_in[:, 416:1952])
    uq = od_w_uq[0].reshape(256, 8, 96)
    uqs = np.zeros_like(uq)
    uqs[:, :, 64:80] = uq[:, :, 80:96]
    uqs[:, :, 80:96] = uq[:, :, 64:80]
    w_uq = np.ascontiguousarray(np.concatenate([uq.reshape(256, 768), uqs.reshape(256, 768)], axis=1))
    w_ukv = np.ascontiguousarray(od_w_ukv[0])
    w_pool = np.ascontiguousarray(od_pool_w[0].transpose(1, 0, 2).reshape(128, 512))
    wout = np.ascontiguousarray(od_w_out[0])
    half = 16
    inv_freq = (np.float32(10000.0) ** (-np.arange(half, dtype=np.float32) / np.float32(half))).astype(np.float32)
    sel = np.zeros((128, 384), np.float32)
    sel[64, 0:64] = 1.0
    sel[0, 128 + 64:128 + 128] = 1.0
    sel[:, 256:384] = 1.0
    lng = np.ascontiguousarray(np.broadcast_to(od_ln_g[0][None, :], (128, D)))
    lnb = np.ascontiguousarray(np.broadcast_to(od_ln_b[0][None, :], (128, D)))
    maps = []
    xTbs = [np.ascontiguousarray(x1[b, :S, :].T) for b in range(2)] if x1 is not None else [None, None]
    posbs = [np.ascontiguousarray(np.broadcast_to(positions[b, :S].reshape(S // 512, 1, 512), (S // 512, 32, 512))
                                  .reshape((S // 512) * 32, 512)).astype(np.int32) for b in range(2)]
    for c in range(NCORES):
        b, s0 = c // 4, (c % 4) * T
        xe = None
        if x1 is not None:
            xe = np.zeros((D, T + 16), np.float32)
            lo, hi = max(0, s0 - 8), min(S, s0 + T + 8)
            xe[:, lo - (s0 - 8):hi - (s0 - 8)] = x1[b, lo:hi, :].T
        smc = np.zeros((128, 80), np.float32)
        smc[:, 0:2] = od_q_norm_g[0].reshape(2, 128).T
        smc[:, 2] = od_kv_norm_g[0]
        smc[64:80, 3] = inv_freq
        smc[80:96, 3] = inv_freq
        smc[64:80, 4] = -1.0
        smc[80:96, 4] = 1.0
        smc[:, 5:9] = od_pool_scale[0].reshape(4, 128).T
        for gi, w in enumerate((2, 4, 8, 16)):
            for j in range(8):
                for side, t in ((0, s0 + j), (1, s0 + T - 8 + j)):
                    lo_ = min(max(t - w // 2, 0), S)
                    hi_ = min(max(t + w - w // 2, 0), S)
                    smc[:, 16 + gi * 16 + side * 8 + j] = np.float32(w) / np.float32(hi_ - lo_)
        maps.append({"xTb": xTbs[b], "xTo": xe, "xtok": (np.ascontiguousarray(x1[b, s0:s0 + T, :]) if x1 is not None else None),
                     "posb": posbs[b],
                     "w_cq": w_cq, "w_kv": w_kv, "w_kr": w_kr, "w_g3": w_g3, "w_uq": w_uq, "w_ukv": w_ukv,
                     "w_pool": w_pool, "wout": wout, "sm_c": smc, "sel": sel, "lng": lng, "lnb": lnb})
    return maps


def build_fused(S=SEQ):
    T = S // 4
    nc = bass.Bass("TRN2", target_bir_lowering=False)

    def inp(name, shape, dt=F32):
        return nc.dram_tensor(name, list(shape), dt, kind="ExternalInput").ap()
    A = {}
    A["xT"] = inp("xT", [D, S + 3])
    A["xT1"] = A["xT"][:, 1:S + 3]
    A["xtok"] = inp("xtok", [S, D])
    A["wg"] = [inp("wg%d" % g, [D, 520]) for g in range(4)]
    A["cvw"] = [inp("cvw%d" % g, [128, 16])[:, :] for g in range(4)]
    A["cvb"] = [inp("cvb%d" % g, [128, 4])[:, :] for g in range(4)]
    A["hp"] = [inp("hp%d" % g, [128, 24])[:, :] for g in range(4)]
    A["cst"] = inp("cst", [128, 512])
    A["msk"] = inp("msk", [128, 1024])
    A["w1"] = inp("w1", [D, 5120])
    A["wout"] = inp("wout", [2048, D])
    A["normg"] = inp("normg", [128, 8])
    A["scw"] = inp("scw", [128, 24])
    A["lng"] = inp("lng", [128, D])
    A["lnb"] = inp("lnb", [128, D])
    A["posb"] = inp("posb", [(S // 512) * 32, 512], I32)
    A["w_cq"] = inp("w_cq", [D, 256])
    A["w_kv"] = inp("w_kv", [D, 128])
    A["w_kr"] = inp("w_kr", [D, 192])
    A["w_g3"] = inp("w_g3", [D, 1536])
    A["w_uq"] = inp("w_uq", [256, 1536])
    A["w_ukv"] = inp("w_ukv", [128, 1024])
    A["w_pool"] = inp("w_pool", [128, 512])
    A["wout_od"] = inp("wout_od", [D, D])
    A["sm_c"] = inp("sm_c", [128, 80])
    A["sel"] = inp("sel", [128, 384])
    A["lng_od"] = inp("lng_od", [128, D])
    A["lnb_od"] = inp("lnb_od", [128, D])
    off = inp("off", [1, 4], I32)
    A["out"] = nc.dram_tensor("out", [T, D], F32, kind="ExternalOutput").ap()
    A["yaT"] = nc.dram_tensor("yaT_s", [D, S], F32).ap()
    A["x1"] = nc.dram_tensor("x1_s", [S, D], F32).ap()
    A["x1T"] = nc.dram_tensor("x1T_s", [(S // 512 + 2) * 1024, 512], F32).ap()
    A["x1HL"] = nc.dram_tensor("x1HL_s", [(S // 512 + 1) * 1024, 8], F32).ap()
    A["x1HR"] = nc.dram_tensor("x1HR_s", [(S // 512 + 1) * 1024, 8], F32).ap()
    A["own_x1T"] = nc.dram_tensor("own_x1T_s", [(T // 512) * 1024, 512], F32).ap()
    A["own_x1"] = nc.dram_tensor("own_x1_s", [T, D], F32).ap()
    A["own_HL"] = nc.dram_tensor("own_HL_s", [(T // 512) * 1024, 8], F32).ap()
    A["own_HR"] = nc.dram_tensor("own_HR_s", [(T // 512) * 1024, 8], F32).ap()
    A["own_pos"] = nc.dram_tensor("own_pos_s", [(T // 512) * 32, 512], I32).ap()

    with ExitStack() as es:
        p = Prog(nc, es)
        regs = [es.enter_context(nc.sync.register("offr%d" % i)) for i in range(3)]
        for i in range(3):
            nc.sync.reg_load(regs[i], off[0:1, i:i + 1])
        NB, NQB = S // 512, T // 512
        b0v = nc.sync.snap(regs[0], min_val=0, max_val=NB - NQB)
        u0v = nc.sync.snap(regs[1], min_val=0, max_val=(NB - NQB) * 64)
        t0v = nc.sync.snap(regs[2], min_val=0, max_val=(S - T) // 8)
        p.prefix = "a_"
        emit_l0a(nc, p, S, A)
        p.prefix = "b_"
        emit_l0b(nc, p, S, A)
        csem = p.dma_sem()
        v = lambda ap, b: ap.rearrange("(a b) t -> a (b t)", b=b)
        p.dma("sp", v(A["own_x1T"], 16), v(A["x1T"], 16)[bass.ds(u0v + 64, NQB * 64), :], sem=csem)
        p.dma("sp", v(A["own_x1"], 8), v(A["x1"], 8)[bass.ds(t0v, T // 8), :], sem=csem)
        p.dma("sp", v(A["own_HL"], 1024), v(A["x1HL"], 1024)[bass.ds(b0v, NQB), :], sem=csem)
        p.dma("sp", v(A["own_HR"], 1024), v(A["x1HR"], 1024)[bass.ds(b0v + 1, NQB), :], sem=csem)
        p.dma("sp", v(A["own_pos"], 32), v(A["posb"], 32)[bass.ds(b0v, NQB), :], sem=csem)
        barrier(p)
        p.prefix = "c_"
        emit_l1(nc, p, S, A)
        p.es = es
        p.finish()
    return nc


def prep_fused(inputs, S=SEQ):
    f = lambda a: np.asarray(a, dtype=np.float32)
    x = f(inputs["x"])[:, :S]
    positions = np.asarray(inputs["positions"], dtype=np.int32)[:, :S]
    T = S // 4
    l0a = prep_l0a(x, f(inputs["ev_w_in"]), f(inputs["ev_conv_w"]), f(inputs["ev_conv_b"]), f(inputs["ev_a_log"]),
                   f(inputs["ev_dt_bias"]), f(inputs["ev_d_skip"]), S=S)
    w_in = f(inputs["ev_w_in"])[0]
    w1 = np.ascontiguousarray(np.concatenate([w_in[:, 0:1024], w_in[:, 3104:7200]], axis=1))
    wout = np.ascontiguousarray(f(inputs["ev_w_out"])[0])
    normg = np.ascontiguousarray(f(inputs["ev_norm_g"])[0].reshape(8, 128).T)
    scw = np.ascontiguousarray(f(inputs["ev_sc_conv_w"])[0].reshape(3, 8, 128).transpose(2, 1, 0).reshape(128, 24))
    lng = np.ascontiguousarray(np.broadcast_to(f(inputs["ev_ln_g"])[0][None, :], (128, D)))
    lnb = np.ascontiguousarray(np.broadcast_to(f(inputs["ev_ln_b"])[0][None, :], (128, D)))
    dummy_x1 = np.zeros((2, 16, D), np.float32)
    l1 = prep_l1(None, positions, f(inputs["od_w_in"]), f(inputs["od_q_norm_g"]), f(inputs["od_w_uq"]), f(inputs["od_kv_norm_g"]),
                 f(inputs["od_w_ukv"]), f(inputs["od_pool_w"]), f(inputs["od_pool_scale"]), f(inputs["od_w_out"]),
                 f(inputs["od_ln_g"]), f(inputs["od_ln_b"]), S=S)
    xtoks = [np.ascontiguousarray(x[b]) for b in range(2)]
    maps = []
    for c in range(NCORES):
        b, q = c // 4, c % 4
        m = {"xT": l0a[4 * b]["xT"], "xtok": xtoks[b], "cst": l0a[0]["cst"], "msk": l0a[0]["msk"],
             "w1": w1, "wout": wout, "normg": normg, "scw": scw, "lng": lng, "lnb": lnb,
             "off": np.array([[q * T // 512, (q * T // 512) * 64, q * T // 8, 0]], np.int32)}
        for g in range(4):
            src = l0a[4 * b + g]
            m["wg%d" % g] = src["wg"]
            m["cvw%d" % g] = src["cvw"]
            m["cvb%d" % g] = src["cvb"]
            m["hp%d" % g] = src["hp"]
        lm = l1[c]
        for k in ("posb", "w_cq", "w_kv", "w_kr", "w_g3", "w_uq", "w_ukv", "w_pool", "sm_c", "sel"):
            m[k] = lm[k]
        m["wout_od"] = lm["wout"]
        m["lng_od"] = lm["lng"]
        m["lnb_od"] = lm["lnb"]
        maps.append(m)
    return maps


def kernel(**inputs):
    T = SEQ // 4
    maps = prep_fused(inputs)
    res = run_bass_kernel_spmd(build_fused(), maps, core_ids=list(range(NCORES)))
    out = np.empty((2, SEQ, D), np.float32)
    for c in range(NCORES):
        out[c // 4, (c % 4) * T:(c % 4 + 1) * T, :] = res.results[c]["out"]
    return out
```

```python
import numpy as np
import concourse.bass as bass
import concourse.mybir as mybir
from concourse.bass_utils import run_bass_kernel_spmd
from contextlib import ExitStack

F32 = mybir.dt.float32
BF16 = mybir.dt.bfloat16
I32 = mybir.dt.int32
AF = mybir.ActivationFunctionType
ALU = mybir.AluOpType
AX = mybir.AxisListType

SAME_ENGINE_SYNC = True

D = 1024
SEQ = 16384
NCORES = 8
ALPHA = 4 ** 0.25
EPS = 1e-5


class Tok:
    __slots__ = ("w", "r", "name")

    def __init__(self, name=""):
        self.w = None
        self.r = {}
        self.name = name


class Prog:
    def __init__(self, nc, es):
        self.nc = nc
        self.es = es
        self.es_top = es
        self.eng = {"pe": nc.tensor, "act": nc.scalar, "dve": nc.vector,
                    "pool": nc.gpsimd, "sp": nc.sync}
        self.sems = {}
        self.cnt = {}
        for k in self.eng:
            self.sems[k] = es.enter_context(nc.semaphore("s_" + k))
            self.cnt[k] = 0
        self.seen = {k: {} for k in self.eng}
        self.ndma = 0
        self.out_dma = []
        self.n_ops = 0
        self.uid = 0

    prefix = ""

    def sb(self, name, shape, dt):
        return self.es.enter_context(self.nc.sbuf_tensor(self.prefix + name, list(shape), dt))

    def ps(self, name, shape, dt=F32):
        return self.es.enter_context(self.nc.psum_tensor(self.prefix + name, list(shape), dt))

    def dma_sem(self):
        k = "d%d" % self.ndma
        self.ndma += 1
        self.sems[k] = self.es_top.enter_context(self.nc.semaphore("s_" + k))
        self.cnt[k] = 0
        return k

    def _wait(self, e, deps):
        for (k, v) in deps:
            if k == e:
                if not SAME_ENGINE_SYNC or e == "pe" or e == "sp":
                    continue
            if self.seen[e].get(k, 0) >= v:
                continue
            self.eng[e].wait_ge(self.sems[k], v)
            self.seen[e][k] = v

    def _deps(self, r, w):
        m = {}
        for t in r:
            if t.w is not None:
                k, v = t.w
                if m.get(k, 0) < v:
                    m[k] = v
        for t in w:
            if t.w is not None:
                k, v = t.w
                if m.get(k, 0) < v:
                    m[k] = v
            for k, v in t.r.items():
                if m.get(k, 0) < v:
                    m[k] = v
        return list(m.items())

    def op(self, e, fn, r=(), w=(), multi=False):
        deps = self._deps(r, w)
        att = None
        if e != "pe" and not multi:
            need = [(k, v) for (k, v) in deps
                    if not (k == e and not SAME_ENGINE_SYNC) and self.seen[e].get(k, 0) < v]
            if need:
                att = need[-1]
                self._wait(e, need[:-1])
        else:
            self._wait(e, deps)
        ins = fn(self.eng[e])
        if att is not None:
            ins._wait_ge(self.sems[att[0]], att[1])
            self.seen[e][att[0]] = att[1]
        self.cnt[e] += 1
        v = self.cnt[e]
        ins.then_inc(self.sems[e], 1)
        for t in r:
            if t.r.get(e, 0) < v:
                t.r[e] = v
        for t in w:
            t.w = (e, v)
            t.r = {}
        self.n_ops += 1
        return ins

    def dma(self, q, out, in_, r=(), w=(), sem=None, is_out=False, nowait=False, **kw):
        if not nowait:
            self._wait(q, self._deps(r, w))
        ins = self.eng[q].dma_start(out=out, in_=in_, **kw)
        self.cnt[sem] += 16
        v = self.cnt[sem]
        ins.then_inc(self.sems[sem], 16)
        for t in r:
            if t.r.get(sem, 0) < v:
                t.r[sem] = v
        for t in w:
            t.w = (sem, v)
            t.r = {}
        if is_out:
            self.out_dma.append((sem, v))
        return ins

    def finish(self, e="sp"):
        m = {}
        for k, v in self.out_dma:
            if m.get(k, 0) < v:
                m[k] = v
        for k, v in m.items():
            self.eng[e].wait_ge(self.sems[k], v)


class Ring:
    def __init__(self, p, name, shape, dt, n, space="sb", dma=False):
        self.bufs = []
        for i in range(n):
            t = p.sb("%s%d" % (name, i), shape, dt) if space == "sb" else p.ps("%s%d" % (name, i), shape, dt)
            self.bufs.append((t, Tok(name + str(i)), p.dma_sem() if dma else None))
        self.i = 0

    def next(self):
        b = self.bufs[self.i % len(self.bufs)]
        self.i += 1
        return b


def load_cast_weight(p, src, dst, stage, K, C, engines=("pool", "act"), cw=1024, tok=None):
    n = 0
    for k in range(K):
        for c0 in range(0, C, cw):
            c1 = min(C, c0 + cw)
            st, stok, ssem = stage.next()
            p.dma("sp", st[:, 0:c1 - c0], src[k * 128:(k + 1) * 128, c0:c1], w=[stok], sem=ssem)
            e = engines[n % len(engines)]
            n += 1
            if e == "act":
                p.op(e, lambda en: en.copy(out=dst[:, k, c0:c1], in_=st[:, 0:c1 - c0]), r=[stok], w=[tok])
            else:
                p.op(e, lambda en: en.tensor_copy(out=dst[:, k, c0:c1], in_=st[:, 0:c1 - c0]), r=[stok], w=[tok])


PI = float(np.pi)
TWO_PI = float(2 * np.pi)
C1 = 6.28125
C2 = float(2 * np.pi - 6.28125)
ATT_SCALE = float(96 ** -0.5)
NEG = -30000.0
L0B_TB = 256


def barrier(p):
    for e in p.eng:
        for k, v in p.cnt.items():
            if k != e and v > 0 and p.seen[e].get(k, 0) < v:
                p.eng[e].wait_ge(p.sems[k], v)
                p.seen[e][k] = v


def layer_norm_tail(p, o, t_o, xk, t_xk, rr, r2, junk, st_r, lng_s, lnb_s, t_c, out_ap, post=None, is_out=True):
    r, t_r, _ = rr.next()
    p.op("dve", lambda e: e.scalar_tensor_tensor(out=r[:], in0=xk[:], scalar=float(ALPHA), in1=o[:], op0=ALU.mult, op1=ALU.add),
         r=[t_xk, t_o], w=[t_r])
    st, t_st, _ = st_r.next()
    jk, t_jk, _ = junk.next()
    p.op("act", lambda e: e.activation(out=jk[:], in_=r[:], func=AF.Identity, accum_out=st[:, 0:1]), r=[t_r], w=[t_jk, t_st], multi=True)
    p.op("act", lambda e: e.activation(out=jk[:], in_=r[:], func=AF.Square, accum_out=st[:, 1:2]), r=[t_r], w=[t_jk, t_st], multi=True)
    p.op("dve", lambda e: e.tensor_scalar(out=st[:, 2:3], in0=st[:, 0:1], scalar1=1.0 / D, scalar2=None, op0=ALU.mult), r=[t_st], w=[t_st])
    p.op("dve", lambda e: e.tensor_tensor(out=st[:, 3:4], in0=st[:, 2:3], in1=st[:, 2:3], op=ALU.mult), r=[t_st], w=[t_st])
    p.op("dve", lambda e: e.scalar_tensor_tensor(out=st[:, 4:5], in0=st[:, 1:2], scalar=1.0 / D, in1=st[:, 3:4], op0=ALU.mult, op1=ALU.subtract),
         r=[t_st], w=[t_st])
    p.op("dve", lambda e: e.tensor_scalar(out=st[:, 4:5], in0=st[:, 4:5], scalar1=float(EPS), scalar2=None, op0=ALU.add), r=[t_st], w=[t_st])
    p.op("act", lambda e: e.activation(out=st[:, 5:6], in_=st[:, 4:5], func=AF.Ln), r=[t_st], w=[t_st])
    p.op("act", lambda e: e.activation(out=st[:, 6:7], in_=st[:, 5:6], func=AF.Exp, scale=-0.5), r=[t_st], w=[t_st])
    q, t_q, osem = r2.next()
    p.op("dve", lambda e: e.tensor_scalar(out=q[:], in0=r[:], scalar1=st[:, 2:3], scalar2=st[:, 6:7], op0=ALU.subtract, op1=ALU.mult),
         r=[t_r, t_st], w=[t_q])
    p.op("pool", lambda e: e.tensor_tensor(out=q[:], in0=q[:], in1=lng_s[:], op=ALU.mult), r=[t_c], w=[t_q])
    p.op("pool", lambda e: e.tensor_tensor(out=q[:], in0=q[:], in1=lnb_s[:], op=ALU.add), r=[t_c], w=[t_q])
    if post is not None:
        post(q, t_q)
    p.dma("act", out_ap, q[:], r=[t_q], w=[], sem=osem, is_out=is_out)


def emit_l0b(nc, p, T, A):
    TB = L0B_TB
    NB = T // TB
    xT, xtok, yaT, w1, wout = A["xT1"], A["xtok"], A["yaT"], A["w1"], A["wout"]
    normg, scw, lng, lnb = A["normg"], A["scw"], A["lng"], A["lnb"]
    out, x1T, cst = A["x1"], A["x1T"], A["cst"]
    x1HL, x1HR = A["x1HL"], A["x1HR"]

    def halo_v(tab, bnd):
        return tab[bnd * 1024:(bnd + 1) * 1024, :].rearrange("(k p) t -> p k t", p=128)
    yaT_v = yaT.rearrange("(k p) t -> p k t", p=128)
    NB5 = T // 512

    def x1T_blk(blk, c0, n):
        return x1T[blk * 1024:(blk + 1) * 1024, c0:c0 + n].rearrange("(k p) t -> p k t", p=128)

    with ExitStack() as es:
        p.es = es
        w1b = p.sb("w1b", [128, 8, 5120], BF16)
        woutb = p.sb("woutb", [128, 16, D], BF16)
        t_w1b, t_woutb = Tok(), Tok()
        stage = Ring(p, "wst", [128, 1024], F32, 1, dma=True)
        normg_s = p.sb("normg_s", [128, 8], F32)
        scw_s = p.sb("scw_s", [128, 24], F32)
        lng_s = p.sb("lng_s", [128, D], F32)
        lnb_s = p.sb("lnb_s", [128, D], F32)
        ones_f = p.sb("ones_f", [128, 128], F32)
        t_c = Tok()
        dc = p.dma_sem()
        p.dma("sp", normg_s[:], normg[:, :], w=[t_c], sem=dc)
        p.dma("sp", scw_s[:], scw[:, :], w=[t_c], sem=dc)
        p.dma("sp", lng_s[:], lng[:, :], w=[t_c], sem=dc)
        p.dma("sp", lnb_s[:], lnb[:, :], w=[t_c], sem=dc)
        t_ones = Tok()
        p.op("dve", lambda e: e.memset(ones_f[:], 1.0), w=[t_ones])
        idf = p.sb("idf", [128, 128], F32)
        p.dma("sp", idf[:], cst[:, 256:384], w=[t_c], sem=dc)
        zt = p.sb("zt", [128, 8, 8], F32)
        t_zt = Tok()
        p.op("dve", lambda e: e.memset(zt[:], 0.0), w=[t_zt])
        zsem = p.dma_sem()
        p.dma("sp", halo_v(x1HL, 0), zt[:], r=[t_zt], sem=zsem)
        p.dma("sp", halo_v(x1HR, NB5), zt[:], r=[t_zt], sem=zsem)
        xtt_r = Ring(p, "xtt", [128, 8, 128], F32, 1, dma=True)
        load_cast_weight(p, w1, w1b, stage, 8, 5120, tok=t_w1b)
        load_cast_weight(p, wout, woutb, stage, 16, D, tok=t_woutb)

        xst = Ring(p, "xst", [128, TB + 2], F32, 3, dma=True)
        xb_r = Ring(p, "xb", [128, 8, TB + 2], BF16, 2)
        yst = Ring(p, "yst", [128, 8, TB], F32, 2, dma=True)
        xtk = Ring(p, "xtk", [128, D], F32, 2, dma=True)
        pp = Ring(p, "pp", [128, 512], F32, 3, space="ps")
        pss = Ring(p, "pss", [128, 512], F32, 1, space="ps")
        po = Ring(p, "po", [128, D], F32, 2, space="ps")
        deferred = []

        def do_post(q, t_q, tok0):
            xtt, t_xtt, xtt_sem = xtt_r.next()
            for hf in range(2):
                tp, t_tp, _ = pp.next()
                for kq in range(4):
                    kk = hf * 4 + kq
                    p.op("pe", lambda e: e.transpose(tp[:, kq * 128:(kq + 1) * 128], q[:, kk * 128:(kk + 1) * 128], idf[:]),
                         r=[t_q, t_c], w=[t_tp])
                p.op("act", lambda e: e.copy(out=xtt[:, hf * 4:(hf + 1) * 4, :], in_=tp[:].rearrange("p (k t) -> p k t", k=4)),
                     r=[t_tp], w=[t_xtt])
            p.dma("act", x1T_blk(tok0 // 512 + 1, tok0 % 512, 128), xtt[:], r=[t_xtt], sem=xtt_sem)
            if tok0 % 512 == 0:
                p.dma("act", halo_v(x1HR, tok0 // 512), xtt[:, :, 0:8], r=[t_xtt], sem=xtt_sem)
            if (tok0 + 128) % 512 == 0:
                p.dma("act", halo_v(x1HL, (tok0 + 128) // 512), xtt[:, :, 120:128], r=[t_xtt], sem=xtt_sem)

        def flush_posts():
            while deferred:
                do_post(*deferred.pop(0))
        g_all = p.sb("g_all", [128, 8, TB], F32)
        t_g = [Tok() for _ in range(8)]
        cat = p.sb("cat", [128, 16, TB], BF16)
        t_cat = [Tok() for _ in range(16)]
        tmp = Ring(p, "tmp", [128, TB + 2], F32, 8)
        rstd = p.sb("rstd", [128, TB], F32)
        t_rstd = Tok()
        halo = Ring(p, "halo", [128, 4], F32, 2)
        rr = Ring(p, "rr", [128, D], F32, 1)
        r2 = Ring(p, "r2", [128, D], F32, 2, dma=True)
        junk = Ring(p, "junk", [128, D], BF16, 1)
        st_r = Ring(p, "stat", [128, 8], F32, 4)
        def load_blk(bj):
            tj = bj * TB
            xb_, t_xb_, _ = xb_r.next()
            for k in range(8):
                st, stok, ssem = xst.next()
                p.dma("sp", st[:], xT[k * 128:(k + 1) * 128, tj:tj + TB + 2], w=[stok], sem=ssem)
                p.op("pool", lambda e: e.tensor_copy(out=xb_[:, k, :], in_=st[:]), r=[stok], w=[t_xb_])
            ya_, t_ya_, ya_sem_ = yst.next()
            p.dma("sp", ya_[:], yaT_v[:, :, tj:tj + TB], w=[t_ya_], sem=ya_sem_)
            return xb_, t_xb_, ya_, t_ya_
        nxt = load_blk(0)
        for bi in range(NB):
            t0 = bi * TB
            xb, t_xb, ya, t_ya = nxt
            if bi + 1 < NB:
                nxt = load_blk(bi + 1)

            ss, t_ss, _ = pss.next()
            for j in range(8):
                z, t_z, _ = pp.next()
                for k in range(8):
                    p.op("pe", lambda e: e.matmul(z[:, 0:TB], lhsT=w1b[:, k, j * 128:(j + 1) * 128],
                                                  rhs=xb[:, k, 1:TB + 1], start=(k == 0), stop=(k == 7)),
                         r=[t_w1b, t_xb], w=[t_z])
                sz, t_sz, _ = tmp.next()
                p.op("act", lambda e: e.activation(out=sz[:, 0:TB], in_=z[:, 0:TB], func=AF.Silu), r=[t_z], w=[t_sz])
                p.op("dve", lambda e: e.tensor_tensor(out=g_all[:, j, :], in0=sz[:, 0:TB], in1=ya[:, j, :], op=ALU.mult),
                     r=[t_sz, t_ya], w=[t_g[j]])
                sq, t_sq, _ = tmp.next()
                p.op("act", lambda e: e.activation(out=sq[:, 0:TB], in_=g_all[:, j, :], func=AF.Square), r=[t_g[j]], w=[t_sq])
                p.op("pe", lambda e: e.matmul(ss[:, 0:TB], lhsT=ones_f[:], rhs=sq[:, 0:TB], start=(j == 0), stop=(j == 7)),
                     r=[t_ones, t_sq], w=[t_ss])
            lnv, t_lnv, _ = tmp.next()
            p.op("dve", lambda e: e.tensor_scalar(out=lnv[:, 0:TB], in0=ss[:, 0:TB], scalar1=1.0 / 1024, scalar2=EPS,
                                                  op0=ALU.mult, op1=ALU.add), r=[t_ss], w=[t_lnv])
            p.op("act", lambda e: e.activation(out=lnv[:, 0:TB], in_=lnv[:, 0:TB], func=AF.Ln), r=[t_lnv], w=[t_lnv])
            p.op("act", lambda e: e.activation(out=rstd[:], in_=lnv[:, 0:TB], func=AF.Exp, scale=-0.5), r=[t_lnv], w=[t_rstd])
            for j in range(8):
                p.op("dve", lambda e: e.scalar_tensor_tensor(out=cat[:, j, :], in0=g_all[:, j, :], scalar=normg_s[:, j:j + 1],
                                                             in1=rstd[:], op0=ALU.mult, op1=ALU.mult),
                     r=[t_g[j], t_rstd, t_c], w=[t_cat[j]])

            flush_posts()
            for j in range(8):
                def proj(grp, lo, n, dst, t_dst, first=True, last=True):
                    for k in range(8):
                        p.op("pe", lambda e: e.matmul(dst, lhsT=w1b[:, k, grp * 1024 + j * 128:grp * 1024 + (j + 1) * 128],
                                                      rhs=xb[:, k, lo:lo + n], start=(k == 0), stop=(k == 7)),
                             r=[t_w1b, t_xb], w=[t_dst])
                cg, t_cg, _ = pp.next()
                proj(2, 0, TB + 2, cg[:, 0:TB + 2], t_cg)
                hh, t_hh, _ = pp.next()
                proj(3, 0, TB + 2, hh[:, 0:TB + 2], t_hh)
                cgs, t_cgs, _ = tmp.next()
                p.op("act", lambda e: e.copy(out=cgs[:, 0:TB + 2], in_=cg[:, 0:TB + 2]), r=[t_cg], w=[t_cgs])
                u, t_u, _ = tmp.next()
                p.op("dve", lambda e: e.tensor_tensor(out=u[:, 0:TB + 2], in0=cgs[:, 0:TB + 2], in1=hh[:, 0:TB + 2], op=ALU.mult),
                     r=[t_cgs, t_hh], w=[t_u])
                c, t_cc, _ = tmp.next()
                p.op("dve", lambda e: e.tensor_scalar(out=c[:, 0:TB], in0=u[:, 0:TB], scalar1=scw_s[:, j * 3:j * 3 + 1], scalar2=None,
                                                      op0=ALU.mult), r=[t_u, t_c], w=[t_cc])
                p.op("dve", lambda e: e.scalar_tensor_tensor(out=c[:, 0:TB], in0=u[:, 1:TB + 1], scalar=scw_s[:, j * 3 + 1:j * 3 + 2],
                                                             in1=c[:, 0:TB], op0=ALU.mult, op1=ALU.add), r=[t_u, t_c], w=[t_cc])
                p.op("dve", lambda e: e.scalar_tensor_tensor(out=c[:, 0:TB], in0=u[:, 2:TB + 2], scalar=scw_s[:, j * 3 + 2:j * 3 + 3],
                                                             in1=c[:, 0:TB], op0=ALU.mult, op1=ALU.add), r=[t_u, t_c], w=[t_cc])
                bg, t_bg, _ = pp.next()
                proj(1, 1, TB, bg[:, 0:TB], t_bg)
                gt, t_gt, _ = pp.next()
                proj(4, 1, TB, gt[:, 0:TB], t_gt)
                sg, t_sg, _ = tmp.next()
                p.op("act", lambda e: e.activation(out=sg[:, 0:TB], in_=gt[:, 0:TB], func=AF.Silu), r=[t_gt], w=[t_sg])
                p.op("dve", lambda e: e.tensor_tensor(out=c[:, 0:TB], in0=c[:, 0:TB], in1=bg[:, 0:TB], op=ALU.mult),
                     r=[t_bg], w=[t_cc])
                p.op("dve", lambda e: e.tensor_tensor(out=cat[:, 8 + j, :], in0=c[:, 0:TB], in1=sg[:, 0:TB], op=ALU.mult),
                     r=[t_cc, t_sg], w=[t_cat[8 + j]])

            outs = []
            for tt in range(TB // 128):
                xk, t_xk, xk_sem = xtk.next()
                p.dma("sp", xk[:], xtok[t0 + tt * 128:t0 + (tt + 1) * 128, :], w=[t_xk], sem=xk_sem)
                o, t_o, _ = po.next()
                for half in range(2):
                    for kc in range(16):
                        p.op("pe", lambda e: e.matmul(o[:, half * 512:(half + 1) * 512], lhsT=cat[:, kc, tt * 128:(tt + 1) * 128],
                                                      rhs=woutb[:, kc, half * 512:(half + 1) * 512], start=(kc == 0), stop=(kc == 15)),
                             r=[t_cat[kc], t_woutb], w=[t_o])
                outs.append((o, t_o, xk, t_xk))
            for tt in range(TB // 128):
                o, t_o, xk, t_xk = outs[tt]
                tok0 = t0 + tt * 128
                layer_norm_tail(p, o, t_o, xk, t_xk, rr, r2, junk, st_r, lng_s, lnb_s, t_c,
                                out[t0 + tt * 128:t0 + (tt + 1) * 128, :],
                                post=(lambda q, t_q, tok0=tok0: deferred.append((q, t_q, tok0))), is_out=False)
        flush_posts()
        barrier(p)


def emit_l0a(nc, p, S, A):
    BLK = 512
    NBLK = S // BLK
    xT, wg_all, cvw_all, cvb_all, hp_all, cst, msk, yaT = (A["xT"], A["wg"], A["cvw"], A["cvb"], A["hp"], A["cst"], A["msk"], A["yaT"])

    with ExitStack() as es:
        p.es = es
        wgb_l = [p.sb("wgb%d" % g, [128, 8, 520], BF16) for g in range(4)]
        t_wgb_l = [Tok() for g in range(4)]
        stage = Ring(p, "wst", [128, 520], F32, 2, dma=True)
        cvw_l = [p.sb("cvw_s%d" % g, [128, 16], F32) for g in range(4)]
        cvb_l = [p.sb("cvb_s%d" % g, [128, 4], F32) for g in range(4)]
        hp_l = [p.sb("hp_s%d" % g, [128, 24], F32) for g in range(4)]
        cst_s = p.sb("cst_s", [128, 512], F32)
        msk_s = p.sb("msk_s", [128, 1024], F32)
        mskb = p.sb("mskb", [128, 1024], BF16)
        identb = p.sb("identb", [128, 128], BF16)
        a_l = [p.sb("a_s%d" % g, [128, 8], F32) for g in range(4)]
        bias32_l = [p.sb("bias32_%d" % g, [128, 2, 4, 4], F32) for g in range(4)]
        a32_l = [p.sb("a32_%d" % g, [128, 2, 4, 4], F32) for g in range(4)]
        dsum_l = [p.sb("dsum%d" % g, [128, 4], F32) for g in range(4)]
        t_c = Tok()
        dc = p.dma_sem()
        for dst, src in ((cst_s, cst), (msk_s, msk)):
            p.dma("sp", dst[:], src[:, :], w=[t_c], sem=dc)
        U = cst_s[:, 0:128]
        UT = cst_s[:, 128:256]
        IDF = cst_s[:, 256:384]
        ONES = cst_s[:, 384:512]
        p.op("dve", lambda e: e.tensor_copy(out=mskb[:], in_=msk_s[:]), r=[t_c], w=[t_c])
        p.op("dve", lambda e: e.tensor_copy(out=identb[:], in_=IDF), r=[t_c], w=[t_c])

        xst = Ring(p, "xst", [128, BLK + 3], F32, 3, dma=True)
        xb_r = Ring(p, "xb", [128, 8, BLK + 3], BF16, 2)
        pG = Ring(p, "pG", [128, 512], F32, 6, space="ps")
        pH = Ring(p, "pHb", [128, 512], F32, 1, space="ps")
        pP = Ring(p, "pPp", [128, 512], F32, 1, space="ps")
        pre_r = Ring(p, "pre", [128, BLK + 3], F32, 3)
        cv_r = Ring(p, "cv", [128, BLK], F32, 2)
        xsf_r = Ring(p, "xsf", [128, 3, BLK], F32, 2)
        btb_r = Ring(p, "btb", [128, BLK], BF16, 2)
        ctb_r = Ring(p, "ctb", [128, BLK], BF16, 2)
        hs_r = Ring(p, "hs", [128, 16], F32, 2)
        dtv_r = Ring(p, "dtv", [128, 6, 16], F32, 2)
        sm_r = Ring(p, "sm", [128, 8, 4], F32, 3)
        W_r = Ring(p, "W", [128, 4, 128], F32, 2)
        E_r = Ring(p, "E", [128, 4, 128], F32, 2)
        M_r = Ring(p, "M", [128, 4, 128], BF16, 2)
        btk_r = Ring(p, "btk", [128, 128], BF16, 2)
        xd_r = Ring(p, "xd", [128, 256], BF16, 2)
        xdw_r = Ring(p, "xdw", [128, 256], BF16, 2)
        y_r = Ring(p, "y", [128, 256], F32, 3, dma=True)
        yT_r = Ring(p, "yT", [128, 256], F32, 3, dma=True)
        yt_r = Ring(p, "yt", [128, 256], F32, 3)
        yl_r = Ring(p, "yl", [128, 256], F32, 2, dma=True)
        H_l = [p.sb("H%d" % g, [128, 256], F32) for g in range(4)]
        Hb_l = [p.sb("Hb%d" % g, [128, 256], BF16) for g in range(4)]
        t_H_l = [Tok() for g in range(4)]
        t_Hb_l = [Tok() for g in range(4)]

        yds_r = Ring(p, "yds", [128, 256], F32, 2)

        def front(k, g, blk, c, dtv, t_dtv, xsf, t_xsf, btb, t_btb, ctb, t_ctb, Tri, t_ya):
            gc = blk * 4 + c
            cs_ = slice(c * 128, (c + 1) * 128)
            dA = dtv[:, 5, 4 * c:4 * c + 4]
            dtc = dtv[:, 4, 4 * c:4 * c + 4]
            W, t_W, _ = W_r.next()
            p.op("dve", lambda e: e.tensor_tensor(out=W[:], in0=Tri.unsqueeze(1).to_broadcast([128, 4, 128]),
                                                  in1=dA.unsqueeze(2).to_broadcast([128, 4, 128]), op=ALU.mult),
                 r=[t_dtv, t_c], w=[t_W])
            Eb, t_Eb, _ = pG.next()
            p.op("pe", lambda e: e.matmul(Eb[:], lhsT=ONES, rhs=W[:].rearrange("p r l -> p (r l)"), start=True, stop=False),
                 r=[t_W, t_c], w=[t_Eb])
            p.op("pe", lambda e: e.matmul(Eb[:], lhsT=identb[:], rhs=mskb[:, 512 * k:512 * (k + 1)], start=False, stop=True),
                 r=[t_c], w=[t_Eb])
            T_, t_T, _ = pG.next()
            Sm, t_Sm = T_[:, 384:512], t_T
            p.op("pe", lambda e: e.matmul(Sm[:, 0:4], lhsT=Tri, rhs=dA, start=True, stop=True), r=[t_dtv, t_c], w=[t_Sm])
            p.op("pe", lambda e: e.matmul(Sm[:, 4:8], lhsT=ONES, rhs=dA, start=True, stop=True), r=[t_dtv, t_c], w=[t_Sm])
            for m in range(3):
                p.op("pe", lambda e: e.transpose(T_[:, m * 128:(m + 1) * 128], xsf[:, m, cs_], IDF), r=[t_xsf, t_c], w=[t_T])
            Cb, t_Cb, _ = pG.next()
            p.op("pe", lambda e: e.matmul(Cb[:, 0:128], lhsT=btb[:, cs_], rhs=ctb[:, cs_], start=True, stop=True),
                 r=[t_btb, t_ctb], w=[t_Cb])
            sm, t_sm, _ = sm_r.next()
            CS, TOT, NCS, ECS, DTE, ETOT, DTW, D_ = [sm[:, i, :] for i in range(8)]
            p.op("act", lambda e: e.copy(out=sm[:, 0:2, :], in_=Sm[:, 0:8].rearrange("p (a r) -> p a r", a=2)), r=[t_Sm], w=[t_sm])
            p.op("dve", lambda e: e.tensor_scalar(out=NCS, in0=CS, scalar1=-1.0, scalar2=None, op0=ALU.mult), r=[t_sm], w=[t_sm])
            E, t_E, _ = E_r.next()
            for r_ in range(4):
                p.op("act", lambda e: e.activation(out=E[:, r_, :], in_=Eb[:, r_ * 128:(r_ + 1) * 128], func=AF.Exp,
                                                   bias=sm[:, 2, r_:r_ + 1]), r=[t_Eb, t_sm], w=[t_E])
            btk, t_btk, _ = btk_r.next()
            p.op("act", lambda e: e.copy(out=btk[:], in_=T_[:, 256:384]), r=[t_T], w=[t_btk])
            M, t_M, _ = M_r.next()
            p.op("dve", lambda e: e.tensor_tensor(out=M[:], in0=E[:], in1=Cb[:, 0:128].unsqueeze(1).to_broadcast([128, 4, 128]),
                                                  op=ALU.mult), r=[t_E, t_Cb], w=[t_M])
            p.op("act", lambda e: e.activation(out=ECS, in_=CS, func=AF.Exp), r=[t_sm], w=[t_sm])
            p.op("dve", lambda e: e.tensor_tensor(out=D_, in0=TOT, in1=CS, op=ALU.subtract), r=[t_sm], w=[t_sm])
            p.op("act", lambda e: e.activation(out=DTE, in_=D_, func=AF.Exp), r=[t_sm], w=[t_sm])
            p.op("act", lambda e: e.activation(out=ETOT, in_=TOT, func=AF.Exp), r=[t_sm], w=[t_sm])
            p.op("dve", lambda e: e.tensor_tensor(out=DTW, in0=DTE, in1=dtc, op=ALU.mult), r=[t_sm, t_dtv], w=[t_sm])
            xd, t_xd, _ = xd_r.next()
            xdw, t_xdw, _ = xdw_r.next()
            xs_tok = T_[:, 0:256].rearrange("p (r q) -> p r q", r=4)
            p.op("dve", lambda e: e.tensor_tensor(out=xd[:].rearrange("p (r q) -> p r q", r=4), in0=xs_tok,
                                                  in1=dtc.unsqueeze(2).to_broadcast([128, 4, 64]), op=ALU.mult),
                 r=[t_T, t_dtv], w=[t_xd])
            p.op("dve", lambda e: e.tensor_tensor(out=xdw[:].rearrange("p (r q) -> p r q", r=4), in0=xs_tok,
                                                  in1=DTW.unsqueeze(2).to_broadcast([128, 4, 64]), op=ALU.mult),
                 r=[t_T, t_sm], w=[t_xdw])
            yds, t_yds = None, None
            if k == 0:
                yds, t_yds, _ = yds_r.next()
                p.op("dve", lambda e: e.tensor_tensor(out=yds[:].rearrange("p (r q) -> p r q", r=4), in0=xs_tok,
                                                      in1=dsum_l[g][:].unsqueeze(2).to_broadcast([128, 4, 64]), op=ALU.mult),
                     r=[t_T, t_c], w=[t_yds])
            return dict(k=k, g=g, gc=gc, cs_=cs_, ctb=ctb, t_ctb=t_ctb, M=M, t_M=t_M, xd=xd, t_xd=t_xd, xdw=xdw, t_xdw=t_xdw,
                        btk=btk, t_btk=t_btk, ECS=ECS, ETOT=ETOT, t_sm=t_sm, yds=yds, t_yds=t_yds, t_ya=t_ya)

        def back(s_):
            k, g, gc, cs_ = s_["k"], s_["g"], s_["gc"], s_["cs_"]
            ctb, t_ctb, M, t_M, xd, t_xd, xdw, t_xdw = (s_["ctb"], s_["t_ctb"], s_["M"], s_["t_M"], s_["xd"], s_["t_xd"],
                                                        s_["xdw"], s_["t_xdw"])
            btk, t_btk, ECS, ETOT, t_sm, yds, t_yds, t_ya = (s_["btk"], s_["t_btk"], s_["ECS"], s_["ETOT"], s_["t_sm"],
                                                             s_["yds"], s_["t_yds"], s_["t_ya"])
            H, Hb, t_H, t_Hb = H_l[g], Hb_l[g], t_H_l[g], t_Hb_l[g]
            Y, t_Y, _ = pG.next()
            for r_ in range(4):
                p.op("pe", lambda e: e.matmul(Y[:, r_ * 64:(r_ + 1) * 64], lhsT=M[:, r_, :], rhs=xd[:, r_ * 64:(r_ + 1) * 64],
                                              start=True, stop=True), r=[t_M, t_xd], w=[t_Y])
            p.op("pe", lambda e: e.matmul(Y[:, 256:512], lhsT=ctb[:, cs_], rhs=Hb[:], start=True, stop=True),
                 r=[t_ctb, t_Hb], w=[t_Y])
            ST, t_ST, _ = pG.next()
            p.op("pe", lambda e: e.matmul(ST[:, 0:256], lhsT=btk[:], rhs=xdw[:], start=True, stop=True),
                 r=[t_btk, t_xdw], w=[t_ST])
            yt, t_yt, _ = yt_r.next()
            p.op("dve", lambda e: e.tensor_tensor(out=yt[:].rearrange("p (r q) -> p r q", r=4),
                                                  in0=Y[:, 256:512].rearrange("p (r q) -> p r q", r=4),
                                                  in1=ECS.unsqueeze(2).to_broadcast([128, 4, 64]), op=ALU.mult),
                 r=[t_Y, t_sm], w=[t_yt])
            yo, t_yo, yo_sem = y_r.next()
            p.op("dve", lambda e: e.tensor_tensor(out=yo[:], in0=yt[:], in1=Y[:, 0:256], op=ALU.add), r=[t_yt, t_Y], w=[t_yo])
            if k == 0:
                p.op("dve", lambda e: e.tensor_tensor(out=yo[:], in0=yo[:], in1=yds[:], op=ALU.add), r=[t_yds], w=[t_yo])
            p.op("dve", lambda e: e.tensor_tensor(out=H[:].rearrange("p (r q) -> p r q", r=4),
                                                  in0=H[:].rearrange("p (r q) -> p r q", r=4),
                                                  in1=ETOT.unsqueeze(2).to_broadcast([128, 4, 64]), op=ALU.mult),
                 r=[t_sm], w=[t_H])
            p.op("dve", lambda e: e.tensor_tensor(out=H[:], in0=H[:], in1=ST[:, 0:256], op=ALU.add), r=[t_ST], w=[t_H])
            p.op("act", lambda e: e.copy(out=Hb[:], in_=H[:]), r=[t_H], w=[t_Hb])
            ydst = yaT[g * 256:(g + 1) * 256, gc * 128:(gc + 1) * 128].rearrange("(j q) t -> q j t", q=128)
            T2, t_T2, _ = pG.next()
            for j in range(2):
                p.op("pe", lambda e: e.transpose(T2[:, j * 128:(j + 1) * 128], yo[:, j * 128:(j + 1) * 128], IDF), r=[t_yo, t_c], w=[t_T2])
            yoT, t_yoT, yoT_sem = yT_r.next()
            if k == 0:
                p.op("act", lambda e: e.copy(out=yoT[:], in_=T2[:, 0:256]), r=[t_T2], w=[t_yoT])
            else:
                yl, t_yl, yl_sem = yl_r.next()
                p.dma("act", yl[:].rearrange("q (j t) -> q j t", j=2), ydst, r=[t_ya[gc]], w=[t_yl], sem=yl_sem)
                p.op("dve", lambda e: e.tensor_tensor(out=yoT[:], in0=T2[:, 0:256], in1=yl[:], op=ALU.add), r=[t_T2, t_yl], w=[t_yoT])
            p.dma("act", ydst, yoT[:].rearrange("q (j t) -> q j t", j=2), r=[t_yoT], w=[t_ya[gc]], sem=yoT_sem)

        for g in range(4):
            cvw_s, cvb_s, hp_s, a_s, bias32, a32, dsum = cvw_l[g], cvb_l[g], hp_l[g], a_l[g], bias32_l[g], a32_l[g], dsum_l[g]
            for dst, src in ((cvw_s, cvw_all[g]), (cvb_s, cvb_all[g]), (hp_s, hp_all[g])):
                p.dma("sp", dst[:], src, w=[t_c], sem=dc)
            p.op("act", lambda e: e.activation(out=a_s[:], in_=hp_s[:, 0:8], func=AF.Exp), r=[t_c], w=[t_c])
            p.op("dve", lambda e: e.tensor_scalar(out=a_s[:], in0=a_s[:], scalar1=-1.0, scalar2=None, op0=ALU.mult), r=[t_c], w=[t_c])
            for k in range(2):
                for c in range(4):
                    p.op("dve", lambda e: e.tensor_copy(out=bias32[:, k, c, :], in_=hp_s[:, 8 + 4 * k:12 + 4 * k]), r=[t_c], w=[t_c])
                    p.op("dve", lambda e: e.tensor_copy(out=a32[:, k, c, :], in_=a_s[:, 4 * k:4 * k + 4]), r=[t_c], w=[t_c])
            p.op("dve", lambda e: e.tensor_tensor(out=dsum[:], in0=hp_s[:, 16:20], in1=hp_s[:, 20:24], op=ALU.add), r=[t_c], w=[t_c])
            load_cast_weight(p, wg_all[g], wgb_l[g], stage, 8, 520, cw=520, engines=("dve", "act"), tok=t_wgb_l[g])
        t_ya_l = [[Tok() for _ in range(S // 128)] for g in range(4)]
        for k in range(2):
            pend = None
            for g in range(4):
                p.op("dve", lambda e: e.memset(H_l[g][:], 0.0), w=[t_H_l[g]])
                p.op("dve", lambda e: e.memset(Hb_l[g][:], 0.0), w=[t_Hb_l[g]])
            Tri = U if k == 0 else UT
            blocks = range(NBLK) if k == 0 else range(NBLK - 1, -1, -1)
            for blk in blocks:
                e0 = blk * BLK
                xb, t_xb, _ = xb_r.next()
                for kk in range(8):
                    st, stok, ssem = xst.next()
                    p.dma("sp", st[:], xT[kk * 128:(kk + 1) * 128, e0:e0 + BLK + 3], w=[stok], sem=ssem)
                    p.op("pool", lambda e: e.tensor_copy(out=xb[:, kk, :], in_=st[:]), r=[stok], w=[t_xb])
                for g in range(4):
                    wgb, t_wgb, cvw_s, cvb_s, bias32, a32, t_ya = wgb_l[g], t_wgb_l[g], cvw_l[g], cvb_l[g], bias32_l[g], a32_l[g], t_ya_l[g]
                    hb, t_hb, _ = pH.next()
                    for c in range(4):
                        for kk in range(8):
                            p.op("pe", lambda e: e.matmul(hb[:, 16 + 4 * c:20 + 4 * c], lhsT=xb[:, kk, 2 + c * 128:2 + (c + 1) * 128],
                                                          rhs=wgb[:, kk, 512 + 4 * k:516 + 4 * k], start=(kk == 0), stop=(kk == 7)),
                                 r=[t_xb, t_wgb], w=[t_hb])
                    dtv, t_dtv, _ = dtv_r.next()
                    V, AV, EE, LL, DT, DA = [dtv[:, i, :] for i in range(6)]
                    b32 = bias32[:, k, :, :].rearrange("p c r -> p (c r)")
                    A32 = a32[:, k, :, :].rearrange("p c r -> p (c r)")
                    p.op("dve", lambda e: e.tensor_tensor(out=V, in0=hb[:, 16:32], in1=b32, op=ALU.add), r=[t_hb, t_c], w=[t_dtv])
                    p.op("dve", lambda e: e.tensor_scalar(out=AV, in0=V, scalar1=-1.0, scalar2=None, op0=ALU.mult), r=[t_dtv], w=[t_dtv])
                    p.op("dve", lambda e: e.tensor_tensor(out=AV, in0=AV, in1=V, op=ALU.max), r=[t_dtv], w=[t_dtv])
                    p.op("act", lambda e: e.activation(out=EE, in_=AV, func=AF.Exp, scale=-1.0), r=[t_dtv], w=[t_dtv])
                    p.op("act", lambda e: e.activation(out=LL, in_=EE, func=AF.Ln, bias=1.0), r=[t_dtv], w=[t_dtv])
                    p.op("dve", lambda e: e.scalar_tensor_tensor(out=DT, in0=V, scalar=0.0, in1=LL, op0=ALU.max, op1=ALU.add), r=[t_dtv], w=[t_dtv])
                    p.op("dve", lambda e: e.tensor_tensor(out=DA, in0=DT, in1=A32, op=ALU.mult), r=[t_dtv, t_c], w=[t_dtv])

                    xsf, t_xsf, _ = xsf_r.next()
                    btb, t_btb, _ = btb_r.next()
                    ctb, t_ctb, _ = ctb_r.next()
                    for m in range(4):
                        P, t_P, _ = pP.next()
                        for kk in range(8):
                            p.op("pe", lambda e: e.matmul(P[:, 0:BLK], lhsT=wgb[:, kk, m * 128:(m + 1) * 128], rhs=xb[:, kk, 0:BLK],
                                                          start=(kk == 0), stop=(kk == 7)), r=[t_xb, t_wgb], w=[t_P])
                        for kk in range(8):
                            p.op("pe", lambda e: e.matmul(hb[:, 4 * m:4 * m + 3], lhsT=wgb[:, kk, m * 128:(m + 1) * 128],
                                                          rhs=xb[:, kk, BLK:BLK + 3], start=(kk == 0), stop=(kk == 7)),
                                 r=[t_xb, t_wgb], w=[t_hb])
                        pre, t_pre, _ = pre_r.next()
                        p.op("act", lambda e: e.copy(out=pre[:, 0:BLK], in_=P[:, 0:BLK]), r=[t_P], w=[t_pre])
                        p.op("act", lambda e: e.copy(out=pre[:, BLK:BLK + 3], in_=hb[:, 4 * m:4 * m + 3]), r=[t_hb], w=[t_pre])
                        cv, t_cv, _ = cv_r.next()
                        p.op("dve", lambda e: e.tensor_scalar(out=cv[:], in0=pre[:, 0:BLK], scalar1=cvw_s[:, 4 * m:4 * m + 1], scalar2=None,
                                                              op0=ALU.mult), r=[t_pre, t_c], w=[t_cv])
                        for tap in range(1, 4):
                            p.op("dve", lambda e: e.scalar_tensor_tensor(out=cv[:], in0=pre[:, tap:tap + BLK],
                                                                         scalar=cvw_s[:, 4 * m + tap:4 * m + tap + 1], in1=cv[:],
                                                                         op0=ALU.mult, op1=ALU.add), r=[t_pre, t_c], w=[t_cv])
                        if m < 3:
                            p.op("act", lambda e: e.activation(out=xsf[:, m, :], in_=cv[:], func=AF.Silu, bias=cvb_s[:, m:m + 1]),
                                 r=[t_cv, t_c], w=[t_xsf])
                            if m == 2:
                                p.op("act", lambda e: e.copy(out=btb[:], in_=xsf[:, 2, :]), r=[t_xsf], w=[t_btb])
                        else:
                            p.op("act", lambda e: e.activation(out=ctb[:], in_=cv[:], func=AF.Silu, bias=cvb_s[:, m:m + 1]),
                                 r=[t_cv, t_c], w=[t_ctb])

                    chunks = range(4) if k == 0 else range(3, -1, -1)
                    for c in chunks:
                        st_ = front(k, g, blk, c, dtv, t_dtv, xsf, t_xsf, btb, t_btb, ctb, t_ctb, Tri, t_ya)
                        if pend is not None:
                            back(pend)
                        pend = st_
            back(pend)
            pend = None
        barrier(p)


def l0a_consts():
    t = np.arange(128)
    U = (t[:, None] <= t[None, :]).astype(np.float32)
    UT = (t[:, None] >= t[None, :]).astype(np.float32)
    I = np.eye(128, dtype=np.float32)
    ones = np.ones((128, 128), np.float32)
    cst = np.ascontiguousarray(np.concatenate([U, UT, I, ones], axis=1))
    mf = np.where(t[None, :] < t[:, None], NEG, 0.0).astype(np.float32)
    mb = np.where(t[None, :] > t[:, None], NEG, 0.0).astype(np.float32)
    msk = np.ascontiguousarray(np.concatenate([np.tile(mf, (1, 4)), np.tile(mb, (1, 4))], axis=1))
    return cst, msk


def prep_l0a(x, ev_w_in, ev_conv_w, ev_conv_b, ev_a_log, ev_dt_bias, ev_d_skip, S=SEQ):
    w_in = ev_w_in[0]
    cw = ev_conv_w[0]
    cb = ev_conv_b[0]
    cst, msk = l0a_consts()
    maps = []
    xTs = []
    for b in range(2):
        xe = np.zeros((D, S + 3), np.float32)
        xe[:, 2:S + 2] = x[b, :S, :].T
        xTs.append(xe)
    for c in range(NCORES):
        b, g = c // 4, c % 4
        xs_cols = 1024 + g * 256 + np.arange(256)
        b_cols = 1024 + 1024 + g * 128 + np.arange(128)
        c_cols = 1024 + 1536 + g * 128 + np.arange(128)
        dt_cols = np.concatenate([3072 + k * 16 + 4 * g + np.arange(4) for k in range(2)])
        cols = np.concatenate([xs_cols, b_cols, c_cols, dt_cols])
        wg = np.ascontiguousarray(w_in[:, cols])
        xbc_idx = cols[:512] - 1024
        cvw = np.ascontiguousarray(cw[:, xbc_idx].reshape(4, 4, 128).transpose(2, 1, 0).reshape(128, 16))
        cvb = np.ascontiguousarray(cb[xbc_idx].reshape(4, 128).T)
        hsel = np.concatenate([np.stack([v[0][k, 4 * g:4 * g + 4] for k in range(2)]).reshape(-1)
                               for v in (ev_a_log, ev_dt_bias, ev_d_skip)])
        hp = np.ascontiguousarray(np.broadcast_to(hsel[None, :], (128, 24))).astype(np.float32)
        maps.append({"xT": xTs[b], "wg": wg, "cvw": cvw, "cvb": cvb, "hp": hp, "cst": cst, "msk": msk})
    return maps


def rope_tables(p, posi, t_posi, n, invf, sgn, t_c, tabs, cosd, sind, t_cos, t_sin):
    R = slice(64, 96)
    ang, t_a, _ = tabs.next()
    nf, t_n, _ = tabs.next()
    ni, t_ni, _ = tabs.next()
    mm, t_m, _ = tabs.next()
    A, N, M = ang[R, 0:n], nf[R, 0:n], mm[R, 0:n]
    NI = ni[R, 0:n].bitcast(I32)
    p.op("dve", lambda e: e.tensor_copy(out=A, in_=posi[R, 0:n]), r=[t_posi], w=[t_a])
    p.op("dve", lambda e: e.tensor_scalar(out=A, in0=A, scalar1=invf[R, 0:1], scalar2=None, op0=ALU.mult), r=[t_c], w=[t_a])
    p.op("dve", lambda e: e.tensor_scalar(out=N, in0=A, scalar1=1.0 / TWO_PI, scalar2=None, op0=ALU.mult), r=[t_a], w=[t_n])
    p.op("dve", lambda e: e.tensor_copy(out=NI, in_=N), r=[t_n], w=[t_ni])
    p.op("dve", lambda e: e.tensor_copy(out=N, in_=NI), r=[t_ni], w=[t_n])
    p.op("dve", lambda e: e.scalar_tensor_tensor(out=A, in0=N, scalar=-C1, in1=A, op0=ALU.mult, op1=ALU.add), r=[t_n], w=[t_a])
    p.op("dve", lambda e: e.scalar_tensor_tensor(out=A, in0=N, scalar=-C2, in1=A, op0=ALU.mult, op1=ALU.add), r=[t_n], w=[t_a])

    def wrap(X, t_x):
        p.op("dve", lambda e: e.tensor_scalar(out=M, in0=X, scalar1=PI, scalar2=None, op0=ALU.is_gt), r=[t_x], w=[t_m])
        p.op("dve", lambda e: e.scalar_tensor_tensor(out=X, in0=M, scalar=-TWO_PI, in1=X, op0=ALU.mult, op1=ALU.add), r=[t_m], w=[t_x])
        p.op("dve", lambda e: e.tensor_scalar(out=M, in0=X, scalar1=-PI, scalar2=None, op0=ALU.is_lt), r=[t_x], w=[t_m])
        p.op("dve", lambda e: e.scalar_tensor_tensor(out=X, in0=M, scalar=TWO_PI, in1=X, op0=ALU.mult, op1=ALU.add), r=[t_m], w=[t_x])
    wrap(A, t_a)
    p.op("act", lambda e: e.activation(out=N, in_=A, func=AF.Sin), r=[t_a], w=[t_n])
    p.op("dve", lambda e: e.tensor_scalar(out=sind, in0=N, scalar1=sgn[R, 0:1], scalar2=None, op0=ALU.mult), r=[t_n, t_c], w=[t_sin])
    p.op("dve", lambda e: e.tensor_scalar(out=A, in0=A, scalar1=PI / 2, scalar2=None, op0=ALU.add), r=[t_a], w=[t_a])
    wrap(A, t_a)
    p.op("act", lambda e: e.activation(out=cosd, in_=A, func=AF.Sin), r=[t_a], w=[t_cos])


def emit_l1(nc, p, S, A):
    T = S // 4
    BLK = 512
    NKB = S // BLK
    NQB = T // BLK
    NK128 = S // 128
    x1T, x1, posb, out = A["x1T"], A["x1"], A["posb"], A["out"]
    w_cq, w_kv, w_kr, w_g3, w_uq, w_ukv, w_pool, wout = (A["w_cq"], A["w_kv"], A["w_kr"], A["w_g3"], A["w_uq"], A["w_ukv"],
                                                         A["w_pool"], A["wout_od"])
    sm_c, sel, lng, lnb = A["sm_c"], A["sel"], A["lng_od"], A["lnb_od"]
    oxT, ox1, oHL, oHR, opos = A["own_x1T"], A["own_x1"], A["own_HL"], A["own_HR"], A["own_pos"]

    def xrows_static(blk, kk):
        return x1T[blk * 1024 + kk * 128:blk * 1024 + (kk + 1) * 128, :]

    def xrows_own(blk, kk):
        return oxT[blk * 1024 + kk * 128:blk * 1024 + (kk + 1) * 128, :]
    xtok = ox1

    with ExitStack() as es:
        p.es = es
        smc = p.sb("smc", [128, 80], F32)
        sel_s = p.sb("sel_s", [128, 384], F32)
        t_c = Tok()
        dc = p.dma_sem()
        p.dma("sp", smc[:], sm_c[:, :], w=[t_c], sem=dc)
        p.dma("sp", sel_s[:], sel[:, :], w=[t_c], sem=dc)
        QG, KVG, INVF, SGN, PSC = smc[:, 0:2], smc[:, 2:3], smc[:, 3:4], smc[:, 4:5], smc[:, 5:9]
        CORR = smc[:, 16:80]
        SEL_E, SEL_O, ONES = sel_s[:, 0:128], sel_s[:, 128:256], sel_s[:, 256:384]
        ycT = p.sb("ycT", [128, 4, T], BF16)
        t_yc = [[Tok() for _ in range(NQB)] for _ in range(8)]
        esA = ExitStack()
        p.es = esA
        ckvn = p.sb("ckvn", [128, S], BF16)
        t_ckvn = [Tok() for _ in range(NKB)]
        Kbuf = p.sb("Kbuf", [96, S], BF16)
        t_kn = [Tok() for _ in range(NKB)]
        t_kr = [Tok() for _ in range(NKB)]
        cqn = p.sb("cqn", [128, 2, T], BF16)
        t_cqn = [Tok() for _ in range(NQB)]
        cosq = p.sb("cosq", [96, T], BF16)
        sinq = p.sb("sinq", [96, T], BF16)
        t_cosq = [Tok() for _ in range(NQB)]
        t_sinq = [Tok() for _ in range(NQB)]
        wuqb = p.sb("wuqb", [128, 2, 1536], BF16)
        wukvb = p.sb("wukvb", [128, 1024], BF16)
        t_wuq, t_wukv = Tok(), Tok()

        with ExitStack() as es1:
            p.es = es1
            wst = Ring(p, "wst", [128, 1536], F32, 1, dma=True)
            wcqb = p.sb("wcqb", [128, 8, 256], BF16)
            wkvb = p.sb("wkvb", [128, 8, 128], BF16)
            wkrb = p.sb("wkrb", [128, 8, 192], BF16)
            t_wcq, t_wkv, t_wkr = Tok(), Tok(), Tok()
            load_cast_weight(p, w_cq, wcqb, wst, 8, 256, cw=256, tok=t_wcq)
            load_cast_weight(p, w_kv, wkvb, wst, 8, 128, cw=128, tok=t_wkv)
            load_cast_weight(p, w_kr, wkrb, wst, 8, 192, cw=192, tok=t_wkr)
            for kc in range(2):
                st, stok, ssem = wst.next()
                p.dma("sp", st[:, 0:1536], w_uq[kc * 128:(kc + 1) * 128, :], w=[stok], sem=ssem)
                p.op("dve", lambda e: e.tensor_scalar(out=wuqb[:, kc, :], in0=st[:, 0:1536], scalar1=QG[:, kc:kc + 1], scalar2=None,
                                                      op0=ALU.mult), r=[stok, t_c], w=[t_wuq])
            st, stok, ssem = wst.next()
            p.dma("sp", st[:, 0:1024], w_ukv[:, :], w=[stok], sem=ssem)
            p.op("dve", lambda e: e.tensor_scalar(out=wukvb[:], in0=st[:, 0:1024], scalar1=KVG, scalar2=None, op0=ALU.mult),
                 r=[stok, t_c], w=[t_wukv])

            xst = Ring(p, "xst", [128, BLK], F32, 3, dma=True)
            xb_r = Ring(p, "xb", [128, 8, BLK], BF16, 2)
            pos_r = Ring(p, "posr", [128, BLK], I32, 2, dma=True)
            tabs = Ring(p, "tabs", [128, BLK], F32, 4)
            cs_r = Ring(p, "csr", [128, BLK], F32, 2)
            sn_r = Ring(p, "snr", [128, BLK], F32, 2)
            sq_r = Ring(p, "sqr", [128, BLK], F32, 3)
            t1_r = Ring(p, "t1r", [128, BLK], F32, 2)
            pA = Ring(p, "pA", [128, 512], F32, 5, space="ps")
            pSS = Ring(p, "pSS", [128, 512], F32, 2, space="ps")

            def load_xblock(rows_fn, blk):
                xb, t_xb, _ = xb_r.next()
                for kk in range(8):
                    st, stok, ssem = xst.next()
                    p.dma("sp", st[:], rows_fn(blk, kk), w=[stok], sem=ssem)
                    p.op("pool", lambda e: e.tensor_copy(out=xb[:, kk, :], in_=st[:]), r=[stok], w=[t_xb])
                return xb, t_xb

            def rstd_of(ss, t_ss, nch):
                r_, t_r, _ = sq_r.next()
                p.op("dve", lambda e: e.tensor_scalar(out=r_[:], in0=ss[:], scalar1=1.0 / nch, scalar2=EPS, op0=ALU.mult, op1=ALU.add),
                     r=[t_ss], w=[t_r])
                p.op("act", lambda e: e.activation(out=r_[:], in_=r_[:], func=AF.Ln), r=[t_r], w=[t_r])
                p.op("act", lambda e: e.activation(out=r_[:], in_=r_[:], func=AF.Exp, scale=-0.5), r=[t_r], w=[t_r])
                return r_, t_r

            for kb in range(NKB):
                c0 = kb * BLK
                xb, t_xb = load_xblock(xrows_static, kb + 1)
                pi_, t_pi, pi_sem = pos_r.next()
                p.dma("sp", pi_[64:96, :], posb[kb * 32:(kb + 1) * 32, :], w=[t_pi], sem=pi_sem)
                ck, t_ck, _ = pA.next()
                ka, t_ka, _ = pA.next()
                kbs, t_kbs, _ = pA.next()
                for kk in range(8):
                    p.op("pe", lambda e: e.matmul(ck[:], lhsT=wkvb[:, kk, :], rhs=xb[:, kk, :], start=(kk == 0), stop=(kk == 7)),
                         r=[t_wkv, t_xb], w=[t_ck])
                for kk in range(8):
                    p.op("pe", lambda e: e.matmul(ka[0:96, :], lhsT=wkrb[:, kk, 0:96], rhs=xb[:, kk, :], start=(kk == 0), stop=(kk == 7)),
                         r=[t_wkr, t_xb], w=[t_ka])
                for kk in range(8):
                    p.op("pe", lambda e: e.matmul(kbs[0:96, :], lhsT=wkrb[:, kk, 96:192], rhs=xb[:, kk, :], start=(kk == 0), stop=(kk == 7)),
                         r=[t_wkr, t_xb], w=[t_kbs])
                sq, t_sq, _ = sq_r.next()
                p.op("act", lambda e: e.activation(out=sq[:], in_=ck[:], func=AF.Square), r=[t_ck], w=[t_sq])
                ss, t_ss, _ = pSS.next()
                p.op("pe", lambda e: e.matmul(ss[:], lhsT=ONES, rhs=sq[:], start=True, stop=True), r=[t_sq, t_c], w=[t_ss])
                rs, t_rs = rstd_of(ss, t_ss, 128)
                p.op("dve", lambda e: e.tensor_tensor(out=ckvn[:, c0:c0 + BLK], in0=ck[:], in1=rs[:], op=ALU.mult),
                     r=[t_ck, t_rs], w=[t_ckvn[kb]])
                cs_, t_cs, _ = cs_r.next()
                sn_, t_sn, _ = sn_r.next()
                rope_tables(p, pi_, t_pi, BLK, INVF, SGN, t_c, tabs, cs_[64:96, :], sn_[64:96, :], t_cs, t_sn)
                t1, t_t1, _ = t1_r.next()
                t2, t_t2, _ = t1_r.next()
                p.op("dve", lambda e: e.tensor_tensor(out=t1[64:96, :], in0=ka[64:96, :], in1=cs_[64:96, :], op=ALU.mult),
                     r=[t_ka, t_cs], w=[t_t1])
                p.op("dve", lambda e: e.tensor_tensor(out=t2[64:96, :], in0=kbs[64:96, :], in1=sn_[64:96, :], op=ALU.mult),
                     r=[t_kbs, t_sn], w=[t_t2])
                p.op("pool", lambda e: e.tensor_tensor(out=Kbuf[64:96, c0:c0 + BLK], in0=t1[64:96, :], in1=t2[64:96, :], op=ALU.add),
                     r=[t_t1, t_t2], w=[t_kr[kb]])

            for qb in range(NQB):
                c0 = qb * BLK
                xb, t_xb = load_xblock(xrows_own, qb)
                pi_, t_pi, pi_sem = pos_r.next()
                p.dma("sp", pi_[64:96, :], opos[qb * 32:(qb + 1) * 32, :], w=[t_pi], sem=pi_sem)
                cqs = []
                ss, t_ss, _ = pSS.next()
                for m in range(2):
                    cq, t_cq, _ = pA.next()
                    for kk in range(8):
                        p.op("pe", lambda e: e.matmul(cq[:], lhsT=wcqb[:, kk, m * 128:(m + 1) * 128], rhs=xb[:, kk, :],
                                                      start=(kk == 0), stop=(kk == 7)), r=[t_wcq, t_xb], w=[t_cq])
                    sq, t_sq, _ = sq_r.next()
                    p.op("act", lambda e: e.activation(out=sq[:], in_=cq[:], func=AF.Square), r=[t_cq], w=[t_sq])
                    p.op("pe", lambda e: e.matmul(ss[:], lhsT=ONES, rhs=sq[:], start=(m == 0), stop=(m == 1)), r=[t_sq, t_c], w=[t_ss])
                    cqs.append((cq, t_cq))
                rs, t_rs = rstd_of(ss, t_ss, 256)
                for m in range(2):
                    cq, t_cq = cqs[m]
                    p.op("dve", lambda e: e.tensor_tensor(out=cqn[:, m, c0:c0 + BLK], in0=cq[:], in1=rs[:], op=ALU.mult),
                         r=[t_cq, t_rs], w=[t_cqn[qb]])
                rope_tables(p, pi_, t_pi, BLK, INVF, SGN, t_c, tabs, cosq[64:96, c0:c0 + BLK], sinq[64:96, c0:c0 + BLK],
                            t_cosq[qb], t_sinq[qb])
        barrier(p)

        with ExitStack() as es3:
            p.es = es3
            Vbuf = p.sb("Vbuf", [128, NK128, 128], BF16)
            t_v = [Tok() for _ in range(NK128 // 8 if NK128 >= 8 else 1)]
            VG = min(8, NK128)
            Q_r = Ring(p, "Q", [96, T], BF16, 2)
            tq_r = Ring(p, "tq", [96, BLK], F32, 4)
            P_r = Ring(p, "P", [128, BLK], BF16, 3)
            osb_r = Ring(p, "osb", [128, BLK], F32, 2)
            rden_r = Ring(p, "rden", [128, BLK], F32, 2)
            pS = Ring(p, "pS", [128, 512], F32, 3, space="ps")
            pO = Ring(p, "pO", [128, 512], F32, 2, space="ps")
            pD = Ring(p, "pD", [128, 512], F32, 1, space="ps")
            pB = Ring(p, "pB", [128, 512], F32, 2, space="ps")
            for h in range(8):
                odd = h % 2
                voff = 64 * odd
                Q, _, _ = Q_r.next()
                t_Q = [Tok() for _ in range(NQB)]
                for qb in range(NQB):
                    c0 = qb * BLK
                    qa, t_qa, _ = pB.next()
                    qs, t_qs, _ = pB.next()
                    for kc in range(2):
                        p.op("pe", lambda e: e.matmul(qa[0:96, :], lhsT=wuqb[:, kc, h * 96:(h + 1) * 96], rhs=cqn[:, kc, c0:c0 + BLK],
                                                      start=(kc == 0), stop=(kc == 1)), r=[t_wuq, t_cqn[qb]], w=[t_qa])
                    for kc in range(2):
                        p.op("pe", lambda e: e.matmul(qs[0:96, :], lhsT=wuqb[:, kc, 768 + h * 96:768 + (h + 1) * 96],
                                                      rhs=cqn[:, kc, c0:c0 + BLK], start=(kc == 0), stop=(kc == 1)),
                             r=[t_wuq, t_cqn[qb]], w=[t_qs])
                    p.op("dve", lambda e: e.tensor_copy(out=Q[0:64, c0:c0 + BLK], in_=qa[0:64, :]), r=[t_qa], w=[t_Q[qb]])
                    t1, t_t1, _ = tq_r.next()
                    t2, t_t2, _ = tq_r.next()
                    p.op("dve", lambda e: e.tensor_tensor(out=t1[64:96, :], in0=qa[64:96, :], in1=cosq[64:96, c0:c0 + BLK], op=ALU.mult),
                         r=[t_qa, t_cosq[qb]], w=[t_t1])
                    p.op("dve", lambda e: e.tensor_tensor(out=t2[64:96, :], in0=qs[64:96, :], in1=sinq[64:96, c0:c0 + BLK], op=ALU.mult),
                         r=[t_qs, t_sinq[qb]], w=[t_t2])
                    p.op("pool", lambda e: e.tensor_tensor(out=Q[64:96, c0:c0 + BLK], in0=t1[64:96, :], in1=t2[64:96, :], op=ALU.add),
                         r=[t_t1, t_t2], w=[t_Q[qb]])
                for kb in range(NKB):
                    c0 = kb * BLK
                    kp, t_kp, _ = pB.next()
                    p.op("pe", lambda e: e.matmul(kp[0:64, :], lhsT=wukvb[:, h * 128:h * 128 + 64], rhs=ckvn[:, c0:c0 + BLK],
                                                  start=True, stop=True), r=[t_wukv, t_ckvn[kb]], w=[t_kp])
                    p.op("dve", lambda e: e.tensor_copy(out=Kbuf[0:64, c0:c0 + BLK], in_=kp[0:64, :]), r=[t_kp], w=[t_kn[kb]])
                for g in range(len(t_v)):
                    vp, t_vp, _ = pB.next()
                    for j in range(VG):
                        k128 = g * VG + j
                        p.op("pe", lambda e: e.matmul(vp[:, j * 64:(j + 1) * 64], lhsT=ckvn[:, k128 * 128:(k128 + 1) * 128],
                                                      rhs=wukvb[:, h * 128 + 64:h * 128 + 128], start=True, stop=True),
                             r=[t_wukv, t_ckvn[k128 // 4]], w=[t_vp])
                    vs = Vbuf[:, g * VG:(g + 1) * VG, :]
                    p.op("pool", lambda e: e.memset(vs[:, :, 64 - voff:128 - voff], 0.0), w=[t_v[g]])
                    p.op("pool", lambda e: e.memset(vs[:, :, 64 - voff:65 - voff], 1.0), w=[t_v[g]])
                    p.op("dve", lambda e: e.tensor_copy(out=vs[:, :, voff:voff + 64], in_=vp[:, 0:VG * 64].rearrange("p (j v) -> p j v", v=64)),
                         r=[t_vp], w=[t_v[g]])
                MV = 128 if odd else 65
                for qb in range(NQB):
                    q0 = qb * BLK
                    O, t_O, _ = pO.next()
                    Sq = {}

                    def issue_S(k128):
                        Sx, t_S, _ = pS.next()
                        kb = k128 // 4
                        p.op("pe", lambda e: e.matmul(Sx[:], lhsT=Kbuf[0:96, k128 * 128:(k128 + 1) * 128], rhs=Q[0:96, q0:q0 + BLK],
                                                      start=True, stop=True), r=[t_kn[kb], t_kr[kb], t_Q[qb]], w=[t_S])
                        Sq[k128] = (Sx, t_S)
                    for k128 in range(min(2, NK128)):
                        issue_S(k128)
                    for k128 in range(NK128):
                        Sx, t_S = Sq.pop(k128)
                        Pt, t_P, _ = P_r.next()
                        p.op("act", lambda e: e.activation(out=Pt[:], in_=Sx[:], func=AF.Exp, scale=ATT_SCALE), r=[t_S], w=[t_P])
                        if k128 + 2 < NK128:
                            issue_S(k128 + 2)
                        p.op("pe", lambda e: e.matmul(O[0:MV, :], lhsT=Vbuf[:, k128, 0:MV], rhs=Pt[:], start=(k128 == 0),
                                                      stop=(k128 == NK128 - 1)), r=[t_v[k128 // VG], t_P], w=[t_O])
                    osb, t_osb, _ = osb_r.next()
                    p.op("dve", lambda e: e.tensor_copy(out=osb[0:MV, :], in_=O[0:MV, :]), r=[t_O], w=[t_osb])
                    Dn, t_D, _ = pD.next()
                    if odd:
                        p.op("pe", lambda e: e.matmul(Dn[:], lhsT=SEL_O, rhs=osb[:], start=True, stop=True), r=[t_osb, t_c], w=[t_D])
                    else:
                        p.op("pe", lambda e: e.matmul(Dn[0:64, :], lhsT=SEL_E[0:65, 0:64], rhs=osb[0:65, :], start=True, stop=True),
                             r=[t_osb, t_c], w=[t_D])
                    rd, t_rd, _ = rden_r.next()
                    PR = slice(voff, voff + 64)
                    p.op("dve", lambda e: e.reciprocal(out=rd[PR, :], in_=Dn[PR, :]), r=[t_D], w=[t_rd])
                    p.op("dve", lambda e: e.tensor_tensor(out=ycT[PR, h // 2, q0:q0 + BLK], in0=osb[PR, :], in1=rd[PR, :], op=ALU.mult),
                         r=[t_osb, t_rd], w=[t_yc[h][qb]])
        barrier(p)
        esA.close()

        with ExitStack() as es4:
            p.es = es4
            wst = Ring(p, "wst4", [128, 1024], F32, 2, dma=True)
            wg3b = p.sb("wg3b", [128, 8, 1536], BF16)
            woutb = p.sb("woutb", [128, 8, D], BF16)
            wpoolb = p.sb("wpoolb", [128, 512], BF16)
            lng_s = p.sb("lng_s", [128, D], F32)
            lnb_s = p.sb("lnb_s", [128, D], F32)
            t_wg3, t_wout, t_wpool, t_ln = Tok(), Tok(), Tok(), Tok()
            dl = p.dma_sem()
            p.dma("sp", lng_s[:], lng[:, :], w=[t_ln], sem=dl)
            p.dma("sp", lnb_s[:], lnb[:, :], w=[t_ln], sem=dl)
            load_cast_weight(p, w_g3, wg3b, wst, 8, 1536, cw=768, tok=t_wg3)
            load_cast_weight(p, wout, woutb, wst, 8, D, tok=t_wout)
            st, stok, ssem = wst.next()
            p.dma("sp", st[:, 0:512], w_pool[:, :], w=[stok], sem=ssem)
            p.op("dve", lambda e: e.tensor_copy(out=wpoolb[:], in_=st[:, 0:512]), r=[stok], w=[t_wpool])
            xst = Ring(p, "xst4", [128, BLK + 16], F32, 3, dma=True)
            xb_r = Ring(p, "xb4", [128, 8, BLK + 16], BF16, 2)
            xtk = Ring(p, "xtk", [128, D], F32, 2, dma=True)
            cat = p.sb("cat", [128, 8, BLK], BF16)
            t_cat = [Tok() for _ in range(8)]
            tmp = Ring(p, "tmp4", [128, BLK + 16], F32, 10)
            hl_r = Ring(p, "hl4", [128, 16], F32, 2)
            pl_r = Ring(p, "pl4", [128, BLK], BF16, 2)
            rr = Ring(p, "rr", [128, D], F32, 2)
            r2 = Ring(p, "r2", [128, D], F32, 2, dma=True)
            junk = Ring(p, "junk", [128, D], BF16, 1)
            st_r = Ring(p, "stat", [128, 8], F32, 4)
            pp = Ring(p, "pp4", [128, 512], F32, 5, space="ps")
            ph = Ring(p, "ph4", [128, 512], F32, 1, space="ps")
            po = Ring(p, "po4", [128, D], F32, 1, space="ps")
            WIN = (2, 4, 8, 16)
            for qb in range(NQB):
                c0 = qb * BLK
                xb, t_xb, _ = xb_r.next()
                for kk in range(8):
                    st, stok, ssem = xst.next()
                    rws = slice(qb * 1024 + kk * 128, qb * 1024 + (kk + 1) * 128)
                    p.dma("sp", st[:, 8:BLK + 8], oxT[rws, :], w=[stok], sem=ssem)
                    p.dma("sp", st[:, 0:8], oHL[rws, :], w=[stok], sem=ssem, nowait=True)
                    p.dma("sp", st[:, BLK + 8:BLK + 16], oHR[rws, :], w=[stok], sem=ssem, nowait=True)
                    p.op("pool", lambda e: e.tensor_copy(out=xb[:, kk, :], in_=st[:]), r=[stok], w=[t_xb])

                def proj(col0, lo, n, dst, t_dst):
                    for kk in range(8):
                        p.op("pe", lambda e: e.matmul(dst, lhsT=wg3b[:, kk, col0:col0 + 128], rhs=xb[:, kk, lo:lo + n],
                                                      start=(kk == 0), stop=(kk == 7)), r=[t_wg3, t_xb], w=[t_dst])
                for j in range(4):
                    gc, t_gc, _ = pp.next()
                    proj(j * 128, 8, BLK, gc[:], t_gc)
                    sg, t_sg, _ = tmp.next()
                    p.op("act", lambda e: e.activation(out=sg[:, 0:BLK], in_=gc[:], func=AF.Silu), r=[t_gc], w=[t_sg])
                    p.op("dve", lambda e: e.tensor_tensor(out=cat[:, j, :], in0=sg[:, 0:BLK], in1=ycT[:, j, c0:c0 + BLK], op=ALU.mult),
                         r=[t_sg, t_yc[2 * j][qb], t_yc[2 * j + 1][qb]], w=[t_cat[j]])
                for gi in range(4):
                    w = WIN[gi]
                    um, t_um, _ = pp.next()
                    proj(512 + gi * 128, 8, BLK, um[:], t_um)
                    hl, t_hl, _ = ph.next()
                    proj(512 + gi * 128, 0, 8, hl[:, 0:8], t_hl)
                    proj(512 + gi * 128, BLK + 8, 8, hl[:, 8:16], t_hl)
                    u, t_u, _ = tmp.next()
                    p.op("act", lambda e: e.copy(out=u[:, 8:BLK + 8], in_=um[:]), r=[t_um], w=[t_u])
                    p.op("act", lambda e: e.copy(out=u[:, 0:8], in_=hl[:, 0:8]), r=[t_hl], w=[t_u])
                    p.op("act", lambda e: e.copy(out=u[:, BLK + 8:BLK + 16], in_=hl[:, 8:16]), r=[t_hl], w=[t_u])
                    cur, t_cur, n, width = u, t_u, BLK + 16, 1
                    while width < w:
                        nxt, t_nxt, _ = tmp.next()
                        n2 = n - width
                        p.op("dve", lambda e: e.tensor_tensor(out=nxt[:, 0:n2], in0=cur[:, 0:n2], in1=cur[:, width:width + n2], op=ALU.add),
                             r=[t_cur], w=[t_nxt])
                        cur, t_cur, n, width = nxt, t_nxt, n2, width * 2
                    s0 = 8 - w // 2
                    pm, t_pm, _ = tmp.next()
                    p.op("dve", lambda e: e.tensor_scalar(out=pm[:, 0:BLK], in0=cur[:, s0:s0 + BLK], scalar1=1.0 / w, scalar2=None, op0=ALU.mult),
                         r=[t_cur], w=[t_pm])
                    if qb == 0:
                        p.op("dve", lambda e: e.tensor_tensor(out=pm[:, 0:8], in0=pm[:, 0:8], in1=CORR[:, gi * 16:gi * 16 + 8], op=ALU.mult),
                             r=[t_c], w=[t_pm])
                    if qb == NQB - 1:
                        p.op("dve", lambda e: e.tensor_tensor(out=pm[:, BLK - 8:BLK], in0=pm[:, BLK - 8:BLK],
                                                              in1=CORR[:, gi * 16 + 8:gi * 16 + 16], op=ALU.mult), r=[t_c], w=[t_pm])
                    pl, t_pl, _ = pl_r.next()
                    p.op("dve", lambda e: e.tensor_tensor(out=pl[:], in0=pm[:, 0:BLK], in1=u[:, 8:BLK + 8], op=ALU.subtract),
                         r=[t_pm, t_u], w=[t_pl])
                    yd, t_yd, _ = pp.next()
                    p.op("pe", lambda e: e.matmul(yd[:], lhsT=wpoolb[:, gi * 128:(gi + 1) * 128], rhs=pl[:], start=True, stop=True),
                         r=[t_wpool, t_pl], w=[t_yd])
                    gd, t_gd, _ = pp.next()
                    proj(1024 + gi * 128, 8, BLK, gd[:], t_gd)
                    sg, t_sg, _ = tmp.next()
                    p.op("act", lambda e: e.activation(out=sg[:, 0:BLK], in_=gd[:], func=AF.Silu), r=[t_gd], w=[t_sg])
                    p.op("dve", lambda e: e.scalar_tensor_tensor(out=cat[:, 4 + gi, :], in0=yd[:], scalar=PSC[:, gi:gi + 1], in1=sg[:, 0:BLK],
                                                                 op0=ALU.mult, op1=ALU.mult), r=[t_yd, t_sg, t_c], w=[t_cat[4 + gi]])
                for tt in range(BLK // 128):
                    xk, t_xk, xk_sem = xtk.next()
                    p.dma("sp", xk[:], xtok[c0 + tt * 128:c0 + (tt + 1) * 128, :], w=[t_xk], sem=xk_sem)
                    o, t_o, _ = po.next()
                    for half in range(2):
                        for kc in range(8):
                            p.op("pe", lambda e: e.matmul(o[:, half * 512:(half + 1) * 512], lhsT=cat[:, kc, tt * 128:(tt + 1) * 128],
                                                          rhs=woutb[:, kc, half * 512:(half + 1) * 512], start=(kc == 0), stop=(kc == 7)),
                                 r=[t_cat[kc], t_wout], w=[t_o])
                    layer_norm_tail(p, o, t_o, xk, t_xk, rr, r2, junk, st_r, lng_s, lnb_s, t_ln,
                                    out[c0 + tt * 128:c0 + (tt + 1) * 128, :])
        barrier(p)


def prep_l1(x1, positions, od_w_in, od_q_norm_g, od_w_uq, od_kv_norm_g, od_w_ukv, od_pool_w, od_pool_scale, od_w_out,
            od_ln_g, od_ln_b, S=SEQ):
    T = S // 4
    w_in = od_w_in[0]
    w_cq = np.ascontiguousarray(w_in[:, 0:256])
    w_kv = np.ascontiguousarray(w_in[:, 256:384])
    kr = w_in[:, 384:416]
    krs = np.concatenate([kr[:, 16:32], kr[:, 0:16]], axis=1)
    z64 = np.zeros((D, 64), np.float32)
    w_kr = np.ascontiguousarray(np.concatenate([z64, kr, z64, krs], axis=1))
    w_g3 = np.ascontiguousarray(w_in[:, 416:1952])
    uq = od_w_uq[0].reshape(256, 8, 96)
    uqs = np.zeros_like(uq)
    uqs[:, :, 64:80] = uq[:, :, 80:96]
    uqs[:, :, 80:96] = uq[:, :, 64:80]
    w_uq = np.ascontiguousarray(np.concatenate([uq.reshape(256, 768), uqs.reshape(256, 768)], axis=1))
    w_ukv = np.ascontiguousarray(od_w_ukv[0])
    w_pool = np.ascontiguousarray(od_pool_w[0].transpose(1, 0, 2).reshape(128, 512))
    wout = np.ascontiguousarray(od_w_out[0])
    half = 16
    inv_freq = (np.float32(10000.0) ** (-np.arange(half, dtype=np.float32) / np.float32(half))).astype(np.float32)
    sel = np.zeros((128, 384), np.float32)
    sel[64, 0:64] = 1.0
    sel[0, 128 + 64:128 + 128] = 1.0
    sel[:, 256:384] = 1.0
    lng = np.ascontiguousarray(np.broadcast_to(od_ln_g[0][None, :], (128, D)))
    lnb = np.ascontiguousarray(np.broadcast_to(od_ln_b[0][None, :], (128, D)))
    maps = []
    xTbs = [np.ascontiguousarray(x1[b, :S, :].T) for b in range(2)] if x1 is not None else [None, None]
    posbs = [np.ascontiguousarray(np.broadcast_to(positions[b, :S].reshape(S // 512, 1, 512), (S // 512, 32, 512))
                                  .reshape((S // 512) * 32, 512)).astype(np.int32) for b in range(2)]
    for c in range(NCORES):
        b, s0 = c // 4, (c % 4) * T
        xe = None
        if x1 is not None:
            xe = np.zeros((D, T + 16), np.float32)
            lo, hi = max(0, s0 - 8), min(S, s0 + T + 8)
            xe[:, lo - (s0 - 8):hi - (s0 - 8)] = x1[b, lo:hi, :].T
        smc = np.zeros((128, 80), np.float32)
        smc[:, 0:2] = od_q_norm_g[0].reshape(2, 128).T
        smc[:, 2] = od_kv_norm_g[0]
        smc[64:80, 3] = inv_freq
        smc[80:96, 3] = inv_freq
        smc[64:80, 4] = -1.0
        smc[80:96, 4] = 1.0
        smc[:, 5:9] = od_pool_scale[0].reshape(4, 128).T
        for gi, w in enumerate((2, 4, 8, 16)):
            for j in range(8):
                for side, t in ((0, s0 + j), (1, s0 + T - 8 + j)):
                    lo_ = min(max(t - w // 2, 0), S)
                    hi_ = min(max(t + w - w // 2, 0), S)
                    smc[:, 16 + gi * 16 + side * 8 + j] = np.float32(w) / np.float32(hi_ - lo_)
        maps.append({"xTb": xTbs[b], "xTo": xe, "xtok": (np.ascontiguousarray(x1[b, s0:s0 + T, :]) if x1 is not None else None),
                     "posb": posbs[b],
                     "w_cq": w_cq, "w_kv": w_kv, "w_kr": w_kr, "w_g3": w_g3, "w_uq": w_uq, "w_ukv": w_ukv,
                     "w_pool": w_pool, "wout": wout, "sm_c": smc, "sel": sel, "lng": lng, "lnb": lnb})
    return maps


def build_fused(S=SEQ):
    T = S // 4
    nc = bass.Bass("TRN2", target_bir_lowering=False)

    def inp(name, shape, dt=F32):
        return nc.dram_tensor(name, list(shape), dt, kind="ExternalInput").ap()
    A = {}
    A["xT"] = inp("xT", [D, S + 3])
    A["xT1"] = A["xT"][:, 1:S + 3]
    A["xtok"] = inp("xtok", [S, D])
    A["wg"] = [inp("wg%d" % g, [D, 520]) for g in range(4)]
    A["cvw"] = [inp("cvw%d" % g, [128, 16])[:, :] for g in range(4)]
    A["cvb"] = [inp("cvb%d" % g, [128, 4])[:, :] for g in range(4)]
    A["hp"] = [inp("hp%d" % g, [128, 24])[:, :] for g in range(4)]
    A["cst"] = inp("cst", [128, 512])
    A["msk"] = inp("msk", [128, 1024])
    A["w1"] = inp("w1", [D, 5120])
    A["wout"] = inp("wout", [2048, D])
    A["normg"] = inp("normg", [128, 8])
    A["scw"] = inp("scw", [128, 24])
    A["lng"] = inp("lng", [128, D])
    A["lnb"] = inp("lnb", [128, D])
    A["posb"] = inp("posb", [(S // 512) * 32, 512], I32)
    A["w_cq"] = inp("w_cq", [D, 256])
    A["w_kv"] = inp("w_kv", [D, 128])
    A["w_kr"] = inp("w_kr", [D, 192])
    A["w_g3"] = inp("w_g3", [D, 1536])
    A["w_uq"] = inp("w_uq", [256, 1536])
    A["w_ukv"] = inp("w_ukv", [128, 1024])
    A["w_pool"] = inp("w_pool", [128, 512])
    A["wout_od"] = inp("wout_od", [D, D])
    A["sm_c"] = inp("sm_c", [128, 80])
    A["sel"] = inp("sel", [128, 384])
    A["lng_od"] = inp("lng_od", [128, D])
    A["lnb_od"] = inp("lnb_od", [128, D])
    off = inp("off", [1, 4], I32)
    A["out"] = nc.dram_tensor("out", [T, D], F32, kind="ExternalOutput").ap()
    A["yaT"] = nc.dram_tensor("yaT_s", [D, S], F32).ap()
    A["x1"] = nc.dram_tensor("x1_s", [S, D], F32).ap()
    A["x1T"] = nc.dram_tensor("x1T_s", [(S // 512 + 2) * 1024, 512], F32).ap()
    A["x1HL"] = nc.dram_tensor("x1HL_s", [(S // 512 + 1) * 1024, 8], F32).ap()
    A["x1HR"] = nc.dram_tensor("x1HR_s", [(S // 512 + 1) * 1024, 8], F32).ap()
    A["own_x1T"] = nc.dram_tensor("own_x1T_s", [(T // 512) * 1024, 512], F32).ap()
    A["own_x1"] = nc.dram_tensor("own_x1_s", [T, D], F32).ap()
    A["own_HL"] = nc.dram_tensor("own_HL_s", [(T // 512) * 1024, 8], F32).ap()
    A["own_HR"] = nc.dram_tensor("own_HR_s", [(T // 512) * 1024, 8], F32).ap()
    A["own_pos"] = nc.dram_tensor("own_pos_s", [(T // 512) * 32, 512], I32).ap()

    with ExitStack() as es:
        p = Prog(nc, es)
        regs = [es.enter_context(nc.sync.register("offr%d" % i)) for i in range(3)]
        for i in range(3):
            nc.sync.reg_load(regs[i], off[0:1, i:i + 1])
        NB, NQB = S // 512, T // 512
        b0v = nc.sync.snap(regs[0], min_val=0, max_val=NB - NQB)
        u0v = nc.sync.snap(regs[1], min_val=0, max_val=(NB - NQB) * 64)
        t0v = nc.sync.snap(regs[2], min_val=0, max_val=(S - T) // 8)
        p.prefix = "a_"
        emit_l0a(nc, p, S, A)
        p.prefix = "b_"
        emit_l0b(nc, p, S, A)
        csem = p.dma_sem()
        v = lambda ap, b: ap.rearrange("(a b) t -> a (b t)", b=b)
        p.dma("sp", v(A["own_x1T"], 16), v(A["x1T"], 16)[bass.ds(u0v + 64, NQB * 64), :], sem=csem)
        p.dma("sp", v(A["own_x1"], 8), v(A["x1"], 8)[bass.ds(t0v, T // 8), :], sem=csem)
        p.dma("sp", v(A["own_HL"], 1024), v(A["x1HL"], 1024)[bass.ds(b0v, NQB), :], sem=csem)
        p.dma("sp", v(A["own_HR"], 1024), v(A["x1HR"], 1024)[bass.ds(b0v + 1, NQB), :], sem=csem)
        p.dma("sp", v(A["own_pos"], 32), v(A["posb"], 32)[bass.ds(b0v, NQB), :], sem=csem)
        barrier(p)
        p.prefix = "c_"
        emit_l1(nc, p, S, A)
        p.es = es
        p.finish()
    return nc


def prep_fused(inputs, S=SEQ):
    f = lambda a: np.asarray(a, dtype=np.float32)
    x = f(inputs["x"])[:, :S]
    positions = np.asarray(inputs["positions"], dtype=np.int32)[:, :S]
    T = S // 4
    l0a = prep_l0a(x, f(inputs["ev_w_in"]), f(inputs["ev_conv_w"]), f(inputs["ev_conv_b"]), f(inputs["ev_a_log"]),
                   f(inputs["ev_dt_bias"]), f(inputs["ev_d_skip"]), S=S)
    w_in = f(inputs["ev_w_in"])[0]
    w1 = np.ascontiguousarray(np.concatenate([w_in[:, 0:1024], w_in[:, 3104:7200]], axis=1))
    wout = np.ascontiguousarray(f(inputs["ev_w_out"])[0])
    normg = np.ascontiguousarray(f(inputs["ev_norm_g"])[0].reshape(8, 128).T)
    scw = np.ascontiguousarray(f(inputs["ev_sc_conv_w"])[0].reshape(3, 8, 128).transpose(2, 1, 0).reshape(128, 24))
    lng = np.ascontiguousarray(np.broadcast_to(f(inputs["ev_ln_g"])[0][None, :], (128, D)))
    lnb = np.ascontiguousarray(np.broadcast_to(f(inputs["ev_ln_b"])[0][None, :], (128, D)))
    dummy_x1 = np.zeros((2, 16, D), np.float32)
    l1 = prep_l1(None, positions, f(inputs["od_w_in"]), f(inputs["od_q_norm_g"]), f(inputs["od_w_uq"]), f(inputs["od_kv_norm_g"]),
                 f(inputs["od_w_ukv"]), f(inputs["od_pool_w"]), f(inputs["od_pool_scale"]), f(inputs["od_w_out"]),
                 f(inputs["od_ln_g"]), f(inputs["od_ln_b"]), S=S)
    xtoks = [np.ascontiguousarray(x[b]) for b in range(2)]
    maps = []
    for c in range(NCORES):
        b, q = c // 4, c % 4
        m = {"xT": l0a[4 * b]["xT"], "xtok": xtoks[b], "cst": l0a[0]["cst"], "msk": l0a[0]["msk"],
             "w1": w1, "wout": wout, "normg": normg, "scw": scw, "lng": lng, "lnb": lnb,
             "off": np.array([[q * T // 512, (q * T // 512) * 64, q * T // 8, 0]], np.int32)}
        for g in range(4):
            src = l0a[4 * b + g]
            m["wg%d" % g] = src["wg"]
            m["cvw%d" % g] = src["cvw"]
            m["cvb%d" % g] = src["cvb"]
            m["hp%d" % g] = src["hp"]
        lm = l1[c]
        for k in ("posb", "w_cq", "w_kv", "w_kr", "w_g3", "w_uq", "w_ukv", "w_pool", "sm_c", "sel"):
            m[k] = lm[k]
        m["wout_od"] = lm["wout"]
        m["lng_od"] = lm["lng"]
        m["lnb_od"] = lm["lnb"]
        maps.append(m)
    return maps


def kernel(**inputs):
    T = SEQ // 4
    maps = prep_fused(inputs)
    res = run_bass_kernel_spmd(build_fused(), maps, core_ids=list(range(NCORES)))
    out = np.empty((2, SEQ, D), np.float32)
    for c in range(NCORES):
        out[c // 4, (c % 4) * T:(c % 4 + 1) * T, :] = res.results[c]["out"]
    return out
```

```python
import numpy as np
import concourse.bass as bass
import concourse.mybir as mybir
from concourse.bass_utils import run_bass_kernel_spmd
from contextlib import ExitStack

F32 = mybir.dt.float32
BF16 = mybir.dt.bfloat16
I32 = mybir.dt.int32
AF = mybir.ActivationFunctionType
ALU = mybir.AluOpType
AX = mybir.AxisListType

SAME_ENGINE_SYNC = True

D = 1024
SEQ = 16384
NCORES = 8
ALPHA = 4 ** 0.25
EPS = 1e-5


class Tok:
    __slots__ = ("w", "r", "name")

    def __init__(self, name=""):
        self.w = None
        self.r = {}
        self.name = name


class Prog:
    def __init__(self, nc, es):
        self.nc = nc
        self.es = es
        self.es_top = es
        self.eng = {"pe": nc.tensor, "act": nc.scalar, "dve": nc.vector,
                    "pool": nc.gpsimd, "sp": nc.sync}
        self.sems = {}
        self.cnt = {}
        for k in self.eng:
            self.sems[k] = es.enter_context(nc.semaphore("s_" + k))
            self.cnt[k] = 0
        self.seen = {k: {} for k in self.eng}
        self.ndma = 0
        self.out_dma = []
        self.n_ops = 0
        self.uid = 0

    prefix = ""

    def sb(self, name, shape, dt):
        return self.es.enter_context(self.nc.sbuf_tensor(self.prefix + name, list(shape), dt))

    def ps(self, name, shape, dt=F32):
        return self.es.enter_context(self.nc.psum_tensor(self.prefix + name, list(shape), dt))

    def dma_sem(self):
        k = "d%d" % self.ndma
        self.ndma += 1
        self.sems[k] = self.es_top.enter_context(self.nc.semaphore("s_" + k))
        self.cnt[k] = 0
        return k

    def _wait(self, e, deps):
        for (k, v) in deps:
            if k == e:
                if not SAME_ENGINE_SYNC or e == "pe" or e == "sp":
                    continue
            if self.seen[e].get(k, 0) >= v:
                continue
            self.eng[e].wait_ge(self.sems[k], v)
            self.seen[e][k] = v

    def _deps(self, r, w):
        m = {}
        for t in r:
            if t.w is not None:
                k, v = t.w
                if m.get(k, 0) < v:
                    m[k] = v
        for t in w:
            if t.w is not None:
                k, v = t.w
                if m.get(k, 0) < v:
                    m[k] = v
            for k, v in t.r.items():
                if m.get(k, 0) < v:
                    m[k] = v
        return list(m.items())

    def op(self, e, fn, r=(), w=(), multi=False):
        deps = self._deps(r, w)
        att = None
        if e != "pe" and not multi:
            need = [(k, v) for (k, v) in deps
                    if not (k == e and not SAME_ENGINE_SYNC) and self.seen[e].get(k, 0) < v]
            if need:
                att = need[-1]
                self._wait(e, need[:-1])
        else:
            self._wait(e, deps)
        ins = fn(self.eng[e])
        if att is not None:
            ins._wait_ge(self.sems[att[0]], att[1])
            self.seen[e][att[0]] = att[1]
        self.cnt[e] += 1
        v = self.cnt[e]
        ins.then_inc(self.sems[e], 1)
        for t in r:
            if t.r.get(e, 0) < v:
                t.r[e] = v
        for t in w:
            t.w = (e, v)
            t.r = {}
        self.n_ops += 1
        return ins

    def dma(self, q, out, in_, r=(), w=(), sem=None, is_out=False, nowait=False, **kw):
        if not nowait:
            self._wait(q, self._deps(r, w))
        ins = self.eng[q].dma_start(out=out, in_=in_, **kw)
        self.cnt[sem] += 16
        v = self.cnt[sem]
        ins.then_inc(self.sems[sem], 16)
        for t in r:
            if t.r.get(sem, 0) < v:
                t.r[sem] = v
        for t in w:
            t.w = (sem, v)
            t.r = {}
        if is_out:
            self.out_dma.append((sem, v))
        return ins

    def finish(self, e="sp"):
        m = {}
        for k, v in self.out_dma:
            if m.get(k, 0) < v:
                m[k] = v
        for k, v in m.items():
            self.eng[e].wait_ge(self.sems[k], v)


class Ring:
    def __init__(self, p, name, shape, dt, n, space="sb", dma=False):
        self.bufs = []
        for i in range(n):
            t = p.sb("%s%d" % (name, i), shape, dt) if space == "sb" else p.ps("%s%d" % (name, i), shape, dt)
            self.bufs.append((t, Tok(name + str(i)), p.dma_sem() if dma else None))
        self.i = 0

    def next(self):
        b = self.bufs[self.i % len(self.bufs)]
        self.i += 1
        return b


def load_cast_weight(p, src, dst, stage, K, C, engines=("act", "dve"), cw=1024, tok=None):
    n = 0
    for k in range(K):
        for c0 in range(0, C, cw):
            c1 = min(C, c0 + cw)
            st, stok, ssem = stage.next()
            p.dma("sp", st[:, 0:c1 - c0], src[k * 128:(k + 1) * 128, c0:c1], w=[stok], sem=ssem)
            e = engines[n % len(engines)]
            n += 1
            if e == "act":
                p.op(e, lambda en: en.copy(out=dst[:, k, c0:c1], in_=st[:, 0:c1 - c0]), r=[stok], w=[tok])
            else:
                p.op(e, lambda en: en.tensor_copy(out=dst[:, k, c0:c1], in_=st[:, 0:c1 - c0]), r=[stok], w=[tok])


PI = float(np.pi)
TWO_PI = float(2 * np.pi)
C1 = 6.28125
C2 = float(2 * np.pi - 6.28125)
ATT_SCALE = float(96 ** -0.5)
NEG = -30000.0
L0B_TB = 256


def barrier(p):
    for e in p.eng:
        for k, v in p.cnt.items():
            if k != e and v > 0 and p.seen[e].get(k, 0) < v:
                p.eng[e].wait_ge(p.sems[k], v)
                p.seen[e][k] = v


def layer_norm_tail(p, o, t_o, xk, t_xk, rr, r2, junk, st_r, lng_s, lnb_s, t_c, out_ap, post=None, is_out=True):
    r, t_r, _ = rr.next()
    p.op("dve", lambda e: e.scalar_tensor_tensor(out=r[:], in0=xk[:], scalar=float(ALPHA), in1=o[:], op0=ALU.mult, op1=ALU.add),
         r=[t_xk, t_o], w=[t_r])
    st, t_st, _ = st_r.next()
    jk, t_jk, _ = junk.next()
    p.op("act", lambda e: e.activation(out=jk[:], in_=r[:], func=AF.Identity, accum_out=st[:, 0:1]), r=[t_r], w=[t_jk, t_st], multi=True)
    p.op("act", lambda e: e.activation(out=jk[:], in_=r[:], func=AF.Square, accum_out=st[:, 1:2]), r=[t_r], w=[t_jk, t_st], multi=True)
    p.op("dve", lambda e: e.tensor_scalar(out=st[:, 2:3], in0=st[:, 0:1], scalar1=1.0 / D, scalar2=None, op0=ALU.mult), r=[t_st], w=[t_st])
    p.op("dve", lambda e: e.tensor_tensor(out=st[:, 3:4], in0=st[:, 2:3], in1=st[:, 2:3], op=ALU.mult), r=[t_st], w=[t_st])
    p.op("dve", lambda e: e.scalar_tensor_tensor(out=st[:, 4:5], in0=st[:, 1:2], scalar=1.0 / D, in1=st[:, 3:4], op0=ALU.mult, op1=ALU.subtract),
         r=[t_st], w=[t_st])
    p.op("dve", lambda e: e.tensor_scalar(out=st[:, 4:5], in0=st[:, 4:5], scalar1=float(EPS), scalar2=None, op0=ALU.add), r=[t_st], w=[t_st])
    p.op("act", lambda e: e.activation(out=st[:, 5:6], in_=st[:, 4:5], func=AF.Ln), r=[t_st], w=[t_st])
    p.op("act", lambda e: e.activation(out=st[:, 6:7], in_=st[:, 5:6], func=AF.Exp, scale=-0.5), r=[t_st], w=[t_st])
    q, t_q, osem = r2.next()
    p.op("dve", lambda e: e.tensor_scalar(out=q[:], in0=r[:], scalar1=st[:, 2:3], scalar2=st[:, 6:7], op0=ALU.subtract, op1=ALU.mult),
         r=[t_r, t_st], w=[t_q])
    p.op("pool", lambda e: e.tensor_tensor(out=q[:], in0=q[:], in1=lng_s[:], op=ALU.mult), r=[t_c], w=[t_q])
    p.op("pool", lambda e: e.tensor_tensor(out=q[:], in0=q[:], in1=lnb_s[:], op=ALU.add), r=[t_c], w=[t_q])
    if post is not None:
        post(q, t_q)
    p.dma("act", out_ap, q[:], r=[t_q], w=[], sem=osem, is_out=is_out)


def emit_l0b(nc, p, T, A):
    TB = L0B_TB
    NB = T // TB
    xT, xtok, yaT, w1, wout = A["xT1"], A["xtok"], A["yaT"], A["w1"], A["wout"]
    normg, scw, lng, lnb = A["normg"], A["scw"], A["lng"], A["lnb"]
    out, x1T, cst = A["x1"], A["x1T"], A["cst"]
    x1HL, x1HR = A["x1HL"], A["x1HR"]

    def halo_v(tab, bnd):
        return tab[bnd * 1024:(bnd + 1) * 1024, :].rearrange("(k p) t -> p k t", p=128)
    yaT_v = yaT.rearrange("(k p) t -> p k t", p=128)
    NB5 = T // 512

    def x1T_blk(blk, c0, n):
        return x1T[blk * 1024:(blk + 1) * 1024, c0:c0 + n].rearrange("(k p) t -> p k t", p=128)

    with ExitStack() as es:
        p.es = es
        w1b = p.sb("w1b", [128, 8, 5120], BF16)
        woutb = p.sb("woutb", [128, 16, D], BF16)
        t_w1b, t_woutb = Tok(), Tok()
        stage = Ring(p, "wst", [128, 1024], F32, 1, dma=True)
        normg_s = p.sb("normg_s", [128, 8], F32)
        scw_s = p.sb("scw_s", [128, 24], F32)
        lng_s = p.sb("lng_s", [128, D], F32)
        lnb_s = p.sb("lnb_s", [128, D], F32)
        ones_f = p.sb("ones_f", [128, 128], F32)
        t_c = Tok()
        dc = p.dma_sem()
        p.dma("sp", normg_s[:], normg[:, :], w=[t_c], sem=dc)
        p.dma("sp", scw_s[:], scw[:, :], w=[t_c], sem=dc)
        p.dma("sp", lng_s[:], lng[:, :], w=[t_c], sem=dc)
        p.dma("sp", lnb_s[:], lnb[:, :], w=[t_c], sem=dc)
        t_ones = Tok()
        p.op("dve", lambda e: e.memset(ones_f[:], 1.0), w=[t_ones])
        idf = p.sb("idf", [128, 128], F32)
        p.dma("sp", idf[:], cst[:, 256:384], w=[t_c], sem=dc)
        zt = p.sb("zt", [128, 8, 8], F32)
        t_zt = Tok()
        p.op("dve", lambda e: e.memset(zt[:], 0.0), w=[t_zt])
        zsem = p.dma_sem()
        p.dma("sp", halo_v(x1HL, 0), zt[:], r=[t_zt], sem=zsem)
        p.dma("sp", halo_v(x1HR, NB5), zt[:], r=[t_zt], sem=zsem)
        xtt_r = Ring(p, "xtt", [128, 8, 128], F32, 1, dma=True)
        load_cast_weight(p, w1, w1b, stage, 8, 5120, tok=t_w1b)
        load_cast_weight(p, wout, woutb, stage, 16, D, tok=t_woutb)

        xst = Ring(p, "xst", [128, TB + 2], F32, 3, dma=True)
        xb_r = Ring(p, "xb", [128, 8, TB + 2], BF16, 2)
        yst = Ring(p, "yst", [128, 8, TB], F32, 2, dma=True)
        xtk = Ring(p, "xtk", [128, D], F32, 2, dma=True)
        pp = Ring(p, "pp", [128, 512], F32, 3, space="ps")
        pss = Ring(p, "pss", [128, 512], F32, 1, space="ps")
        po = Ring(p, "po", [128, D], F32, 2, space="ps")
        deferred = []

        def do_post(q, t_q, tok0):
            xtt, t_xtt, xtt_sem = xtt_r.next()
            for hf in range(2):
                tp, t_tp, _ = pp.next()
                for kq in range(4):
                    kk = hf * 4 + kq
                    p.op("pe", lambda e: e.transpose(tp[:, kq * 128:(kq + 1) * 128], q[:, kk * 128:(kk + 1) * 128], idf[:]),
                         r=[t_q, t_c], w=[t_tp])
                p.op("act", lambda e: e.copy(out=xtt[:, hf * 4:(hf + 1) * 4, :], in_=tp[:].rearrange("p (k t) -> p k t", k=4)),
                     r=[t_tp], w=[t_xtt])
            p.dma("act", x1T_blk(tok0 // 512 + 1, tok0 % 512, 128), xtt[:], r=[t_xtt], sem=xtt_sem)
            if tok0 % 512 == 0:
                p.dma("act", halo_v(x1HR, tok0 // 512), xtt[:, :, 0:8], r=[t_xtt], sem=xtt_sem)
            if (tok0 + 128) % 512 == 0:
                p.dma("act", halo_v(x1HL, (tok0 + 128) // 512), xtt[:, :, 120:128], r=[t_xtt], sem=xtt_sem)

        def flush_posts():
            while deferred:
                do_post(*deferred.pop(0))
        g_all = p.sb("g_all", [128, 8, TB], F32)
        t_g = [Tok() for _ in range(8)]
        cat = p.sb("cat", [128, 16, TB], BF16)
        t_cat = [Tok() for _ in range(16)]
        tmp = Ring(p, "tmp", [128, TB + 2], F32, 8)
        rstd = p.sb("rstd", [128, TB], F32)
        t_rstd = Tok()
        halo = Ring(p, "halo", [128, 4], F32, 2)
        rr = Ring(p, "rr", [128, D], F32, 1)
        r2 = Ring(p, "r2", [128, D], F32, 2, dma=True)
        junk = Ring(p, "junk", [128, D], BF16, 1)
        st_r = Ring(p, "stat", [128, 8], F32, 4)
        def load_blk(bj):
            tj = bj * TB
            xb_, t_xb_, _ = xb_r.next()
            for k in range(8):
                st, stok, ssem = xst.next()
                p.dma("sp", st[:], xT[k * 128:(k + 1) * 128, tj:tj + TB + 2], w=[stok], sem=ssem)
                p.op("pool", lambda e: e.tensor_copy(out=xb_[:, k, :], in_=st[:]), r=[stok], w=[t_xb_])
            ya_, t_ya_, ya_sem_ = yst.next()
            p.dma("sp", ya_[:], yaT_v[:, :, tj:tj + TB], w=[t_ya_], sem=ya_sem_)
            return xb_, t_xb_, ya_, t_ya_
        nxt = load_blk(0)
        for bi in range(NB):
            t0 = bi * TB
            xb, t_xb, ya, t_ya = nxt
            if bi + 1 < NB:
                nxt = load_blk(bi + 1)

            ss, t_ss, _ = pss.next()
            for j in range(8):
                z, t_z, _ = pp.next()
                for k in range(8):
                    p.op("pe", lambda e: e.matmul(z[:, 0:TB], lhsT=w1b[:, k, j * 128:(j + 1) * 128],
                                                  rhs=xb[:, k, 1:TB + 1], start=(k == 0), stop=(k == 7)),
                         r=[t_w1b, t_xb], w=[t_z])
                sz, t_sz, _ = tmp.next()
                p.op("act", lambda e: e.activation(out=sz[:, 0:TB], in_=z[:, 0:TB], func=AF.Silu), r=[t_z], w=[t_sz])
                p.op("dve", lambda e: e.tensor_tensor(out=g_all[:, j, :], in0=sz[:, 0:TB], in1=ya[:, j, :], op=ALU.mult),
                     r=[t_sz, t_ya], w=[t_g[j]])
                sq, t_sq, _ = tmp.next()
                p.op("act", lambda e: e.activation(out=sq[:, 0:TB], in_=g_all[:, j, :], func=AF.Square), r=[t_g[j]], w=[t_sq])
                p.op("pe", lambda e: e.matmul(ss[:, 0:TB], lhsT=ones_f[:], rhs=sq[:, 0:TB], start=(j == 0), stop=(j == 7)),
                     r=[t_ones, t_sq], w=[t_ss])
            lnv, t_lnv, _ = tmp.next()
            p.op("dve", lambda e: e.tensor_scalar(out=lnv[:, 0:TB], in0=ss[:, 0:TB], scalar1=1.0 / 1024, scalar2=EPS,
                                                  op0=ALU.mult, op1=ALU.add), r=[t_ss], w=[t_lnv])
            p.op("act", lambda e: e.activation(out=lnv[:, 0:TB], in_=lnv[:, 0:TB], func=AF.Ln), r=[t_lnv], w=[t_lnv])
            p.op("act", lambda e: e.activation(out=rstd[:], in_=lnv[:, 0:TB], func=AF.Exp, scale=-0.5), r=[t_lnv], w=[t_rstd])
            for j in range(8):
                p.op("dve", lambda e: e.scalar_tensor_tensor(out=cat[:, j, :], in0=g_all[:, j, :], scalar=normg_s[:, j:j + 1],
                                                             in1=rstd[:], op0=ALU.mult, op1=ALU.mult),
                     r=[t_g[j], t_rstd, t_c], w=[t_cat[j]])

            flush_posts()
            for j in range(8):
                def proj(grp, lo, n, dst, t_dst, first=True, last=True):
                    for k in range(8):
                        p.op("pe", lambda e: e.matmul(dst, lhsT=w1b[:, k, grp * 1024 + j * 128:grp * 1024 + (j + 1) * 128],
                                                      rhs=xb[:, k, lo:lo + n], start=(k == 0), stop=(k == 7)),
                             r=[t_w1b, t_xb], w=[t_dst])
                cg, t_cg, _ = pp.next()
                proj(2, 0, TB + 2, cg[:, 0:TB + 2], t_cg)
                hh, t_hh, _ = pp.next()
                proj(3, 0, TB + 2, hh[:, 0:TB + 2], t_hh)
                cgs, t_cgs, _ = tmp.next()
                p.op("act", lambda e: e.copy(out=cgs[:, 0:TB + 2], in_=cg[:, 0:TB + 2]), r=[t_cg], w=[t_cgs])
                u, t_u, _ = tmp.next()
                p.op("dve", lambda e: e.tensor_tensor(out=u[:, 0:TB + 2], in0=cgs[:, 0:TB + 2], in1=hh[:, 0:TB + 2], op=ALU.mult),
                     r=[t_cgs, t_hh], w=[t_u])
                c, t_cc, _ = tmp.next()
                p.op("dve", lambda e: e.tensor_scalar(out=c[:, 0:TB], in0=u[:, 0:TB], scalar1=scw_s[:, j * 3:j * 3 + 1], scalar2=None,
                                                      op0=ALU.mult), r=[t_u, t_c], w=[t_cc])
                p.op("dve", lambda e: e.scalar_tensor_tensor(out=c[:, 0:TB], in0=u[:, 1:TB + 1], scalar=scw_s[:, j * 3 + 1:j * 3 + 2],
                                                             in1=c[:, 0:TB], op0=ALU.mult, op1=ALU.add), r=[t_u, t_c], w=[t_cc])
                p.op("dve", lambda e: e.scalar_tensor_tensor(out=c[:, 0:TB], in0=u[:, 2:TB + 2], scalar=scw_s[:, j * 3 + 2:j * 3 + 3],
                                                             in1=c[:, 0:TB], op0=ALU.mult, op1=ALU.add), r=[t_u, t_c], w=[t_cc])
                bg, t_bg, _ = pp.next()
                proj(1, 1, TB, bg[:, 0:TB], t_bg)
                gt, t_gt, _ = pp.next()
                proj(4, 1, TB, gt[:, 0:TB], t_gt)
                sg, t_sg, _ = tmp.next()
                p.op("act", lambda e: e.activation(out=sg[:, 0:TB], in_=gt[:, 0:TB], func=AF.Silu), r=[t_gt], w=[t_sg])
                p.op("dve", lambda e: e.tensor_tensor(out=c[:, 0:TB], in0=c[:, 0:TB], in1=bg[:, 0:TB], op=ALU.mult),
                     r=[t_bg], w=[t_cc])
                p.op("dve", lambda e: e.tensor_tensor(out=cat[:, 8 + j, :], in0=c[:, 0:TB], in1=sg[:, 0:TB], op=ALU.mult),
                     r=[t_cc, t_sg], w=[t_cat[8 + j]])

            outs = []
            for tt in range(TB // 128):
                xk, t_xk, xk_sem = xtk.next()
                p.dma("sp", xk[:], xtok[t0 + tt * 128:t0 + (tt + 1) * 128, :], w=[t_xk], sem=xk_sem)
                o, t_o, _ = po.next()
                for half in range(2):
                    for kc in range(16):
                        p.op("pe", lambda e: e.matmul(o[:, half * 512:(half + 1) * 512], lhsT=cat[:, kc, tt * 128:(tt + 1) * 128],
                                                      rhs=woutb[:, kc, half * 512:(half + 1) * 512], start=(kc == 0), stop=(kc == 15)),
                             r=[t_cat[kc], t_woutb], w=[t_o])
                outs.append((o, t_o, xk, t_xk))
            for tt in range(TB // 128):
                o, t_o, xk, t_xk = outs[tt]
                tok0 = t0 + tt * 128
                layer_norm_tail(p, o, t_o, xk, t_xk, rr, r2, junk, st_r, lng_s, lnb_s, t_c,
                                out[t0 + tt * 128:t0 + (tt + 1) * 128, :],
                                post=(lambda q, t_q, tok0=tok0: deferred.append((q, t_q, tok0))), is_out=False)
        flush_posts()
        barrier(p)


def emit_l0a(nc, p, S, A):
    BLK = 512
    NBLK = S // BLK
    xT, wg_all, cvw_all, cvb_all, hp_all, cst, msk, yaT = (A["xT"], A["wg"], A["cvw"], A["cvb"], A["hp"], A["cst"], A["msk"], A["yaT"])

    with ExitStack() as es:
        p.es = es
        wgb_l = [p.sb("wgb%d" % g, [128, 8, 520], BF16) for g in range(4)]
        t_wgb_l = [Tok() for g in range(4)]
        stage = Ring(p, "wst", [128, 520], F32, 2, dma=True)
        cvw_l = [p.sb("cvw_s%d" % g, [128, 16], F32) for g in range(4)]
        cvb_l = [p.sb("cvb_s%d" % g, [128, 4], F32) for g in range(4)]
        hp_l = [p.sb("hp_s%d" % g, [128, 24], F32) for g in range(4)]
        cst_s = p.sb("cst_s", [128, 512], F32)
        msk_s = p.sb("msk_s", [128, 1024], F32)
        mskb = p.sb("mskb", [128, 1024], BF16)
        identb = p.sb("identb", [128, 128], BF16)
        a_l = [p.sb("a_s%d" % g, [128, 8], F32) for g in range(4)]
        bias32_l = [p.sb("bias32_%d" % g, [128, 2, 4, 4], F32) for g in range(4)]
        a32_l = [p.sb("a32_%d" % g, [128, 2, 4, 4], F32) for g in range(4)]
        dsum_l = [p.sb("dsum%d" % g, [128, 4], F32) for g in range(4)]
        t_c = Tok()
        dc = p.dma_sem()
        for dst, src in ((cst_s, cst), (msk_s, msk)):
            p.dma("sp", dst[:], src[:, :], w=[t_c], sem=dc)
        U = cst_s[:, 0:128]
        UT = cst_s[:, 128:256]
        IDF = cst_s[:, 256:384]
        ONES = cst_s[:, 384:512]
        p.op("dve", lambda e: e.tensor_copy(out=mskb[:], in_=msk_s[:]), r=[t_c], w=[t_c])
        p.op("dve", lambda e: e.tensor_copy(out=identb[:], in_=IDF), r=[t_c], w=[t_c])

        xst = Ring(p, "xst", [128, BLK + 3], F32, 3, dma=True)
        xb_r = Ring(p, "xb", [128, 8, BLK + 3], BF16, 2)
        pG = Ring(p, "pG", [128, 512], F32, 6, space="ps")
        pH = Ring(p, "pHb", [128, 512], F32, 1, space="ps")
        pP = Ring(p, "pPp", [128, 512], F32, 1, space="ps")
        pre_r = Ring(p, "pre", [128, BLK + 3], F32, 3)
        cv_r = Ring(p, "cv", [128, BLK], F32, 2)
        xsf_r = Ring(p, "xsf", [128, 3, BLK], F32, 2)
        btb_r = Ring(p, "btb", [128, BLK], BF16, 2)
        ctb_r = Ring(p, "ctb", [128, BLK], BF16, 2)
        hs_r = Ring(p, "hs", [128, 16], F32, 2)
        dtv_r = Ring(p, "dtv", [128, 6, 16], F32, 2)
        sm_r = Ring(p, "sm", [128, 8, 4], F32, 3)
        W_r = Ring(p, "W", [128, 4, 128], F32, 2)
        E_r = Ring(p, "E", [128, 4, 128], F32, 2)
        M_r = Ring(p, "M", [128, 4, 128], BF16, 2)
        btk_r = Ring(p, "btk", [128, 128], BF16, 2)
        xd_r = Ring(p, "xd", [128, 256], BF16, 2)
        xdw_r = Ring(p, "xdw", [128, 256], BF16, 2)
        y_r = Ring(p, "y", [128, 256], F32, 3, dma=True)
        yT_r = Ring(p, "yT", [128, 256], F32, 3, dma=True)
        yt_r = Ring(p, "yt", [128, 256], F32, 3)
        yl_r = Ring(p, "yl", [128, 256], F32, 2, dma=True)
        H_l = [p.sb("H%d" % g, [128, 256], F32) for g in range(4)]
        Hb_l = [p.sb("Hb%d" % g, [128, 256], BF16) for g in range(4)]
        t_H_l = [Tok() for g in range(4)]
        t_Hb_l = [Tok() for g in range(4)]

        yds_r = Ring(p, "yds", [128, 256], F32, 2)

        def front(k, g, blk, c, dtv, t_dtv, xsf, t_xsf, btb, t_btb, ctb, t_ctb, Tri, t_ya):
            gc = blk * 4 + c
            cs_ = slice(c * 128, (c + 1) * 128)
            dA = dtv[:, 5, 4 * c:4 * c + 4]
            dtc = dtv[:, 4, 4 * c:4 * c + 4]
            W, t_W, _ = W_r.next()
            p.op("dve", lambda e: e.tensor_tensor(out=W[:], in0=Tri.unsqueeze(1).to_broadcast([128, 4, 128]),
                                                  in1=dA.unsqueeze(2).to_broadcast([128, 4, 128]), op=ALU.mult),
                 r=[t_dtv, t_c], w=[t_W])
            Eb, t_Eb, _ = pG.next()
            p.op("pe", lambda e: e.matmul(Eb[:], lhsT=ONES, rhs=W[:].rearrange("p r l -> p (r l)"), start=True, stop=False),
                 r=[t_W, t_c], w=[t_Eb])
            p.op("pe", lambda e: e.matmul(Eb[:], lhsT=identb[:], rhs=mskb[:, 512 * k:512 * (k + 1)], start=False, stop=True),
                 r=[t_c], w=[t_Eb])
            T_, t_T, _ = pG.next()
            Sm, t_Sm = T_[:, 384:512], t_T
            p.op("pe", lambda e: e.matmul(Sm[:, 0:4], lhsT=Tri, rhs=dA, start=True, stop=True), r=[t_dtv, t_c], w=[t_Sm])
            p.op("pe", lambda e: e.matmul(Sm[:, 4:8], lhsT=ONES, rhs=dA, start=True, stop=True), r=[t_dtv, t_c], w=[t_Sm])
            for m in range(3):
                p.op("pe", lambda e: e.transpose(T_[:, m * 128:(m + 1) * 128], xsf[:, m, cs_], IDF), r=[t_xsf, t_c], w=[t_T])
            Cb, t_Cb, _ = pG.next()
            p.op("pe", lambda e: e.matmul(Cb[:, 0:128], lhsT=btb[:, cs_], rhs=ctb[:, cs_], start=True, stop=True),
                 r=[t_btb, t_ctb], w=[t_Cb])
            sm, t_sm, _ = sm_r.next()
            CS, TOT, NCS, ECS, DTE, ETOT, DTW, D_ = [sm[:, i, :] for i in range(8)]
            p.op("act", lambda e: e.copy(out=sm[:, 0:2, :], in_=Sm[:, 0:8].rearrange("p (a r) -> p a r", a=2)), r=[t_Sm], w=[t_sm])
            p.op("dve", lambda e: e.tensor_scalar(out=NCS, in0=CS, scalar1=-1.0, scalar2=None, op0=ALU.mult), r=[t_sm], w=[t_sm])
            E, t_E, _ = E_r.next()
            for r_ in range(4):
                p.op("act", lambda e: e.activation(out=E[:, r_, :], in_=Eb[:, r_ * 128:(r_ + 1) * 128], func=AF.Exp,
                                                   bias=sm[:, 2, r_:r_ + 1]), r=[t_Eb, t_sm], w=[t_E])
            btk, t_btk, _ = btk_r.next()
            p.op("act", lambda e: e.copy(out=btk[:], in_=T_[:, 256:384]), r=[t_T], w=[t_btk])
            M, t_M, _ = M_r.next()
            p.op("dve", lambda e: e.tensor_tensor(out=M[:], in0=E[:], in1=Cb[:, 0:128].unsqueeze(1).to_broadcast([128, 4, 128]),
                                                  op=ALU.mult), r=[t_E, t_Cb], w=[t_M])
            p.op("act", lambda e: e.activation(out=ECS, in_=CS, func=AF.Exp), r=[t_sm], w=[t_sm])
            p.op("dve", lambda e: e.tensor_tensor(out=D_, in0=TOT, in1=CS, op=ALU.subtract), r=[t_sm], w=[t_sm])
            p.op("act", lambda e: e.activation(out=DTE, in_=D_, func=AF.Exp), r=[t_sm], w=[t_sm])
            p.op("act", lambda e: e.activation(out=ETOT, in_=TOT, func=AF.Exp), r=[t_sm], w=[t_sm])
            p.op("dve", lambda e: e.tensor_tensor(out=DTW, in0=DTE, in1=dtc, op=ALU.mult), r=[t_sm, t_dtv], w=[t_sm])
            xd, t_xd, _ = xd_r.next()
            xdw, t_xdw, _ = xdw_r.next()
            xs_tok = T_[:, 0:256].rearrange("p (r q) -> p r q", r=4)
            p.op("dve", lambda e: e.tensor_tensor(out=xd[:].rearrange("p (r q) -> p r q", r=4), in0=xs_tok,
                                                  in1=dtc.unsqueeze(2).to_broadcast([128, 4, 64]), op=ALU.mult),
                 r=[t_T, t_dtv], w=[t_xd])
            p.op("dve", lambda e: e.tensor_tensor(out=xdw[:].rearrange("p (r q) -> p r q", r=4), in0=xs_tok,
                                                  in1=DTW.unsqueeze(2).to_broadcast([128, 4, 64]), op=ALU.mult),
                 r=[t_T, t_sm], w=[t_xdw])
            yds, t_yds = None, None
            if k == 0:
                yds, t_yds, _ = yds_r.next()
                p.op("dve", lambda e: e.tensor_tensor(out=yds[:].rearrange("p (r q) -> p r q", r=4), in0=xs_tok,
                                                      in1=dsum_l[g][:].unsqueeze(2).to_broadcast([128, 4, 64]), op=ALU.mult),
                     r=[t_T, t_c], w=[t_yds])
            return dict(k=k, g=g, gc=gc, cs_=cs_, ctb=ctb, t_ctb=t_ctb, M=M, t_M=t_M, xd=xd, t_xd=t_xd, xdw=xdw, t_xdw=t_xdw,
                        btk=btk, t_btk=t_btk, ECS=ECS, ETOT=ETOT, t_sm=t_sm, yds=yds, t_yds=t_yds, t_ya=t_ya)

        def back(s_):
            k, g, gc, cs_ = s_["k"], s_["g"], s_["gc"], s_["cs_"]
            ctb, t_ctb, M, t_M, xd, t_xd, xdw, t_xdw = (s_["ctb"], s_["t_ctb"], s_["M"], s_["t_M"], s_["xd"], s_["t_xd"],
                                                        s_["xdw"], s_["t_xdw"])
            btk, t_btk, ECS, ETOT, t_sm, yds, t_yds, t_ya = (s_["btk"], s_["t_btk"], s_["ECS"], s_["ETOT"], s_["t_sm"],
                                                             s_["yds"], s_["t_yds"], s_["t_ya"])
            H, Hb, t_H, t_Hb = H_l[g], Hb_l[g], t_H_l[g], t_Hb_l[g]
            Y, t_Y, _ = pG.next()
            for r_ in range(4):
                p.op("pe", lambda e: e.matmul(Y[:, r_ * 64:(r_ + 1) * 64], lhsT=M[:, r_, :], rhs=xd[:, r_ * 64:(r_ + 1) * 64],
                                              start=True, stop=True), r=[t_M, t_xd], w=[t_Y])
            p.op("pe", lambda e: e.matmul(Y[:, 256:512], lhsT=ctb[:, cs_], rhs=Hb[:], start=True, stop=True),
                 r=[t_ctb, t_Hb], w=[t_Y])
            ST, t_ST, _ = pG.next()
            p.op("pe", lambda e: e.matmul(ST[:, 0:256], lhsT=btk[:], rhs=xdw[:], start=True, stop=True),
                 r=[t_btk, t_xdw], w=[t_ST])
            yt, t_yt, _ = yt_r.next()
            p.op("dve", lambda e: e.tensor_tensor(out=yt[:].rearrange("p (r q) -> p r q", r=4),
                                                  in0=Y[:, 256:512].rearrange("p (r q) -> p r q", r=4),
                                                  in1=ECS.unsqueeze(2).to_broadcast([128, 4, 64]), op=ALU.mult),
                 r=[t_Y, t_sm], w=[t_yt])
            yo, t_yo, yo_sem = y_r.next()
            p.op("dve", lambda e: e.tensor_tensor(out=yo[:], in0=yt[:], in1=Y[:, 0:256], op=ALU.add), r=[t_yt, t_Y], w=[t_yo])
            if k == 0:
                p.op("dve", lambda e: e.tensor_tensor(out=yo[:], in0=yo[:], in1=yds[:], op=ALU.add), r=[t_yds], w=[t_yo])
            p.op("dve", lambda e: e.tensor_tensor(out=H[:].rearrange("p (r q) -> p r q", r=4),
                                                  in0=H[:].rearrange("p (r q) -> p r q", r=4),
                                                  in1=ETOT.unsqueeze(2).to_broadcast([128, 4, 64]), op=ALU.mult),
                 r=[t_sm], w=[t_H])
            p.op("dve", lambda e: e.tensor_tensor(out=H[:], in0=H[:], in1=ST[:, 0:256], op=ALU.add), r=[t_ST], w=[t_H])
            p.op("act", lambda e: e.copy(out=Hb[:], in_=H[:]), r=[t_H], w=[t_Hb])
            ydst = yaT[g * 256:(g + 1) * 256, gc * 128:(gc + 1) * 128].rearrange("(j q) t -> q j t", q=128)
            T2, t_T2, _ = pG.next()
            for j in range(2):
                p.op("pe", lambda e: e.transpose(T2[:, j * 128:(j + 1) * 128], yo[:, j * 128:(j + 1) * 128], IDF), r=[t_yo, t_c], w=[t_T2])
            yoT, t_yoT, yoT_sem = yT_r.next()
            if k == 0:
                p.op("act", lambda e: e.copy(out=yoT[:], in_=T2[:, 0:256]), r=[t_T2], w=[t_yoT])
            else:
                yl, t_yl, yl_sem = yl_r.next()
                p.dma("act", yl[:].rearrange("q (j t) -> q j t", j=2), ydst, r=[t_ya[gc]], w=[t_yl], sem=yl_sem)
                p.op("dve", lambda e: e.tensor_tensor(out=yoT[:], in0=T2[:, 0:256], in1=yl[:], op=ALU.add), r=[t_T2, t_yl], w=[t_yoT])
            p.dma("act", ydst, yoT[:].rearrange("q (j t) -> q j t", j=2), r=[t_yoT], w=[t_ya[gc]], sem=yoT_sem)

        for g in range(4):
            cvw_s, cvb_s, hp_s, a_s, bias32, a32, dsum = cvw_l[g], cvb_l[g], hp_l[g], a_l[g], bias32_l[g], a32_l[g], dsum_l[g]
            for dst, src in ((cvw_s, cvw_all[g]), (cvb_s, cvb_all[g]), (hp_s, hp_all[g])):
                p.dma("sp", dst[:], src, w=[t_c], sem=dc)
            p.op("act", lambda e: e.activation(out=a_s[:], in_=hp_s[:, 0:8], func=AF.Exp), r=[t_c], w=[t_c])
            p.op("dve", lambda e: e.tensor_scalar(out=a_s[:], in0=a_s[:], scalar1=-1.0, scalar2=None, op0=ALU.mult), r=[t_c], w=[t_c])
            for k in range(2):
                for c in range(4):
                    p.op("dve", lambda e: e.tensor_copy(out=bias32[:, k, c, :], in_=hp_s[:, 8 + 4 * k:12 + 4 * k]), r=[t_c], w=[t_c])
                    p.op("dve", lambda e: e.tensor_copy(out=a32[:, k, c, :], in_=a_s[:, 4 * k:4 * k + 4]), r=[t_c], w=[t_c])
            p.op("dve", lambda e: e.tensor_tensor(out=dsum[:], in0=hp_s[:, 16:20], in1=hp_s[:, 20:24], op=ALU.add), r=[t_c], w=[t_c])
            load_cast_weight(p, wg_all[g], wgb_l[g], stage, 8, 520, cw=520, engines=("dve", "act"), tok=t_wgb_l[g])
        t_ya_l = [[Tok() for _ in range(S // 128)] for g in range(4)]
        for k in range(2):
            pend = None
            for g in range(4):
                p.op("dve", lambda e: e.memset(H_l[g][:], 0.0), w=[t_H_l[g]])
                p.op("dve", lambda e: e.memset(Hb_l[g][:], 0.0), w=[t_Hb_l[g]])
            Tri = U if k == 0 else UT
            blocks = range(NBLK) if k == 0 else range(NBLK - 1, -1, -1)
            for blk in blocks:
                e0 = blk * BLK
                xb, t_xb, _ = xb_r.next()
                for kk in range(8):
                    st, stok, ssem = xst.next()
                    p.dma("sp", st[:], xT[kk * 128:(kk + 1) * 128, e0:e0 + BLK + 3], w=[stok], sem=ssem)
                    p.op("pool", lambda e: e.tensor_copy(out=xb[:, kk, :], in_=st[:]), r=[stok], w=[t_xb])
                for g in range(4):
                    wgb, t_wgb, cvw_s, cvb_s, bias32, a32, t_ya = wgb_l[g], t_wgb_l[g], cvw_l[g], cvb_l[g], bias32_l[g], a32_l[g], t_ya_l[g]
                    hb, t_hb, _ = pH.next()
                    for c in range(4):
                        for kk in range(8):
                            p.op("pe", lambda e: e.matmul(hb[:, 16 + 4 * c:20 + 4 * c], lhsT=xb[:, kk, 2 + c * 128:2 + (c + 1) * 128],
                                                          rhs=wgb[:, kk, 512 + 4 * k:516 + 4 * k], start=(kk == 0), stop=(kk == 7)),
                                 r=[t_xb, t_wgb], w=[t_hb])
                    dtv, t_dtv, _ = dtv_r.next()
                    V, AV, EE, LL, DT, DA = [dtv[:, i, :] for i in range(6)]
                    b32 = bias32[:, k, :, :].rearrange("p c r -> p (c r)")
                    A32 = a32[:, k, :, :].rearrange("p c r -> p (c r)")
                    p.op("dve", lambda e: e.tensor_tensor(out=V, in0=hb[:, 16:32], in1=b32, op=ALU.add), r=[t_hb, t_c], w=[t_dtv])
                    p.op("dve", lambda e: e.tensor_scalar(out=AV, in0=V, scalar1=-1.0, scalar2=None, op0=ALU.mult), r=[t_dtv], w=[t_dtv])
                    p.op("dve", lambda e: e.tensor_tensor(out=AV, in0=AV, in1=V, op=ALU.max), r=[t_dtv], w=[t_dtv])
                    p.op("act", lambda e: e.activation(out=EE, in_=AV, func=AF.Exp, scale=-1.0), r=[t_dtv], w=[t_dtv])
                    p.op("act", lambda e: e.activation(out=LL, in_=EE, func=AF.Ln, bias=1.0), r=[t_dtv], w=[t_dtv])
                    p.op("dve", lambda e: e.scalar_tensor_tensor(out=DT, in0=V, scalar=0.0, in1=LL, op0=ALU.max, op1=ALU.add), r=[t_dtv], w=[t_dtv])
                    p.op("dve", lambda e: e.tensor_tensor(out=DA, in0=DT, in1=A32, op=ALU.mult), r=[t_dtv, t_c], w=[t_dtv])

                    xsf, t_xsf, _ = xsf_r.next()
                    btb, t_btb, _ = btb_r.next()
                    ctb, t_ctb, _ = ctb_r.next()
                    for m in range(4):
                        P, t_P, _ = pP.next()
                        for kk in range(8):
                            p.op("pe", lambda e: e.matmul(P[:, 0:BLK], lhsT=wgb[:, kk, m * 128:(m + 1) * 128], rhs=xb[:, kk, 0:BLK],
                                                          start=(kk == 0), stop=(kk == 7)), r=[t_xb, t_wgb], w=[t_P])
                        for kk in range(8):
                            p.op("pe", lambda e: e.matmul(hb[:, 4 * m:4 * m + 3], lhsT=wgb[:, kk, m * 128:(m + 1) * 128],
                                                          rhs=xb[:, kk, BLK:BLK + 3], start=(kk == 0), stop=(kk == 7)),
                                 r=[t_xb, t_wgb], w=[t_hb])
                        pre, t_pre, _ = pre_r.next()
                        p.op("act", lambda e: e.copy(out=pre[:, 0:BLK], in_=P[:, 0:BLK]), r=[t_P], w=[t_pre])
                        p.op("act", lambda e: e.copy(out=pre[:, BLK:BLK + 3], in_=hb[:, 4 * m:4 * m + 3]), r=[t_hb], w=[t_pre])
                        cv, t_cv, _ = cv_r.next()
                        p.op("dve", lambda e: e.tensor_scalar(out=cv[:], in0=pre[:, 0:BLK], scalar1=cvw_s[:, 4 * m:4 * m + 1], scalar2=None,
                                                              op0=ALU.mult), r=[t_pre, t_c], w=[t_cv])
                        for tap in range(1, 4):
                            p.op("dve", lambda e: e.scalar_tensor_tensor(out=cv[:], in0=pre[:, tap:tap + BLK],
                                                                         scalar=cvw_s[:, 4 * m + tap:4 * m + tap + 1], in1=cv[:],
                                                                         op0=ALU.mult, op1=ALU.add), r=[t_pre, t_c], w=[t_cv])
                        if m < 3:
                            p.op("act", lambda e: e.activation(out=xsf[:, m, :], in_=cv[:], func=AF.Silu, bias=cvb_s[:, m:m + 1]),
                                 r=[t_cv, t_c], w=[t_xsf])
                            if m == 2:
                                p.op("act", lambda e: e.copy(out=btb[:], in_=xsf[:, 2, :]), r=[t_xsf], w=[t_btb])
                        else:
                            p.op("act", lambda e: e.activation(out=ctb[:], in_=cv[:], func=AF.Silu, bias=cvb_s[:, m:m + 1]),
                                 r=[t_cv, t_c], w=[t_ctb])

                    chunks = range(4) if k == 0 else range(3, -1, -1)
                    for c in chunks:
                        st_ = front(k, g, blk, c, dtv, t_dtv, xsf, t_xsf, btb, t_btb, ctb, t_ctb, Tri, t_ya)
                        if pend is not None:
                            back(pend)
                        pend = st_
            back(pend)
            pend = None
        barrier(p)


def l0a_consts():
    t = np.arange(128)
    U = (t[:, None] <= t[None, :]).astype(np.float32)
    UT = (t[:, None] >= t[None, :]).astype(np.float32)
    I = np.eye(128, dtype=np.float32)
    ones = np.ones((128, 128), np.float32)
    cst = np.ascontiguousarray(np.concatenate([U, UT, I, ones], axis=1))
    mf = np.where(t[None, :] < t[:, None], NEG, 0.0).astype(np.float32)
    mb = np.where(t[None, :] > t[:, None], NEG, 0.0).astype(np.float32)
    msk = np.ascontiguousarray(np.concatenate([np.tile(mf, (1, 4)), np.tile(mb, (1, 4))], axis=1))
    return cst, msk


def prep_l0a(x, ev_w_in, ev_conv_w, ev_conv_b, ev_a_log, ev_dt_bias, ev_d_skip, S=SEQ):
    w_in = ev_w_in[0]
    cw = ev_conv_w[0]
    cb = ev_conv_b[0]
    cst, msk = l0a_consts()
    maps = []
    xTs = []
    for b in range(2):
        xe = np.zeros((D, S + 3), np.float32)
        xe[:, 2:S + 2] = x[b, :S, :].T
        xTs.append(xe)
    for c in range(NCORES):
        b, g = c // 4, c % 4
        xs_cols = 1024 + g * 256 + np.arange(256)
        b_cols = 1024 + 1024 + g * 128 + np.arange(128)
        c_cols = 1024 + 1536 + g * 128 + np.arange(128)
        dt_cols = np.concatenate([3072 + k * 16 + 4 * g + np.arange(4) for k in range(2)])
        cols = np.concatenate([xs_cols, b_cols, c_cols, dt_cols])
        wg = np.ascontiguousarray(w_in[:, cols])
        xbc_idx = cols[:512] - 1024
        cvw = np.ascontiguousarray(cw[:, xbc_idx].reshape(4, 4, 128).transpose(2, 1, 0).reshape(128, 16))
        cvb = np.ascontiguousarray(cb[xbc_idx].reshape(4, 128).T)
        hsel = np.concatenate([np.stack([v[0][k, 4 * g:4 * g + 4] for k in range(2)]).reshape(-1)
                               for v in (ev_a_log, ev_dt_bias, ev_d_skip)])
        hp = np.ascontiguousarray(np.broadcast_to(hsel[None, :], (128, 24))).astype(np.float32)
        maps.append({"xT": xTs[b], "wg": wg, "cvw": cvw, "cvb": cvb, "hp": hp, "cst": cst, "msk": msk})
    return maps


def rope_tables(p, posi, t_posi, n, invf, sgn, t_c, tabs, cosd, sind, t_cos, t_sin):
    R = slice(64, 96)
    ang, t_a, _ = tabs.next()
    nf, t_n, _ = tabs.next()
    ni, t_ni, _ = tabs.next()
    mm, t_m, _ = tabs.next()
    A, N, M = ang[R, 0:n], nf[R, 0:n], mm[R, 0:n]
    NI = ni[R, 0:n].bitcast(I32)
    p.op("dve", lambda e: e.tensor_copy(out=A, in_=posi[R, 0:n]), r=[t_posi], w=[t_a])
    p.op("dve", lambda e: e.tensor_scalar(out=A, in0=A, scalar1=invf[R, 0:1], scalar2=None, op0=ALU.mult), r=[t_c], w=[t_a])
    p.op("dve", lambda e: e.tensor_scalar(out=N, in0=A, scalar1=1.0 / TWO_PI, scalar2=None, op0=ALU.mult), r=[t_a], w=[t_n])
    p.op("dve", lambda e: e.tensor_copy(out=NI, in_=N), r=[t_n], w=[t_ni])
    p.op("dve", lambda e: e.tensor_copy(out=N, in_=NI), r=[t_ni], w=[t_n])
    p.op("dve", lambda e: e.scalar_tensor_tensor(out=A, in0=N, scalar=-C1, in1=A, op0=ALU.mult, op1=ALU.add), r=[t_n], w=[t_a])
    p.op("dve", lambda e: e.scalar_tensor_tensor(out=A, in0=N, scalar=-C2, in1=A, op0=ALU.mult, op1=ALU.add), r=[t_n], w=[t_a])

    def wrap(X, t_x):
        p.op("dve", lambda e: e.tensor_scalar(out=M, in0=X, scalar1=PI, scalar2=None, op0=ALU.is_gt), r=[t_x], w=[t_m])
        p.op("dve", lambda e: e.scalar_tensor_tensor(out=X, in0=M, scalar=-TWO_PI, in1=X, op0=ALU.mult, op1=ALU.add), r=[t_m], w=[t_x])
        p.op("dve", lambda e: e.tensor_scalar(out=M, in0=X, scalar1=-PI, scalar2=None, op0=ALU.is_lt), r=[t_x], w=[t_m])
        p.op("dve", lambda e: e.scalar_tensor_tensor(out=X, in0=M, scalar=TWO_PI, in1=X, op0=ALU.mult, op1=ALU.add), r=[t_m], w=[t_x])
    wrap(A, t_a)
    p.op("act", lambda e: e.activation(out=N, in_=A, func=AF.Sin), r=[t_a], w=[t_n])
    p.op("dve", lambda e: e.tensor_scalar(out=sind, in0=N, scalar1=sgn[R, 0:1], scalar2=None, op0=ALU.mult), r=[t_n, t_c], w=[t_sin])
    p.op("dve", lambda e: e.tensor_scalar(out=A, in0=A, scalar1=PI / 2, scalar2=None, op0=ALU.add), r=[t_a], w=[t_a])
    wrap(A, t_a)
    p.op("act", lambda e: e.activation(out=cosd, in_=A, func=AF.Sin), r=[t_a], w=[t_cos])


def emit_l1(nc, p, S, A):
    T = S // 4
    BLK = 512
    NKB = S // BLK
    NQB = T // BLK
    NK128 = S // 128
    x1T, x1, posb, out = A["x1T"], A["x1"], A["posb"], A["out"]
    w_cq, w_kv, w_kr, w_g3, w_uq, w_ukv, w_pool, wout = (A["w_cq"], A["w_kv"], A["w_kr"], A["w_g3"], A["w_uq"], A["w_ukv"],
                                                         A["w_pool"], A["wout_od"])
    sm_c, sel, lng, lnb = A["sm_c"], A["sel"], A["lng_od"], A["lnb_od"]
    oxT, ox1, oHL, oHR, opos = A["own_x1T"], A["own_x1"], A["own_HL"], A["own_HR"], A["own_pos"]

    def xrows_static(blk, kk):
        return x1T[blk * 1024 + kk * 128:blk * 1024 + (kk + 1) * 128, :]

    def xrows_own(blk, kk):
        return oxT[blk * 1024 + kk * 128:blk * 1024 + (kk + 1) * 128, :]
    xtok = ox1

    with ExitStack() as es:
        p.es = es
        smc = p.sb("smc", [128, 80], F32)
        sel_s = p.sb("sel_s", [128, 384], F32)
        t_c = Tok()
        dc = p.dma_sem()
        p.dma("sp", smc[:], sm_c[:, :], w=[t_c], sem=dc)
        p.dma("sp", sel_s[:], sel[:, :], w=[t_c], sem=dc)
        QG, KVG, INVF, SGN, PSC = smc[:, 0:2], smc[:, 2:3], smc[:, 3:4], smc[:, 4:5], smc[:, 5:9]
        CORR = smc[:, 16:80]
        SEL_E, SEL_O, ONES = sel_s[:, 0:128], sel_s[:, 128:256], sel_s[:, 256:384]
        ycT = p.sb("ycT", [128, 4, T], BF16)
        t_yc = [[Tok() for _ in range(NQB)] for _ in range(8)]
        esA = ExitStack()
        p.es = esA
        ckvn = p.sb("ckvn", [128, S], BF16)
        t_ckvn = [Tok() for _ in range(NKB)]
        Kbuf = p.sb("Kbuf", [96, S], BF16)
        t_kn = [Tok() for _ in range(NKB)]
        t_kr = [Tok() for _ in range(NKB)]
        cqn = p.sb("cqn", [128, 2, T], BF16)
        t_cqn = [Tok() for _ in range(NQB)]
        cosq = p.sb("cosq", [96, T], BF16)
        sinq = p.sb("sinq", [96, T], BF16)
        t_cosq = [Tok() for _ in range(NQB)]
        t_sinq = [Tok() for _ in range(NQB)]
        wuqb = p.sb("wuqb", [128, 2, 1536], BF16)
        wukvb = p.sb("wukvb", [128, 1024], BF16)
        t_wuq, t_wukv = Tok(), Tok()

        with ExitStack() as es1:
            p.es = es1
            wst = Ring(p, "wst", [128, 1536], F32, 1, dma=True)
            wcqb = p.sb("wcqb", [128, 8, 256], BF16)
            wkvb = p.sb("wkvb", [128, 8, 128], BF16)
            wkrb = p.sb("wkrb", [128, 8, 192], BF16)
            t_wcq, t_wkv, t_wkr = Tok(), Tok(), Tok()
            load_cast_weight(p, w_cq, wcqb, wst, 8, 256, cw=256, tok=t_wcq)
            load_cast_weight(p, w_kv, wkvb, wst, 8, 128, cw=128, tok=t_wkv)
            load_cast_weight(p, w_kr, wkrb, wst, 8, 192, cw=192, tok=t_wkr)
            for kc in range(2):
                st, stok, ssem = wst.next()
                p.dma("sp", st[:, 0:1536], w_uq[kc * 128:(kc + 1) * 128, :], w=[stok], sem=ssem)
                p.op("dve", lambda e: e.tensor_scalar(out=wuqb[:, kc, :], in0=st[:, 0:1536], scalar1=QG[:, kc:kc + 1], scalar2=None,
                                                      op0=ALU.mult), r=[stok, t_c], w=[t_wuq])
            st, stok, ssem = wst.next()
            p.dma("sp", st[:, 0:1024], w_ukv[:, :], w=[stok], sem=ssem)
            p.op("dve", lambda e: e.tensor_scalar(out=wukvb[:], in0=st[:, 0:1024], scalar1=KVG, scalar2=None, op0=ALU.mult),
                 r=[stok, t_c], w=[t_wukv])

            xst = Ring(p, "xst", [128, BLK], F32, 3, dma=True)
            xb_r = Ring(p, "xb", [128, 8, BLK], BF16, 2)
            pos_r = Ring(p, "posr", [128, BLK], I32, 2, dma=True)
            tabs = Ring(p, "tabs", [128, BLK], F32, 4)
            cs_r = Ring(p, "csr", [128, BLK], F32, 2)
            sn_r = Ring(p, "snr", [128, BLK], F32, 2)
            sq_r = Ring(p, "sqr", [128, BLK], F32, 3)
            t1_r = Ring(p, "t1r", [128, BLK], F32, 2)
            pA = Ring(p, "pA", [128, 512], F32, 5, space="ps")
            pSS = Ring(p, "pSS", [128, 512], F32, 2, space="ps")

            def load_xblock(rows_fn, blk):
                xb, t_xb, _ = xb_r.next()
                for kk in range(8):
                    st, stok, ssem = xst.next()
                    p.dma("sp", st[:], rows_fn(blk, kk), w=[stok], sem=ssem)
                    if kk % 2 == 0:
                        p.op("act", lambda e: e.copy(out=xb[:, kk, :], in_=st[:]), r=[stok], w=[t_xb])
                    else:
                        p.op("pool", lambda e: e.tensor_copy(out=xb[:, kk, :], in_=st[:]), r=[stok], w=[t_xb])
                return xb, t_xb

            def rstd_of(ss, t_ss, nch):
                r_, t_r, _ = sq_r.next()
                p.op("dve", lambda e: e.tensor_scalar(out=r_[:], in0=ss[:], scalar1=1.0 / nch, scalar2=EPS, op0=ALU.mult, op1=ALU.add),
                     r=[t_ss], w=[t_r])
                p.op("act", lambda e: e.activation(out=r_[:], in_=r_[:], func=AF.Ln), r=[t_r], w=[t_r])
                p.op("act", lambda e: e.activation(out=r_[:], in_=r_[:], func=AF.Exp, scale=-0.5), r=[t_r], w=[t_r])
                return r_, t_r

            for kb in range(NKB):
                c0 = kb * BLK
                xb, t_xb = load_xblock(xrows_static, kb + 1)
                pi_, t_pi, pi_sem = pos_r.next()
                p.dma("sp", pi_[64:96, :], posb[kb * 32:(kb + 1) * 32, :], w=[t_pi], sem=pi_sem)
                ck, t_ck, _ = pA.next()
                ka, t_ka, _ = pA.next()
                kbs, t_kbs, _ = pA.next()
                for kk in range(8):
                    p.op("pe", lambda e: e.matmul(ck[:], lhsT=wkvb[:, kk, :], rhs=xb[:, kk, :], start=(kk == 0), stop=(kk == 7)),
                         r=[t_wkv, t_xb], w=[t_ck])
                for kk in range(8):
                    p.op("pe", lambda e: e.matmul(ka[0:96, :], lhsT=wkrb[:, kk, 0:96], rhs=xb[:, kk, :], start=(kk == 0), stop=(kk == 7)),
                         r=[t_wkr, t_xb], w=[t_ka])
                for kk in range(8):
                    p.op("pe", lambda e: e.matmul(kbs[0:96, :], lhsT=wkrb[:, kk, 96:192], rhs=xb[:, kk, :], start=(kk == 0), stop=(kk == 7)),
                         r=[t_wkr, t_xb], w=[t_kbs])
                sq, t_sq, _ = sq_r.next()
                p.op("act", lambda e: e.activation(out=sq[:], in_=ck[:], func=AF.Square), r=[t_ck], w=[t_sq])
                ss, t_ss, _ = pSS.next()
                p.op("pe", lambda e: e.matmul(ss[:], lhsT=ONES, rhs=sq[:], start=True, stop=True), r=[t_sq, t_c], w=[t_ss])
                rs, t_rs = rstd_of(ss, t_ss, 128)
                p.op("dve", lambda e: e.tensor_tensor(out=ckvn[:, c0:c0 + BLK], in0=ck[:], in1=rs[:], op=ALU.mult),
                     r=[t_ck, t_rs], w=[t_ckvn[kb]])
                cs_, t_cs, _ = cs_r.next()
                sn_, t_sn, _ = sn_r.next()
                rope_tables(p, pi_, t_pi, BLK, INVF, SGN, t_c, tabs, cs_[64:96, :], sn_[64:96, :], t_cs, t_sn)
                t1, t_t1, _ = t1_r.next()
                t2, t_t2, _ = t1_r.next()
                p.op("dve", lambda e: e.tensor_tensor(out=t1[64:96, :], in0=ka[64:96, :], in1=cs_[64:96, :], op=ALU.mult),
                     r=[t_ka, t_cs], w=[t_t1])
                p.op("dve", lambda e: e.tensor_tensor(out=t2[64:96, :], in0=kbs[64:96, :], in1=sn_[64:96, :], op=ALU.mult),
                     r=[t_kbs, t_sn], w=[t_t2])
                p.op("pool", lambda e: e.tensor_tensor(out=Kbuf[64:96, c0:c0 + BLK], in0=t1[64:96, :], in1=t2[64:96, :], op=ALU.add),
                     r=[t_t1, t_t2], w=[t_kr[kb]])

            for qb in range(NQB):
                c0 = qb * BLK
                xb, t_xb = load_xblock(xrows_own, qb)
                pi_, t_pi, pi_sem = pos_r.next()
                p.dma("sp", pi_[64:96, :], opos[qb * 32:(qb + 1) * 32, :], w=[t_pi], sem=pi_sem)
                cqs = []
                ss, t_ss, _ = pSS.next()
                for m in range(2):
                    cq, t_cq, _ = pA.next()
                    for kk in range(8):
                        p.op("pe", lambda e: e.matmul(cq[:], lhsT=wcqb[:, kk, m * 128:(m + 1) * 128], rhs=xb[:, kk, :],
                                                      start=(kk == 0), stop=(kk == 7)), r=[t_wcq, t_xb], w=[t_cq])
                    sq, t_sq, _ = sq_r.next()
                    p.op("act", lambda e: e.activation(out=sq[:], in_=cq[:], func=AF.Square), r=[t_cq], w=[t_sq])
                    p.op("pe", lambda e: e.matmul(ss[:], lhsT=ONES, rhs=sq[:], start=(m == 0), stop=(m == 1)), r=[t_sq, t_c], w=[t_ss])
                    cqs.append((cq, t_cq))
                rs, t_rs = rstd_of(ss, t_ss, 256)
                for m in range(2):
                    cq, t_cq = cqs[m]
                    p.op("dve", lambda e: e.tensor_tensor(out=cqn[:, m, c0:c0 + BLK], in0=cq[:], in1=rs[:], op=ALU.mult),
                         r=[t_cq, t_rs], w=[t_cqn[qb]])
                rope_tables(p, pi_, t_pi, BLK, INVF, SGN, t_c, tabs, cosq[64:96, c0:c0 + BLK], sinq[64:96, c0:c0 + BLK],
                            t_cosq[qb], t_sinq[qb])
        barrier(p)

        with ExitStack() as es3:
            p.es = es3
            Vbuf = p.sb("Vbuf", [128, NK128, 128], BF16)
            t_v = [Tok() for _ in range(NK128 // 8 if NK128 >= 8 else 1)]
            VG = min(8, NK128)
            Q_r = Ring(p, "Q", [96, T], BF16, 2)
            tq_r = Ring(p, "tq", [96, BLK], F32, 4)
            P_r = Ring(p, "P", [128, BLK], BF16, 3)
            osb_r = Ring(p, "osb", [128, BLK], F32, 2)
            rden_r = Ring(p, "rden", [128, BLK], F32, 2)
            pS = Ring(p, "pS", [128, 512], F32, 3, space="ps")
            pO = Ring(p, "pO", [128, 512], F32, 2, space="ps")
            pD = Ring(p, "pD", [128, 512], F32, 1, space="ps")
            pB = Ring(p, "pB", [128, 512], F32, 2, space="ps")
            for h in range(8):
                odd = h % 2
                voff = 64 * odd
                Q, _, _ = Q_r.next()
                t_Q = [Tok() for _ in range(NQB)]
                for qb in range(NQB):
                    c0 = qb * BLK
                    qa, t_qa, _ = pB.next()
                    qs, t_qs, _ = pB.next()
                    for kc in range(2):
                        p.op("pe", lambda e: e.matmul(qa[0:96, :], lhsT=wuqb[:, kc, h * 96:(h + 1) * 96], rhs=cqn[:, kc, c0:c0 + BLK],
                                                      start=(kc == 0), stop=(kc == 1)), r=[t_wuq, t_cqn[qb]], w=[t_qa])
                    for kc in range(2):
                        p.op("pe", lambda e: e.matmul(qs[0:96, :], lhsT=wuqb[:, kc, 768 + h * 96:768 + (h + 1) * 96],
                                                      rhs=cqn[:, kc, c0:c0 + BLK], start=(kc == 0), stop=(kc == 1)),
                             r=[t_wuq, t_cqn[qb]], w=[t_qs])
                    p.op("dve", lambda e: e.tensor_copy(out=Q[0:64, c0:c0 + BLK], in_=qa[0:64, :]), r=[t_qa], w=[t_Q[qb]])
                    t1, t_t1, _ = tq_r.next()
                    t2, t_t2, _ = tq_r.next()
                    p.op("dve", lambda e: e.tensor_tensor(out=t1[64:96, :], in0=qa[64:96, :], in1=cosq[64:96, c0:c0 + BLK], op=ALU.mult),
                         r=[t_qa, t_cosq[qb]], w=[t_t1])
                    p.op("dve", lambda e: e.tensor_tensor(out=t2[64:96, :], in0=qs[64:96, :], in1=sinq[64:96, c0:c0 + BLK], op=ALU.mult),
                         r=[t_qs, t_sinq[qb]], w=[t_t2])
                    p.op("pool", lambda e: e.tensor_tensor(out=Q[64:96, c0:c0 + BLK], in0=t1[64:96, :], in1=t2[64:96, :], op=ALU.add),
                         r=[t_t1, t_t2], w=[t_Q[qb]])
                for kb in range(NKB):
                    c0 = kb * BLK
                    kp, t_kp, _ = pB.next()
                    p.op("pe", lambda e: e.matmul(kp[0:64, :], lhsT=wukvb[:, h * 128:h * 128 + 64], rhs=ckvn[:, c0:c0 + BLK],
                                                  start=True, stop=True), r=[t_wukv, t_ckvn[kb]], w=[t_kp])
                    p.op("dve", lambda e: e.tensor_copy(out=Kbuf[0:64, c0:c0 + BLK], in_=kp[0:64, :]), r=[t_kp], w=[t_kn[kb]])
                for g in range(len(t_v)):
                    vp, t_vp, _ = pB.next()
                    for j in range(VG):
                        k128 = g * VG + j
                        p.op("pe", lambda e: e.matmul(vp[:, j * 64:(j + 1) * 64], lhsT=ckvn[:, k128 * 128:(k128 + 1) * 128],
                                                      rhs=wukvb[:, h * 128 + 64:h * 128 + 128], start=True, stop=True),
                             r=[t_wukv, t_ckvn[k128 // 4]], w=[t_vp])
                    vs = Vbuf[:, g * VG:(g + 1) * VG, :]
                    p.op("pool", lambda e: e.memset(vs[:, :, 64 - voff:128 - voff], 0.0), w=[t_v[g]])
                    p.op("pool", lambda e: e.memset(vs[:, :, 64 - voff:65 - voff], 1.0), w=[t_v[g]])
                    p.op("dve", lambda e: e.tensor_copy(out=vs[:, :, voff:voff + 64], in_=vp[:, 0:VG * 64].rearrange("p (j v) -> p j v", v=64)),
                         r=[t_vp], w=[t_v[g]])
                MV = 128 if odd else 65
                for qb in range(NQB):
                    q0 = qb * BLK
                    O, t_O, _ = pO.next()
                    Sq = {}

                    def issue_S(k128):
                        Sx, t_S, _ = pS.next()
                        kb = k128 // 4
                        p.op("pe", lambda e: e.matmul(Sx[:], lhsT=Kbuf[0:96, k128 * 128:(k128 + 1) * 128], rhs=Q[0:96, q0:q0 + BLK],
                                                      start=True, stop=True), r=[t_kn[kb], t_kr[kb], t_Q[qb]], w=[t_S])
                        Sq[k128] = (Sx, t_S)
                    for k128 in range(min(2, NK128)):
                        issue_S(k128)
                    for k128 in range(NK128):
                        Sx, t_S = Sq.pop(k128)
                        Pt, t_P, _ = P_r.next()
                        p.op("act", lambda e: e.activation(out=Pt[:], in_=Sx[:], func=AF.Exp, scale=ATT_SCALE), r=[t_S], w=[t_P])
                        if k128 + 2 < NK128:
                            issue_S(k128 + 2)
                        p.op("pe", lambda e: e.matmul(O[0:MV, :], lhsT=Vbuf[:, k128, 0:MV], rhs=Pt[:], start=(k128 == 0),
                                                      stop=(k128 == NK128 - 1)), r=[t_v[k128 // VG], t_P], w=[t_O])
                    osb, t_osb, _ = osb_r.next()
                    p.op("dve", lambda e: e.tensor_copy(out=osb[0:MV, :], in_=O[0:MV, :]), r=[t_O], w=[t_osb])
                    Dn, t_D, _ = pD.next()
                    if odd:
                        p.op("pe", lambda e: e.matmul(Dn[:], lhsT=SEL_O, rhs=osb[:], start=True, stop=True), r=[t_osb, t_c], w=[t_D])
                    else:
                        p.op("pe", lambda e: e.matmul(Dn[0:64, :], lhsT=SEL_E[0:65, 0:64], rhs=osb[0:65, :], start=True, stop=True),
                             r=[t_osb, t_c], w=[t_D])
                    rd, t_rd, _ = rden_r.next()
                    PR = slice(voff, voff + 64)
                    p.op("dve", lambda e: e.reciprocal(out=rd[PR, :], in_=Dn[PR, :]), r=[t_D], w=[t_rd])
                    p.op("dve", lambda e: e.tensor_tensor(out=ycT[PR, h // 2, q0:q0 + BLK], in0=osb[PR, :], in1=rd[PR, :], op=ALU.mult),
                         r=[t_osb, t_rd], w=[t_yc[h][qb]])
        barrier(p)
        esA.close()

        with ExitStack() as es4:
            p.es = es4
            wst = Ring(p, "wst4", [128, 1024], F32, 2, dma=True)
            wg3b = p.sb("wg3b", [128, 8, 1536], BF16)
            woutb = p.sb("woutb", [128, 8, D], BF16)
            wpoolb = p.sb("wpoolb", [128, 512], BF16)
            lng_s = p.sb("lng_s", [128, D], F32)
            lnb_s = p.sb("lnb_s", [128, D], F32)
            t_wg3, t_wout, t_wpool, t_ln = Tok(), Tok(), Tok(), Tok()
            dl = p.dma_sem()
            p.dma("sp", lng_s[:], lng[:, :], w=[t_ln], sem=dl)
            p.dma("sp", lnb_s[:], lnb[:, :], w=[t_ln], sem=dl)
            load_cast_weight(p, w_g3, wg3b, wst, 8, 1536, cw=768, tok=t_wg3)
            load_cast_weight(p, wout, woutb, wst, 8, D, tok=t_wout)
            st, stok, ssem = wst.next()
            p.dma("sp", st[:, 0:512], w_pool[:, :], w=[stok], sem=ssem)
            p.op("dve", lambda e: e.tensor_copy(out=wpoolb[:], in_=st[:, 0:512]), r=[stok], w=[t_wpool])
            xst = Ring(p, "xst4", [128, BLK + 16], F32, 3, dma=True)
            xb_r = Ring(p, "xb4", [128, 8, BLK + 16], BF16, 2)
            xtk = Ring(p, "xtk", [128, D], F32, 2, dma=True)
            cat = p.sb("cat", [128, 8, BLK], BF16)
            t_cat = [Tok() for _ in range(8)]
            tmp = Ring(p, "tmp4", [128, BLK + 16], F32, 10)
            hl_r = Ring(p, "hl4", [128, 16], F32, 2)
            pl_r = Ring(p, "pl4", [128, BLK], BF16, 2)
            rr = Ring(p, "rr", [128, D], F32, 2)
            r2 = Ring(p, "r2", [128, D], F32, 2, dma=True)
            junk = Ring(p, "junk", [128, D], BF16, 1)
            st_r = Ring(p, "stat", [128, 8], F32, 4)
            pp = Ring(p, "pp4", [128, 512], F32, 5, space="ps")
            ph = Ring(p, "ph4", [128, 512], F32, 1, space="ps")
            po = Ring(p, "po4", [128, D], F32, 1, space="ps")
            WIN = (2, 4, 8, 16)
            for qb in range(NQB):
                c0 = qb * BLK
                xb, t_xb, _ = xb_r.next()
                for kk in range(8):
                    st, stok, ssem = xst.next()
                    rws = slice(qb * 1024 + kk * 128, qb * 1024 + (kk + 1) * 128)
                    p.dma("sp", st[:, 8:BLK + 8], oxT[rws, :], w=[stok], sem=ssem)
                    p.dma("sp", st[:, 0:8], oHL[rws, :], w=[stok], sem=ssem, nowait=True)
                    p.dma("sp", st[:, BLK + 8:BLK + 16], oHR[rws, :], w=[stok], sem=ssem, nowait=True)
                    p.op("pool", lambda e: e.tensor_copy(out=xb[:, kk, :], in_=st[:]), r=[stok], w=[t_xb])

                def proj(col0, lo, n, dst, t_dst):
                    for kk in range(8):
                        p.op("pe", lambda e: e.matmul(dst, lhsT=wg3b[:, kk, col0:col0 + 128], rhs=xb[:, kk, lo:lo + n],
                                                      start=(kk == 0), stop=(kk == 7)), r=[t_wg3, t_xb], w=[t_dst])
                for j in range(4):
                    gc, t_gc, _ = pp.next()
                    proj(j * 128, 8, BLK, gc[:], t_gc)
                    sg, t_sg, _ = tmp.next()
                    p.op("act", lambda e: e.activation(out=sg[:, 0:BLK], in_=gc[:], func=AF.Silu), r=[t_gc], w=[t_sg])
                    p.op("dve", lambda e: e.tensor_tensor(out=cat[:, j, :], in0=sg[:, 0:BLK], in1=ycT[:, j, c0:c0 + BLK], op=ALU.mult),
                         r=[t_sg, t_yc[2 * j][qb], t_yc[2 * j + 1][qb]], w=[t_cat[j]])
                for gi in range(4):
                    w = WIN[gi]
                    um, t_um, _ = pp.next()
                    proj(512 + gi * 128, 8, BLK, um[:], t_um)
                    hl, t_hl, _ = ph.next()
                    proj(512 + gi * 128, 0, 8, hl[:, 0:8], t_hl)
                    proj(512 + gi * 128, BLK + 8, 8, hl[:, 8:16], t_hl)
                    u, t_u, _ = tmp.next()
                    p.op("act", lambda e: e.copy(out=u[:, 8:BLK + 8], in_=um[:]), r=[t_um], w=[t_u])
                    p.op("act", lambda e: e.copy(out=u[:, 0:8], in_=hl[:, 0:8]), r=[t_hl], w=[t_u])
                    p.op("act", lambda e: e.copy(out=u[:, BLK + 8:BLK + 16], in_=hl[:, 8:16]), r=[t_hl], w=[t_u])
                    cur, t_cur, n, width = u, t_u, BLK + 16, 1
                    while width < w:
                        nxt, t_nxt, _ = tmp.next()
                        n2 = n - width
                        p.op("dve", lambda e: e.tensor_tensor(out=nxt[:, 0:n2], in0=cur[:, 0:n2], in1=cur[:, width:width + n2], op=ALU.add),
                             r=[t_cur], w=[t_nxt])
                        cur, t_cur, n, width = nxt, t_nxt, n2, width * 2
                    s0 = 8 - w // 2
                    pm, t_pm, _ = tmp.next()
                    p.op("dve", lambda e: e.tensor_scalar(out=pm[:, 0:BLK], in0=cur[:, s0:s0 + BLK], scalar1=1.0 / w, scalar2=None, op0=ALU.mult),
                         r=[t_cur], w=[t_pm])
                    if qb == 0:
                        p.op("dve", lambda e: e.tensor_tensor(out=pm[:, 0:8], in0=pm[:, 0:8], in1=CORR[:, gi * 16:gi * 16 + 8], op=ALU.mult),
                             r=[t_c], w=[t_pm])
                    if qb == NQB - 1:
                        p.op("dve", lambda e: e.tensor_tensor(out=pm[:, BLK - 8:BLK], in0=pm[:, BLK - 8:BLK],
                                                              in1=CORR[:, gi * 16 + 8:gi * 16 + 16], op=ALU.mult), r=[t_c], w=[t_pm])
                    pl, t_pl, _ = pl_r.next()
                    p.op("dve", lambda e: e.tensor_tensor(out=pl[:], in0=pm[:, 0:BLK], in1=u[:, 8:BLK + 8], op=ALU.subtract),
                         r=[t_pm, t_u], w=[t_pl])
                    yd, t_yd, _ = pp.next()
                    p.op("pe", lambda e: e.matmul(yd[:], lhsT=wpoolb[:, gi * 128:(gi + 1) * 128], rhs=pl[:], start=True, stop=True),
                         r=[t_wpool, t_pl], w=[t_yd])
                    gd, t_gd, _ = pp.next()
                    proj(1024 + gi * 128, 8, BLK, gd[:], t_gd)
                    sg, t_sg, _ = tmp.next()
                    p.op("act", lambda e: e.activation(out=sg[:, 0:BLK], in_=gd[:], func=AF.Silu), r=[t_gd], w=[t_sg])
                    p.op("dve", lambda e: e.scalar_tensor_tensor(out=cat[:, 4 + gi, :], in0=yd[:], scalar=PSC[:, gi:gi + 1], in1=sg[:, 0:BLK],
                                                                 op0=ALU.mult, op1=ALU.mult), r=[t_yd, t_sg, t_c], w=[t_cat[4 + gi]])
                for tt in range(BLK // 128):
                    xk, t_xk, xk_sem = xtk.next()
                    p.dma("sp", xk[:], xtok[c0 + tt * 128:c0 + (tt + 1) * 128, :], w=[t_xk], sem=xk_sem)
                    o, t_o, _ = po.next()
                    for half in range(2):
                        for kc in range(8):
                            p.op("pe", lambda e: e.matmul(o[:, half * 512:(half + 1) * 512], lhsT=cat[:, kc, tt * 128:(tt + 1) * 128],
                                                          rhs=woutb[:, kc, half * 512:(half + 1) * 512], start=(kc == 0), stop=(kc == 7)),
                                 r=[t_cat[kc], t_wout], w=[t_o])
                    layer_norm_tail(p, o, t_o, xk, t_xk, rr, r2, junk, st_r, lng_s, lnb_s, t_ln,
                                    out[c0 + tt * 128:c0 + (tt + 1) * 128, :])
        barrier(p)


def prep_l1(x1, positions, od_w_in, od_q_norm_g, od_w_uq, od_kv_norm_g, od_w_ukv, od_pool_w, od_pool_scale, od_w_out,
            od_ln_g, od_ln_b, S=SEQ):
    T = S // 4
    w_in = od_w_in[0]
    w_cq = np.ascontiguousarray(w_in[:, 0:256])
    w_kv = np.ascontiguousarray(w_in[:, 256:384])
    kr = w_in[:, 384:416]
    krs = np.concatenate([kr[:, 16:32], kr[:, 0:16]], axis=1)
    z64 = np.zeros((D, 64), np.float32)
    w_kr = np.ascontiguousarray(np.concatenate([z64, kr, z64, krs], axis=1))
    w_g3 = np.ascontiguousarray(w_in[:, 416:1952])
    uq = od_w_uq[0].reshape(256, 8, 96)
    uqs = np.zeros_like(uq)
    uqs[:, :, 64:80] = uq[:, :, 80:96]
    uqs[:, :, 80:96] = uq[:, :, 64:80]
    w_uq = np.ascontiguousarray(np.concatenate([uq.reshape(256, 768), uqs.reshape(256, 768)], axis=1))
    w_ukv = np.ascontiguousarray(od_w_ukv[0])
    w_pool = np.ascontiguousarray(od_pool_w[0].transpose(1, 0, 2).reshape(128, 512))
    wout = np.ascontiguousarray(od_w_out[0])
    half = 16
    inv_freq = (np.float32(10000.0) ** (-np.arange(half, dtype=np.float32) / np.float32(half))).astype(np.float32)
    sel = np.zeros((128, 384), np.float32)
    sel[64, 0:64] = 1.0
    sel[0, 128 + 64:128 + 128] = 1.0
    sel[:, 256:384] = 1.0
    lng = np.ascontiguousarray(np.broadcast_to(od_ln_g[0][None, :], (128, D)))
    lnb = np.ascontiguousarray(np.broadcast_to(od_ln_b[0][None, :], (128, D)))
    maps = []
    xTbs = [np.ascontiguousarray(x1[b, :S, :].T) for b in range(2)] if x1 is not None else [None, None]
    posbs = [np.ascontiguousarray(np.broadcast_to(positions[b, :S].reshape(S // 512, 1, 512), (S // 512, 32, 512))
                                  .reshape((S // 512) * 32, 512)).astype(np.int32) for b in range(2)]
    for c in range(NCORES):
        b, s0 = c // 4, (c % 4) * T
        xe = None
        if x1 is not None:
            xe = np.zeros((D, T + 16), np.float32)
            lo, hi = max(0, s0 - 8), min(S, s0 + T + 8)
            xe[:, lo - (s0 - 8):hi - (s0 - 8)] = x1[b, lo:hi, :].T
        smc = np.zeros((128, 80), np.float32)
        smc[:, 0:2] = od_q_norm_g[0].reshape(2, 128).T
        smc[:, 2] = od_kv_norm_g[0]
        smc[64:80, 3] = inv_freq
        smc[80:96, 3] = inv_freq
        smc[64:80, 4] = -1.0
        smc[80:96, 4] = 1.0
        smc[:, 5:9] = od_pool_scale[0].reshape(4, 128).T
        for gi, w in enumerate((2, 4, 8, 16)):
            for j in range(8):
                for side, t in ((0, s0 + j), (1, s0 + T - 8 + j)):
                    lo_ = min(max(t - w // 2, 0), S)
                    hi_ = min(max(t + w - w // 2, 0), S)
                    smc[:, 16 + gi * 16 + side * 8 + j] = np.float32(w) / np.float32(hi_ - lo_)
        maps.append({"xTb": xTbs[b], "xTo": xe, "xtok": (np.ascontiguousarray(x1[b, s0:s0 + T, :]) if x1 is not None else None),
                     "posb": posbs[b],
                     "w_cq": w_cq, "w_kv": w_kv, "w_kr": w_kr, "w_g3": w_g3, "w_uq": w_uq, "w_ukv": w_ukv,
                     "w_pool": w_pool, "wout": wout, "sm_c": smc, "sel": sel, "lng": lng, "lnb": lnb})
    return maps


def build_fused(S=SEQ):
    T = S // 4
    nc = bass.Bass("TRN2", target_bir_lowering=False)

    def inp(name, shape, dt=F32):
        return nc.dram_tensor(name, list(shape), dt, kind="ExternalInput").ap()
    A = {}
    A["xT"] = inp("xT", [D, S + 3])
    A["xT1"] = A["xT"][:, 1:S + 3]
    A["xtok"] = inp("xtok", [S, D])
    A["wg"] = [inp("wg%d" % g, [D, 520]) for g in range(4)]
    A["cvw"] = [inp("cvw%d" % g, [128, 16])[:, :] for g in range(4)]
    A["cvb"] = [inp("cvb%d" % g, [128, 4])[:, :] for g in range(4)]
    A["hp"] = [inp("hp%d" % g, [128, 24])[:, :] for g in range(4)]
    A["cst"] = inp("cst", [128, 512])
    A["msk"] = inp("msk", [128, 1024])
    A["w1"] = inp("w1", [D, 5120])
    A["wout"] = inp("wout", [2048, D])
    A["normg"] = inp("normg", [128, 8])
    A["scw"] = inp("scw", [128, 24])
    A["lng"] = inp("lng", [128, D])
    A["lnb"] = inp("lnb", [128, D])
    A["posb"] = inp("posb", [(S // 512) * 32, 512], I32)
    A["w_cq"] = inp("w_cq", [D, 256])
    A["w_kv"] = inp("w_kv", [D, 128])
    A["w_kr"] = inp("w_kr", [D, 192])
    A["w_g3"] = inp("w_g3", [D, 1536])
    A["w_uq"] = inp("w_uq", [256, 1536])
    A["w_ukv"] = inp("w_ukv", [128, 1024])
    A["w_pool"] = inp("w_pool", [128, 512])
    A["wout_od"] = inp("wout_od", [D, D])
    A["sm_c"] = inp("sm_c", [128, 80])
    A["sel"] = inp("sel", [128, 384])
    A["lng_od"] = inp("lng_od", [128, D])
    A["lnb_od"] = inp("lnb_od", [128, D])
    off = inp("off", [1, 4], I32)
    A["out"] = nc.dram_tensor("out", [T, D], F32, kind="ExternalOutput").ap()
    A["yaT"] = nc.dram_tensor("yaT_s", [D, S], F32).ap()
    A["x1"] = nc.dram_tensor("x1_s", [S, D], F32).ap()
    A["x1T"] = nc.dram_tensor("x1T_s", [(S // 512 + 2) * 1024, 512], F32).ap()
    A["x1HL"] = nc.dram_tensor("x1HL_s", [(S // 512 + 1) * 1024, 8], F32).ap()
    A["x1HR"] = nc.dram_tensor("x1HR_s", [(S // 512 + 1) * 1024, 8], F32).ap()
    A["own_x1T"] = nc.dram_tensor("own_x1T_s", [(T // 512) * 1024, 512], F32).ap()
    A["own_x1"] = nc.dram_tensor("own_x1_s", [T, D], F32).ap()
    A["own_HL"] = nc.dram_tensor("own_HL_s", [(T // 512) * 1024, 8], F32).ap()
    A["own_HR"] = nc.dram_tensor("own_HR_s", [(T // 512) * 1024, 8], F32).ap()
    A["own_pos"] = nc.dram_tensor("own_pos_s", [(T // 512) * 32, 512], I32).ap()

    with ExitStack() as es:
        p = Prog(nc, es)
        regs = [es.enter_context(nc.sync.register("offr%d" % i)) for i in range(3)]
        for i in range(3):
            nc.sync.reg_load(regs[i], off[0:1, i:i + 1])
        NB, NQB = S // 512, T // 512
        b0v = nc.sync.snap(regs[0], min_val=0, max_val=NB - NQB)
        u0v = nc.sync.snap(regs[1], min_val=0, max_val=(NB - NQB) * 64)
        t0v = nc.sync.snap(regs[2], min_val=0, max_val=(S - T) // 8)
        p.prefix = "a_"
        emit_l0a(nc, p, S, A)
        p.prefix = "b_"
        emit_l0b(nc, p, S, A)
        csem = p.dma_sem()
        v = lambda ap, b: ap.rearrange("(a b) t -> a (b t)", b=b)
        p.dma("sp", v(A["own_x1T"], 16), v(A["x1T"], 16)[bass.ds(u0v + 64, NQB * 64), :], sem=csem)
        p.dma("sp", v(A["own_x1"], 8), v(A["x1"], 8)[bass.ds(t0v, T // 8), :], sem=csem)
        p.dma("sp", v(A["own_HL"], 1024), v(A["x1HL"], 1024)[bass.ds(b0v, NQB), :], sem=csem)
        p.dma("sp", v(A["own_HR"], 1024), v(A["x1HR"], 1024)[bass.ds(b0v + 1, NQB), :], sem=csem)
        p.dma("sp", v(A["own_pos"], 32), v(A["posb"], 32)[bass.ds(b0v, NQB), :], sem=csem)
        barrier(p)
        p.prefix = "c_"
        emit_l1(nc, p, S, A)
        p.es = es
        p.finish()
    return nc


def prep_fused(inputs, S=SEQ):
    f = lambda a: np.asarray(a, dtype=np.float32)
    x = f(inputs["x"])[:, :S]
    positions = np.asarray(inputs["positions"], dtype=np.int32)[:, :S]
    T = S // 4
    l0a = prep_l0a(x, f(inputs["ev_w_in"]), f(inputs["ev_conv_w"]), f(inputs["ev_conv_b"]), f(inputs["ev_a_log"]),
                   f(inputs["ev_dt_bias"]), f(inputs["ev_d_skip"]), S=S)
    w_in = f(inputs["ev_w_in"])[0]
    w1 = np.ascontiguousarray(np.concatenate([w_in[:, 0:1024], w_in[:, 3104:7200]], axis=1))
    wout = np.ascontiguousarray(f(inputs["ev_w_out"])[0])
    normg = np.ascontiguousarray(f(inputs["ev_norm_g"])[0].reshape(8, 128).T)
    scw = np.ascontiguousarray(f(inputs["ev_sc_conv_w"])[0].reshape(3, 8, 128).transpose(2, 1, 0).reshape(128, 24))
    lng = np.ascontiguousarray(np.broadcast_to(f(inputs["ev_ln_g"])[0][None, :], (128, D)))
    lnb = np.ascontiguousarray(np.broadcast_to(f(inputs["ev_ln_b"])[0][None, :], (128, D)))
    dummy_x1 = np.zeros((2, 16, D), np.float32)
    l1 = prep_l1(None, positions, f(inputs["od_w_in"]), f(inputs["od_q_norm_g"]), f(inputs["od_w_uq"]), f(inputs["od_kv_norm_g"]),
                 f(inputs["od_w_ukv"]), f(inputs["od_pool_w"]), f(inputs["od_pool_scale"]), f(inputs["od_w_out"]),
                 f(inputs["od_ln_g"]), f(inputs["od_ln_b"]), S=S)
    xtoks = [np.ascontiguousarray(x[b]) for b in range(2)]
    maps = []
    for c in range(NCORES):
        b, q = c // 4, c % 4
        m = {"xT": l0a[4 * b]["xT"], "xtok": xtoks[b], "cst": l0a[0]["cst"], "msk": l0a[0]["msk"],
             "w1": w1, "wout": wout, "normg": normg, "scw": scw, "lng": lng, "lnb": lnb,
             "off": np.array([[q * T // 512, (q * T // 512) * 64, q * T // 8, 0]], np.int32)}
        for g in range(4):
            src = l0a[4 * b + g]
            m["wg%d" % g] = src["wg"]
            m["cvw%d" % g] = src["cvw"]
            m["cvb%d" % g] = src["cvb"]
            m["hp%d" % g] = src["hp"]
        lm = l1[c]
        for k in ("posb", "w_cq", "w_kv", "w_kr", "w_g3", "w_uq", "w_ukv", "w_pool", "sm_c", "sel"):
            m[k] = lm[k]
        m["wout_od"] = lm["wout"]
        m["lng_od"] = lm["lng"]
        m["lnb_od"] = lm["lnb"]
        maps.append(m)
    return maps


def kernel(**inputs):
    T = SEQ // 4
    maps = prep_fused(inputs)
    res = run_bass_kernel_spmd(build_fused(), maps, core_ids=list(range(NCORES)))
    out = np.empty((2, SEQ, D), np.float32)
    for c in range(NCORES):
        out[c // 4, (c % 4) * T:(c % 4 + 1) * T, :] = res.results[c]["out"]
    return out
```

```python
import numpy as np
import concourse.bass as bass
import concourse.mybir as mybir
from concourse.bass_utils import run_bass_kernel_spmd
from contextlib import ExitStack

F32 = mybir.dt.float32
BF16 = mybir.dt.bfloat16
I32 = mybir.dt.int32
AF = mybir.ActivationFunctionType
ALU = mybir.AluOpType
AX = mybir.AxisListType

SAME_ENGINE_SYNC = True

D = 1024
SEQ = 16384
NCORES = 8
ALPHA = 4 ** 0.25
EPS = 1e-5


class Tok:
    __slots__ = ("w", "r", "name")

    def __init__(self, name=""):
        self.w = None
        self.r = {}
        self.name = name


class Prog:
    def __init__(self, nc, es):
        self.nc = nc
        self.es = es
        self.es_top = es
        self.eng = {"pe": nc.tensor, "act": nc.scalar, "dve": nc.vector,
                    "pool": nc.gpsimd, "sp": nc.sync}
        self.sems = {}
        self.cnt = {}
        for k in self.eng:
            self.sems[k] = es.enter_context(nc.semaphore("s_" + k))
            self.cnt[k] = 0
        self.seen = {k: {} for k in self.eng}
        self.ndma = 0
        self.out_dma = []
        self.n_ops = 0
        self.uid = 0

    prefix = ""

    def sb(self, name, shape, dt):
        return self.es.enter_context(self.nc.sbuf_tensor(self.prefix + name, list(shape), dt))

    def ps(self, name, shape, dt=F32):
        return self.es.enter_context(self.nc.psum_tensor(self.prefix + name, list(shape), dt))

    def dma_sem(self):
        k = "d%d" % self.ndma
        self.ndma += 1
        self.sems[k] = self.es_top.enter_context(self.nc.semaphore("s_" + k))
        self.cnt[k] = 0
        return k

    def _wait(self, e, deps):
        for (k, v) in deps:
            if k == e:
                if not SAME_ENGINE_SYNC or e == "pe" or e == "sp":
                    continue
            if self.seen[e].get(k, 0) >= v:
                continue
            self.eng[e].wait_ge(self.sems[k], v)
            self.seen[e][k] = v

    def _deps(self, r, w):
        m = {}
        for t in r:
            if t.w is not None:
                k, v = t.w
                if m.get(k, 0) < v:
                    m[k] = v
        for t in w:
            if t.w is not None:
                k, v = t.w
                if m.get(k, 0) < v:
                    m[k] = v
            for k, v in t.r.items():
                if m.get(k, 0) < v:
                    m[k] = v
        return list(m.items())

    def op(self, e, fn, r=(), w=(), multi=False):
        deps = self._deps(r, w)
        att = None
        if e != "pe" and not multi:
            need = [(k, v) for (k, v) in deps
                    if not (k == e and not SAME_ENGINE_SYNC) and self.seen[e].get(k, 0) < v]
            if need:
                att = need[-1]
                self._wait(e, need[:-1])
        else:
            self._wait(e, deps)
        ins = fn(self.eng[e])
        if att is not None:
            ins._wait_ge(self.sems[att[0]], att[1])
            self.seen[e][att[0]] = att[1]
        self.cnt[e] += 1
        v = self.cnt[e]
        ins.then_inc(self.sems[e], 1)
        for t in r:
            if t.r.get(e, 0) < v:
                t.r[e] = v
        for t in w:
            t.w = (e, v)
            t.r = {}
        self.n_ops += 1
        return ins

    def dma(self, q, out, in_, r=(), w=(), sem=None, is_out=False, nowait=False, **kw):
        if not nowait:
            self._wait(q, self._deps(r, w))
        ins = self.eng[q].dma_start(out=out, in_=in_, **kw)
        self.cnt[sem] += 16
        v = self.cnt[sem]
        ins.then_inc(self.sems[sem], 16)
        for t in r:
            if t.r.get(sem, 0) < v:
                t.r[sem] = v
        for t in w:
            t.w = (sem, v)
            t.r = {}
        if is_out:
            self.out_dma.append((sem, v))
        return ins

    def finish(self, e="sp"):
        m = {}
        for k, v in self.out_dma:
            if m.get(k, 0) < v:
                m[k] = v
        for k, v in m.items():
            self.eng[e].wait_ge(self.sems[k], v)


class Ring:
    def __init__(self, p, name, shape, dt, n, space="sb", dma=False):
        self.bufs = []
        for i in range(n):
            t = p.sb("%s%d" % (name, i), shape, dt) if space == "sb" else p.ps("%s%d" % (name, i), shape, dt)
            self.bufs.append((t, Tok(name + str(i)), p.dma_sem() if dma else None))
        self.i = 0

    def next(self):
        b = self.bufs[self.i % len(self.bufs)]
        self.i += 1
        return b


def load_cast_weight(p, src, dst, stage, K, C, engines=("act", "dve"), cw=1024, tok=None):
    n = 0
    for k in range(K):
        for c0 in range(0, C, cw):
            c1 = min(C, c0 + cw)
            st, stok, ssem = stage.next()
            p.dma("sp", st[:, 0:c1 - c0], src[k * 128:(k + 1) * 128, c0:c1], w=[stok], sem=ssem)
            e = engines[n % len(engines)]
            n += 1
            if e == "act":
                p.op(e, lambda en: en.copy(out=dst[:, k, c0:c1], in_=st[:, 0:c1 - c0]), r=[stok], w=[tok])
            else:
                p.op(e, lambda en: en.tensor_copy(out=dst[:, k, c0:c1], in_=st[:, 0:c1 - c0]), r=[stok], w=[tok])


PI = float(np.pi)
TWO_PI = float(2 * np.pi)
C1 = 6.28125
C2 = float(2 * np.pi - 6.28125)
ATT_SCALE = float(96 ** -0.5)
NEG = -30000.0
L0B_TB = 256


def barrier(p):
    for e in p.eng:
        for k, v in p.cnt.items():
            if k != e and v > 0 and p.seen[e].get(k, 0) < v:
                p.eng[e].wait_ge(p.sems[k], v)
                p.seen[e][k] = v


def layer_norm_tail(p, o, t_o, xk, t_xk, rr, r2, junk, st_r, lng_s, lnb_s, t_c, out_ap, post=None, is_out=True, gb_eng="pool"):
    r, t_r, _ = rr.next()
    p.op("dve", lambda e: e.scalar_tensor_tensor(out=r[:], in0=xk[:], scalar=float(ALPHA), in1=o[:], op0=ALU.mult, op1=ALU.add),
         r=[t_xk, t_o], w=[t_r])
    st, t_st, _ = st_r.next()
    jk, t_jk, _ = junk.next()
    p.op("act", lambda e: e.activation(out=jk[:], in_=r[:], func=AF.Identity, accum_out=st[:, 0:1]), r=[t_r], w=[t_jk, t_st], multi=True)
    p.op("act", lambda e: e.activation(out=jk[:], in_=r[:], func=AF.Square, accum_out=st[:, 1:2]), r=[t_r], w=[t_jk, t_st], multi=True)
    p.op("dve", lambda e: e.tensor_scalar(out=st[:, 2:3], in0=st[:, 0:1], scalar1=1.0 / D, scalar2=None, op0=ALU.mult), r=[t_st], w=[t_st])
    p.op("dve", lambda e: e.tensor_tensor(out=st[:, 3:4], in0=st[:, 2:3], in1=st[:, 2:3], op=ALU.mult), r=[t_st], w=[t_st])
    p.op("dve", lambda e: e.scalar_tensor_tensor(out=st[:, 4:5], in0=st[:, 1:2], scalar=1.0 / D, in1=st[:, 3:4], op0=ALU.mult, op1=ALU.subtract),
         r=[t_st], w=[t_st])
    p.op("dve", lambda e: e.tensor_scalar(out=st[:, 4:5], in0=st[:, 4:5], scalar1=float(EPS), scalar2=None, op0=ALU.add), r=[t_st], w=[t_st])
    p.op("act", lambda e: e.activation(out=st[:, 5:6], in_=st[:, 4:5], func=AF.Ln), r=[t_st], w=[t_st])
    p.op("act", lambda e: e.activation(out=st[:, 6:7], in_=st[:, 5:6], func=AF.Exp, scale=-0.5), r=[t_st], w=[t_st])
    q, t_q, osem = r2.next()
    p.op("dve", lambda e: e.tensor_scalar(out=q[:], in0=r[:], scalar1=st[:, 2:3], scalar2=st[:, 6:7], op0=ALU.subtract, op1=ALU.mult),
         r=[t_r, t_st], w=[t_q])
    p.op(gb_eng, lambda e: e.tensor_tensor(out=q[:], in0=q[:], in1=lng_s[:], op=ALU.mult), r=[t_c], w=[t_q])
    p.op(gb_eng, lambda e: e.tensor_tensor(out=q[:], in0=q[:], in1=lnb_s[:], op=ALU.add), r=[t_c], w=[t_q])
    if post is not None:
        post(q, t_q)
    p.dma("act", out_ap, q[:], r=[t_q], w=[], sem=osem, is_out=is_out)


def emit_l0b(nc, p, T, A):
    TB = L0B_TB
    NB = T // TB
    xT, xtok, yaT, w1, wout = A["xT1"], A["xtok"], A["yaT"], A["w1"], A["wout"]
    normg, scw, lng, lnb = A["normg"], A["scw"], A["lng"], A["lnb"]
    out, x1T, cst = A["x1"], A["x1T"], A["cst"]
    x1HL, x1HR = A["x1HL"], A["x1HR"]

    def halo_v(tab, bnd):
        return tab[bnd * 1024:(bnd + 1) * 1024, :].rearrange("(k p) t -> p k t", p=128)
    yaT_v = yaT.rearrange("(k p) t -> p k t", p=128)
    NB5 = T // 512

    def x1T_blk(blk, c0, n):
        return x1T[blk * 1024:(blk + 1) * 1024, c0:c0 + n].rearrange("(k p) t -> p k t", p=128)

    with ExitStack() as es:
        p.es = es
        w1b = p.sb("w1b", [128, 8, 5120], BF16)
        woutb = p.sb("woutb", [128, 16, D], BF16)
        t_w1b, t_woutb = Tok(), Tok()
        stage = Ring(p, "wst", [128, 1024], F32, 1, dma=True)
        normg_s = p.sb("normg_s", [128, 8], F32)
        scw_s = p.sb("scw_s", [128, 24], F32)
        lng_s = p.sb("lng_s", [128, D], F32)
        lnb_s = p.sb("lnb_s", [128, D], F32)
        ones_f = p.sb("ones_f", [128, 128], F32)
        t_c = Tok()
        dc = p.dma_sem()
        p.dma("sp", normg_s[:], normg[:, :], w=[t_c], sem=dc)
        p.dma("sp", scw_s[:], scw[:, :], w=[t_c], sem=dc)
        p.dma("sp", lng_s[:], lng[:, :], w=[t_c], sem=dc)
        p.dma("sp", lnb_s[:], lnb[:, :], w=[t_c], sem=dc)
        t_ones = Tok()
        p.op("dve", lambda e: e.memset(ones_f[:], 1.0), w=[t_ones])
        idf = p.sb("idf", [128, 128], F32)
        p.dma("sp", idf[:], cst[:, 256:384], w=[t_c], sem=dc)
        zt = p.sb("zt", [128, 8, 8], F32)
        t_zt = Tok()
        p.op("dve", lambda e: e.memset(zt[:], 0.0), w=[t_zt])
        zsem = p.dma_sem()
        p.dma("sp", halo_v(x1HL, 0), zt[:], r=[t_zt], sem=zsem)
        p.dma("sp", halo_v(x1HR, NB5), zt[:], r=[t_zt], sem=zsem)
        xtt_r = Ring(p, "xtt", [128, 8, 128], F32, 1, dma=True)
        load_cast_weight(p, w1, w1b, stage, 8, 5120, tok=t_w1b)
        load_cast_weight(p, wout, woutb, stage, 16, D, tok=t_woutb)

        xst = Ring(p, "xst", [128, TB + 2], F32, 3, dma=True)
        xb_r = Ring(p, "xb", [128, 8, TB + 2], BF16, 2)
        yst = Ring(p, "yst", [128, 8, TB], F32, 2, dma=True)
        xtk = Ring(p, "xtk", [128, D], F32, 2, dma=True)
        pp = Ring(p, "pp", [128, 512], F32, 3, space="ps")
        pss = Ring(p, "pss", [128, 512], F32, 1, space="ps")
        po = Ring(p, "po", [128, D], F32, 2, space="ps")
        deferred = []

        def do_post(q, t_q, tok0):
            xtt, t_xtt, xtt_sem = xtt_r.next()
            for hf in range(2):
                tp, t_tp, _ = pp.next()
                for kq in range(4):
                    kk = hf * 4 + kq
                    p.op("pe", lambda e: e.transpose(tp[:, kq * 128:(kq + 1) * 128], q[:, kk * 128:(kk + 1) * 128], idf[:]),
                         r=[t_q, t_c], w=[t_tp])
                p.op("act", lambda e: e.copy(out=xtt[:, hf * 4:(hf + 1) * 4, :], in_=tp[:].rearrange("p (k t) -> p k t", k=4)),
                     r=[t_tp], w=[t_xtt])
            p.dma("act", x1T_blk(tok0 // 512 + 1, tok0 % 512, 128), xtt[:], r=[t_xtt], sem=xtt_sem)
            if tok0 % 512 == 0:
                p.dma("act", halo_v(x1HR, tok0 // 512), xtt[:, :, 0:8], r=[t_xtt], sem=xtt_sem)
            if (tok0 + 128) % 512 == 0:
                p.dma("act", halo_v(x1HL, (tok0 + 128) // 512), xtt[:, :, 120:128], r=[t_xtt], sem=xtt_sem)

        def flush_posts():
            while deferred:
                do_post(*deferred.pop(0))
        g_all = p.sb("g_all", [128, 8, TB], F32)
        t_g = [Tok() for _ in range(8)]
        cat = p.sb("cat", [128, 16, TB], BF16)
        t_cat = [Tok() for _ in range(16)]
        tmp = Ring(p, "tmp", [128, TB + 2], F32, 8)
        rstd = p.sb("rstd", [128, TB], F32)
        t_rstd = Tok()
        halo = Ring(p, "halo", [128, 4], F32, 2)
        rr = Ring(p, "rr", [128, D], F32, 1)
        r2 = Ring(p, "r2", [128, D], F32, 2, dma=True)
        junk = Ring(p, "junk", [128, D], BF16, 1)
        st_r = Ring(p, "stat", [128, 8], F32, 4)
        def load_blk(bj):
            tj = bj * TB
            xb_, t_xb_, _ = xb_r.next()
            for k in range(8):
                st, stok, ssem = xst.next()
                p.dma("sp", st[:], xT[k * 128:(k + 1) * 128, tj:tj + TB + 2], w=[stok], sem=ssem)
                p.op("pool", lambda e: e.tensor_copy(out=xb_[:, k, :], in_=st[:]), r=[stok], w=[t_xb_])
            ya_, t_ya_, ya_sem_ = yst.next()
            p.dma("sp", ya_[:], yaT_v[:, :, tj:tj + TB], w=[t_ya_], sem=ya_sem_)
            return xb_, t_xb_, ya_, t_ya_
        nxt = load_blk(0)
        for bi in range(NB):
            t0 = bi * TB
            xb, t_xb, ya, t_ya = nxt
            if bi + 1 < NB:
                nxt = load_blk(bi + 1)

            ss, t_ss, _ = pss.next()
            for j in range(8):
                z, t_z, _ = pp.next()
                for k in range(8):
                    p.op("pe", lambda e: e.matmul(z[:, 0:TB], lhsT=w1b[:, k, j * 128:(j + 1) * 128],
                                                  rhs=xb[:, k, 1:TB + 1], start=(k == 0), stop=(k == 7)),
                         r=[t_w1b, t_xb], w=[t_z])
                sz, t_sz, _ = tmp.next()
                p.op("act", lambda e: e.activation(out=sz[:, 0:TB], in_=z[:, 0:TB], func=AF.Silu), r=[t_z], w=[t_sz])
                p.op("dve", lambda e: e.tensor_tensor(out=g_all[:, j, :], in0=sz[:, 0:TB], in1=ya[:, j, :], op=ALU.mult),
                     r=[t_sz, t_ya], w=[t_g[j]])
                sq, t_sq, _ = tmp.next()
                p.op("act", lambda e: e.activation(out=sq[:, 0:TB], in_=g_all[:, j, :], func=AF.Square), r=[t_g[j]], w=[t_sq])
                p.op("pe", lambda e: e.matmul(ss[:, 0:TB], lhsT=ones_f[:], rhs=sq[:, 0:TB], start=(j == 0), stop=(j == 7)),
                     r=[t_ones, t_sq], w=[t_ss])
            lnv, t_lnv, _ = tmp.next()
            p.op("dve", lambda e: e.tensor_scalar(out=lnv[:, 0:TB], in0=ss[:, 0:TB], scalar1=1.0 / 1024, scalar2=EPS,
                                                  op0=ALU.mult, op1=ALU.add), r=[t_ss], w=[t_lnv])
            p.op("act", lambda e: e.activation(out=lnv[:, 0:TB], in_=lnv[:, 0:TB], func=AF.Ln), r=[t_lnv], w=[t_lnv])
            p.op("act", lambda e: e.activation(out=rstd[:], in_=lnv[:, 0:TB], func=AF.Exp, scale=-0.5), r=[t_lnv], w=[t_rstd])
            for j in range(8):
                p.op("dve", lambda e: e.scalar_tensor_tensor(out=cat[:, j, :], in0=g_all[:, j, :], scalar=normg_s[:, j:j + 1],
                                                             in1=rstd[:], op0=ALU.mult, op1=ALU.mult),
                     r=[t_g[j], t_rstd, t_c], w=[t_cat[j]])

            flush_posts()
            for j in range(8):
                def proj(grp, lo, n, dst, t_dst, first=True, last=True):
                    for k in range(8):
                        p.op("pe", lambda e: e.matmul(dst, lhsT=w1b[:, k, grp * 1024 + j * 128:grp * 1024 + (j + 1) * 128],
                                                      rhs=xb[:, k, lo:lo + n], start=(k == 0), stop=(k == 7)),
                             r=[t_w1b, t_xb], w=[t_dst])
                cg, t_cg, _ = pp.next()
                proj(2, 0, TB + 2, cg[:, 0:TB + 2], t_cg)
                hh, t_hh, _ = pp.next()
                proj(3, 0, TB + 2, hh[:, 0:TB + 2], t_hh)
                cgs, t_cgs, _ = tmp.next()
                p.op("act", lambda e: e.copy(out=cgs[:, 0:TB + 2], in_=cg[:, 0:TB + 2]), r=[t_cg], w=[t_cgs])
                u, t_u, _ = tmp.next()
                p.op("dve", lambda e: e.tensor_tensor(out=u[:, 0:TB + 2], in0=cgs[:, 0:TB + 2], in1=hh[:, 0:TB + 2], op=ALU.mult),
                     r=[t_cgs, t_hh], w=[t_u])
                c, t_cc, _ = tmp.next()
                p.op("dve", lambda e: e.tensor_scalar(out=c[:, 0:TB], in0=u[:, 0:TB], scalar1=scw_s[:, j * 3:j * 3 + 1], scalar2=None,
                                                      op0=ALU.mult), r=[t_u, t_c], w=[t_cc])
                p.op("dve", lambda e: e.scalar_tensor_tensor(out=c[:, 0:TB], in0=u[:, 1:TB + 1], scalar=scw_s[:, j * 3 + 1:j * 3 + 2],
                                                             in1=c[:, 0:TB], op0=ALU.mult, op1=ALU.add), r=[t_u, t_c], w=[t_cc])
                p.op("dve", lambda e: e.scalar_tensor_tensor(out=c[:, 0:TB], in0=u[:, 2:TB + 2], scalar=scw_s[:, j * 3 + 2:j * 3 + 3],
                                                             in1=c[:, 0:TB], op0=ALU.mult, op1=ALU.add), r=[t_u, t_c], w=[t_cc])
                bg, t_bg, _ = pp.next()
                proj(1, 1, TB, bg[:, 0:TB], t_bg)
                gt, t_gt, _ = pp.next()
                proj(4, 1, TB, gt[:, 0:TB], t_gt)
                sg, t_sg, _ = tmp.next()
                p.op("act", lambda e: e.activation(out=sg[:, 0:TB], in_=gt[:, 0:TB], func=AF.Silu), r=[t_gt], w=[t_sg])
                p.op("dve", lambda e: e.tensor_tensor(out=c[:, 0:TB], in0=c[:, 0:TB], in1=bg[:, 0:TB], op=ALU.mult),
                     r=[t_bg], w=[t_cc])
                p.op("dve", lambda e: e.tensor_tensor(out=cat[:, 8 + j, :], in0=c[:, 0:TB], in1=sg[:, 0:TB], op=ALU.mult),
                     r=[t_cc, t_sg], w=[t_cat[8 + j]])

            outs = []
            for tt in range(TB // 128):
                xk, t_xk, xk_sem = xtk.next()
                p.dma("sp", xk[:], xtok[t0 + tt * 128:t0 + (tt + 1) * 128, :], w=[t_xk], sem=xk_sem)
                o, t_o, _ = po.next()
                for half in range(2):
                    for kc in range(16):
                        p.op("pe", lambda e: e.matmul(o[:, half * 512:(half + 1) * 512], lhsT=cat[:, kc, tt * 128:(tt + 1) * 128],
                                                      rhs=woutb[:, kc, half * 512:(half + 1) * 512], start=(kc == 0), stop=(kc == 15)),
                             r=[t_cat[kc], t_woutb], w=[t_o])
                outs.append((o, t_o, xk, t_xk))
            for tt in range(TB // 128):
                o, t_o, xk, t_xk = outs[tt]
                tok0 = t0 + tt * 128
                layer_norm_tail(p, o, t_o, xk, t_xk, rr, r2, junk, st_r, lng_s, lnb_s, t_c,
                                out[t0 + tt * 128:t0 + (tt + 1) * 128, :],
                                post=(lambda q, t_q, tok0=tok0: deferred.append((q, t_q, tok0))), is_out=False)
        flush_posts()
        barrier(p)


def emit_l0a(nc, p, S, A):
    BLK = 512
    NBLK = S // BLK
    xT, wg_all, cvw_all, cvb_all, hp_all, cst, msk, yaT = (A["xT"], A["wg"], A["cvw"], A["cvb"], A["hp"], A["cst"], A["msk"], A["yaT"])

    with ExitStack() as es:
        p.es = es
        wgb_l = [p.sb("wgb%d" % g, [128, 8, 520], BF16) for g in range(4)]
        t_wgb_l = [Tok() for g in range(4)]
        stage = Ring(p, "wst", [128, 520], F32, 2, dma=True)
        cvw_l = [p.sb("cvw_s%d" % g, [128, 16], F32) for g in range(4)]
        cvb_l = [p.sb("cvb_s%d" % g, [128, 4], F32) for g in range(4)]
        hp_l = [p.sb("hp_s%d" % g, [128, 24], F32) for g in range(4)]
        cst_s = p.sb("cst_s", [128, 512], F32)
        msk_s = p.sb("msk_s", [128, 1024], F32)
        mskb = p.sb("mskb", [128, 1024], BF16)
        identb = p.sb("identb", [128, 128], BF16)
        a_l = [p.sb("a_s%d" % g, [128, 8], F32) for g in range(4)]
        bias32_l = [p.sb("bias32_%d" % g, [128, 2, 4, 4], F32) for g in range(4)]
        a32_l = [p.sb("a32_%d" % g, [128, 2, 4, 4], F32) for g in range(4)]
        dsum_l = [p.sb("dsum%d" % g, [128, 4], F32) for g in range(4)]
        t_c = Tok()
        dc = p.dma_sem()
        for dst, src in ((cst_s, cst), (msk_s, msk)):
            p.dma("sp", dst[:], src[:, :], w=[t_c], sem=dc)
        U = cst_s[:, 0:128]
        UT = cst_s[:, 128:256]
        IDF = cst_s[:, 256:384]
        ONES = cst_s[:, 384:512]
        p.op("dve", lambda e: e.tensor_copy(out=mskb[:], in_=msk_s[:]), r=[t_c], w=[t_c])
        p.op("dve", lambda e: e.tensor_copy(out=identb[:], in_=IDF), r=[t_c], w=[t_c])

        xst = Ring(p, "xst", [128, BLK + 3], F32, 3, dma=True)
        xb_r = Ring(p, "xb", [128, 8, BLK + 3], BF16, 2)
        pG = Ring(p, "pG", [128, 512], F32, 6, space="ps")
        pH = Ring(p, "pHb", [128, 512], F32, 1, space="ps")
        pP = Ring(p, "pPp", [128, 512], F32, 1, space="ps")
        pre_r = Ring(p, "pre", [128, BLK + 3], F32, 3)
        cv_r = Ring(p, "cv", [128, BLK], F32, 2)
        xsf_r = Ring(p, "xsf", [128, 3, BLK], F32, 2)
        btb_r = Ring(p, "btb", [128, BLK], BF16, 2)
        ctb_r = Ring(p, "ctb", [128, BLK], BF16, 2)
        hs_r = Ring(p, "hs", [128, 16], F32, 2)
        dtv_r = Ring(p, "dtv", [128, 6, 16], F32, 2)
        sm_r = Ring(p, "sm", [128, 8, 4], F32, 3)
        W_r = Ring(p, "W", [128, 4, 128], F32, 2)
        E_r = Ring(p, "E", [128, 4, 128], F32, 2)
        M_r = Ring(p, "M", [128, 4, 128], BF16, 2)
        btk_r = Ring(p, "btk", [128, 128], BF16, 2)
        xd_r = Ring(p, "xd", [128, 256], BF16, 2)
        xdw_r = Ring(p, "xdw", [128, 256], BF16, 2)
        y_r = Ring(p, "y", [128, 256], F32, 3, dma=True)
        yT_r = Ring(p, "yT", [128, 256], F32, 3, dma=True)
        yt_r = Ring(p, "yt", [128, 256], F32, 3)
        yl_r = Ring(p, "yl", [128, 256], F32, 2, dma=True)
        H_l = [p.sb("H%d" % g, [128, 256], F32) for g in range(4)]
        Hb_l = [p.sb("Hb%d" % g, [128, 256], BF16) for g in range(4)]
        t_H_l = [Tok() for g in range(4)]
        t_Hb_l = [Tok() for g in range(4)]

        yds_r = Ring(p, "yds", [128, 256], F32, 2)

        def front(k, g, blk, c, dtv, t_dtv, xsf, t_xsf, btb, t_btb, ctb, t_ctb, Tri, t_ya):
            gc = blk * 4 + c
            cs_ = slice(c * 128, (c + 1) * 128)
            dA = dtv[:, 5, 4 * c:4 * c + 4]
            dtc = dtv[:, 4, 4 * c:4 * c + 4]
            W, t_W, _ = W_r.next()
            p.op("dve", lambda e: e.tensor_tensor(out=W[:], in0=Tri.unsqueeze(1).to_broadcast([128, 4, 128]),
                                                  in1=dA.unsqueeze(2).to_broadcast([128, 4, 128]), op=ALU.mult),
                 r=[t_dtv, t_c], w=[t_W])
            Eb, t_Eb, _ = pG.next()
            p.op("pe", lambda e: e.matmul(Eb[:], lhsT=ONES, rhs=W[:].rearrange("p r l -> p (r l)"), start=True, stop=False),
                 r=[t_W, t_c], w=[t_Eb])
            p.op("pe", lambda e: e.matmul(Eb[:], lhsT=identb[:], rhs=mskb[:, 512 * k:512 * (k + 1)], start=False, stop=True),
                 r=[t_c], w=[t_Eb])
            T_, t_T, _ = pG.next()
            Sm, t_Sm = T_[:, 384:512], t_T
            p.op("pe", lambda e: e.matmul(Sm[:, 0:4], lhsT=Tri, rhs=dA, start=True, stop=True), r=[t_dtv, t_c], w=[t_Sm])
            p.op("pe", lambda e: e.matmul(Sm[:, 4:8], lhsT=ONES, rhs=dA, start=True, stop=True), r=[t_dtv, t_c], w=[t_Sm])
            for m in range(3):
                p.op("pe", lambda e: e.transpose(T_[:, m * 128:(m + 1) * 128], xsf[:, m, cs_], IDF), r=[t_xsf, t_c], w=[t_T])
            Cb, t_Cb, _ = pG.next()
            p.op("pe", lambda e: e.matmul(Cb[:, 0:128], lhsT=btb[:, cs_], rhs=ctb[:, cs_], start=True, stop=True),
                 r=[t_btb, t_ctb], w=[t_Cb])
            sm, t_sm, _ = sm_r.next()
            CS, TOT, NCS, ECS, DTE, ETOT, DTW, D_ = [sm[:, i, :] for i in range(8)]
            p.op("act", lambda e: e.copy(out=sm[:, 0:2, :], in_=Sm[:, 0:8].rearrange("p (a r) -> p a r", a=2)), r=[t_Sm], w=[t_sm])
            p.op("dve", lambda e: e.tensor_scalar(out=NCS, in0=CS, scalar1=-1.0, scalar2=None, op0=ALU.mult), r=[t_sm], w=[t_sm])
            E, t_E, _ = E_r.next()
            for r_ in range(4):
                p.op("act", lambda e: e.activation(out=E[:, r_, :], in_=Eb[:, r_ * 128:(r_ + 1) * 128], func=AF.Exp,
                                                   bias=sm[:, 2, r_:r_ + 1]), r=[t_Eb, t_sm], w=[t_E])
            btk, t_btk, _ = btk_r.next()
            p.op("act", lambda e: e.copy(out=btk[:], in_=T_[:, 256:384]), r=[t_T], w=[t_btk])
            M, t_M, _ = M_r.next()
            p.op("dve", lambda e: e.tensor_tensor(out=M[:], in0=E[:], in1=Cb[:, 0:128].unsqueeze(1).to_broadcast([128, 4, 128]),
                                                  op=ALU.mult), r=[t_E, t_Cb], w=[t_M])
            p.op("act", lambda e: e.activation(out=ECS, in_=CS, func=AF.Exp), r=[t_sm], w=[t_sm])
            p.op("dve", lambda e: e.tensor_tensor(out=D_, in0=TOT, in1=CS, op=ALU.subtract), r=[t_sm], w=[t_sm])
            p.op("act", lambda e: e.activation(out=DTE, in_=D_, func=AF.Exp), r=[t_sm], w=[t_sm])
            p.op("act", lambda e: e.activation(out=ETOT, in_=TOT, func=AF.Exp), r=[t_sm], w=[t_sm])
            p.op("dve", lambda e: e.tensor_tensor(out=DTW, in0=DTE, in1=dtc, op=ALU.mult), r=[t_sm, t_dtv], w=[t_sm])
            xd, t_xd, _ = xd_r.next()
            xdw, t_xdw, _ = xdw_r.next()
            xs_tok = T_[:, 0:256].rearrange("p (r q) -> p r q", r=4)
            p.op("dve", lambda e: e.tensor_tensor(out=xd[:].rearrange("p (r q) -> p r q", r=4), in0=xs_tok,
                                                  in1=dtc.unsqueeze(2).to_broadcast([128, 4, 64]), op=ALU.mult),
                 r=[t_T, t_dtv], w=[t_xd])
            p.op("dve", lambda e: e.tensor_tensor(out=xdw[:].rearrange("p (r q) -> p r q", r=4), in0=xs_tok,
                                                  in1=DTW.unsqueeze(2).to_broadcast([128, 4, 64]), op=ALU.mult),
                 r=[t_T, t_sm], w=[t_xdw])
            yds, t_yds = None, None
            if k == 0:
                yds, t_yds, _ = yds_r.next()
                p.op("dve", lambda e: e.tensor_tensor(out=yds[:].rearrange("p (r q) -> p r q", r=4), in0=xs_tok,
                                                      in1=dsum_l[g][:].unsqueeze(2).to_broadcast([128, 4, 64]), op=ALU.mult),
                     r=[t_T, t_c], w=[t_yds])
            return dict(k=k, g=g, gc=gc, cs_=cs_, ctb=ctb, t_ctb=t_ctb, M=M, t_M=t_M, xd=xd, t_xd=t_xd, xdw=xdw, t_xdw=t_xdw,
                        btk=btk, t_btk=t_btk, ECS=ECS, ETOT=ETOT, t_sm=t_sm, yds=yds, t_yds=t_yds, t_ya=t_ya)

        def back(s_):
            k, g, gc, cs_ = s_["k"], s_["g"], s_["gc"], s_["cs_"]
            ctb, t_ctb, M, t_M, xd, t_xd, xdw, t_xdw = (s_["ctb"], s_["t_ctb"], s_["M"], s_["t_M"], s_["xd"], s_["t_xd"],
                                                        s_["xdw"], s_["t_xdw"])
            btk, t_btk, ECS, ETOT, t_sm, yds, t_yds, t_ya = (s_["btk"], s_["t_btk"], s_["ECS"], s_["ETOT"], s_["t_sm"],
                                                             s_["yds"], s_["t_yds"], s_["t_ya"])
            H, Hb, t_H, t_Hb = H_l[g], Hb_l[g], t_H_l[g], t_Hb_l[g]
            Y, t_Y, _ = pG.next()
            for r_ in range(4):
                p.op("pe", lambda e: e.matmul(Y[:, r_ * 64:(r_ + 1) * 64], lhsT=M[:, r_, :], rhs=xd[:, r_ * 64:(r_ + 1) * 64],
                                              start=True, stop=True), r=[t_M, t_xd], w=[t_Y])
            p.op("pe", lambda e: e.matmul(Y[:, 256:512], lhsT=ctb[:, cs_], rhs=Hb[:], start=True, stop=True),
                 r=[t_ctb, t_Hb], w=[t_Y])
            ST, t_ST, _ = pG.next()
            p.op("pe", lambda e: e.matmul(ST[:, 0:256], lhsT=btk[:], rhs=xdw[:], start=True, stop=True),
                 r=[t_btk, t_xdw], w=[t_ST])
            yt, t_yt, _ = yt_r.next()
            p.op("dve", lambda e: e.tensor_tensor(out=yt[:].rearrange("p (r q) -> p r q", r=4),
                                                  in0=Y[:, 256:512].rearrange("p (r q) -> p r q", r=4),
                                                  in1=ECS.unsqueeze(2).to_broadcast([128, 4, 64]), op=ALU.mult),
                 r=[t_Y, t_sm], w=[t_yt])
            yo, t_yo, yo_sem = y_r.next()
            p.op("dve", lambda e: e.tensor_tensor(out=yo[:], in0=yt[:], in1=Y[:, 0:256], op=ALU.add), r=[t_yt, t_Y], w=[t_yo])
            if k == 0:
                p.op("dve", lambda e: e.tensor_tensor(out=yo[:], in0=yo[:], in1=yds[:], op=ALU.add), r=[t_yds], w=[t_yo])
            p.op("dve", lambda e: e.tensor_tensor(out=H[:].rearrange("p (r q) -> p r q", r=4),
                                                  in0=H[:].rearrange("p (r q) -> p r q", r=4),
                                                  in1=ETOT.unsqueeze(2).to_broadcast([128, 4, 64]), op=ALU.mult),
                 r=[t_sm], w=[t_H])
            p.op("dve", lambda e: e.tensor_tensor(out=H[:], in0=H[:], in1=ST[:, 0:256], op=ALU.add), r=[t_ST], w=[t_H])
            p.op("act", lambda e: e.copy(out=Hb[:], in_=H[:]), r=[t_H], w=[t_Hb])
            ydst = yaT[g * 256:(g + 1) * 256, gc * 128:(gc + 1) * 128].rearrange("(j q) t -> q j t", q=128)
            T2, t_T2, _ = pG.next()
            for j in range(2):
                p.op("pe", lambda e: e.transpose(T2[:, j * 128:(j + 1) * 128], yo[:, j * 128:(j + 1) * 128], IDF), r=[t_yo, t_c], w=[t_T2])
            yoT, t_yoT, yoT_sem = yT_r.next()
            if k == 0:
                p.op("act", lambda e: e.copy(out=yoT[:], in_=T2[:, 0:256]), r=[t_T2], w=[t_yoT])
            else:
                yl, t_yl, yl_sem = yl_r.next()
                p.dma("act", yl[:].rearrange("q (j t) -> q j t", j=2), ydst, r=[t_ya[gc]], w=[t_yl], sem=yl_sem)
                p.op("dve", lambda e: e.tensor_tensor(out=yoT[:], in0=T2[:, 0:256], in1=yl[:], op=ALU.add), r=[t_T2, t_yl], w=[t_yoT])
            p.dma("act", ydst, yoT[:].rearrange("q (j t) -> q j t", j=2), r=[t_yoT], w=[t_ya[gc]], sem=yoT_sem)

        for g in range(4):
            cvw_s, cvb_s, hp_s, a_s, bias32, a32, dsum = cvw_l[g], cvb_l[g], hp_l[g], a_l[g], bias32_l[g], a32_l[g], dsum_l[g]
            for dst, src in ((cvw_s, cvw_all[g]), (cvb_s, cvb_all[g]), (hp_s, hp_all[g])):
                p.dma("sp", dst[:], src, w=[t_c], sem=dc)
            p.op("act", lambda e: e.activation(out=a_s[:], in_=hp_s[:, 0:8], func=AF.Exp), r=[t_c], w=[t_c])
            p.op("dve", lambda e: e.tensor_scalar(out=a_s[:], in0=a_s[:], scalar1=-1.0, scalar2=None, op0=ALU.mult), r=[t_c], w=[t_c])
            for k in range(2):
                for c in range(4):
                    p.op("dve", lambda e: e.tensor_copy(out=bias32[:, k, c, :], in_=hp_s[:, 8 + 4 * k:12 + 4 * k]), r=[t_c], w=[t_c])
                    p.op("dve", lambda e: e.tensor_copy(out=a32[:, k, c, :], in_=a_s[:, 4 * k:4 * k + 4]), r=[t_c], w=[t_c])
            p.op("dve", lambda e: e.tensor_tensor(out=dsum[:], in0=hp_s[:, 16:20], in1=hp_s[:, 20:24], op=ALU.add), r=[t_c], w=[t_c])
            load_cast_weight(p, wg_all[g], wgb_l[g], stage, 8, 520, cw=520, engines=("dve", "act"), tok=t_wgb_l[g])
        t_ya_l = [[Tok() for _ in range(S // 128)] for g in range(4)]
        for k in range(2):
            pend = None
            for g in range(4):
                p.op("dve", lambda e: e.memset(H_l[g][:], 0.0), w=[t_H_l[g]])
                p.op("dve", lambda e: e.memset(Hb_l[g][:], 0.0), w=[t_Hb_l[g]])
            Tri = U if k == 0 else UT
            blocks = range(NBLK) if k == 0 else range(NBLK - 1, -1, -1)
            for blk in blocks:
                e0 = blk * BLK
                xb, t_xb, _ = xb_r.next()
                for kk in range(8):
                    st, stok, ssem = xst.next()
                    p.dma("sp", st[:], xT[kk * 128:(kk + 1) * 128, e0:e0 + BLK + 3], w=[stok], sem=ssem)
                    p.op("pool", lambda e: e.tensor_copy(out=xb[:, kk, :], in_=st[:]), r=[stok], w=[t_xb])
                for g in range(4):
                    wgb, t_wgb, cvw_s, cvb_s, bias32, a32, t_ya = wgb_l[g], t_wgb_l[g], cvw_l[g], cvb_l[g], bias32_l[g], a32_l[g], t_ya_l[g]
                    hb, t_hb, _ = pH.next()
                    for c in range(4):
                        for kk in range(8):
                            p.op("pe", lambda e: e.matmul(hb[:, 16 + 4 * c:20 + 4 * c], lhsT=xb[:, kk, 2 + c * 128:2 + (c + 1) * 128],
                                                          rhs=wgb[:, kk, 512 + 4 * k:516 + 4 * k], start=(kk == 0), stop=(kk == 7)),
                                 r=[t_xb, t_wgb], w=[t_hb])
                    dtv, t_dtv, _ = dtv_r.next()
                    V, AV, EE, LL, DT, DA = [dtv[:, i, :] for i in range(6)]
                    b32 = bias32[:, k, :, :].rearrange("p c r -> p (c r)")
                    A32 = a32[:, k, :, :].rearrange("p c r -> p (c r)")
                    p.op("dve", lambda e: e.tensor_tensor(out=V, in0=hb[:, 16:32], in1=b32, op=ALU.add), r=[t_hb, t_c], w=[t_dtv])
                    p.op("dve", lambda e: e.tensor_scalar(out=AV, in0=V, scalar1=-1.0, scalar2=None, op0=ALU.mult), r=[t_dtv], w=[t_dtv])
                    p.op("dve", lambda e: e.tensor_tensor(out=AV, in0=AV, in1=V, op=ALU.max), r=[t_dtv], w=[t_dtv])
                    p.op("act", lambda e: e.activation(out=EE, in_=AV, func=AF.Exp, scale=-1.0), r=[t_dtv], w=[t_dtv])
                    p.op("act", lambda e: e.activation(out=LL, in_=EE, func=AF.Ln, bias=1.0), r=[t_dtv], w=[t_dtv])
                    p.op("dve", lambda e: e.scalar_tensor_tensor(out=DT, in0=V, scalar=0.0, in1=LL, op0=ALU.max, op1=ALU.add), r=[t_dtv], w=[t_dtv])
                    p.op("dve", lambda e: e.tensor_tensor(out=DA, in0=DT, in1=A32, op=ALU.mult), r=[t_dtv, t_c], w=[t_dtv])

                    xsf, t_xsf, _ = xsf_r.next()
                    btb, t_btb, _ = btb_r.next()
                    ctb, t_ctb, _ = ctb_r.next()
                    for m in range(4):
                        P, t_P, _ = pP.next()
                        for kk in range(8):
                            p.op("pe", lambda e: e.matmul(P[:, 0:BLK], lhsT=wgb[:, kk, m * 128:(m + 1) * 128], rhs=xb[:, kk, 0:BLK],
                                                          start=(kk == 0), stop=(kk == 7)), r=[t_xb, t_wgb], w=[t_P])
                        for kk in range(8):
                            p.op("pe", lambda e: e.matmul(hb[:, 4 * m:4 * m + 3], lhsT=wgb[:, kk, m * 128:(m + 1) * 128],
                                                          rhs=xb[:, kk, BLK:BLK + 3], start=(kk == 0), stop=(kk == 7)),
                                 r=[t_xb, t_wgb], w=[t_hb])
                        pre, t_pre, _ = pre_r.next()
                        p.op("act", lambda e: e.copy(out=pre[:, 0:BLK], in_=P[:, 0:BLK]), r=[t_P], w=[t_pre])
                        p.op("act", lambda e: e.copy(out=pre[:, BLK:BLK + 3], in_=hb[:, 4 * m:4 * m + 3]), r=[t_hb], w=[t_pre])
                        cv, t_cv, _ = cv_r.next()
                        p.op("dve", lambda e: e.tensor_scalar(out=cv[:], in0=pre[:, 0:BLK], scalar1=cvw_s[:, 4 * m:4 * m + 1], scalar2=None,
                                                              op0=ALU.mult), r=[t_pre, t_c], w=[t_cv])
                        for tap in range(1, 4):
                            p.op("dve", lambda e: e.scalar_tensor_tensor(out=cv[:], in0=pre[:, tap:tap + BLK],
                                                                         scalar=cvw_s[:, 4 * m + tap:4 * m + tap + 1], in1=cv[:],
                                                                         op0=ALU.mult, op1=ALU.add), r=[t_pre, t_c], w=[t_cv])
                        if m < 3:
                            p.op("act", lambda e: e.activation(out=xsf[:, m, :], in_=cv[:], func=AF.Silu, bias=cvb_s[:, m:m + 1]),
                                 r=[t_cv, t_c], w=[t_xsf])
                            if m == 2:
                                p.op("act", lambda e: e.copy(out=btb[:], in_=xsf[:, 2, :]), r=[t_xsf], w=[t_btb])
                        else:
                            p.op("act", lambda e: e.activation(out=ctb[:], in_=cv[:], func=AF.Silu, bias=cvb_s[:, m:m + 1]),
                                 r=[t_cv, t_c], w=[t_ctb])

                    chunks = range(4) if k == 0 else range(3, -1, -1)
                    for c in chunks:
                        st_ = front(k, g, blk, c, dtv, t_dtv, xsf, t_xsf, btb, t_btb, ctb, t_ctb, Tri, t_ya)
                        if pend is not None:
                            back(pend)
                        pend = st_
            back(pend)
            pend = None
        barrier(p)


def l0a_consts():
    t = np.arange(128)
    U = (t[:, None] <= t[None, :]).astype(np.float32)
    UT = (t[:, None] >= t[None, :]).astype(np.float32)
    I = np.eye(128, dtype=np.float32)
    ones = np.ones((128, 128), np.float32)
    cst = np.ascontiguousarray(np.concatenate([U, UT, I, ones], axis=1))
    mf = np.where(t[None, :] < t[:, None], NEG, 0.0).astype(np.float32)
    mb = np.where(t[None, :] > t[:, None], NEG, 0.0).astype(np.float32)
    msk = np.ascontiguousarray(np.concatenate([np.tile(mf, (1, 4)), np.tile(mb, (1, 4))], axis=1))
    return cst, msk


def prep_l0a(x, ev_w_in, ev_conv_w, ev_conv_b, ev_a_log, ev_dt_bias, ev_d_skip, S=SEQ):
    w_in = ev_w_in[0]
    cw = ev_conv_w[0]
    cb = ev_conv_b[0]
    cst, msk = l0a_consts()
    maps = []
    xTs = []
    for b in range(2):
        xe = np.zeros((D, S + 3), np.float32)
        xe[:, 2:S + 2] = x[b, :S, :].T
        xTs.append(xe)
    for c in range(NCORES):
        b, g = c // 4, c % 4
        xs_cols = 1024 + g * 256 + np.arange(256)
        b_cols = 1024 + 1024 + g * 128 + np.arange(128)
        c_cols = 1024 + 1536 + g * 128 + np.arange(128)
        dt_cols = np.concatenate([3072 + k * 16 + 4 * g + np.arange(4) for k in range(2)])
        cols = np.concatenate([xs_cols, b_cols, c_cols, dt_cols])
        wg = np.ascontiguousarray(w_in[:, cols])
        xbc_idx = cols[:512] - 1024
        cvw = np.ascontiguousarray(cw[:, xbc_idx].reshape(4, 4, 128).transpose(2, 1, 0).reshape(128, 16))
        cvb = np.ascontiguousarray(cb[xbc_idx].reshape(4, 128).T)
        hsel = np.concatenate([np.stack([v[0][k, 4 * g:4 * g + 4] for k in range(2)]).reshape(-1)
                               for v in (ev_a_log, ev_dt_bias, ev_d_skip)])
        hp = np.ascontiguousarray(np.broadcast_to(hsel[None, :], (128, 24))).astype(np.float32)
        maps.append({"xT": xTs[b], "wg": wg, "cvw": cvw, "cvb": cvb, "hp": hp, "cst": cst, "msk": msk})
    return maps


def rope_tables(p, posi, t_posi, n, invf, sgn, t_c, tabs, cosd, sind, t_cos, t_sin):
    R = slice(64, 96)
    ang, t_a, _ = tabs.next()
    nf, t_n, _ = tabs.next()
    ni, t_ni, _ = tabs.next()
    mm, t_m, _ = tabs.next()
    A, N, M = ang[R, 0:n], nf[R, 0:n], mm[R, 0:n]
    NI = ni[R, 0:n].bitcast(I32)
    p.op("dve", lambda e: e.tensor_copy(out=A, in_=posi[R, 0:n]), r=[t_posi], w=[t_a])
    p.op("dve", lambda e: e.tensor_scalar(out=A, in0=A, scalar1=invf[R, 0:1], scalar2=None, op0=ALU.mult), r=[t_c], w=[t_a])
    p.op("dve", lambda e: e.tensor_scalar(out=N, in0=A, scalar1=1.0 / TWO_PI, scalar2=None, op0=ALU.mult), r=[t_a], w=[t_n])
    p.op("dve", lambda e: e.tensor_copy(out=NI, in_=N), r=[t_n], w=[t_ni])
    p.op("dve", lambda e: e.tensor_copy(out=N, in_=NI), r=[t_ni], w=[t_n])
    p.op("dve", lambda e: e.scalar_tensor_tensor(out=A, in0=N, scalar=-C1, in1=A, op0=ALU.mult, op1=ALU.add), r=[t_n], w=[t_a])
    p.op("dve", lambda e: e.scalar_tensor_tensor(out=A, in0=N, scalar=-C2, in1=A, op0=ALU.mult, op1=ALU.add), r=[t_n], w=[t_a])

    def wrap(X, t_x):
        p.op("dve", lambda e: e.tensor_scalar(out=M, in0=X, scalar1=PI, scalar2=None, op0=ALU.is_gt), r=[t_x], w=[t_m])
        p.op("dve", lambda e: e.scalar_tensor_tensor(out=X, in0=M, scalar=-TWO_PI, in1=X, op0=ALU.mult, op1=ALU.add), r=[t_m], w=[t_x])
        p.op("dve", lambda e: e.tensor_scalar(out=M, in0=X, scalar1=-PI, scalar2=None, op0=ALU.is_lt), r=[t_x], w=[t_m])
        p.op("dve", lambda e: e.scalar_tensor_tensor(out=X, in0=M, scalar=TWO_PI, in1=X, op0=ALU.mult, op1=ALU.add), r=[t_m], w=[t_x])
    wrap(A, t_a)
    p.op("act", lambda e: e.activation(out=N, in_=A, func=AF.Sin), r=[t_a], w=[t_n])
    p.op("dve", lambda e: e.tensor_scalar(out=sind, in0=N, scalar1=sgn[R, 0:1], scalar2=None, op0=ALU.mult), r=[t_n, t_c], w=[t_sin])
    p.op("dve", lambda e: e.tensor_scalar(out=A, in0=A, scalar1=PI / 2, scalar2=None, op0=ALU.add), r=[t_a], w=[t_a])
    wrap(A, t_a)
    p.op("act", lambda e: e.activation(out=cosd, in_=A, func=AF.Sin), r=[t_a], w=[t_cos])


def emit_l1(nc, p, S, A):
    T = S // 4
    BLK = 512
    NKB = S // BLK
    NQB = T // BLK
    NK128 = S // 128
    x1T, x1, posb, out = A["x1T"], A["x1"], A["posb"], A["out"]
    w_cq, w_kv, w_kr, w_g3, w_uq, w_ukv, w_pool, wout = (A["w_cq"], A["w_kv"], A["w_kr"], A["w_g3"], A["w_uq"], A["w_ukv"],
                                                         A["w_pool"], A["wout_od"])
    sm_c, sel, lng, lnb = A["sm_c"], A["sel"], A["lng_od"], A["lnb_od"]
    oxT, ox1, oHL, oHR, opos = A["own_x1T"], A["own_x1"], A["own_HL"], A["own_HR"], A["own_pos"]

    def xrows_static(blk, kk):
        return x1T[blk * 1024 + kk * 128:blk * 1024 + (kk + 1) * 128, :]

    def xrows_own(blk, kk):
        return oxT[blk * 1024 + kk * 128:blk * 1024 + (kk + 1) * 128, :]
    xtok = ox1

    with ExitStack() as es:
        p.es = es
        smc = p.sb("smc", [128, 80], F32)
        sel_s = p.sb("sel_s", [128, 384], F32)
        t_c = Tok()
        dc = p.dma_sem()
        p.dma("sp", smc[:], sm_c[:, :], w=[t_c], sem=dc)
        p.dma("sp", sel_s[:], sel[:, :], w=[t_c], sem=dc)
        QG, KVG, INVF, SGN, PSC = smc[:, 0:2], smc[:, 2:3], smc[:, 3:4], smc[:, 4:5], smc[:, 5:9]
        CORR = smc[:, 16:80]
        SEL_E, SEL_O, ONES = sel_s[:, 0:128], sel_s[:, 128:256], sel_s[:, 256:384]
        ycT = p.sb("ycT", [128, 4, T], BF16)
        t_yc = [[Tok() for _ in range(NQB)] for _ in range(8)]
        esA = ExitStack()
        p.es = esA
        ckvn = p.sb("ckvn", [128, S], BF16)
        t_ckvn = [Tok() for _ in range(NKB)]
        Kbuf = p.sb("Kbuf", [96, S], BF16)
        t_kn = [Tok() for _ in range(NKB)]
        t_kr = [Tok() for _ in range(NKB)]
        cqn = p.sb("cqn", [128, 2, T], BF16)
        t_cqn = [Tok() for _ in range(NQB)]
        cosq = p.sb("cosq", [96, T], BF16)
        sinq = p.sb("sinq", [96, T], BF16)
        t_cosq = [Tok() for _ in range(NQB)]
        t_sinq = [Tok() for _ in range(NQB)]
        wuqb = p.sb("wuqb", [128, 2, 1536], BF16)
        wukvb = p.sb("wukvb", [128, 1024], BF16)
        t_wuq, t_wukv = Tok(), Tok()

        with ExitStack() as es1:
            p.es = es1
            wst = Ring(p, "wst", [128, 1536], F32, 1, dma=True)
            wcqb = p.sb("wcqb", [128, 8, 256], BF16)
            wkvb = p.sb("wkvb", [128, 8, 128], BF16)
            wkrb = p.sb("wkrb", [128, 8, 192], BF16)
            t_wcq, t_wkv, t_wkr = Tok(), Tok(), Tok()
            load_cast_weight(p, w_cq, wcqb, wst, 8, 256, cw=256, tok=t_wcq)
            load_cast_weight(p, w_kv, wkvb, wst, 8, 128, cw=128, tok=t_wkv)
            load_cast_weight(p, w_kr, wkrb, wst, 8, 192, cw=192, tok=t_wkr)
            for kc in range(2):
                st, stok, ssem = wst.next()
                p.dma("sp", st[:, 0:1536], w_uq[kc * 128:(kc + 1) * 128, :], w=[stok], sem=ssem)
                p.op("dve", lambda e: e.tensor_scalar(out=wuqb[:, kc, :], in0=st[:, 0:1536], scalar1=QG[:, kc:kc + 1], scalar2=None,
                                                      op0=ALU.mult), r=[stok, t_c], w=[t_wuq])
            st, stok, ssem = wst.next()
            p.dma("sp", st[:, 0:1024], w_ukv[:, :], w=[stok], sem=ssem)
            p.op("dve", lambda e: e.tensor_scalar(out=wukvb[:], in0=st[:, 0:1024], scalar1=KVG, scalar2=None, op0=ALU.mult),
                 r=[stok, t_c], w=[t_wukv])

            xst = Ring(p, "xst", [128, BLK], F32, 3, dma=True)
            xb_r = Ring(p, "xb", [128, 8, BLK], BF16, 2)
            pos_r = Ring(p, "posr", [128, BLK], I32, 2, dma=True)
            tabs = Ring(p, "tabs", [128, BLK], F32, 4)
            cs_r = Ring(p, "csr", [128, BLK], F32, 2)
            sn_r = Ring(p, "snr", [128, BLK], F32, 2)
            sq_r = Ring(p, "sqr", [128, BLK], F32, 3)
            t1_r = Ring(p, "t1r", [128, BLK], F32, 2)
            pA = Ring(p, "pA", [128, 512], F32, 5, space="ps")
            pSS = Ring(p, "pSS", [128, 512], F32, 2, space="ps")

            def load_xblock(rows_fn, blk):
                xb, t_xb, _ = xb_r.next()
                for kk in range(8):
                    st, stok, ssem = xst.next()
                    p.dma("sp", st[:], rows_fn(blk, kk), w=[stok], sem=ssem)
                    if kk % 2 == 0:
                        p.op("act", lambda e: e.copy(out=xb[:, kk, :], in_=st[:]), r=[stok], w=[t_xb])
                    else:
                        p.op("pool", lambda e: e.tensor_copy(out=xb[:, kk, :], in_=st[:]), r=[stok], w=[t_xb])
                return xb, t_xb

            def rstd_of(ss, t_ss, nch):
                r_, t_r, _ = sq_r.next()
                p.op("dve", lambda e: e.tensor_scalar(out=r_[:], in0=ss[:], scalar1=1.0 / nch, scalar2=EPS, op0=ALU.mult, op1=ALU.add),
                     r=[t_ss], w=[t_r])
                p.op("act", lambda e: e.activation(out=r_[:], in_=r_[:], func=AF.Ln), r=[t_r], w=[t_r])
                p.op("act", lambda e: e.activation(out=r_[:], in_=r_[:], func=AF.Exp, scale=-0.5), r=[t_r], w=[t_r])
                return r_, t_r

            for kb in range(NKB):
                c0 = kb * BLK
                xb, t_xb = load_xblock(xrows_static, kb + 1)
                pi_, t_pi, pi_sem = pos_r.next()
                p.dma("sp", pi_[64:96, :], posb[kb * 32:(kb + 1) * 32, :], w=[t_pi], sem=pi_sem)
                ck, t_ck, _ = pA.next()
                ka, t_ka, _ = pA.next()
                kbs, t_kbs, _ = pA.next()
                for kk in range(8):
                    p.op("pe", lambda e: e.matmul(ck[:], lhsT=wkvb[:, kk, :], rhs=xb[:, kk, :], start=(kk == 0), stop=(kk == 7)),
                         r=[t_wkv, t_xb], w=[t_ck])
                for kk in range(8):
                    p.op("pe", lambda e: e.matmul(ka[0:96, :], lhsT=wkrb[:, kk, 0:96], rhs=xb[:, kk, :], start=(kk == 0), stop=(kk == 7)),
                         r=[t_wkr, t_xb], w=[t_ka])
                for kk in range(8):
                    p.op("pe", lambda e: e.matmul(kbs[0:96, :], lhsT=wkrb[:, kk, 96:192], rhs=xb[:, kk, :], start=(kk == 0), stop=(kk == 7)),
                         r=[t_wkr, t_xb], w=[t_kbs])
                sq, t_sq, _ = sq_r.next()
                p.op("act", lambda e: e.activation(out=sq[:], in_=ck[:], func=AF.Square), r=[t_ck], w=[t_sq])
                ss, t_ss, _ = pSS.next()
                p.op("pe", lambda e: e.matmul(ss[:], lhsT=ONES, rhs=sq[:], start=True, stop=True), r=[t_sq, t_c], w=[t_ss])
                rs, t_rs = rstd_of(ss, t_ss, 128)
                p.op("dve", lambda e: e.tensor_tensor(out=ckvn[:, c0:c0 + BLK], in0=ck[:], in1=rs[:], op=ALU.mult),
                     r=[t_ck, t_rs], w=[t_ckvn[kb]])
                cs_, t_cs, _ = cs_r.next()
                sn_, t_sn, _ = sn_r.next()
                rope_tables(p, pi_, t_pi, BLK, INVF, SGN, t_c, tabs, cs_[64:96, :], sn_[64:96, :], t_cs, t_sn)
                t1, t_t1, _ = t1_r.next()
                t2, t_t2, _ = t1_r.next()
                p.op("dve", lambda e: e.tensor_tensor(out=t1[64:96, :], in0=ka[64:96, :], in1=cs_[64:96, :], op=ALU.mult),
                     r=[t_ka, t_cs], w=[t_t1])
                p.op("dve", lambda e: e.tensor_tensor(out=t2[64:96, :], in0=kbs[64:96, :], in1=sn_[64:96, :], op=ALU.mult),
                     r=[t_kbs, t_sn], w=[t_t2])
                p.op("pool", lambda e: e.tensor_tensor(out=Kbuf[64:96, c0:c0 + BLK], in0=t1[64:96, :], in1=t2[64:96, :], op=ALU.add),
                     r=[t_t1, t_t2], w=[t_kr[kb]])

            for qb in range(NQB):
                c0 = qb * BLK
                xb, t_xb = load_xblock(xrows_own, qb)
                pi_, t_pi, pi_sem = pos_r.next()
                p.dma("sp", pi_[64:96, :], opos[qb * 32:(qb + 1) * 32, :], w=[t_pi], sem=pi_sem)
                cqs = []
                ss, t_ss, _ = pSS.next()
                for m in range(2):
                    cq, t_cq, _ = pA.next()
                    for kk in range(8):
                        p.op("pe", lambda e: e.matmul(cq[:], lhsT=wcqb[:, kk, m * 128:(m + 1) * 128], rhs=xb[:, kk, :],
                                                      start=(kk == 0), stop=(kk == 7)), r=[t_wcq, t_xb], w=[t_cq])
                    sq, t_sq, _ = sq_r.next()
                    p.op("act", lambda e: e.activation(out=sq[:], in_=cq[:], func=AF.Square), r=[t_cq], w=[t_sq])
                    p.op("pe", lambda e: e.matmul(ss[:], lhsT=ONES, rhs=sq[:], start=(m == 0), stop=(m == 1)), r=[t_sq, t_c], w=[t_ss])
                    cqs.append((cq, t_cq))
                rs, t_rs = rstd_of(ss, t_ss, 256)
                for m in range(2):
                    cq, t_cq = cqs[m]
                    p.op("dve", lambda e: e.tensor_tensor(out=cqn[:, m, c0:c0 + BLK], in0=cq[:], in1=rs[:], op=ALU.mult),
                         r=[t_cq, t_rs], w=[t_cqn[qb]])
                rope_tables(p, pi_, t_pi, BLK, INVF, SGN, t_c, tabs, cosq[64:96, c0:c0 + BLK], sinq[64:96, c0:c0 + BLK],
                            t_cosq[qb], t_sinq[qb])
        barrier(p)

        with ExitStack() as es3:
            p.es = es3
            Vbuf = p.sb("Vbuf", [128, NK128, 128], BF16)
            t_v = [Tok() for _ in range(NK128 // 8 if NK128 >= 8 else 1)]
            VG = min(8, NK128)
            Q_r = Ring(p, "Q", [96, T], BF16, 2)
            tq_r = Ring(p, "tq", [96, BLK], F32, 4)
            P_r = Ring(p, "P", [128, BLK], BF16, 3)
            osb_r = Ring(p, "osb", [128, BLK], F32, 2)
            rden_r = Ring(p, "rden", [128, BLK], F32, 2)
            pS = Ring(p, "pS", [128, 512], F32, 3, space="ps")
            pO = Ring(p, "pO", [128, 512], F32, 2, space="ps")
            pD = Ring(p, "pD", [128, 512], F32, 1, space="ps")
            pB = Ring(p, "pB", [128, 512], F32, 2, space="ps")
            for h in range(8):
                odd = h % 2
                voff = 64 * odd
                Q, _, _ = Q_r.next()
                t_Q = [Tok() for _ in range(NQB)]
                for qb in range(NQB):
                    c0 = qb * BLK
                    qa, t_qa, _ = pB.next()
                    qs, t_qs, _ = pB.next()
                    for kc in range(2):
                        p.op("pe", lambda e: e.matmul(qa[0:96, :], lhsT=wuqb[:, kc, h * 96:(h + 1) * 96], rhs=cqn[:, kc, c0:c0 + BLK],
                                                      start=(kc == 0), stop=(kc == 1)), r=[t_wuq, t_cqn[qb]], w=[t_qa])
                    for kc in range(2):
                        p.op("pe", lambda e: e.matmul(qs[0:96, :], lhsT=wuqb[:, kc, 768 + h * 96:768 + (h + 1) * 96],
                                                      rhs=cqn[:, kc, c0:c0 + BLK], start=(kc == 0), stop=(kc == 1)),
                             r=[t_wuq, t_cqn[qb]], w=[t_qs])
                    p.op("dve", lambda e: e.tensor_copy(out=Q[0:64, c0:c0 + BLK], in_=qa[0:64, :]), r=[t_qa], w=[t_Q[qb]])
                    t1, t_t1, _ = tq_r.next()
                    t2, t_t2, _ = tq_r.next()
                    p.op("dve", lambda e: e.tensor_tensor(out=t1[64:96, :], in0=qa[64:96, :], in1=cosq[64:96, c0:c0 + BLK], op=ALU.mult),
                         r=[t_qa, t_cosq[qb]], w=[t_t1])
                    p.op("dve", lambda e: e.tensor_tensor(out=t2[64:96, :], in0=qs[64:96, :], in1=sinq[64:96, c0:c0 + BLK], op=ALU.mult),
                         r=[t_qs, t_sinq[qb]], w=[t_t2])
                    p.op("pool", lambda e: e.tensor_tensor(out=Q[64:96, c0:c0 + BLK], in0=t1[64:96, :], in1=t2[64:96, :], op=ALU.add),
                         r=[t_t1, t_t2], w=[t_Q[qb]])
                for kb in range(NKB):
                    c0 = kb * BLK
                    kp, t_kp, _ = pB.next()
                    p.op("pe", lambda e: e.matmul(kp[0:64, :], lhsT=wukvb[:, h * 128:h * 128 + 64], rhs=ckvn[:, c0:c0 + BLK],
                                                  start=True, stop=True), r=[t_wukv, t_ckvn[kb]], w=[t_kp])
                    p.op("dve", lambda e: e.tensor_copy(out=Kbuf[0:64, c0:c0 + BLK], in_=kp[0:64, :]), r=[t_kp], w=[t_kn[kb]])
                for g in range(len(t_v)):
                    vp, t_vp, _ = pB.next()
                    for j in range(VG):
                        k128 = g * VG + j
                        p.op("pe", lambda e: e.matmul(vp[:, j * 64:(j + 1) * 64], lhsT=ckvn[:, k128 * 128:(k128 + 1) * 128],
                                                      rhs=wukvb[:, h * 128 + 64:h * 128 + 128], start=True, stop=True),
                             r=[t_wukv, t_ckvn[k128 // 4]], w=[t_vp])
                    vs = Vbuf[:, g * VG:(g + 1) * VG, :]
                    p.op("pool", lambda e: e.memset(vs[:, :, 64 - voff:128 - voff], 0.0), w=[t_v[g]])
                    p.op("pool", lambda e: e.memset(vs[:, :, 64 - voff:65 - voff], 1.0), w=[t_v[g]])
                    p.op("dve", lambda e: e.tensor_copy(out=vs[:, :, voff:voff + 64], in_=vp[:, 0:VG * 64].rearrange("p (j v) -> p j v", v=64)),
                         r=[t_vp], w=[t_v[g]])
                MV = 128 if odd else 65
                for qb in range(NQB):
                    q0 = qb * BLK
                    O, t_O, _ = pO.next()
                    Sq = {}

                    def issue_S(k128):
                        Sx, t_S, _ = pS.next()
                        kb = k128 // 4
                        p.op("pe", lambda e: e.matmul(Sx[:], lhsT=Kbuf[0:96, k128 * 128:(k128 + 1) * 128], rhs=Q[0:96, q0:q0 + BLK],
                                                      start=True, stop=True), r=[t_kn[kb], t_kr[kb], t_Q[qb]], w=[t_S])
                        Sq[k128] = (Sx, t_S)
                    for k128 in range(min(2, NK128)):
                        issue_S(k128)
                    for k128 in range(NK128):
                        Sx, t_S = Sq.pop(k128)
                        Pt, t_P, _ = P_r.next()
                        p.op("act", lambda e: e.activation(out=Pt[:], in_=Sx[:], func=AF.Exp, scale=ATT_SCALE), r=[t_S], w=[t_P])
                        if k128 + 2 < NK128:
                            issue_S(k128 + 2)
                        p.op("pe", lambda e: e.matmul(O[0:MV, :], lhsT=Vbuf[:, k128, 0:MV], rhs=Pt[:], start=(k128 == 0),
                                                      stop=(k128 == NK128 - 1)), r=[t_v[k128 // VG], t_P], w=[t_O])
                    osb, t_osb, _ = osb_r.next()
                    p.op("dve", lambda e: e.tensor_copy(out=osb[0:MV, :], in_=O[0:MV, :]), r=[t_O], w=[t_osb])
                    Dn, t_D, _ = pD.next()
                    if odd:
                        p.op("pe", lambda e: e.matmul(Dn[:], lhsT=SEL_O, rhs=osb[:], start=True, stop=True), r=[t_osb, t_c], w=[t_D])
                    else:
                        p.op("pe", lambda e: e.matmul(Dn[0:64, :], lhsT=SEL_E[0:65, 0:64], rhs=osb[0:65, :], start=True, stop=True),
                             r=[t_osb, t_c], w=[t_D])
                    rd, t_rd, _ = rden_r.next()
                    PR = slice(voff, voff + 64)
                    p.op("dve", lambda e: e.reciprocal(out=rd[PR, :], in_=Dn[PR, :]), r=[t_D], w=[t_rd])
                    p.op("dve", lambda e: e.tensor_tensor(out=ycT[PR, h // 2, q0:q0 + BLK], in0=osb[PR, :], in1=rd[PR, :], op=ALU.mult),
                         r=[t_osb, t_rd], w=[t_yc[h][qb]])
        barrier(p)
        esA.close()

        with ExitStack() as es4:
            p.es = es4
            wst = Ring(p, "wst4", [128, 1024], F32, 2, dma=True)
            wg3b = p.sb("wg3b", [128, 8, 1536], BF16)
            woutb = p.sb("woutb", [128, 8, D], BF16)
            wpoolb = p.sb("wpoolb", [128, 512], BF16)
            lng_s = p.sb("lng_s", [128, D], F32)
            lnb_s = p.sb("lnb_s", [128, D], F32)
            t_wg3, t_wout, t_wpool, t_ln = Tok(), Tok(), Tok(), Tok()
            dl = p.dma_sem()
            p.dma("sp", lng_s[:], lng[:, :], w=[t_ln], sem=dl)
            p.dma("sp", lnb_s[:], lnb[:, :], w=[t_ln], sem=dl)
            load_cast_weight(p, w_g3, wg3b, wst, 8, 1536, cw=768, tok=t_wg3)
            load_cast_weight(p, wout, woutb, wst, 8, D, tok=t_wout)
            st, stok, ssem = wst.next()
            p.dma("sp", st[:, 0:512], w_pool[:, :], w=[stok], sem=ssem)
            p.op("dve", lambda e: e.tensor_copy(out=wpoolb[:], in_=st[:, 0:512]), r=[stok], w=[t_wpool])
            xst = Ring(p, "xst4", [128, BLK + 16], F32, 3, dma=True)
            xb_r = Ring(p, "xb4", [128, 8, BLK + 16], BF16, 2)
            xtk = Ring(p, "xtk", [128, D], F32, 2, dma=True)
            cat = p.sb("cat", [128, 8, BLK], BF16)
            t_cat = [Tok() for _ in range(8)]
            tmp = Ring(p, "tmp4", [128, BLK + 16], F32, 10)
            hl_r = Ring(p, "hl4", [128, 16], F32, 2)
            pl_r = Ring(p, "pl4", [128, BLK], BF16, 2)
            rr = Ring(p, "rr", [128, D], F32, 2)
            r2 = Ring(p, "r2", [128, D], F32, 2, dma=True)
            junk = Ring(p, "junk", [128, D], BF16, 1)
            st_r = Ring(p, "stat", [128, 8], F32, 4)
            pp = Ring(p, "pp4", [128, 512], F32, 5, space="ps")
            ph = Ring(p, "ph4", [128, 512], F32, 1, space="ps")
            po = Ring(p, "po4", [128, D], F32, 1, space="ps")
            WIN = (2, 4, 8, 16)
            for qb in range(NQB):
                c0 = qb * BLK
                xb, t_xb, _ = xb_r.next()
                for kk in range(8):
                    st, stok, ssem = xst.next()
                    rws = slice(qb * 1024 + kk * 128, qb * 1024 + (kk + 1) * 128)
                    p.dma("sp", st[:, 8:BLK + 8], oxT[rws, :], w=[stok], sem=ssem)
                    p.dma("sp", st[:, 0:8], oHL[rws, :], w=[stok], sem=ssem, nowait=True)
                    p.dma("sp", st[:, BLK + 8:BLK + 16], oHR[rws, :], w=[stok], sem=ssem, nowait=True)
                    p.op("pool", lambda e: e.tensor_copy(out=xb[:, kk, :], in_=st[:]), r=[stok], w=[t_xb])

                def proj(col0, lo, n, dst, t_dst):
                    for kk in range(8):
                        p.op("pe", lambda e: e.matmul(dst, lhsT=wg3b[:, kk, col0:col0 + 128], rhs=xb[:, kk, lo:lo + n],
                                                      start=(kk == 0), stop=(kk == 7)), r=[t_wg3, t_xb], w=[t_dst])
                for j in range(4):
                    gc, t_gc, _ = pp.next()
                    proj(j * 128, 8, BLK, gc[:], t_gc)
                    sg, t_sg, _ = tmp.next()
                    p.op("act", lambda e: e.activation(out=sg[:, 0:BLK], in_=gc[:], func=AF.Silu), r=[t_gc], w=[t_sg])
                    p.op("dve", lambda e: e.tensor_tensor(out=cat[:, j, :], in0=sg[:, 0:BLK], in1=ycT[:, j, c0:c0 + BLK], op=ALU.mult),
                         r=[t_sg, t_yc[2 * j][qb], t_yc[2 * j + 1][qb]], w=[t_cat[j]])
                for gi in range(4):
                    w = WIN[gi]
                    um, t_um, _ = pp.next()
                    proj(512 + gi * 128, 8, BLK, um[:], t_um)
                    hl, t_hl, _ = ph.next()
                    proj(512 + gi * 128, 0, 8, hl[:, 0:8], t_hl)
                    proj(512 + gi * 128, BLK + 8, 8, hl[:, 8:16], t_hl)
                    u, t_u, _ = tmp.next()
                    p.op("act", lambda e: e.copy(out=u[:, 8:BLK + 8], in_=um[:]), r=[t_um], w=[t_u])
                    p.op("act", lambda e: e.copy(out=u[:, 0:8], in_=hl[:, 0:8]), r=[t_hl], w=[t_u])
                    p.op("act", lambda e: e.copy(out=u[:, BLK + 8:BLK + 16], in_=hl[:, 8:16]), r=[t_hl], w=[t_u])
                    cur, t_cur, n, width = u, t_u, BLK + 16, 1
                    while width < w:
                        nxt, t_nxt, _ = tmp.next()
                        n2 = n - width
                        p.op("dve", lambda e: e.tensor_tensor(out=nxt[:, 0:n2], in0=cur[:, 0:n2], in1=cur[:, width:width + n2], op=ALU.add),
                             r=[t_cur], w=[t_nxt])
                        cur, t_cur, n, width = nxt, t_nxt, n2, width * 2
                    s0 = 8 - w // 2
                    pm, t_pm, _ = tmp.next()
                    p.op("dve", lambda e: e.tensor_scalar(out=pm[:, 0:BLK], in0=cur[:, s0:s0 + BLK], scalar1=1.0 / w, scalar2=None, op0=ALU.mult),
                         r=[t_cur], w=[t_pm])
                    if qb == 0:
                        p.op("dve", lambda e: e.tensor_tensor(out=pm[:, 0:8], in0=pm[:, 0:8], in1=CORR[:, gi * 16:gi * 16 + 8], op=ALU.mult),
                             r=[t_c], w=[t_pm])
                    if qb == NQB - 1:
                        p.op("dve", lambda e: e.tensor_tensor(out=pm[:, BLK - 8:BLK], in0=pm[:, BLK - 8:BLK],
                                                              in1=CORR[:, gi * 16 + 8:gi * 16 + 16], op=ALU.mult), r=[t_c], w=[t_pm])
                    pl, t_pl, _ = pl_r.next()
                    p.op("dve", lambda e: e.tensor_tensor(out=pl[:], in0=pm[:, 0:BLK], in1=u[:, 8:BLK + 8], op=ALU.subtract),
                         r=[t_pm, t_u], w=[t_pl])
                    yd, t_yd, _ = pp.next()
                    p.op("pe", lambda e: e.matmul(yd[:], lhsT=wpoolb[:, gi * 128:(gi + 1) * 128], rhs=pl[:], start=True, stop=True),
                         r=[t_wpool, t_pl], w=[t_yd])
                    gd, t_gd, _ = pp.next()
                    proj(1024 + gi * 128, 8, BLK, gd[:], t_gd)
                    sg, t_sg, _ = tmp.next()
                    p.op("act", lambda e: e.activation(out=sg[:, 0:BLK], in_=gd[:], func=AF.Silu), r=[t_gd], w=[t_sg])
                    p.op("dve", lambda e: e.scalar_tensor_tensor(out=cat[:, 4 + gi, :], in0=yd[:], scalar=PSC[:, gi:gi + 1], in1=sg[:, 0:BLK],
                                                                 op0=ALU.mult, op1=ALU.mult), r=[t_yd, t_sg, t_c], w=[t_cat[4 + gi]])
                for tt in range(BLK // 128):
                    xk, t_xk, xk_sem = xtk.next()
                    p.dma("sp", xk[:], xtok[c0 + tt * 128:c0 + (tt + 1) * 128, :], w=[t_xk], sem=xk_sem)
                    o, t_o, _ = po.next()
                    for half in range(2):
                        for kc in range(8):
                            p.op("pe", lambda e: e.matmul(o[:, half * 512:(half + 1) * 512], lhsT=cat[:, kc, tt * 128:(tt + 1) * 128],
                                                          rhs=woutb[:, kc, half * 512:(half + 1) * 512], start=(kc == 0), stop=(kc == 7)),
                                 r=[t_cat[kc], t_wout], w=[t_o])
                    layer_norm_tail(p, o, t_o, xk, t_xk, rr, r2, junk, st_r, lng_s, lnb_s, t_ln,
                                    out[c0 + tt * 128:c0 + (tt + 1) * 128, :], gb_eng="dve")
        barrier(p)


def prep_l1(x1, positions, od_w_in, od_q_norm_g, od_w_uq, od_kv_norm_g, od_w_ukv, od_pool_w, od_pool_scale, od_w_out,
            od_ln_g, od_ln_b, S=SEQ):
    T = S // 4
    w_in = od_w_in[0]
    w_cq = np.ascontiguousarray(w_in[:, 0:256])
    w_kv = np.ascontiguousarray(w_in[:, 256:384])
    kr = w_in[:, 384:416]
    krs = np.concatenate([kr[:, 16:32], kr[:, 0:16]], axis=1)
    z64 = np.zeros((D, 64), np.float32)
    w_kr = np.ascontiguousarray(np.concatenate([z64, kr, z64, krs], axis=1))
    w_g3 = np.ascontiguousarray(w_in[:, 416:1952])
    uq = od_w_uq[0].reshape(256, 8, 96)
    uqs = np.zeros_like(uq)
    uqs[:, :, 64:80] = uq[:, :, 80:96]
    uqs[:, :, 80:96] = uq[:, :, 64:80]
    w_uq = np.ascontiguousarray(np.concatenate([uq.reshape(256, 768), uqs.reshape(256, 768)], axis=1))
    w_ukv = np.ascontiguousarray(od_w_ukv[0])
    w_pool = np.ascontiguousarray(od_pool_w[0].transpose(1, 0, 2).reshape(128, 512))
    wout = np.ascontiguousarray(od_w_out[0])
    half = 16
    inv_freq = (np.float32(10000.0) ** (-np.arange(half, dtype=np.float32) / np.float32(half))).astype(np.float32)
    sel = np.zeros((128, 384), np.float32)
    sel[64, 0:64] = 1.0
    sel[0, 128 + 64:128 + 128] = 1.0
    sel[:, 256:384] = 1.0
    lng = np.ascontiguousarray(np.broadcast_to(od_ln_g[0][None, :], (128, D)))
    lnb = np.ascontiguousarray(np.broadcast_to(od_ln_b[0][None, :], (128, D)))
    maps = []
    xTbs = [np.ascontiguousarray(x1[b, :S, :].T) for b in range(2)] if x1 is not None else [None, None]
    posbs = [np.ascontiguousarray(np.broadcast_to(positions[b, :S].reshape(S // 512, 1, 512), (S // 512, 32, 512))
                                  .reshape((S // 512) * 32, 512)).astype(np.int32) for b in range(2)]
    for c in range(NCORES):
        b, s0 = c // 4, (c % 4) * T
        xe = None
        if x1 is not None:
            xe = np.zeros((D, T + 16), np.float32)
            lo, hi = max(0, s0 - 8), min(S, s0 + T + 8)
            xe[:, lo - (s0 - 8):hi - (s0 - 8)] = x1[b, lo:hi, :].T
        smc = np.zeros((128, 80), np.float32)
        smc[:, 0:2] = od_q_norm_g[0].reshape(2, 128).T
        smc[:, 2] = od_kv_norm_g[0]
        smc[64:80, 3] = inv_freq
        smc[80:96, 3] = inv_freq
        smc[64:80, 4] = -1.0
        smc[80:96, 4] = 1.0
        smc[:, 5:9] = od_pool_scale[0].reshape(4, 128).T
        for gi, w in enumerate((2, 4, 8, 16)):
            for j in range(8):
                for side, t in ((0, s0 + j), (1, s0 + T - 8 + j)):
                    lo_ = min(max(t - w // 2, 0), S)
                    hi_ = min(max(t + w - w // 2, 0), S)
                    smc[:, 16 + gi * 16 + side * 8 + j] = np.float32(w) / np.float32(hi_ - lo_)
        maps.append({"xTb": xTbs[b], "xTo": xe, "xtok": (np.ascontiguousarray(x1[b, s0:s0 + T, :]) if x1 is not None else None),
                     "posb": posbs[b],
                     "w_cq": w_cq, "w_kv": w_kv, "w_kr": w_kr, "w_g3": w_g3, "w_uq": w_uq, "w_ukv": w_ukv,
                     "w_pool": w_pool, "wout": wout, "sm_c": smc, "sel": sel, "lng": lng, "lnb": lnb})
    return maps


def build_fused(S=SEQ):
    T = S // 4
    nc = bass.Bass("TRN2", target_bir_lowering=False)

    def inp(name, shape, dt=F32):
        return nc.dram_tensor(name, list(shape), dt, kind="ExternalInput").ap()
    A = {}
    A["xT"] = inp("xT", [D, S + 3])
    A["xT1"] = A["xT"][:, 1:S + 3]
    A["xtok"] = inp("xtok", [S, D])
    A["wg"] = [inp("wg%d" % g, [D, 520]) for g in range(4)]
    A["cvw"] = [inp("cvw%d" % g, [128, 16])[:, :] for g in range(4)]
    A["cvb"] = [inp("cvb%d" % g, [128, 4])[:, :] for g in range(4)]
    A["hp"] = [inp("hp%d" % g, [128, 24])[:, :] for g in range(4)]
    A["cst"] = inp("cst", [128, 512])
    A["msk"] = inp("msk", [128, 1024])
    A["w1"] = inp("w1", [D, 5120])
    A["wout"] = inp("wout", [2048, D])
    A["normg"] = inp("normg", [128, 8])
    A["scw"] = inp("scw", [128, 24])
    A["lng"] = inp("lng", [128, D])
    A["lnb"] = inp("lnb", [128, D])
    A["posb"] = inp("posb", [(S // 512) * 32, 512], I32)
    A["w_cq"] = inp("w_cq", [D, 256])
    A["w_kv"] = inp("w_kv", [D, 128])
    A["w_kr"] = inp("w_kr", [D, 192])
    A["w_g3"] = inp("w_g3", [D, 1536])
    A["w_uq"] = inp("w_uq", [256, 1536])
    A["w_ukv"] = inp("w_ukv", [128, 1024])
    A["w_pool"] = inp("w_pool", [128, 512])
    A["wout_od"] = inp("wout_od", [D, D])
    A["sm_c"] = inp("sm_c", [128, 80])
    A["sel"] = inp("sel", [128, 384])
    A["lng_od"] = inp("lng_od", [128, D])
    A["lnb_od"] = inp("lnb_od", [128, D])
    off = inp("off", [1, 4], I32)
    A["out"] = nc.dram_tensor("out", [T, D], F32, kind="ExternalOutput").ap()
    A["yaT"] = nc.dram_tensor("yaT_s", [D, S], F32).ap()
    A["x1"] = nc.dram_tensor("x1_s", [S, D], F32).ap()
    A["x1T"] = nc.dram_tensor("x1T_s", [(S // 512 + 2) * 1024, 512], F32).ap()
    A["x1HL"] = nc.dram_tensor("x1HL_s", [(S // 512 + 1) * 1024, 8], F32).ap()
    A["x1HR"] = nc.dram_tensor("x1HR_s", [(S // 512 + 1) * 1024, 8], F32).ap()
    A["own_x1T"] = nc.dram_tensor("own_x1T_s", [(T // 512) * 1024, 512], F32).ap()
    A["own_x1"] = nc.dram_tensor("own_x1_s", [T, D], F32).ap()
    A["own_HL"] = nc.dram_tensor("own_HL_s", [(T // 512) * 1024, 8], F32).ap()
    A["own_HR"] = nc.dram_tensor("own_HR_s", [(T // 512) * 1024, 8], F32).ap()
    A["own_pos"] = nc.dram_tensor("own_pos_s", [(T // 512) * 32, 512], I32).ap()

    with ExitStack() as es:
        p = Prog(nc, es)
        regs = [es.enter_context(nc.sync.register("offr%d" % i)) for i in range(3)]
        for i in range(3):
            nc.sync.reg_load(regs[i], off[0:1, i:i + 1])
        NB, NQB = S // 512, T // 512
        b0v = nc.sync.snap(regs[0], min_val=0, max_val=NB - NQB)
        u0v = nc.sync.snap(regs[1], min_val=0, max_val=(NB - NQB) * 64)
        t0v = nc.sync.snap(regs[2], min_val=0, max_val=(S - T) // 8)
        p.prefix = "a_"
        emit_l0a(nc, p, S, A)
        p.prefix = "b_"
        emit_l0b(nc, p, S, A)
        csem = p.dma_sem()
        v = lambda ap, b: ap.rearrange("(a b) t -> a (b t)", b=b)
        p.dma("sp", v(A["own_x1T"], 16), v(A["x1T"], 16)[bass.ds(u0v + 64, NQB * 64), :], sem=csem)
        p.dma("sp", v(A["own_x1"], 8), v(A["x1"], 8)[bass.ds(t0v, T // 8), :], sem=csem)
        p.dma("sp", v(A["own_HL"], 1024), v(A["x1HL"], 1024)[bass.ds(b0v, NQB), :], sem=csem)
        p.dma("sp", v(A["own_HR"], 1024), v(A["x1HR"], 1024)[bass.ds(b0v + 1, NQB), :], sem=csem)
        p.dma("sp", v(A["own_pos"], 32), v(A["posb"], 32)[bass.ds(b0v, NQB), :], sem=csem)
        barrier(p)
        p.prefix = "c_"
        emit_l1(nc, p, S, A)
        p.es = es
        p.finish()
    return nc


def prep_fused(inputs, S=SEQ):
    f = lambda a: np.asarray(a, dtype=np.float32)
    x = f(inputs["x"])[:, :S]
    positions = np.asarray(inputs["positions"], dtype=np.int32)[:, :S]
    T = S // 4
    l0a = prep_l0a(x, f(inputs["ev_w_in"]), f(inputs["ev_conv_w"]), f(inputs["ev_conv_b"]), f(inputs["ev_a_log"]),
                   f(inputs["ev_dt_bias"]), f(inputs["ev_d_skip"]), S=S)
    w_in = f(inputs["ev_w_in"])[0]
    w1 = np.ascontiguousarray(np.concatenate([w_in[:, 0:1024], w_in[:, 3104:7200]], axis=1))
    wout = np.ascontiguousarray(f(inputs["ev_w_out"])[0])
    normg = np.ascontiguousarray(f(inputs["ev_norm_g"])[0].reshape(8, 128).T)
    scw = np.ascontiguousarray(f(inputs["ev_sc_conv_w"])[0].reshape(3, 8, 128).transpose(2, 1, 0).reshape(128, 24))
    lng = np.ascontiguousarray(np.broadcast_to(f(inputs["ev_ln_g"])[0][None, :], (128, D)))
    lnb = np.ascontiguousarray(np.broadcast_to(f(inputs["ev_ln_b"])[0][None, :], (128, D)))
    dummy_x1 = np.zeros((2, 16, D), np.float32)
    l1 = prep_l1(None, positions, f(inputs["od_w_in"]), f(inputs["od_q_norm_g"]), f(inputs["od_w_uq"]), f(inputs["od_kv_norm_g"]),
                 f(inputs["od_w_ukv"]), f(inputs["od_pool_w"]), f(inputs["od_pool_scale"]), f(inputs["od_w_out"]),
                 f(inputs["od_ln_g"]), f(inputs["od_ln_b"]), S=S)
    xtoks = [np.ascontiguousarray(x[b]) for b in range(2)]
    maps = []
    for c in range(NCORES):
        b, q = c // 4, c % 4
        m = {"xT": l0a[4 * b]["xT"], "xtok": xtoks[b], "cst": l0a[0]["cst"], "msk": l0a[0]["msk"],
             "w1": w1, "wout": wout, "normg": normg, "scw": scw, "lng": lng, "lnb": lnb,
             "off": np.array([[q * T // 512, (q * T // 512) * 64, q * T // 8, 0]], np.int32)}
        for g in range(4):
            src = l0a[4 * b + g]
            m["wg%d" % g] = src["wg"]
            m["cvw%d" % g] = src["cvw"]
            m["cvb%d" % g] = src["cvb"]
            m["hp%d" % g] = src["hp"]
        lm = l1[c]
        for k in ("posb", "w_cq", "w_kv", "w_kr", "w_g3", "w_uq", "w_ukv", "w_pool", "sm_c", "sel"):
            m[k] = lm[k]
        m["wout_od"] = lm["wout"]
        m["lng_od"] = lm["lng"]
        m["lnb_od"] = lm["lnb"]
        maps.append(m)
    return maps


def kernel(**inputs):
    T = SEQ // 4
    maps = prep_fused(inputs)
    res = run_bass_kernel_spmd(build_fused(), maps, core_ids=list(range(NCORES)))
    out = np.empty((2, SEQ, D), np.float32)
    for c in range(NCORES):
        out[c // 4, (c % 4) * T:(c % 4 + 1) * T, :] = res.results[c]["out"]
    return out
```
